# Optimizing a Trainium2 kernel written in Bass

```python
import jax, jax.numpy as jnp
from jax import lax
import numpy as np

D_MODEL = 1024
BATCH = 16
SEQ = 256
DEPTH = 2
DEC_BATCH = 2
DEC_SEQ = 4096
PAST_LEN = 256

GRID_W = 64
POS_BASE = 10000.0
MIX_W = 256
N_GROUPS = 4
GROUP_W = MIX_W // N_GROUPS
POOL_WINDOWS = (2, 4, 8, 16)
RWKV_HEADS = 4
RWKV_HEAD = MIX_W // RWKV_HEADS
RWKV_W_RANK = 64
RWKV_A_RANK = 32
RWKV_G_RANK = 64
RWKV_DECAY_SCALE = 0.606531
RWKV_GN_EPS = 64e-5
GLA_HEADS = 4
GLA_DK = 32
GLA_DV = MIX_W // GLA_HEADS
GLA_RANK = 16
GLA_CHUNK = 64
GLA_GATE_NORM = 16.0
N_BRANCH = 4
D_FF = 4 * D_MODEL
EPS = 1e-6
IN_SIZES = (MIX_W, MIX_W, 3 * MIX_W, 2 * RWKV_W_RANK, 2 * RWKV_A_RANK, RWKV_G_RANK,
            GLA_HEADS * GLA_DK, GLA_HEADS * GLA_DK, GLA_HEADS * GLA_DV, GLA_HEADS * GLA_DV,
            2 * GLA_RANK, N_BRANCH * D_MODEL)
P_IN = sum(IN_SIZES)

kernel_name = 'hybrid_pool_fourier_rwkv7_gla_diffusion_step'

F32 = jnp.float32


def _rmsnorm(x, g):
    xf = x.astype(F32)
    return xf * lax.rsqrt(jnp.mean(xf * xf, axis=-1, keepdims=True) + EPS) * g.astype(F32)


def _split_cols(u, sizes):
    out, start = [], 0
    for s in sizes:
        out.append(u[..., start:start + s])
        start += s
    return out


def _pos_embed_2d(n_tok, d):
    rows = n_tok // GRID_W
    rr, cc = jnp.meshgrid(jnp.arange(rows, dtype=F32), jnp.arange(GRID_W, dtype=F32), indexing='ij')
    rr = rr.reshape(-1)
    cc = cc.reshape(-1)
    quarter = d // 4
    omega = 1.0 / (POS_BASE ** (jnp.arange(quarter, dtype=F32) / quarter))
    ar = rr[:, None] * omega
    ac = cc[:, None] * omega
    return jnp.concatenate([jnp.sin(ar), jnp.cos(ar), jnp.sin(ac), jnp.cos(ac)], axis=-1)


def _pool_mixer(z, w, scale):
    b_, L, _ = z.shape
    csum = jnp.concatenate([jnp.zeros((b_, 1, MIX_W), F32), jnp.cumsum(z, axis=1)], axis=1)
    t = jnp.arange(L)
    parts = []
    for gi, win in enumerate(POOL_WINDOWS):
        sl = slice(gi * GROUP_W, (gi + 1) * GROUP_W)
        lo = jnp.clip(t - win // 2, 0, L - 1)
        hi = jnp.clip(t + (win - win // 2) - 1, 0, L - 1)
        s = jnp.take(csum[..., sl], hi + 1, axis=1) - jnp.take(csum[..., sl], lo, axis=1)
        cnt = (hi - lo + 1).astype(F32)[None, :, None]
        parts.append(s / cnt - z[..., sl])
    pooled = jnp.stack(parts, axis=2)
    y = jnp.einsum('blgc,gcd->blgd', pooled, w)
    return y.reshape(b_, L, MIX_W) * scale


def _fourier_mixer(z):
    b_, L, _ = z.shape
    zg = z.astype(F32).reshape(b_, L, N_GROUPS, GROUP_W)
    f = jnp.fft.fft2(zg, axes=(1, 3), norm='ortho')
    return jnp.real(f).astype(F32).reshape(b_, L, MIX_W)


def _token_shift(z):
    zp = jnp.pad(z, ((0, 0), (1, 1), (0, 0)))
    return 0.5 * (zp[:, :-2] + zp[:, 2:])


def _rwkv_scan(r, k, v, w, kk, a, s0, reverse):
    def step(S, inp):
        r_t, k_t, v_t, w_t, kk_t, a_t = inp
        sa = jnp.einsum('bhvk,bhk->bhv', S, -kk_t)
        S = S * w_t[:, :, None, :] + sa[..., None] * (kk_t * a_t)[:, :, None, :] + v_t[..., None] * k_t[:, :, None, :]
        return S, jnp.einsum('bhvk,bhk->bhv', S, r_t)
    xs = tuple(jnp.swapaxes(t, 0, 1) for t in (r, k, v, w, kk, a))
    s_fin, o = lax.scan(step, s0, xs, reverse=reverse)
    return jnp.swapaxes(o, 0, 1), s_fin


def _gla_chunked(q, k, v, log_a, s0):
    b_, L, H, _ = q.shape
    dv = v.shape[-1]
    n = L // GLA_CHUNK

    def chunks(t):
        return t.reshape(b_, n, GLA_CHUNK, H, t.shape[-1]).transpose(1, 0, 3, 2, 4)

    causal = jnp.tril(jnp.ones((GLA_CHUNK, GLA_CHUNK), dtype=bool))[None, None, :, :, None]

    def step(S, inp):
        qc, kc, vc, gc = inp
        bcum = jnp.cumsum(gc, axis=2)
        diff = bcum[:, :, :, None, :] - bcum[:, :, None, :, :]
        decay = jnp.exp(jnp.where(causal, diff, -jnp.inf))
        att = jnp.einsum('bhtd,bhsd,bhtsd->bhts', qc, kc, decay)
        o = jnp.einsum('bhts,bhse->bhte', att, vc) + jnp.einsum('bhtd,bhde->bhte', qc * jnp.exp(bcum), S)
        b_last = bcum[:, :, -1:, :]
        S = jnp.exp(b_last[:, :, 0, :])[..., None] * S + jnp.einsum('bhsd,bhse->bhde', kc * jnp.exp(b_last - bcum), vc)
        return S, o

    s_fin, o = lax.scan(step, s0, (chunks(q), chunks(k), chunks(v), chunks(log_a)))
    return o.transpose(1, 0, 3, 2, 4).reshape(b_, L, H, dv), s_fin


def _head_groupnorm(o):
    mu = jnp.mean(o, axis=-1, keepdims=True)
    var = jnp.mean(jnp.square(o - mu), axis=-1, keepdims=True)
    return (o - mu) * lax.rsqrt(var + RWKV_GN_EPS)


def _mixer_block(h, p, s_rwkv0, s_gla0):
    b_, L, _ = h.shape
    u = jnp.einsum('bld,dp->blp', h, p['w_in']).astype(F32)
    (z_pool, z_four, z_rkv, c_w, c_a, c_g, gq, gk, gv, g_out, c_al, m_log) = _split_cols(u, IN_SIZES)

    y_a = _pool_mixer(z_pool, p['pool_w'], p['pool_scale'])

    y_b = _fourier_mixer(z_four)

    hn = (b_, L, RWKV_HEADS, RWKV_HEAD)
    z_rkv = z_rkv + p['rwkv_mu'] * (_token_shift(z_rkv) - z_rkv)
    r = z_rkv[..., :MIX_W].reshape(hn)
    k = z_rkv[..., MIX_W:2 * MIX_W].reshape(hn)
    v = z_rkv[..., 2 * MIX_W:].reshape(hn)
    kk = k * p['rwkv_kk'].reshape(RWKV_HEADS, RWKV_HEAD)
    kk = kk * lax.rsqrt(jnp.sum(kk * kk, axis=-1, keepdims=True) + EPS)
    w_log = p['rwkv_w0'] + jnp.einsum('bldr,drc->bldc', jnp.tanh(c_w.reshape(b_, L, 2, RWKV_W_RANK)), p['rwkv_bw'])
    decay = jnp.exp(-RWKV_DECAY_SCALE * jax.nn.sigmoid(w_log)).reshape(b_, L, 2, RWKV_HEADS, RWKV_HEAD)
    iclr = jax.nn.sigmoid(p['rwkv_a0'] + jnp.einsum('bldr,drc->bldc', c_a.reshape(b_, L, 2, RWKV_A_RANK), p['rwkv_ba']))
    iclr = iclr.reshape(b_, L, 2, RWKV_HEADS, RWKV_HEAD)
    k_dir = k[:, :, None] * (1.0 + (iclr - 1.0) * p['rwkv_ka'].reshape(RWKV_HEADS, RWKV_HEAD))
    o_f, s_rf = _rwkv_scan(r, k_dir[:, :, 0], v, decay[:, :, 0], kk, iclr[:, :, 0], s_rwkv0[:, 0], False)
    o_bk, s_rb = _rwkv_scan(r, k_dir[:, :, 1], v, decay[:, :, 1], kk, iclr[:, :, 1], s_rwkv0[:, 1], True)
    o_c = _head_groupnorm(o_f + o_bk) * p['rwkv_gn'].reshape(RWKV_HEADS, RWKV_HEAD)
    bonus = jnp.sum(jnp.sum(r[:, :, None] * k_dir * p['rwkv_rk'], axis=-1, keepdims=True), axis=2)
    o_c = o_c + bonus * v
    gate_c = jnp.einsum('blr,rc->blc', jax.nn.sigmoid(c_g), p['rwkv_bg'])
    y_c = o_c.reshape(b_, L, MIX_W) * gate_c

    q = gq.reshape(b_, L, GLA_HEADS, GLA_DK) * (GLA_DK ** -0.5)
    kg = gk.reshape(b_, L, GLA_HEADS, GLA_DK)
    vg = gv.reshape(b_, L, GLA_HEADS, GLA_DV)
    log_a = jax.nn.log_sigmoid(jnp.einsum('bldr,drc->bldc', c_al.reshape(b_, L, 2, GLA_RANK), p['gla_ab'])
                               + p['gla_abias']) / GLA_GATE_NORM
    log_a = log_a.reshape(b_, L, 2, GLA_HEADS, GLA_DK)
    og_f, s_gf = _gla_chunked(q, kg, vg, log_a[:, :, 0], s_gla0[:, 0])
    og_rev, s_gb = _gla_chunked(jnp.flip(q, 1), jnp.flip(kg, 1), jnp.flip(vg, 1), jnp.flip(log_a[:, :, 1], 1), s_gla0[:, 1])
    og = og_f + jnp.flip(og_rev, 1)
    og = og * lax.rsqrt(jnp.mean(og * og, axis=-1, keepdims=True) + EPS) * p['gla_norm'].reshape(GLA_HEADS, GLA_DV)
    y_d = og.reshape(b_, L, MIX_W) * jax.nn.silu(g_out)

    ys = jnp.stack([y_a, y_b, y_c, y_d], axis=2)
    branch = jnp.einsum('blic,icd->blid', ys, p['w_branch'])
    gates = jax.nn.sigmoid(m_log.reshape(b_, L, N_BRANCH, D_MODEL))
    merged = jnp.sum(gates * branch, axis=2)
    out = jnp.einsum('bld,de->ble', merged, p['w_out'])
    return out, jnp.stack([s_rf, s_rb], axis=1), jnp.stack([s_gf, s_gb], axis=1)


def _layer(x, cond, p, s_rwkv0, s_gla0):
    mod = jnp.einsum('bd,de->be', jax.nn.silu(cond.astype(F32)), p['ada_w']) + p['ada_b']
    sh1, sc1, g1, sh2, sc2, g2 = jnp.split(mod[:, None, :].astype(F32), 6, axis=-1)
    h = _rmsnorm(x, p['norm1_g']) * (1.0 + sc1) + sh1
    mix, s_r, s_g = _mixer_block(h, p, s_rwkv0, s_gla0)
    xf = x.astype(F32) + g1 * mix
    h = _rmsnorm(xf, p['norm2_g']) * (1.0 + sc2) + sh2
    ff = jnp.einsum('blf,fd->bld', jnp.square(jax.nn.relu(jnp.einsum('bld,df->blf', h, p['mlp_w1']))), p['mlp_w2'])
    xf = xf + g2 * ff
    return xf.astype(x.dtype), s_r, s_g


def setup_inputs(seed: int = 0) -> dict:
    key = jax.random.key(seed)
    ks = jax.random.split(key, 40)

    def nrm(i, shape, scale):
        return jax.random.normal(ks[i], shape, F32) * scale

    def gain(i, shape):
        return 1.0 + 0.1 * jax.random.normal(ks[i], shape, F32)

    return {
        'x_prompt': nrm(0, (BATCH, SEQ, D_MODEL), 1.0),
        'x_sample': nrm(1, (DEC_BATCH, DEC_SEQ, D_MODEL), 1.0),
        'state_rwkv': nrm(2, (DEC_BATCH, DEPTH, 2, RWKV_HEADS, RWKV_HEAD, RWKV_HEAD), 0.5),
        'state_gla': nrm(3, (DEC_BATCH, DEPTH, 2, GLA_HEADS, GLA_DK, GLA_DV), 1.0),
        'c': nrm(4, (DEC_BATCH, D_MODEL), 1.0),
        'c_ctx': nrm(5, (D_MODEL,), 1.0),
        'ada_w': nrm(6, (DEPTH, D_MODEL, 6 * D_MODEL), 0.5 * D_MODEL ** -0.5),
        'ada_b': nrm(7, (DEPTH, 6 * D_MODEL), 0.02),
        'norm1_g': gain(8, (DEPTH, D_MODEL)),
        'norm2_g': gain(9, (DEPTH, D_MODEL)),
        'w_in': nrm(10, (DEPTH, D_MODEL, P_IN), D_MODEL ** -0.5),
        'pool_w': nrm(11, (DEPTH, N_GROUPS, GROUP_W, GROUP_W), GROUP_W ** -0.5),
        'pool_scale': gain(12, (DEPTH, MIX_W)),
        'rwkv_mu': jax.random.uniform(ks[13], (DEPTH, 3 * MIX_W), F32),
        'rwkv_w0': nrm(14, (DEPTH, 2, MIX_W), 0.5),
        'rwkv_bw': nrm(15, (DEPTH, 2, RWKV_W_RANK, MIX_W), 0.1 * RWKV_W_RANK ** -0.5),
        'rwkv_a0': nrm(16, (DEPTH, 2, MIX_W), 0.5),
        'rwkv_ba': nrm(17, (DEPTH, 2, RWKV_A_RANK, MIX_W), 0.5 * RWKV_A_RANK ** -0.5),
        'rwkv_kk': gain(18, (DEPTH, MIX_W)),
        'rwkv_ka': gain(19, (DEPTH, MIX_W)),
        'rwkv_bg': nrm(20, (DEPTH, RWKV_G_RANK, MIX_W), RWKV_G_RANK ** -0.5),
        'rwkv_rk': nrm(21, (DEPTH, RWKV_HEADS, RWKV_HEAD), 0.1),
        'rwkv_gn': gain(22, (DEPTH, MIX_W)),
        'gla_ab': nrm(23, (DEPTH, 2, GLA_RANK, GLA_HEADS * GLA_DK), 0.5 * GLA_RANK ** -0.5),
        'gla_abias': nrm(24, (DEPTH, 2, GLA_HEADS * GLA_DK), 0.5),
        'gla_norm': gain(25, (DEPTH, MIX_W)),
        'w_branch': nrm(26, (DEPTH, N_BRANCH, MIX_W, D_MODEL), MIX_W ** -0.5),
        'w_out': nrm(27, (DEPTH, D_MODEL, D_MODEL), D_MODEL ** -0.5),
        'mlp_w1': nrm(28, (DEPTH, D_MODEL, D_FF), D_MODEL ** -0.5),
        'mlp_w2': nrm(29, (DEPTH, D_FF, D_MODEL), D_FF ** -0.5),
        'final_g': gain(30, (D_MODEL,)),
    }


def reference(x_prompt, x_sample, state_rwkv, state_gla, c, c_ctx, ada_w, ada_b, norm1_g, norm2_g,
              w_in, pool_w, pool_scale, rwkv_mu, rwkv_w0, rwkv_bw, rwkv_a0, rwkv_ba, rwkv_kk, rwkv_ka,
              rwkv_bg, rwkv_rk, rwkv_gn, gla_ab, gla_abias, gla_norm, w_branch, w_out, mlp_w1, mlp_w2,
              final_g):
    def layer_params(l):
        return {'ada_w': ada_w[l], 'ada_b': ada_b[l], 'norm1_g': norm1_g[l], 'norm2_g': norm2_g[l],
                'w_in': w_in[l], 'pool_w': pool_w[l], 'pool_scale': pool_scale[l], 'rwkv_mu': rwkv_mu[l],
                'rwkv_w0': rwkv_w0[l], 'rwkv_bw': rwkv_bw[l], 'rwkv_a0': rwkv_a0[l], 'rwkv_ba': rwkv_ba[l],
                'rwkv_kk': rwkv_kk[l], 'rwkv_ka': rwkv_ka[l], 'rwkv_bg': rwkv_bg[l], 'rwkv_rk': rwkv_rk[l],
                'rwkv_gn': rwkv_gn[l], 'gla_ab': gla_ab[l], 'gla_abias': gla_abias[l], 'gla_norm': gla_norm[l],
                'w_branch': w_branch[l], 'w_out': w_out[l], 'mlp_w1': mlp_w1[l], 'mlp_w2': mlp_w2[l]}

    xp = x_prompt
    bp = x_prompt.shape[0]
    new_r, new_g = [], []
    for l in range(DEPTH):
        xp, s_r, s_g = _layer(xp, c_ctx[None, :], layer_params(l),
                              jnp.zeros((bp, 2, RWKV_HEADS, RWKV_HEAD, RWKV_HEAD), F32),
                              jnp.zeros((bp, 2, GLA_HEADS, GLA_DK, GLA_DV), F32))
        new_r.append(s_r)
        new_g.append(s_g)
    y_prompt = _rmsnorm(xp, final_g).astype(x_prompt.dtype)
    new_state_rwkv = jnp.stack(new_r, axis=1)
    new_state_gla = jnp.stack(new_g, axis=1)

    n_tok = x_sample.shape[1]
    xs = (x_sample.astype(F32) + _pos_embed_2d(n_tok, D_MODEL)[None]).astype(x_sample.dtype)
    for l in range(DEPTH):
        xs, _, _ = _layer(xs, c, layer_params(l), state_rwkv[:, l].astype(F32), state_gla[:, l].astype(F32))
    y_sample = _rmsnorm(xs, final_g).astype(x_sample.dtype)

    return (y_prompt, y_sample, new_state_rwkv, new_state_gla)
```

```python
import contextlib
import numpy as np
import concourse.bass as bass
import concourse.mybir as mybir
from concourse.bass_utils import run_bass_kernel_spmd

F32 = mybir.dt.float32
BF16 = mybir.dt.bfloat16
ALU = mybir.AluOpType
AF = mybir.ActivationFunctionType
AX = mybir.AxisListType

D = 1024
T = 4096
NT = 8
TN = 512
DEPTH = 2
P_IN = 6432
NMIX = 2336
N_CORES = 8
EPS = 1e-6


class Em:
    ENGS = ("pe", "act", "dve", "pool", "sp")

    def __init__(self, nc):
        self.nc = nc
        self.ops = {e: [] for e in self.ENGS}
        self.cnt = {e: 0 for e in self.ENGS}
        self.seen = {e: {} for e in self.ENGS}
        self.last_w = {}
        self.readers = {}
        self.dma_sems = {}
        self.free_sems = []
        self.pool_sems = {}
        self.pool_names = set()
        self.sem_names = ["c_" + e for e in self.ENGS]
        self.final_tokens = []
        self.marks = []

    def _deps(self, eng, reads, writes):
        toks = []
        for k in reads:
            t = self.last_w.get(k)
            if t is not None:
                toks.append(t)
        for k in writes:
            t = self.last_w.get(k)
            if t is not None:
                toks.append(t)
            toks.extend(self.readers.get(k, ()))
        seen = self.seen[eng]
        best = {}
        own = "c_" + eng
        for (s, v) in toks:
            if eng == "pe" and s == own:
                continue
            if seen.get(s, 0) < v and best.get(s, 0) < v:
                best[s] = v
        waits = []
        for s, v in best.items():
            seen[s] = v
            waits.append((s, v))
        return waits

    def _commit(self, tok, reads, writes):
        for k in reads:
            self.readers.setdefault(k, []).append(tok)
        for k in writes:
            self.last_w[k] = tok
            self.readers[k] = []

    def op(self, eng, fn, reads=(), writes=()):
        waits = self._deps(eng, reads, writes)
        self.cnt[eng] += 1
        tok = ("c_" + eng, self.cnt[eng])
        self.ops[eng].append((waits, fn, ("c_" + eng, 1)))
        self._commit(tok, reads, writes)
        return tok

    def dma(self, q, fn, semkey, reads=(), writes=(), final=False):
        if semkey not in self.dma_sems:
            if q == "sp" and self.free_sems:
                self.dma_sems[semkey] = self.free_sems.pop()
            elif q != "sp" and semkey in self.pool_sems:
                self.dma_sems[semkey] = self.pool_sems[semkey]
            else:
                name = "d%d" % (len(self.sem_names) - len(self.ENGS))
                self.dma_sems[semkey] = [name, 0]
                self.sem_names.append(name)
        ent = self.dma_sems[semkey]
        if q != "sp":
            self.pool_names.add(ent[0])
        waits = self._deps(q, reads, writes)
        ent[1] += 16
        tok = (ent[0], ent[1])
        self.ops[q].append((waits, fn, (ent[0], 16)))
        self._commit(tok, reads, writes)
        if final:
            self.final_tokens.append(tok)
        return tok

    def barrier(self, label=""):
        self.marks.append((label, dict(self.cnt)))
        targets = [("c_" + e, self.cnt[e]) for e in self.ENGS if self.cnt[e] > 0]
        targets += [(n, c) for (n, c) in self.dma_sems.values() if c > 0]
        for e in self.ENGS:
            waits = []
            for (s, v) in targets:
                if e == "pe" and s == "c_pe":
                    continue
                if self.seen[e].get(s, 0) < v:
                    self.seen[e][s] = v
                    waits.append((s, v))
            if waits:
                self.ops[e].append((waits, None, None))
        for k_, ent_ in self.dma_sems.items():
            if ent_[0] in self.pool_names:
                self.pool_sems[k_] = ent_
            else:
                self.free_sems.append(ent_)
        self.dma_sems = {}

    def build(self):
        nc = self.nc
        with contextlib.ExitStack() as st:
            print("Em: %d semaphores, ops:" % len(self.sem_names), {e: len(v) for e, v in self.ops.items()})
            sems = {n: st.enter_context(nc.semaphore(n)) for n in self.sem_names}
            fin = {}
            for (s, v) in self.final_tokens:
                fin[s] = max(fin.get(s, 0), v)
            block = st.enter_context(nc.Block())

            def runner(ename):
                def f(e):
                    for waits, fn, inc in self.ops[ename]:
                        for (s, v) in waits:
                            e.wait_ge(sems[s], v)
                        if fn is not None:
                            fn(e).then_inc(sems[inc[0]], inc[1])
                    if ename == "sp":
                        for s, v in fin.items():
                            e.wait_ge(sems[s], v)
                return f
            block.tensor(runner("pe"))
            block.scalar(runner("act"))
            block.vector(runner("dve"))
            block.gpsimd(runner("pool"))
            block.sync(runner("sp"))


class Rec:
    def __init__(self):
        self.items = []

    def op(self, eng, fn, reads=(), writes=()):
        self.items.append(("op", eng, fn, list(reads), list(writes)))

    def dma(self, q, fn, semkey, reads=(), writes=(), final=False):
        self.items.append(("dma", q, fn, semkey, list(reads), list(writes), final))


def merge_streams(em, recs):
    pos = [0] * len(recs)
    tot = [max(1, len(r.items)) for r in recs]
    while True:
        best = None
        for i, r in enumerate(recs):
            if pos[i] < len(r.items):
                frac = pos[i] / tot[i]
                if best is None or frac < best[0]:
                    best = (frac, i)
        if best is None:
            break
        i = best[1]
        it = recs[i].items[pos[i]]; pos[i] += 1
        if it[0] == "op":
            em.op(it[1], it[2], reads=it[3], writes=it[4])
        else:
            em.dma(it[1], it[2], it[3], reads=it[4], writes=it[5], final=it[6])


class Arena:
    def __init__(self, handle_bf16, nelem):
        self.h = handle_bf16
        self.n = nelem
        self.off = 0

    def reset(self):
        self.off = 0

    def alloc(self, shape_free, dt):
        n = int(np.prod(shape_free))
        nb = n * (2 if dt == F32 else 1)
        nb = (nb + 15) // 16 * 16
        assert self.off + nb <= self.n, ("arena overflow", self.off, nb, self.n)
        ap = self.h[:, self.off:self.off + n * (2 if dt == F32 else 1)]
        self.off += nb
        if dt == F32:
            ap = ap.bitcast(F32)
        if len(shape_free) == 2:
            ap = ap.rearrange("p (a b) -> p a b", b=shape_free[1])
        elif len(shape_free) == 3:
            ap = ap.rearrange("p (a b c) -> p a b c", b=shape_free[1], c=shape_free[2])
        return ap


def build_program():
    nc = bass.Bass("TRN2", target_bir_lowering=False)
    dI = lambda n, sh, dt=F32: nc.dram_tensor(n, sh, dt, kind="ExternalInput").ap()
    DBG = False
    NL = DEPTH
    dS = lambda n, sh, dt=F32: nc.dram_tensor(n, sh, dt, kind="Internal").ap()
    dO = lambda n, sh, dt=F32: nc.dram_tensor(n, sh, dt, kind="ExternalOutput").ap()
    xT = dI("xT", [D, T]); pos = dI("pos", [D, T])
    cond = dI("cond", [128, 8]); kap = dI("kap", [128, 1])
    ada_w = dI("ada_w", [DEPTH, D, 6 * D]); ada_b = dI("ada_b", [DEPTH, 128, 48])
    n1g = dI("n1g", [DEPTH, 128, 8]); n2g = dI("n2g", [DEPTH, 128, 8]); fg = dI("fg", [128, 8])
    w_in = dI("w_in", [DEPTH, D, P_IN]); w_br = dI("w_br", [DEPTH, D, D]); w_out = dI("w_out", [DEPTH, D, D])
    w1 = dI("w1", [DEPTH, D, 4 * D]); w2 = dI("w2", [DEPTH, 4 * D, D])
    yT = dO("yT", [D, T])
    PRM_N = 1568; ROW_N = 1280
    prm_d = dI("prm", [DEPTH, 128, PRM_N]); rows_d = dI("rows", [DEPTH, 1, ROW_N])
    lr64_d = dI("lr64", [DEPTH, 64, 768]); lr32_d = dI("lr32", [DEPTH, 32, 512]); lr16_d = dI("lr16", [DEPTH, 16, 256])
    pool_w_d = dI("pool_w", [DEPTH, 4, 64, 64])
    ident_d = dI("ident", [128, 128]); maskc_d = dI("maskc", [128, 8, 512])
    invc_d = dI("invc", [2, 128, T]); csm_d = dI("csm", [2, 128, 512])
    CL_d = dI("CL", [T, T], BF16); SL_d = dI("SL", [T, T], BF16)
    sr0_d = dI("sr0", [DEPTH, 2, 64, 4, 64]); sg0_d = dI("sg0", [DEPTH, 2, 32, 4, 64])
    osr_d = dO("osr", [DEPTH, 2, 16, 64, 4, 64]); osg_d = dO("osg", [DEPTH, 2, 16, 32, 4, 64])
    ZM = dS("ZM", [768, T])
    xres = dS("xres", [D, T])
    U = dS("U", [19 * 128, T])
    G = dS("G", [4 * D, T], BF16)
    Y = dS("Y", [D, T], BF16)
    H2 = dS("H2", [D, T], BF16)

    with contextlib.ExitStack() as st:
        arena_h = st.enter_context(nc.sbuf_tensor("arena", [128, 104000], BF16))
        cst_h = st.enter_context(nc.sbuf_tensor("cst", [128, 1024], F32))
        onesb = st.enter_context(nc.sbuf_tensor("onesb", [128, 128], BF16))
        psum = [st.enter_context(nc.psum_tensor("ps%d" % i, [128, 512], F32)) for i in range(8)]
        ar = Arena(arena_h, 104000)
        em = Em(nc)
        c_silu = cst_h[:, 0:8]; c_mod = cst_h[:, 8:56]; c_gsc1 = cst_h[:, 56:64]; c_gsc2 = cst_h[:, 64:72]
        c_n1g = cst_h[:, 72:80]; c_n2g = cst_h[:, 80:88]; c_fg = cst_h[:, 88:96]
        c_kap = cst_h[:, 96:97]; c_eps = cst_h[:, 97:98]; c_adab = cst_h[:, 100:148]
        c_nk = cst_h[:, 98:99]; c_gneps = cst_h[:, 99:100]
        em.dma("sp", lambda e: e.dma_start(out=c_silu, in_=cond), "c_silu", writes=["c_silu"])
        em.dma("sp", lambda e: e.dma_start(out=c_kap, in_=kap), "c_kap", writes=["c_kap"])
        em.dma("sp", lambda e: e.dma_start(out=c_fg, in_=fg), "c_fg", writes=["c_fg"])
        em.op("dve", lambda e: e.memset(c_eps, EPS), writes=["c_eps"])
        em.op("dve", lambda e: e.memset(c_gneps, 64e-5), writes=["c_gneps"])
        em.op("dve", lambda e: e.tensor_scalar(out=c_nk, in0=c_kap, scalar1=-1.0, scalar2=None, op0=ALU.add), reads=["c_kap"], writes=["c_nk"])
        em.op("dve", lambda e: e.memset(onesb[:], 1.0 / D), writes=["onesb"])
        em.op("act", lambda e: e.activation(out=c_silu, in_=c_silu, func=AF.Silu), reads=["c_silu"], writes=["c_silu"])

        pctr = [0]

        def pbank():
            pctr[0] = (pctr[0] + 1) % 8
            return pctr[0]

        def rmsnorm_tile(xt, xkey, gsc, shift, out_bf, outkey, tmp, xsq, rstd, tag):
            em.op("act", lambda e: e.activation(out=xsq, in_=xt, func=AF.Square), reads=[xkey], writes=[tag + "xsq"])
            b = pbank()
            for c in range(8):
                em.op("pe", lambda e, c=c, b=b: e.matmul(psum[b][:], lhsT=onesb[:], rhs=xsq[:, c, :], start=(c == 0), stop=(c == 7)),
                      reads=[tag + "xsq", "onesb"], writes=["ps%d" % b])
            em.op("act", lambda e, b=b: e.activation(out=rstd, in_=psum[b][:], func=AF.Sqrt, bias=c_eps, scale=1.0),
                  reads=["ps%d" % b, "c_eps"], writes=[tag + "rstd"])
            em.op("dve", lambda e: e.reciprocal(out=rstd, in_=rstd), reads=[tag + "rstd"], writes=[tag + "rstd"])
            for c in range(8):
                em.op("dve", lambda e, c=c: e.tensor_tensor(out=tmp[:, c, :], in0=xt[:, c, :], in1=rstd, op=ALU.mult),
                      reads=[xkey, tag + "rstd"], writes=[tag + "tmp%d" % c])
                if shift is not None:
                    em.op("act", lambda e, c=c: e.activation(out=out_bf[:, c, :], in_=tmp[:, c, :], func=AF.Identity,
                                                             bias=shift[:, c:c + 1], scale=gsc[:, c:c + 1]),
                          reads=[tag + "tmp%d" % c, "mod"], writes=[outkey])
                else:
                    em.op("act", lambda e, c=c: e.activation(out=out_bf[:, c, :], in_=tmp[:, c, :], func=AF.Identity,
                                                             bias=0.0, scale=gsc[:, c:c + 1]),
                          reads=[tag + "tmp%d" % c, "mod"], writes=[outkey])

        fm = lambda ap: ap.rearrange("(c p) t -> p c t", p=128)
        cx = CX()
        cx.nc = nc; cx.em = em; cx.ar = ar; cx.psum = psum; cx.pbank = pbank; cx.U = U; cx.Y = Y; cx.ZM = ZM
        cx.c_kap = c_kap; cx.c_nk = c_nk; cx.c_eps = c_eps; cx.c_gneps = c_gneps
        cx.PRM_N = PRM_N; cx.ROW_N = ROW_N; cx.prm = prm_d; cx.rows = rows_d
        cx.lr64_d = lr64_d; cx.lr32_d = lr32_d; cx.lr16_d = lr16_d; cx.pool_w = pool_w_d
        cx.ident_d = ident_d; cx.maskc_d = maskc_d; cx.invc = invc_d; cx.csm = csm_d; cx.CL = CL_d; cx.SL = SL_d
        cx.sr0 = sr0_d; cx.sg0 = sg0_d; cx.osr = osr_d; cx.osg = osg_d

        for l in range(NL):
            last = (l == NL - 1)
            em.barrier(); ar.reset()
            em.dma("sp", lambda e, l=l: e.dma_start(out=c_adab, in_=ada_b[l]), "c_adab", writes=["c_adab"])
            em.dma("sp", lambda e, l=l: e.dma_start(out=c_n1g, in_=n1g[l]), "c_n1g", writes=["c_n1g"])
            em.dma("sp", lambda e, l=l: e.dma_start(out=c_n2g, in_=n2g[l]), "c_n2g", writes=["c_n2g"])
            awt = [ar.alloc([8, 768], F32) for _ in range(2)]
            aw_v = ada_w[l].rearrange("(c p) n -> p c n", p=128)
            pb = pbank()
            for g in range(8):
                wt = awt[g % 2]; key = "awt%d" % (g % 2)
                em.dma("sp", lambda e, g=g, wt=wt, aw_v=aw_v: e.dma_start(out=wt, in_=aw_v[:, :, g * 768:(g + 1) * 768]), key, writes=[key])
                for m in range(6):
                    col = g * 6 + m
                    for kc in range(8):
                        em.op("pe", lambda e, wt=wt, m=m, kc=kc, col=col, pb=pb: e.matmul(
                            psum[pb][:, col:col + 1], lhsT=wt[:, kc, m * 128:(m + 1) * 128], rhs=c_silu[:, kc:kc + 1],
                            start=(kc == 0), stop=(kc == 7)), reads=[key, "c_silu"], writes=["ps%d" % pb])
            em.op("dve", lambda e, pb=pb: e.tensor_tensor(out=c_mod, in0=psum[pb][:, 0:48], in1=c_adab, op=ALU.add),
                  reads=["ps%d" % pb, "c_adab"], writes=["mod"])
            em.op("dve", lambda e: e.scalar_tensor_tensor(out=c_gsc1, in0=c_mod[:, 8:16], scalar=1.0, in1=c_n1g, op0=ALU.add, op1=ALU.mult),
                  reads=["mod", "c_n1g"], writes=["mod"])
            em.op("dve", lambda e: e.scalar_tensor_tensor(out=c_gsc2, in0=c_mod[:, 32:40], scalar=1.0, in1=c_n2g, op0=ALU.add, op1=ALU.mult),
                  reads=["mod", "c_n2g"], writes=["mod"])
            if DBG and l == 0:
                dbg_mod = nc.dram_tensor("dbg_mod", [128, 64], F32, kind="ExternalOutput").ap()
                em.dma("sp", lambda e: e.dma_start(out=dbg_mod, in_=cst_h[:, 8:72]), "dbgmod", reads=["mod"], final=True)
            sh1 = c_mod[:, 0:8]; g1 = c_mod[:, 16:24]; sh2 = c_mod[:, 24:32]; g2 = c_mod[:, 40:48]

            em.barrier(); ar.reset()
            hT = ar.alloc([8, T], BF16)
            xb = [ar.alloc([8, TN], F32) for _ in range(2)]
            pb_ = [ar.alloc([8, TN], F32) for _ in range(2)]
            tmp = ar.alloc([8, TN], F32); xsq = ar.alloc([8, TN], BF16); rstd = ar.alloc([TN], F32)
            for j in range(NT):
                xt = xb[j % 2]; xk = "xb%d" % (j % 2)
                sl = slice(j * TN, (j + 1) * TN)
                if l == 0:
                    pt = pb_[j % 2]; pk = "pb%d" % (j % 2)
                    em.dma("sp", lambda e, xt=xt, sl=sl: e.dma_start(out=xt, in_=fm(xT)[:, :, sl]), xk, writes=[xk])
                    em.dma("sp", lambda e, pt=pt, sl=sl: e.dma_start(out=pt, in_=fm(pos)[:, :, sl]), pk, writes=[pk])
                    em.op("pool", lambda e, xt=xt, pt=pt: e.tensor_tensor(out=xt, in0=xt, in1=pt, op=ALU.add), reads=[xk, pk], writes=[xk])
                    em.dma("sp", lambda e, xt=xt, sl=sl: e.dma_start(out=fm(xres)[:, :, sl], in_=xt), xk + "w", reads=[xk], writes=["xres%d" % j])
                else:
                    em.dma("sp", lambda e, xt=xt, sl=sl: e.dma_start(out=xt, in_=fm(xres)[:, :, sl]), xk, reads=["xres%d" % j], writes=[xk])
                rmsnorm_tile(xt, xk, c_gsc1, sh1, hT[:, :, sl], "hT%d" % j, tmp, xsq, rstd, "n1")

            chunks = [(m * 128, 128, "U", m) for m in range(18)] + [(2304, 32, "U", 18)] + \
                     [(NMIX + m * 128, 128, "G", m) for m in range(32)]
            wbuf = [ar.alloc([8, 512], BF16) for _ in range(2)]
            stg = [ar.alloc([TN], F32) for _ in range(4)]
            stb = [ar.alloc([TN], BF16) for _ in range(4)]
            win_v = w_in[l].rearrange("(c p) n -> p c n", p=128)
            groups = []
            cur = []
            for ch in chunks:
                if cur and (len(cur) == 4 or cur[-1][2] != ch[2] or cur[-1][1] != 128):
                    groups.append(cur); cur = []
                cur.append(ch)
            groups.append(cur)
            sctr = 0
            for gi, grp in enumerate(groups):
                wt = wbuf[gi % 2]; wk = "wbuf%d" % (gi % 2)
                c0 = grp[0][0]; ncol = sum(c[1] for c in grp)
                em.dma("pool", lambda e, wt=wt, c0=c0, ncol=ncol, win_v=win_v: e.dma_start(out=wt[:, :, 0:ncol], in_=win_v[:, :, c0:c0 + ncol]), wk, writes=[wk])
                for j in range(NT):
                    sl = slice(j * TN, (j + 1) * TN)
                    for (cs, cn, kind, mi) in grp:
                        b = pbank(); o = cs - c0
                        for kc in range(8):
                            em.op("pe", lambda e, wt=wt, o=o, cn=cn, kc=kc, b=b, sl=sl: e.matmul(
                                psum[b][0:cn, :], lhsT=wt[:, kc, o:o + cn], rhs=hT[:, kc, sl], start=(kc == 0), stop=(kc == 7)),
                                reads=[wk, "hT%d" % j], writes=["ps%d" % b])
                        si = sctr % 4; sctr += 1
                        if kind == "U":
                            s_ = stg[si]; sk = "stg%d" % si
                            em.op("dve", lambda e, s_=s_, b=b, cn=cn: e.tensor_copy(out=s_[0:cn, :], in_=psum[b][0:cn, :]), reads=["ps%d" % b], writes=[sk])
                            em.dma("sp", lambda e, s_=s_, mi=mi, cn=cn, sl=sl: e.dma_start(out=U[mi * 128:mi * 128 + cn, sl], in_=s_[0:cn, :]),
                                   sk + "w", reads=[sk], writes=["U%d_%d" % (mi, j)])
                        else:
                            s_ = stb[si]; sk = "stb%d" % si
                            em.op("act", lambda e, s_=s_, b=b: e.activation(out=s_, in_=psum[b][:], func=AF.Sigmoid), reads=["ps%d" % b], writes=[sk])
                            em.dma("sp", lambda e, s_=s_, mi=mi, sl=sl: e.dma_start(out=G[mi * 128:(mi + 1) * 128, sl], in_=s_),
                                   sk + "w", reads=[sk], writes=["G%d" % j])

            em.barrier(); ar.reset()
            mixers(cx, l)

            em.barrier(); ar.reset()
            wbr = ar.alloc([8, D], BF16); wo = ar.alloc([8, D], BF16)
            em.dma("pool", lambda e, l=l: e.dma_start(out=wbr, in_=w_br[l].rearrange("(c p) n -> p c n", p=128)), "wbr", writes=["wbr"])
            em.dma("pool", lambda e, l=l: e.dma_start(out=wo, in_=w_out[l].rearrange("(c p) n -> p c n", p=128)), "wo", writes=["wo"])
            Yt = [ar.alloc([8, TN], BF16) for _ in range(2)]
            Gt = [ar.alloc([32, TN], BF16) for _ in range(2)]
            xb = [ar.alloc([8, TN], F32) for _ in range(2)]
            mgs = [ar.alloc([8, TN], BF16) for _ in range(2)]
            accs = [ar.alloc([TN], F32) for _ in range(2)]; tms = [ar.alloc([TN], F32) for _ in range(2)]
            tmp = ar.alloc([8, TN], F32); xsq = ar.alloc([8, TN], BF16); rstd = ar.alloc([TN], F32)
            h2 = [ar.alloc([8, TN], BF16) for _ in range(1)]
            for j in range(NT):
                sl = slice(j * TN, (j + 1) * TN)
                yt = Yt[j % 2]; gt = Gt[j % 2]; xt = xb[j % 2]; hh = h2[0]
                yk = "Yt%d" % (j % 2); gk = "Gt%d" % (j % 2); xk = "x3_%d" % (j % 2); hk = "h2_0"
                em.dma("sp", lambda e, yt=yt, sl=sl: e.dma_start(out=yt, in_=fm(Y)[:, :, sl]), yk, reads=["Y%d" % j], writes=[yk])
                em.dma("sp", lambda e, gt=gt, sl=sl: e.dma_start(out=gt, in_=fm(G)[:, :, sl]), gk, reads=["G%d" % j], writes=[gk])
                em.dma("sp", lambda e, xt=xt, sl=sl: e.dma_start(out=xt, in_=fm(xres)[:, :, sl]), xk, reads=["xres%d" % j], writes=[xk])
                mg = mgs[j % 2]; mgk = "mg%d" % (j % 2)
                for dc in range(8):
                    acc = accs[dc % 2]; acck = "acc%d" % (dc % 2)
                    for i in range(4):
                        b = pbank()
                        for cc in range(2):
                            em.op("pe", lambda e, i=i, cc=cc, dc=dc, b=b, yt=yt: e.matmul(
                                psum[b][:], lhsT=wbr[:, i * 2 + cc, dc * 128:(dc + 1) * 128], rhs=yt[:, i * 2 + cc, :],
                                start=(cc == 0), stop=(cc == 1)), reads=["wbr", yk], writes=["ps%d" % b])
                        dst = acc if i == 0 else tms[(i - 1) % 2]
                        dstk = acck if i == 0 else "tm%d" % ((i - 1) % 2)
                        em.op("dve", lambda e, b=b, i=i, dc=dc, gt=gt, dst=dst: e.tensor_tensor(out=dst, in0=psum[b][:], in1=gt[:, i * 8 + dc, :], op=ALU.mult),
                              reads=["ps%d" % b, gk], writes=[dstk])
                        if i > 0:
                            o_ = mg[:, dc, :] if i == 3 else acc
                            em.op("pool", lambda e, o_=o_, acc=acc, dst=dst: e.tensor_tensor(out=o_, in0=acc, in1=dst, op=ALU.add),
                                  reads=[acck, dstk], writes=[mgk if i == 3 else acck])
                for d2 in range(8):
                    b = pbank()
                    for dc in range(8):
                        em.op("pe", lambda e, d2=d2, dc=dc, b=b, mg=mg: e.matmul(psum[b][:], lhsT=wo[:, dc, d2 * 128:(d2 + 1) * 128], rhs=mg[:, dc, :],
                                                                          start=(dc == 0), stop=(dc == 7)), reads=["wo", mgk], writes=["ps%d" % b])
                    em.op("dve", lambda e, d2=d2, b=b, xt=xt: e.scalar_tensor_tensor(out=xt[:, d2, :], in0=psum[b][:], scalar=g1[:, d2:d2 + 1], in1=xt[:, d2, :],
                                                                                   op0=ALU.mult, op1=ALU.add), reads=["ps%d" % b, xk, "mod"], writes=[xk])
                em.dma("sp", lambda e, xt=xt, sl=sl: e.dma_start(out=fm(xres)[:, :, sl], in_=xt), xk + "w", reads=[xk], writes=["xres%d" % j])
                rmsnorm_tile(xt, xk, c_gsc2, sh2, hh, hk, tmp, xsq, rstd, "n2")
                em.dma("sp", lambda e, hh=hh, sl=sl: e.dma_start(out=fm(H2)[:, :, sl], in_=hh), hk + "w", reads=[hk], writes=["H2_%d" % j])

            if DBG and l == 0:
                em.barrier()
                dd = nc.dram_tensor("dY", [D, T], BF16, kind="ExternalOutput").ap()
                em.dma("sp", lambda e, dd=dd: e.dma_start(out=dd, in_=Y), "dbgY", final=True)
                dd2 = nc.dram_tensor("dX", [D, T], F32, kind="ExternalOutput").ap()
                em.dma("sp", lambda e, dd2=dd2: e.dma_start(out=dd2, in_=xres), "dbgX", final=True)
            em.barrier(); ar.reset()
            w1s = ar.alloc([8, 4 * D], BF16); w2s = ar.alloc([32, D], BF16)
            for q in range(4):
                em.dma("pool", lambda e, l=l, q=q: e.dma_start(out=w1s[:, :, q * 1024:(q + 1) * 1024],
                                                               in_=w1[l].rearrange("(c p) n -> p c n", p=128)[:, :, q * 1024:(q + 1) * 1024]), "w1s", writes=["w1s"])
                em.dma("pool", lambda e, l=l, q=q: e.dma_start(out=w2s[:, q * 8:(q + 1) * 8, :],
                                                               in_=w2[l].rearrange("(c p) n -> p c n", p=128)[:, q * 8:(q + 1) * 8, :]), "w2s", writes=["w2s"])
            hid_raw = ar.alloc([32 * TN], BF16)
            hid = hid_raw.rearrange("p (a b) -> p a b", b=TN)
            rl = [ar.alloc([TN], F32) for _ in range(2)]
            xb1 = ar.alloc([8, TN], F32); h2t = ar.alloc([8, TN], BF16)
            if last:
                tmp = hid_raw[:, 0:16 * TN].bitcast(F32).rearrange("p (a b) -> p a b", b=TN)
                xsq = ar.alloc([8, TN], BF16); rstd = ar.alloc([TN], F32)
            for j in range(NT):
                sl = slice(j * TN, (j + 1) * TN)
                em.dma("sp", lambda e, sl=sl: e.dma_start(out=h2t, in_=fm(H2)[:, :, sl]), "h2t", reads=["H2_%d" % j], writes=["h2t"])
                em.dma("sp", lambda e, sl=sl: e.dma_start(out=xb1, in_=fm(xres)[:, :, sl]), "xb1", reads=["xres%d" % j], writes=["xb1"])
                for fc in range(32):
                    b = pbank()
                    for kc in range(8):
                        em.op("pe", lambda e, fc=fc, kc=kc, b=b: e.matmul(psum[b][:], lhsT=w1s[:, kc, fc * 128:(fc + 1) * 128], rhs=h2t[:, kc, :],
                                                                          start=(kc == 0), stop=(kc == 7)), reads=["w1s", "h2t"], writes=["ps%d" % b])
                    r_ = rl[fc % 2]; rk = "rl%d" % (fc % 2)
                    em.op("act", lambda e, b=b, r_=r_: e.activation(out=r_, in_=psum[b][:], func=AF.Relu), reads=["ps%d" % b], writes=[rk])
                    em.op("pool", lambda e, r_=r_, fc=fc: e.tensor_tensor(out=hid[:, fc, :], in0=r_, in1=r_, op=ALU.mult), reads=[rk], writes=["hid"])
                for d2 in range(8):
                    b = pbank()
                    for fc in range(32):
                        em.op("pe", lambda e, fc=fc, d2=d2, b=b: e.matmul(psum[b][:], lhsT=w2s[:, fc, d2 * 128:(d2 + 1) * 128], rhs=hid[:, fc, :],
                                                                          start=(fc == 0), stop=(fc == 31)), reads=["w2s", "hid"], writes=["ps%d" % b])
                    em.op("dve", lambda e, d2=d2, b=b: e.scalar_tensor_tensor(out=xb1[:, d2, :], in0=psum[b][:], scalar=g2[:, d2:d2 + 1], in1=xb1[:, d2, :],
                                                                            op0=ALU.mult, op1=ALU.add), reads=["ps%d" % b, "xb1", "mod"], writes=["xb1"])
                if not last:
                    em.dma("sp", lambda e, sl=sl: e.dma_start(out=fm(xres)[:, :, sl], in_=xb1), "xb1w", reads=["xb1"], writes=["xres%d" % j])
                else:
                    em.op("act", lambda e: e.activation(out=xsq, in_=xb1, func=AF.Square), reads=["xb1"], writes=["fxsq"])
                    b = pbank()
                    for c in range(8):
                        em.op("pe", lambda e, c=c, b=b: e.matmul(psum[b][:], lhsT=onesb[:], rhs=xsq[:, c, :], start=(c == 0), stop=(c == 7)),
                              reads=["fxsq", "onesb"], writes=["ps%d" % b])
                    em.op("act", lambda e, b=b: e.activation(out=rstd, in_=psum[b][:], func=AF.Sqrt, bias=c_eps, scale=1.0),
                          reads=["ps%d" % b, "c_eps"], writes=["frstd"])
                    em.op("dve", lambda e: e.reciprocal(out=rstd, in_=rstd), reads=["frstd"], writes=["frstd"])
                    for c in range(8):
                        em.op("dve", lambda e, c=c: e.scalar_tensor_tensor(out=tmp[:, c, :], in0=xb1[:, c, :], scalar=c_fg[:, c:c + 1], in1=rstd,
                                                                         op0=ALU.mult, op1=ALU.mult), reads=["xb1", "frstd", "c_fg"], writes=["hid"])
                    em.dma("sp", lambda e, sl=sl: e.dma_start(out=fm(yT)[:, :, sl], in_=tmp), "yT_w", reads=["hid"], writes=["yT%d" % j], final=True)
        em.build()
        global _EM
        _EM = em
    return nc


class CX:
    pass


def mixers(cx, l):
    em, ar, psum, pbank, U, Y = cx.em, cx.ar, cx.psum, cx.pbank, cx.U, cx.Y
    c_kap, c_nk, c_eps = cx.c_kap, cx.c_nk, cx.c_eps

    ar.reset()
    c = mk_consts(cx, l, 'p')
    p_psc = c.psc

    zp = ar.alloc([2, 16, 272], F32)
    Wa = ar.alloc([16, 272], F32); Wb = ar.alloc([16, 272], F32)
    pooled = ar.alloc([2, T], F32)
    invc = ar.alloc([2, T], F32)
    PW = ar.alloc([2, 128], F32)
    yst = [ar.alloc([TN], BF16) for _ in range(2)]
    em.op("pool", lambda e: e.memset(zp, 0.0), writes=["zp"])
    em.op("pool", lambda e: e.memset(PW, 0.0), writes=["PW"])
    for ct in range(2):
        em.dma("sp", lambda e, ct=ct: e.dma_start(out=invc[:, ct, :], in_=cx.invc[ct]), "invc", writes=["invc"])
        Uv = U[ct * 128:(ct + 1) * 128, :].rearrange("p (b s) -> p b s", s=256)
        em.dma("sp", lambda e, ct=ct, Uv=Uv: e.dma_start(out=zp[:, ct, :, 8:264], in_=Uv), "zp", writes=["zp"])
        em.dma("sp", lambda e, ct=ct, Uv=Uv: e.dma_start(out=zp[:, ct, 1:16, 0:8], in_=Uv[:, 0:15, 248:256]), "zp", writes=["zp"])
        em.dma("sp", lambda e, ct=ct, Uv=Uv: e.dma_start(out=zp[:, ct, 0:15, 264:272], in_=Uv[:, 1:16, 0:8]), "zp", writes=["zp"])
    for g in range(4):
        r0 = (g % 2) * 64
        em.dma("sp", lambda e, g=g, r0=r0, l=l: e.dma_start(out=PW[r0:r0 + 64, g // 2, r0:r0 + 64], in_=cx.pool_w[l, g]), "PW", writes=["PW"])
    for ct in range(2):
        for (a, b) in ((0, 8), (264, 272)):
            em.op("dve", lambda e, ct=ct, a=a, b=b: e.tensor_scalar(out=zp[:, ct, :, a:b], in0=zp[:, ct, :, a:b], scalar1=c_kap, scalar2=None, op0=ALU.mult),
                  reads=["zp", "c_kap"], writes=["zp"])
    pv = lambda ct: pooled[:, ct, :].rearrange("p (b s) -> p b s", s=256)
    iv = lambda ct: invc[:, ct, :].rearrange("p (b s) -> p b s", s=256)

    def take(W, wk, ct, r0):
        em.op("dve", lambda e: e.tensor_tensor(out=pv(ct)[r0:r0 + 64], in0=W[r0:r0 + 64, :, 8:264], in1=iv(ct)[r0:r0 + 64], op=ALU.mult),
              reads=[wk, "invc"], writes=["pooled"])
        em.op("dve", lambda e: e.tensor_tensor(out=pv(ct)[r0:r0 + 64], in0=pv(ct)[r0:r0 + 64], in1=zp[r0:r0 + 64, ct, :, 8:264], op=ALU.subtract),
              reads=["pooled", "zp"], writes=["pooled"])

    for ct in range(2):
        Z = zp[:, ct]
        em.op("dve", lambda e, Z=Z: e.tensor_tensor(out=Wa[:, :, 1:272], in0=Z[:, :, 0:271], in1=Z[:, :, 1:272], op=ALU.add), reads=["zp"], writes=["Wa"])
        if ct == 0:
            take(Wa, "Wa", 0, 0)
        em.op("dve", lambda e: e.tensor_tensor(out=Wb[:, :, 2:271], in0=Wa[:, :, 1:270], in1=Wa[:, :, 3:272], op=ALU.add), reads=["Wa"], writes=["Wb"])
        if ct == 0:
            take(Wb, "Wb", 0, 64)
        else:
            em.op("dve", lambda e: e.tensor_tensor(out=Wa[:, :, 4:269], in0=Wb[:, :, 2:267], in1=Wb[:, :, 6:271], op=ALU.add), reads=["Wb"], writes=["Wa"])
            take(Wa, "Wa", 1, 0)
            em.op("dve", lambda e: e.tensor_tensor(out=Wb[:, :, 8:265], in0=Wa[:, :, 4:261], in1=Wa[:, :, 12:269], op=ALU.add), reads=["Wa"], writes=["Wb"])
            take(Wb, "Wb", 1, 64)
    n = 0
    for j in range(NT):
        sl = slice(j * TN, (j + 1) * TN)
        for ct in range(2):
            b = pbank(); i = n % 2; n += 1
            em.op("pe", lambda e, b=b, ct=ct, sl=sl: e.matmul(psum[b][:], lhsT=PW[:, ct, :], rhs=pooled[:, ct, sl], start=True, stop=True),
                  reads=["PW", "pooled"], writes=["ps%d" % b])
            em.op("act", lambda e, b=b, ct=ct, i=i: e.activation(out=yst[i], in_=psum[b][:], func=AF.Identity, bias=0.0, scale=p_psc[:, ct:ct + 1]),
                  reads=["ps%d" % b], writes=["yst%d" % i])
            em.dma("sp", lambda e, ct=ct, sl=sl, i=i: e.dma_start(out=Y[ct * 128:(ct + 1) * 128, sl], in_=yst[i]), "yst%dw" % i, reads=["yst%d" % i])

    em.barrier(); ar.reset()
    zf = ar.alloc([2, T], BF16)
    csm = ar.alloc([2, 512], BF16)
    zcs = ar.alloc([32, 512], BF16)
    CLb = [ar.alloc([16, 512], BF16) for _ in range(2)]
    SLb = [ar.alloc([16, 512], BF16) for _ in range(2)]
    fst = [ar.alloc([TN], BF16) for _ in range(2)]
    for ct in range(2):
        em.dma("pool", lambda e, ct=ct: e.dma_start(out=zf[:, ct, :], in_=U[256 + ct * 128:256 + (ct + 1) * 128, :]), "zf", writes=["zf"])
        em.dma("pool", lambda e, ct=ct: e.dma_start(out=csm[:, ct, :], in_=cx.csm[ct]), "csm", writes=["csm"])
    for tc in range(32):
        b = pbank()
        for ct in range(2):
            em.op("pe", lambda e, b=b, ct=ct, tc=tc: e.matmul(psum[b][:], lhsT=zf[:, ct, tc * 128:(tc + 1) * 128], rhs=csm[:, ct, :], start=(ct == 0), stop=(ct == 1)),
                  reads=["zf", "csm"], writes=["ps%d" % b])
        if tc % 2 == 0:
            em.op("act", lambda e, b=b, tc=tc: e.activation(out=zcs[:, tc, :], in_=psum[b][:], func=AF.Identity), reads=["ps%d" % b], writes=["zcs"])
        else:
            em.op("dve", lambda e, b=b, tc=tc: e.tensor_copy(out=zcs[:, tc, :], in_=psum[b][:]), reads=["ps%d" % b], writes=["zcs"])
    CLv = cx.CL.rearrange("(tc p) f -> p tc f", p=128)
    SLv = cx.SL.rearrange("(tc p) f -> p tc f", p=128)
    n = 0
    for ft in range(8):
        fsl = slice(ft * 512, (ft + 1) * 512)
        bb = [pbank(), pbank()]
        for th in range(2):
            i = n % 2; n += 1
            em.dma("sp", lambda e, i=i, th=th, fsl=fsl: e.dma_start(out=CLb[i], in_=CLv[:, th * 16:(th + 1) * 16, fsl]), "CLb%d" % i, writes=["CLb%d" % i])
            em.dma("sp", lambda e, i=i, th=th, fsl=fsl: e.dma_start(out=SLb[i], in_=SLv[:, th * 16:(th + 1) * 16, fsl]), "SLb%d" % i, writes=["SLb%d" % i])
            for half in range(2):
                b = bb[half]
                for tcl in range(16):
                    tc = th * 16 + tcl
                    em.op("pe", lambda e, b=b, i=i, tc=tc, tcl=tcl, half=half, th=th: e.matmul(
                        psum[b][:], lhsT=zcs[:, tc, half * 128:(half + 1) * 128], rhs=CLb[i][:, tcl, :], start=(th == 0 and tcl == 0), stop=False),
                        reads=["zcs", "CLb%d" % i], writes=["ps%d" % b])
                    em.op("pe", lambda e, b=b, i=i, tc=tc, tcl=tcl, half=half, th=th: e.matmul(
                        psum[b][:], lhsT=zcs[:, tc, 256 + half * 128:256 + (half + 1) * 128], rhs=SLb[i][:, tcl, :], start=False, stop=(th == 1 and tcl == 15)),
                        reads=["zcs", "SLb%d" % i], writes=["ps%d" % b])
        for half in range(2):
            b = bb[half]
            em.op("act", lambda e, b=b, half=half: e.activation(out=fst[half], in_=psum[b][:], func=AF.Identity), reads=["ps%d" % b], writes=["fst%d" % half])
            em.dma("sp", lambda e, half=half, fsl=fsl: e.dma_start(out=Y[256 + half * 128:256 + (half + 1) * 128, fsl], in_=fst[half]), "fst%dw" % half, reads=["fst%d" % half])

    em.barrier(); ar.reset()
    c = mk_consts(cx, l, "m")
    rwkv_prepass(cx, l, c)
    recs = []
    for fn, banks in ((mix_gla, [0, 1, 2]), (mix_rwkv, [3, 4, 5, 6, 7])):
        rec = Rec()
        sub = CX(); sub.__dict__.update(cx.__dict__)
        sub.em = rec
        st_ = [0]

        def pb(banks=banks, st_=st_):
            st_[0] = (st_[0] + 1) % len(banks)
            return banks[st_[0]]
        sub.pbank = pb
        fn(sub, l, c)
        recs.append(rec)
    merge_streams(em, recs)


def mk_consts(cx, l, tag):
    em, ar = cx.em, cx.ar
    c = CX()
    c.ident = ar.alloc([128], F32)
    c.ident4 = ar.alloc([512], F32)
    c.maskc = ar.alloc([8, 512], F32)
    c.prm = ar.alloc([cx.PRM_N], F32)
    c.rows = ar.alloc([cx.ROW_N], F32)
    c.lr64 = ar.alloc([768], F32); c.lr32 = ar.alloc([512], F32); c.lr16 = ar.alloc([256], F32)
    c.ones_row = ar.alloc([128], F32); c.ones_col = ar.alloc([4], F32)
    k = "K" + tag
    em.dma("sp", lambda e: e.dma_start(out=c.ident, in_=cx.ident_d), k + "a", writes=[k])
    for h in range(4):
        em.dma("sp", lambda e, h=h: e.dma_start(out=c.ident4[:, h * 128:(h + 1) * 128], in_=cx.ident_d), k + "b", writes=[k + "i4%d" % h])
    em.dma("sp", lambda e: e.dma_start(out=c.maskc, in_=cx.maskc_d), k + "c", writes=[k + "m"])
    em.dma("sp", lambda e, l=l: e.dma_start(out=c.prm, in_=cx.prm[l]), k + "d", writes=[k + "p"])
    em.dma("sp", lambda e, l=l: e.dma_start(out=c.rows[0:1, :], in_=cx.rows[l]), k + "e", writes=[k + "r"])
    em.dma("sp", lambda e, l=l: e.dma_start(out=c.lr64[0:64, :], in_=cx.lr64_d[l]), k + "f", writes=[k + "l64"])
    em.dma("sp", lambda e, l=l: e.dma_start(out=c.lr32[0:32, :], in_=cx.lr32_d[l]), k + "g", writes=[k + "l32"])
    em.dma("sp", lambda e, l=l: e.dma_start(out=c.lr16[0:16, :], in_=cx.lr16_d[l]), k + "h", writes=[k + "l16"])
    em.op("dve", lambda e: e.memset(c.ones_row, 1.0), writes=[k + "o1"])
    em.op("dve", lambda e: e.memset(c.ones_col, 1.0), writes=[k + "o2"])
    BC0 = 32
    p = c.prm
    c.psc = p[:, 0:2]; c.mu = p[:, 2:8]; c.hm = p[:, 8:14]; c.om = p[:, 14:20]
    c.kkw = p[:, BC0:BC0 + 256]; c.ka = p[:, BC0 + 256:BC0 + 512]; c.oka = p[:, BC0 + 512:BC0 + 768]
    c.rk = p[:, BC0 + 768:BC0 + 1024]; c.gn = p[:, BC0 + 1024:BC0 + 1280]; c.gln = p[:, BC0 + 1280:BC0 + 1536]
    em.op("dve", lambda e: e.tensor_scalar(out=c.hm, in0=c.mu, scalar1=0.5, scalar2=None, op0=ALU.mult), reads=[k + "p"], writes=[k + "p"])
    em.op("dve", lambda e: e.tensor_scalar(out=c.om, in0=c.mu, scalar1=-1.0, scalar2=1.0, op0=ALU.mult, op1=ALU.add), reads=[k + "p"], writes=[k + "p"])
    em.op("dve", lambda e: e.tensor_scalar(out=c.oka, in0=c.ka, scalar1=-1.0, scalar2=1.0, op0=ALU.mult, op1=ALU.add), reads=[k + "p"], writes=[k + "p"])
    em.barrier()
    return c


def mix_gla(cx, l, c):
    em, ar, psum, pbank, U, Y = cx.em, cx.ar, cx.psum, cx.pbank, cx.U, cx.Y
    S = [ar.alloc([4, 64], F32) for _ in range(2)]
    OG = ar.alloc([32, 256], F32)
    fmt = [ar.alloc([6, 128], F32) for _ in range(2)]
    cal = [ar.alloc([128], F32) for _ in range(2)]
    tm = [ar.alloc([512], F32) for _ in range(2)]
    go = ar.alloc([256], F32)
    ee = ar.alloc([128], F32); lg = ar.alloc([128], F32)
    Eq = ar.alloc([128], F32); Ek = ar.alloc([128], F32); Eb = ar.alloc([128], F32)
    qt = ar.alloc([128], F32); kh = ar.alloc([128], F32); kb = ar.alloc([128], F32)
    qT = ar.alloc([512], F32); kT = ar.alloc([512], F32)
    AT = ar.alloc([512], F32)
    etot = ar.alloc([4], F32)
    og = ar.alloc([256], F32); sq = ar.alloc([256], F32); ss = ar.alloc([4], F32); sgo = ar.alloc([256], F32)
    yd = ar.alloc([256], F32); ydT = [ar.alloc([2, 128], BF16) for _ in range(2)]
    Ufm = U[1536:2304, :].rearrange("(c p) t -> p c t", p=128)
    Yv = Y[768:1024, :].rearrange("(c p) t -> p c t", p=128)
    ab = lambda d: c.lr16[0:16, d * 128:(d + 1) * 128]
    abrow = lambda d: c.rows[0:1, 1024 + d * 128:1024 + (d + 1) * 128]
    for d in range(2):
        order = list(range(32)) if d == 0 else list(range(31, -1, -1))
        Sd = S[d]; Sk = "gS%d" % d
        for n, tc in enumerate(order):
            par = n % 2
            tsl = slice(tc * 128, (tc + 1) * 128)
            f_ = fmt[par]; fk = "gfmt%d" % par; ca = cal[par]; ck = "gcal%d" % par; t_ = tm[par]; tk = "gtm%d" % par
            em.dma("sp", lambda e, f_=f_, tsl=tsl: e.dma_start(out=f_, in_=Ufm[:, :, tsl]), fk, writes=[fk])
            em.dma("sp", lambda e, ca=ca, tsl=tsl, d=d: e.dma_start(out=ca[0:16, :], in_=U[2304 + 16 * d:2304 + 16 * d + 16, tsl]), ck, writes=[ck])
            if n == 0:
                em.dma("sp", lambda e, Sd=Sd, d=d, l=l: e.dma_start(out=Sd[0:32], in_=cx.sg0[l, d]), Sk + "i", writes=[Sk])
            elif n % 2 == 0:
                em.op("dve", lambda e, Sd=Sd: e.tensor_scalar(out=Sd[0:32], in0=Sd[0:32], scalar1=cx.c_kap[0:32], scalar2=None, op0=ALU.mult),
                      reads=[Sk], writes=[Sk])
            b = pbank()
            for cc in range(4):
                em.op("pe", lambda e, b=b, cc=cc, f_=f_: e.matmul(psum[b][:, cc * 128:(cc + 1) * 128], lhsT=f_[:, cc, :], rhs=c.ident, start=True, stop=True),
                      reads=[fk], writes=["ps%d" % b])
            em.op("act", lambda e, b=b, t_=t_: e.activation(out=t_, in_=psum[b][:], func=AF.Identity), reads=["ps%d" % b], writes=[tk])
            if d == 1:
                b = pbank()
                for cc in range(2):
                    em.op("pe", lambda e, b=b, cc=cc, f_=f_: e.matmul(psum[b][:, cc * 128:(cc + 1) * 128], lhsT=f_[:, 4 + cc, :], rhs=c.ident, start=True, stop=True),
                          reads=[fk], writes=["ps%d" % b])
                em.op("act", lambda e, b=b: e.activation(out=sgo, in_=psum[b][:, 0:256], func=AF.Silu), reads=["ps%d" % b], writes=["gsgo"])
            b = pbank()
            em.op("pe", lambda e, b=b, ca=ca, d=d: e.matmul(psum[b][:, 0:128], lhsT=ca[0:16, :], rhs=ab(d), start=True, stop=False), reads=[ck], writes=["ps%d" % b])
            em.op("pe", lambda e, b=b, d=d: e.matmul(psum[b][:, 0:128], lhsT=c.ones_row[0:1, :], rhs=abrow(d), start=False, stop=True), reads=[], writes=["ps%d" % b])
            em.op("act", lambda e, b=b: e.activation(out=ee, in_=psum[b][:, 0:128], func=AF.Exp, scale=-1.0), reads=["ps%d" % b], writes=["gee"])
            em.op("act", lambda e: e.activation(out=lg, in_=ee, func=AF.Ln, bias=1.0, scale=1.0), reads=["gee"], writes=["glg"])
            b = pbank()
            em.op("pe", lambda e, b=b, d=d: e.matmul(psum[b][:, 0:128], lhsT=c.maskc[:, d * 4 + 0, 0:128], rhs=lg, start=True, stop=True), reads=["glg"], writes=["ps%d" % b])
            em.op("pe", lambda e, b=b, d=d: e.matmul(psum[b][:, 128:256], lhsT=c.maskc[:, d * 4 + 3, 0:128], rhs=lg, start=True, stop=True), reads=["glg"], writes=["ps%d" % b])
            em.op("act", lambda e, b=b: e.activation(out=Eq, in_=psum[b][:, 0:128], func=AF.Exp, scale=-1.0 / 16), reads=["ps%d" % b], writes=["gEq"])
            em.op("act", lambda e, b=b: e.activation(out=Ek, in_=psum[b][:, 0:128], func=AF.Exp, scale=1.0 / 16), reads=["ps%d" % b], writes=["gEk"])
            em.op("act", lambda e, b=b: e.activation(out=Eb, in_=psum[b][:, 128:256], func=AF.Exp, scale=-1.0 / 16), reads=["ps%d" % b], writes=["gEb"])
            em.op("dve", lambda e, t_=t_: e.scalar_tensor_tensor(out=qt, in0=t_[:, 0:128], scalar=32 ** -0.5, in1=Eq, op0=ALU.mult, op1=ALU.mult), reads=[tk, "gEq"], writes=["gqt"])
            em.op("dve", lambda e, t_=t_: e.tensor_tensor(out=kh, in0=t_[:, 128:256], in1=Ek, op=ALU.mult), reads=[tk, "gEk"], writes=["gkh"])
            em.op("pool", lambda e, t_=t_: e.tensor_tensor(out=kb, in0=t_[:, 128:256], in1=Eb, op=ALU.mult), reads=[tk, "gEb"], writes=["gkb"])
            b = pbank()
            for h in range(4):
                em.op("pe", lambda e, b=b, h=h: e.matmul(psum[b][0:32, h * 128:(h + 1) * 128], lhsT=qt[:, h * 32:(h + 1) * 32], rhs=c.ident, start=True, stop=True),
                      reads=["gqt"], writes=["ps%d" % b])
            em.op("act", lambda e, b=b: e.activation(out=qT[0:32], in_=psum[b][0:32], func=AF.Identity), reads=["ps%d" % b], writes=["gqT"])
            b = pbank()
            for h in range(4):
                em.op("pe", lambda e, b=b, h=h: e.matmul(psum[b][0:32, h * 128:(h + 1) * 128], lhsT=kh[:, h * 32:(h + 1) * 32], rhs=c.ident, start=True, stop=True),
                      reads=["gkh"], writes=["ps%d" % b])
            em.op("dve", lambda e, b=b: e.tensor_copy(out=kT[0:32], in_=psum[b][0:32]), reads=["ps%d" % b], writes=["gkT"])
            b = pbank()
            for h in range(4):
                em.op("pe", lambda e, b=b, h=h: e.matmul(psum[b][:, h * 128:(h + 1) * 128], lhsT=kT[0:32, h * 128:(h + 1) * 128], rhs=qT[0:32, h * 128:(h + 1) * 128], start=True, stop=True),
                      reads=["gkT", "gqT"], writes=["ps%d" % b])
            em.op("dve", lambda e, b=b, d=d: e.tensor_tensor(out=AT, in0=psum[b][:], in1=c.maskc[:, d * 4 + 0, :], op=ALU.mult), reads=["ps%d" % b], writes=["gAT"])
            b = pbank()
            for h in range(4):
                em.op("pe", lambda e, b=b, h=h: e.matmul(psum[b][0:32, h:h + 1], lhsT=lg[:, h * 32:(h + 1) * 32], rhs=c.ones_col[:, 0:1], start=True, stop=True),
                      reads=["glg"], writes=["ps%d" % b])
            em.op("act", lambda e, b=b: e.activation(out=etot[0:32], in_=psum[b][0:32, 0:4], func=AF.Exp, scale=-1.0 / 16), reads=["ps%d" % b], writes=["getot"])
            bO = pbank()
            for h in range(4):
                em.op("pe", lambda e, bO=bO, h=h, t_=t_: e.matmul(psum[bO][:, h * 64:(h + 1) * 64], lhsT=AT[:, h * 128:(h + 1) * 128], rhs=t_[:, 256 + h * 64:256 + (h + 1) * 64], start=True, stop=False),
                      reads=["gAT", tk], writes=["ps%d" % bO])
                em.op("pe", lambda e, bO=bO, h=h, Sd=Sd: e.matmul(psum[bO][:, h * 64:(h + 1) * 64], lhsT=qT[0:32, h * 128:(h + 1) * 128], rhs=Sd[0:32, h, :], start=False, stop=True),
                      reads=["gqT", Sk], writes=["ps%d" % bO])
            bS = pbank()
            for h in range(4):
                em.op("pe", lambda e, bS=bS, h=h, t_=t_: e.matmul(psum[bS][0:32, h * 64:(h + 1) * 64], lhsT=kb[:, h * 32:(h + 1) * 32], rhs=t_[:, 256 + h * 64:256 + (h + 1) * 64], start=True, stop=True),
                      reads=["gkb", tk], writes=["ps%d" % bS])
            for h in range(4):
                em.op("dve", lambda e, bS=bS, h=h, Sd=Sd: e.scalar_tensor_tensor(out=Sd[0:32, h, :], in0=Sd[0:32, h, :], scalar=etot[0:32, h:h + 1], in1=psum[bS][0:32, h * 64:(h + 1) * 64],
                                                                            op0=ALU.mult, op1=ALU.add), reads=[Sk, "getot", "ps%d" % bS], writes=[Sk])
            if n % 2 == 1:
                blk = tc // 2
                em.dma("sp", lambda e, Sd=Sd, d=d, l=l, blk=blk: e.dma_start(out=cx.osg[l, d, blk], in_=Sd[0:32]), Sk + "o", reads=[Sk], final=True)
            if d == 0:
                em.op("act", lambda e, bO=bO, tc=tc: e.activation(out=OG[:, tc, :], in_=psum[bO][:, 0:256], func=AF.Identity), reads=["ps%d" % bO], writes=["gOG"])
            else:
                em.op("dve", lambda e, bO=bO, tc=tc: e.tensor_tensor(out=og, in0=psum[bO][:, 0:256], in1=OG[:, tc, :], op=ALU.add), reads=["ps%d" % bO, "gOG"], writes=["gog"])
                em.op("pool", lambda e: e.tensor_tensor(out=sq, in0=og, in1=og, op=ALU.mult), reads=["gog"], writes=["gsq"])
                em.op("dve", lambda e: e.tensor_reduce(out=ss, in_=sq.rearrange("p (h v) -> p h v", v=64), axis=AX.X, op=ALU.add), reads=["gsq"], writes=["gss"])
                em.op("act", lambda e: e.activation(out=ss, in_=ss, func=AF.Sqrt, bias=cx.c_eps, scale=1.0 / 64), reads=["gss"], writes=["gss"])
                em.op("dve", lambda e: e.reciprocal(out=ss, in_=ss), reads=["gss"], writes=["gss"])
                em.op("dve", lambda e: e.tensor_tensor(out=yd.rearrange("p (h v) -> p h v", v=64), in0=og.rearrange("p (h v) -> p h v", v=64),
                                                       in1=ss.unsqueeze(2).to_broadcast([128, 4, 64]), op=ALU.mult), reads=["gog", "gss"], writes=["gyd"])
                em.op("pool", lambda e: e.tensor_tensor(out=yd, in0=yd, in1=c.gln, op=ALU.mult), reads=["gyd"], writes=["gyd"])
                em.op("pool", lambda e: e.tensor_tensor(out=yd, in0=yd, in1=sgo, op=ALU.mult), reads=["gyd", "gsgo"], writes=["gyd"])
                b = pbank()
                for cc in range(2):
                    em.op("pe", lambda e, b=b, cc=cc: e.matmul(psum[b][:, cc * 128:(cc + 1) * 128], lhsT=yd[:, cc * 128:(cc + 1) * 128], rhs=c.ident, start=True, stop=True),
                          reads=["gyd"], writes=["ps%d" % b])
                yT_ = ydT[par]; yk = "gydT%d" % par
                em.op("act", lambda e, b=b, yT_=yT_: e.activation(out=yT_, in_=psum[b][:, 0:256].rearrange("p (c t) -> p c t", t=128), func=AF.Identity), reads=["ps%d" % b], writes=[yk])
                em.dma("sp", lambda e, yT_=yT_, tsl=tsl: e.dma_start(out=Yv[:, :, tsl], in_=yT_), yk + "w", reads=[yk])


def rwkv_prepass(cx, l, c):
    em, ar, psum, pbank, U, Y = cx.em, cx.ar, cx.psum, cx.pbank, cx.U, cx.Y
    ZM = cx.ZM
    mark = ar.off
    zt = [ar.alloc([6, 514], F32) for _ in range(2)]
    sh = ar.alloc([6, 512], F32)
    zo = [ar.alloc([6, 512], F32) for _ in range(2)]
    Uz = U[512:1280, :].rearrange("(c p) t -> p c t", p=128)
    ZMv = ZM.rearrange("(c p) t -> p c t", p=128)
    for j in range(NT):
        z = zt[j % 2]; zk = "rzt%d" % (j % 2); o_ = zo[j % 2]; ok = "rzo%d" % (j % 2)
        t0 = j * 512
        if j == 0:
            em.op("pool", lambda e, z=z: e.memset(z[:, :, 0:1], 0.0), writes=[zk])
            em.dma("sp", lambda e, z=z: e.dma_start(out=z[:, :, 1:514], in_=Uz[:, :, 0:513]), zk, writes=[zk])
        elif j == NT - 1:
            em.op("pool", lambda e, z=z: e.memset(z[:, :, 513:514], 0.0), writes=[zk])
            em.dma("sp", lambda e, z=z, t0=t0: e.dma_start(out=z[:, :, 0:513], in_=Uz[:, :, t0 - 1:t0 + 512]), zk, writes=[zk])
        else:
            em.dma("sp", lambda e, z=z, t0=t0: e.dma_start(out=z, in_=Uz[:, :, t0 - 1:t0 + 513]), zk, writes=[zk])
        em.op("dve", lambda e, z=z: e.tensor_tensor(out=sh, in0=z[:, :, 0:512], in1=z[:, :, 2:514], op=ALU.add), reads=[zk], writes=["rsh"])
        em.op("dve", lambda e, z=z: e.scalar_tensor_tensor(out=sh[:, :, 0:512:256], in0=z[:, :, 0:512:256], scalar=cx.c_nk, in1=sh[:, :, 0:512:256], op0=ALU.mult, op1=ALU.add),
              reads=[zk, "rsh"], writes=["rsh"])
        em.op("dve", lambda e, z=z: e.scalar_tensor_tensor(out=sh[:, :, 255:512:256], in0=z[:, :, 257:514:256], scalar=cx.c_nk, in1=sh[:, :, 255:512:256], op0=ALU.mult, op1=ALU.add),
              reads=[zk, "rsh"], writes=["rsh"])
        for cc in range(6):
            em.op("pool", lambda e, cc=cc: e.tensor_scalar(out=sh[:, cc, :], in0=sh[:, cc, :], scalar1=c.hm[:, cc:cc + 1], scalar2=None, op0=ALU.mult), reads=["rsh"], writes=["rsh"])
            em.op("dve", lambda e, cc=cc, z=z, o_=o_: e.scalar_tensor_tensor(out=o_[:, cc, :], in0=z[:, cc, 1:513], scalar=c.om[:, cc:cc + 1], in1=sh[:, cc, :], op0=ALU.mult, op1=ALU.add),
                  reads=[zk, "rsh"], writes=[ok])
        em.dma("sp", lambda e, o_=o_, t0=t0: e.dma_start(out=ZMv[:, :, t0:t0 + 512], in_=o_), ok + "w", reads=[ok])
    em.barrier()
    ar.off = mark


def mix_rwkv(cx, l, c):
    em, ar, psum, pbank, U, Y = cx.em, cx.ar, cx.psum, cx.pbank, cx.U, cx.Y
    CD = 0.606531
    ZM = cx.ZM
    ZMv = ZM.rearrange("(c p) t -> p c t", p=128)
    A = lambda shp: ar.alloc(shp, F32)
    H = [A([4, 64]) for _ in range(2)]
    OR = A([32, 256]); BON = A([32, 4])
    zc = [A([6, 128]) for _ in range(2)]
    cw = [A([128]) for _ in range(2)]; ca = [A([128]) for _ in range(2)]; cg = [A([128]) for _ in range(2)]
    rk_ = A([512]); v_ = A([256])
    tcw = A([128]); sg = A([256]); aa = A([256])
    kw = A([256]); sq = A([256]); ss = A([4]); kk = A([256]); t1 = A([256]); kdir = A([256]); bb = A([256])
    Ex = A([256]); E1 = A([256]); E2 = A([256]); E3 = A([256]); E4 = A([256])
    kkt = A([256]); bh = A([256]); kh = A([256]); rt = A([256]); Kbar = A([256]); Bbn = A([256]); t2 = A([256]); bsum = A([4])
    kktT = A([512]); bhT = A([512]); khT = A([512]); rtT = A([512])
    Mp = [A([512]) for _ in range(2)]; Np = [A([512]) for _ in range(2)]; X = A([512])
    AkkT = A([512]); ArkT = A([512]); ArbT = A([512])
    WT = A([512]); Y0 = A([256]); U0 = A([256]); Uu = A([256]); pc = A([4])
    o = A([256]); s1 = A([4]); cen = A([256]); s2 = A([4]); bon = A([4]); sgc = A([128]); yc = A([256])
    ycT = [ar.alloc([2, 128], BF16) for _ in range(2)]
    Yv = Y[512:768, :].rearrange("(c p) t -> p c t", p=128)
    bw = lambda d: c.lr64[0:64, d * 256:(d + 1) * 256]
    bg = c.lr64[0:64, 512:768]
    ba = lambda d: c.lr32[0:32, d * 256:(d + 1) * 256]
    w0row = lambda d: c.rows[0:1, d * 256:(d + 1) * 256]
    a0row = lambda d: c.rows[0:1, 512 + d * 256:512 + (d + 1) * 256]
    h3 = lambda ap: ap.rearrange("p (h v) -> p h v", v=64)
    bc4 = lambda ap4: ap4.unsqueeze(2).to_broadcast([128, 4, 64])

    def ew(eng, fn, reads, writes):
        em.op(eng, fn, reads=reads, writes=writes)

    def mm4(b, lhs, rhs, reads, npart=128, width=128):
        for h in range(4):
            em.op("pe", lambda e, h=h: e.matmul(psum[b][0:npart, h * width:(h + 1) * width], lhsT=lhs(h), rhs=rhs(h), start=True, stop=True),
                  reads=reads, writes=["ps%d" % b])

    blk128 = lambda t: (lambda h: t[:, h * 128:(h + 1) * 128])
    blk64 = lambda t: (lambda h: t[:, h * 64:(h + 1) * 64])
    blkT = lambda t: (lambda h: t[0:64, h * 128:(h + 1) * 128])

    for d in range(2):
        order = list(range(32)) if d == 0 else list(range(31, -1, -1))
        Hd = H[d]; Hk = "rH%d" % d
        for n, tc in enumerate(order):
            par = n % 2
            tsl = slice(tc * 128, (tc + 1) * 128)
            z = zc[par]; zk = "rzc%d" % par; cw_ = cw[par]; cwk = "rcw%d" % par; ca_ = ca[par]; cak = "rca%d" % par; cg_ = cg[par]; cgk = "rcg%d" % par
            em.dma("sp", lambda e, z=z, tsl=tsl: e.dma_start(out=z, in_=ZMv[:, :, tsl]), zk, writes=[zk])
            em.dma("sp", lambda e, cw_=cw_, tsl=tsl, d=d: e.dma_start(out=cw_[0:64, :], in_=U[1280 + 64 * d:1280 + 64 * d + 64, tsl]), cwk, writes=[cwk])
            em.dma("sp", lambda e, ca_=ca_, tsl=tsl, d=d: e.dma_start(out=ca_[0:32, :], in_=U[1408 + 32 * d:1408 + 32 * d + 32, tsl]), cak, writes=[cak])
            if d == 1:
                em.dma("sp", lambda e, cg_=cg_, tsl=tsl: e.dma_start(out=cg_[0:64, :], in_=U[1472:1536, tsl]), cgk, writes=[cgk])
            if n == 0:
                em.dma("sp", lambda e, Hd=Hd, d=d, l=l: e.dma_start(out=Hd[0:64], in_=cx.sr0[l, d]), Hk + "i", writes=[Hk])
            elif n % 2 == 0:
                ew("dve", lambda e, Hd=Hd: e.tensor_scalar(out=Hd[0:64], in0=Hd[0:64], scalar1=cx.c_kap[0:64], scalar2=None, op0=ALU.mult), [Hk], [Hk])
            b = pbank()
            for cc in range(4):
                em.op("pe", lambda e, b=b, cc=cc, z=z: e.matmul(psum[b][:, cc * 128:(cc + 1) * 128], lhsT=z[:, cc, :], rhs=c.ident, start=True, stop=True), reads=[zk], writes=["ps%d" % b])
            ew("act", lambda e, b=b: e.activation(out=rk_, in_=psum[b][:], func=AF.Identity), ["ps%d" % b], ["rrk"])
            b = pbank()
            for cc in range(2):
                em.op("pe", lambda e, b=b, cc=cc, z=z: e.matmul(psum[b][:, cc * 128:(cc + 1) * 128], lhsT=z[:, 4 + cc, :], rhs=c.ident, start=True, stop=True), reads=[zk], writes=["ps%d" % b])
            ew("act", lambda e, b=b: e.activation(out=v_, in_=psum[b][:, 0:256], func=AF.Identity), ["ps%d" % b], ["rv"])
            r_ = rk_[:, 0:256]; k_ = rk_[:, 256:512]
            ew("act", lambda e, cw_=cw_: e.activation(out=tcw[0:64], in_=cw_[0:64], func=AF.Tanh), [cwk], ["rtcw"])
            b = pbank()
            em.op("pe", lambda e, b=b, d=d: e.matmul(psum[b][:, 0:256], lhsT=tcw[0:64, :], rhs=bw(d), start=True, stop=False), reads=["rtcw"], writes=["ps%d" % b])
            em.op("pe", lambda e, b=b, d=d: e.matmul(psum[b][:, 0:256], lhsT=c.ones_row[0:1, :], rhs=w0row(d), start=False, stop=True), reads=[], writes=["ps%d" % b])
            em.op("pe", lambda e, b=b, d=d, ca_=ca_: e.matmul(psum[b][:, 256:512], lhsT=ca_[0:32, :], rhs=ba(d), start=True, stop=False), reads=[cak], writes=["ps%d" % b])
            em.op("pe", lambda e, b=b, d=d: e.matmul(psum[b][:, 256:512], lhsT=c.ones_row[0:1, :], rhs=a0row(d), start=False, stop=True), reads=[], writes=["ps%d" % b])
            ew("act", lambda e, b=b: e.activation(out=sg, in_=psum[b][:, 0:256], func=AF.Sigmoid), ["ps%d" % b], ["rsg"])
            ew("act", lambda e, b=b: e.activation(out=aa, in_=psum[b][:, 256:512], func=AF.Sigmoid), ["ps%d" % b], ["raa"])
            ew("dve", lambda e: e.tensor_tensor(out=kw, in0=k_, in1=c.kkw, op=ALU.mult), ["rrk"], ["rkw"])
            ew("pool", lambda e: e.tensor_tensor(out=sq, in0=kw, in1=kw, op=ALU.mult), ["rkw"], ["rsq"])
            ew("dve", lambda e: e.tensor_reduce(out=ss, in_=h3(sq), axis=AX.X, op=ALU.add), ["rsq"], ["rss"])
            ew("act", lambda e: e.activation(out=ss, in_=ss, func=AF.Sqrt, bias=cx.c_eps, scale=1.0), ["rss"], ["rss"])
            ew("dve", lambda e: e.reciprocal(out=ss, in_=ss), ["rss"], ["rss"])
            ew("dve", lambda e: e.tensor_tensor(out=h3(kk), in0=h3(kw), in1=bc4(ss), op=ALU.mult), ["rkw", "rss"], ["rkk"])
            ew("pool", lambda e: e.tensor_tensor(out=t1, in0=aa, in1=c.ka, op=ALU.mult), ["raa"], ["rt1"])
            ew("pool", lambda e: e.tensor_tensor(out=t1, in0=t1, in1=c.oka, op=ALU.add), ["rt1"], ["rt1"])
            ew("pool", lambda e: e.tensor_tensor(out=kdir, in0=k_, in1=t1, op=ALU.mult), ["rrk", "rt1"], ["rkdir"])
            ew("dve", lambda e: e.tensor_tensor(out=bb, in0=kk, in1=aa, op=ALU.mult), ["rkk", "raa"], ["rbb"])
            b = pbank()
            em.op("pe", lambda e, b=b, d=d: e.matmul(psum[b][:, 0:256], lhsT=c.maskc[:, d * 4 + 0, 0:128], rhs=sg, start=True, stop=True), reads=["rsg"], writes=["ps%d" % b])
            em.op("pe", lambda e, b=b, d=d: e.matmul(psum[b][:, 256:512], lhsT=c.maskc[:, d * 4 + 3, 0:128], rhs=sg, start=True, stop=True), reads=["rsg"], writes=["ps%d" % b])
            ew("dve", lambda e, b=b: e.tensor_tensor(out=Ex, in0=psum[b][:, 0:256], in1=sg, op=ALU.subtract), ["ps%d" % b, "rsg"], ["rEx"])
            ew("act", lambda e: e.activation(out=E1, in_=Ex, func=AF.Exp, scale=-CD), ["rEx"], ["rE1"])
            ew("act", lambda e, b=b: e.activation(out=E2, in_=psum[b][:, 0:256], func=AF.Exp, scale=CD), ["ps%d" % b], ["rE2"])
            ew("act", lambda e, b=b: e.activation(out=E3, in_=psum[b][:, 0:256], func=AF.Exp, scale=-CD), ["ps%d" % b], ["rE3"])
            ew("act", lambda e, b=b: e.activation(out=E4, in_=psum[b][:, 256:512], func=AF.Exp, scale=-CD), ["ps%d" % b], ["rE4"])
            ew("dve", lambda e: e.tensor_tensor(out=kkt, in0=kk, in1=E1, op=ALU.mult), ["rkk", "rE1"], ["rkkt"])
            ew("pool", lambda e: e.tensor_tensor(out=bh, in0=bb, in1=E2, op=ALU.mult), ["rbb", "rE2"], ["rbh"])
            ew("dve", lambda e: e.tensor_tensor(out=kh, in0=kdir, in1=E2, op=ALU.mult), ["rkdir", "rE2"], ["rkh"])
            ew("pool", lambda e: e.tensor_tensor(out=rt, in0=r_, in1=E3, op=ALU.mult), ["rrk", "rE3"], ["rrt"])
            ew("dve", lambda e: e.tensor_tensor(out=Kbar, in0=kdir, in1=E4, op=ALU.mult), ["rkdir", "rE4"], ["rKbar"])
            ew("dve", lambda e: e.scalar_tensor_tensor(out=Bbn, in0=bb, scalar=-1.0, in1=E4, op0=ALU.mult, op1=ALU.mult), ["rbb", "rE4"], ["rBbn"])
            ew("pool", lambda e: e.tensor_tensor(out=t2, in0=r_, in1=kdir, op=ALU.mult), ["rrk", "rkdir"], ["rt2"])
            ew("pool", lambda e: e.tensor_tensor(out=t2, in0=t2, in1=c.rk, op=ALU.mult), ["rt2"], ["rt2"])
            if d == 0:
                ew("dve", lambda e, tc=tc: e.tensor_reduce(out=BON[:, tc, :], in_=h3(t2), axis=AX.X, op=ALU.add), ["rt2"], ["rBON"])
            else:
                ew("dve", lambda e: e.tensor_reduce(out=bsum, in_=h3(t2), axis=AX.X, op=ALU.add), ["rt2"], ["rbsum"])
            for ti, (src, sk, dst, dk_) in enumerate(((kkt, "rkkt", kktT, "rkktT"), (bh, "rbh", bhT, "rbhT"), (kh, "rkh", khT, "rkhT"), (rt, "rrt", rtT, "rrtT"))):
                b = pbank()
                mm4(b, blk64(src), lambda h: c.ident, [sk], npart=64)
                if ti % 2 == 0:
                    ew("act", lambda e, b=b, dst=dst: e.activation(out=dst[0:64], in_=psum[b][0:64], func=AF.Identity), ["ps%d" % b], [dk_])
                else:
                    ew("dve", lambda e, b=b, dst=dst: e.tensor_copy(out=dst[0:64], in_=psum[b][0:64]), ["ps%d" % b], [dk_])
            def amat(dst, dkey, lhsT_t, lk, rhs_t, rkey, kind, d=d):
                b = pbank()
                mm4(b, blkT(lhsT_t), blkT(rhs_t), [lk, rkey])
                ew("dve", lambda e, b=b: e.tensor_tensor(out=dst, in0=psum[b][:], in1=c.maskc[:, d * 4 + kind, :], op=ALU.mult), ["ps%d" % b], [dkey])
            amat(Mp[0], "rMp0", bhT, "rbhT", kktT, "rkktT", 2)
            amat(Np[0], "rNp0", kktT, "rkktT", bhT, "rbhT", 3)
            amat(AkkT, "rAkkT", khT, "rkhT", kktT, "rkktT", 2)
            amat(ArkT, "rArkT", khT, "rkhT", rtT, "rrtT", 0)
            amat(ArbT, "rArbT", bhT, "rbhT", rtT, "rrtT", 1)
            ew("pool", lambda e: e.tensor_tensor(out=X, in0=c.ident4, in1=Mp[0], op=ALU.subtract), ["rMp0"], ["rX"])
            cur = 0
            for jj in range(1, 7):
                nxt = 1 - cur
                if jj < 6:
                    b = pbank()
                    mm4(b, blk128(Np[cur]), blk128(Mp[cur]), ["rNp%d" % cur, "rMp%d" % cur])
                    ew("act", lambda e, b=b, nxt=nxt: e.activation(out=Mp[nxt], in_=psum[b][:], func=AF.Identity), ["ps%d" % b], ["rMp%d" % nxt])
                b = pbank()
                mm4(b, blk128(Mp[cur]), blk128(Np[cur]), ["rNp%d" % cur, "rMp%d" % cur])
                ew("dve", lambda e, b=b, nxt=nxt: e.tensor_copy(out=Np[nxt], in_=psum[b][:]), ["ps%d" % b], ["rNp%d" % nxt])
                b = pbank()
                mm4(b, blk128(Np[nxt]), blk128(X), ["rNp%d" % nxt, "rX"])
                ew("dve", lambda e, b=b: e.tensor_tensor(out=X, in0=X, in1=psum[b][:], op=ALU.add), ["ps%d" % b, "rX"], ["rX"])
                cur = nxt
            b = pbank()
            mm4(b, blk64(kkt), blk128(X), ["rkkt", "rX"], npart=64)
            ew("act", lambda e, b=b: e.activation(out=WT[0:64], in_=psum[b][0:64], func=AF.Identity), ["ps%d" % b], ["rWT"])
            b = pbank()
            mm4(b, blk128(AkkT), blk64(v_), ["rAkkT", "rv"], width=64)
            ew("act", lambda e, b=b: e.activation(out=Y0, in_=psum[b][:, 0:256], func=AF.Identity), ["ps%d" % b], ["rY0"])
            b = pbank()
            mm4(b, blk128(X), blk64(Y0), ["rX", "rY0"], width=64)
            ew("dve", lambda e, b=b: e.tensor_copy(out=U0, in_=psum[b][:, 0:256]), ["ps%d" % b], ["rU0"])
            b = pbank()
            mm4(b, blk64(sg), lambda h: c.ones_col[:, 0:1], ["rsg"], npart=64, width=1)
            ew("act", lambda e, b=b: e.activation(out=pc[0:64], in_=psum[b][0:64, 0:4], func=AF.Exp, scale=-CD), ["ps%d" % b], ["rpc"])
            b = pbank()
            mm4(b, blkT(WT), lambda h, Hd=Hd: Hd[0:64, h, :], ["rWT", Hk], width=64)
            ew("dve", lambda e, b=b: e.tensor_tensor(out=Uu, in0=psum[b][:, 0:256], in1=U0, op=ALU.add), ["ps%d" % b, "rU0"], ["rUu"])
            bO = pbank()
            for h in range(4):
                osl = slice(h * 64, (h + 1) * 64)
                em.op("pe", lambda e, h=h, osl=osl, Hd=Hd, bO=bO: e.matmul(psum[bO][:, osl], lhsT=rtT[0:64, h * 128:(h + 1) * 128], rhs=Hd[0:64, h, :], start=True, stop=False), reads=["rrtT", Hk], writes=["ps%d" % bO])
                em.op("pe", lambda e, h=h, osl=osl, bO=bO: e.matmul(psum[bO][:, osl], lhsT=ArkT[:, h * 128:(h + 1) * 128], rhs=v_[:, osl], start=False, stop=False), reads=["rArkT", "rv"], writes=["ps%d" % bO])
                em.op("pe", lambda e, h=h, osl=osl, bO=bO: e.matmul(psum[bO][:, osl], lhsT=ArbT[:, h * 128:(h + 1) * 128], rhs=Uu[:, osl], start=False, stop=True), reads=["rArbT", "rUu"], writes=["ps%d" % bO])
            bH = pbank()
            for h in range(4):
                osl = slice(h * 64, (h + 1) * 64)
                em.op("pe", lambda e, h=h, osl=osl, bH=bH: e.matmul(psum[bH][0:64, osl], lhsT=Kbar[:, osl], rhs=v_[:, osl], start=True, stop=False), reads=["rKbar", "rv"], writes=["ps%d" % bH])
                em.op("pe", lambda e, h=h, osl=osl, bH=bH: e.matmul(psum[bH][0:64, osl], lhsT=Bbn[:, osl], rhs=Uu[:, osl], start=False, stop=True), reads=["rBbn", "rUu"], writes=["ps%d" % bH])
            for h in range(4):
                ew("dve", lambda e, h=h, Hd=Hd, bH=bH: e.scalar_tensor_tensor(out=Hd[0:64, h, :], in0=Hd[0:64, h, :], scalar=pc[0:64, h:h + 1], in1=psum[bH][0:64, h * 64:(h + 1) * 64], op0=ALU.mult, op1=ALU.add),
                   [Hk, "rpc", "ps%d" % bH], [Hk])
            if n % 2 == 1:
                blk = tc // 2
                em.dma("sp", lambda e, Hd=Hd, d=d, l=l, blk=blk: e.dma_start(out=cx.osr[l, d, blk], in_=Hd[0:64]), Hk + "o", reads=[Hk], final=True)
            if d == 0:
                ew("act", lambda e, tc=tc, bO=bO: e.activation(out=OR[:, tc, :], in_=psum[bO][:, 0:256], func=AF.Identity), ["ps%d" % bO], ["rOR"])
                continue
            ew("dve", lambda e, tc=tc, bO=bO: e.tensor_tensor(out=o, in0=psum[bO][:, 0:256], in1=OR[:, tc, :], op=ALU.add), ["ps%d" % bO, "rOR"], ["ro"])
            ew("dve", lambda e: e.tensor_reduce(out=s1, in_=h3(o), axis=AX.X, op=ALU.add), ["ro"], ["rs1"])
            ew("dve", lambda e: e.tensor_scalar(out=s1, in0=s1, scalar1=1.0 / 64, scalar2=None, op0=ALU.mult), ["rs1"], ["rs1"])
            ew("dve", lambda e: e.tensor_tensor(out=h3(cen), in0=h3(o), in1=bc4(s1), op=ALU.subtract), ["ro", "rs1"], ["rcen"])
            ew("pool", lambda e: e.tensor_tensor(out=sq, in0=cen, in1=cen, op=ALU.mult), ["rcen"], ["rsq"])
            ew("dve", lambda e: e.tensor_reduce(out=s2, in_=h3(sq), axis=AX.X, op=ALU.add), ["rsq"], ["rs2"])
            ew("act", lambda e: e.activation(out=s2, in_=s2, func=AF.Sqrt, bias=cx.c_gneps, scale=1.0 / 64), ["rs2"], ["rs2"])
            ew("dve", lambda e: e.reciprocal(out=s2, in_=s2), ["rs2"], ["rs2"])
            ew("dve", lambda e: e.tensor_tensor(out=h3(cen), in0=h3(cen), in1=bc4(s2), op=ALU.mult), ["rcen", "rs2"], ["rcen"])
            ew("pool", lambda e: e.tensor_tensor(out=cen, in0=cen, in1=c.gn, op=ALU.mult), ["rcen"], ["rcen"])
            ew("dve", lambda e, tc=tc: e.tensor_tensor(out=bon, in0=BON[:, tc, :], in1=bsum, op=ALU.add), ["rBON", "rbsum"], ["rbon"])
            ew("dve", lambda e: e.tensor_tensor(out=h3(t2), in0=h3(v_), in1=bc4(bon), op=ALU.mult), ["rv", "rbon", "rt2"], ["rt2"])
            ew("pool", lambda e: e.tensor_tensor(out=cen, in0=cen, in1=t2, op=ALU.add), ["rcen", "rt2"], ["rcen"])
            ew("act", lambda e, cg_=cg_: e.activation(out=sgc[0:64], in_=cg_[0:64], func=AF.Sigmoid), [cgk], ["rsgc"])
            b = pbank()
            em.op("pe", lambda e, b=b: e.matmul(psum[b][:, 0:256], lhsT=sgc[0:64, :], rhs=bg, start=True, stop=True), reads=["rsgc"], writes=["ps%d" % b])
            ew("dve", lambda e, b=b: e.tensor_tensor(out=yc, in0=psum[b][:, 0:256], in1=cen, op=ALU.mult), ["ps%d" % b, "rcen"], ["ryc"])
            b = pbank()
            for cc in range(2):
                em.op("pe", lambda e, b=b, cc=cc: e.matmul(psum[b][:, cc * 128:(cc + 1) * 128], lhsT=yc[:, cc * 128:(cc + 1) * 128], rhs=c.ident, start=True, stop=True), reads=["ryc"], writes=["ps%d" % b])
            yT_ = ycT[par]; yk = "rycT%d" % par
            ew("act", lambda e, b=b, yT_=yT_: e.activation(out=yT_, in_=psum[b][:, 0:256].rearrange("p (c t) -> p c t", t=128), func=AF.Identity), ["ps%d" % b], [yk])
            em.dma("sp", lambda e, yT_=yT_, tsl=tsl: e.dma_start(out=Yv[:, :, tsl], in_=yT_), yk + "w", reads=[yk])


_NC = None
_EM = None
_LAST = None


def kernel(**inputs):
    global _NC
    f = lambda k: np.ascontiguousarray(np.asarray(inputs[k], dtype=np.float32))
    x_prompt = f("x_prompt"); x_sample = f("x_sample"); c = f("c"); c_ctx = f("c_ctx")
    jobs = []
    ntok = x_sample.shape[1]
    rows = ntok // 64
    rr, cc = np.meshgrid(np.arange(rows, dtype=np.float32), np.arange(64, dtype=np.float32), indexing="ij")
    rr = rr.reshape(-1); cc = cc.reshape(-1)
    quarter = D // 4
    omega = (1.0 / (np.float32(10000.0) ** (np.arange(quarter, dtype=np.float32) / np.float32(quarter)))).astype(np.float32)
    arr = rr[:, None] * omega; acc = cc[:, None] * omega
    pos = np.concatenate([np.sin(arr), np.cos(arr), np.sin(acc), np.cos(acc)], axis=-1).astype(np.float32)
    lay8 = lambda v: np.ascontiguousarray(v.reshape(-1, 128).T)
    common = {
        "ada_w": f("ada_w"),
        "ada_b": np.ascontiguousarray(f("ada_b").reshape(DEPTH, 48, 128).transpose(0, 2, 1)),
        "n1g": np.ascontiguousarray(f("norm1_g").reshape(DEPTH, 8, 128).transpose(0, 2, 1)),
        "n2g": np.ascontiguousarray(f("norm2_g").reshape(DEPTH, 8, 128).transpose(0, 2, 1)),
        "fg": lay8(f("final_g")),
        "w_in": f("w_in"), "w_br": f("w_branch").reshape(DEPTH, D, D), "w_out": f("w_out"),
        "w1": f("mlp_w1"), "w2": f("mlp_w2"),
    }
    import ml_dtypes
    g = lambda k: f(k)
    def bcast(v):
        return np.broadcast_to(v.reshape(1, -1), (128, v.size))
    prm = np.zeros((DEPTH, 128, 1568), np.float32); rows_ = np.zeros((DEPTH, 1, 1280), np.float32)
    lr64 = np.zeros((DEPTH, 64, 768), np.float32); lr32 = np.zeros((DEPTH, 32, 512), np.float32); lr16 = np.zeros((DEPTH, 16, 256), np.float32)
    for l in range(DEPTH):
        prm[l, :, 0:2] = g("pool_scale")[l].reshape(2, 128).T
        prm[l, :, 2:8] = g("rwkv_mu")[l].reshape(6, 128).T
        for i, nm in enumerate(("rwkv_kk", "rwkv_ka", None, "rwkv_rk", "rwkv_gn", "gla_norm")):
            if nm is not None:
                prm[l, :, 32 + i * 256:32 + (i + 1) * 256] = bcast(g(nm)[l])
        rows_[l, 0, 0:512] = g("rwkv_w0")[l].reshape(-1); rows_[l, 0, 512:1024] = g("rwkv_a0")[l].reshape(-1)
        rows_[l, 0, 1024:1280] = g("gla_abias")[l].reshape(-1)
        lr64[l, :, 0:256] = g("rwkv_bw")[l, 0]; lr64[l, :, 256:512] = g("rwkv_bw")[l, 1]; lr64[l, :, 512:768] = g("rwkv_bg")[l]
        lr32[l, :, 0:256] = g("rwkv_ba")[l, 0]; lr32[l, :, 256:512] = g("rwkv_ba")[l, 1]
        lr16[l, :, 0:128] = g("gla_ab")[l, 0]; lr16[l, :, 128:256] = g("gla_ab")[l, 1]
    ii = np.arange(128)[:, None]; jj = np.arange(128)[None, :]
    maskc = np.zeros((128, 8, 512), np.float32)
    for d_ in range(2):
        incl = (ii <= jj) if d_ == 0 else (ii >= jj)
        strict = (ii < jj) if d_ == 0 else (ii > jj)
        after = (ii > jj) if d_ == 0 else (ii < jj)
        for kind, m in enumerate((incl, incl, strict, after)):
            mm_ = np.tile(m.astype(np.float32), (1, 4))
            maskc[:, d_ * 4 + kind, :] = -mm_ if kind == 1 else mm_
    ch = np.arange(256); gch = ch // 64; cidx = ch % 64
    kk_ = np.arange(64)
    csm = np.zeros((256, 512), np.float64)
    for gi in range(4):
        rws = np.where(gch == gi)[0]
        ang = 2 * np.pi * np.outer(cidx[rws], kk_) / 64.0
        csm[np.ix_(rws, gi * 64 + kk_)] = np.cos(ang) / 8.0
        csm[np.ix_(rws, 256 + gi * 64 + kk_)] = -np.sin(ang) / 8.0
    csm = csm.reshape(2, 128, 512).astype(np.float32)

    def job_consts(L):
        tt = np.arange(T)
        s_ = tt % L
        invc = np.zeros((2, 128, T), np.float32)
        for gi, win in enumerate((2, 4, 8, 16)):
            lo = np.clip(s_ - win // 2, 0, L - 1); hi = np.clip(s_ + (win - win // 2) - 1, 0, L - 1)
            invc[gi // 2, (gi % 2) * 64:(gi % 2) * 64 + 64, :] = (1.0 / (hi - lo + 1))[None, :]
        CLm = np.zeros((T, T), np.float32); SLm = np.zeros((T, T), np.float32)
        sl_ = np.arange(L)
        mmod = np.outer(sl_, sl_) % L
        cb = (np.cos(2 * np.pi * mmod / L) / np.sqrt(L)).astype(np.float32)
        sb_ = (np.sin(2 * np.pi * mmod / L) / np.sqrt(L)).astype(np.float32)
        for b_ in range(T // L):
            CLm[b_ * L:(b_ + 1) * L, b_ * L:(b_ + 1) * L] = cb
            SLm[b_ * L:(b_ + 1) * L, b_ * L:(b_ + 1) * L] = sb_
        return invc, CLm.astype(ml_dtypes.bfloat16), SLm.astype(ml_dtypes.bfloat16)
    invc_s, CL_s, SL_s = job_consts(T)
    invc_p, CL_p, SL_p = job_consts(256)
    common.update(prm=prm, rows=rows_, lr64=lr64, lr32=lr32, lr16=lr16, pool_w=g("pool_w"),
                  ident=np.eye(128, dtype=np.float32), maskc=maskc, csm=csm)
    st_r = g("state_rwkv"); st_g = g("state_gla")
    for s in range(2):
        common_s = dict(common, invc=invc_s, CL=CL_s, SL=SL_s,
                        sr0=np.ascontiguousarray(st_r[s].transpose(0, 1, 4, 2, 3)),
                        sg0=np.ascontiguousarray(st_g[s].transpose(0, 1, 3, 2, 4)))
        jobs.append(dict(common_s, xT=np.ascontiguousarray(x_sample[s].T), pos=np.ascontiguousarray(pos.T),
                         cond=lay8(c[s]), kap=np.ones((128, 1), np.float32)))
    xp = x_prompt.reshape(-1, D)
    common = dict(common, invc=invc_p, CL=CL_p, SL=SL_p, sr0=np.zeros((DEPTH, 2, 64, 4, 64), np.float32),
                  sg0=np.zeros((DEPTH, 2, 32, 4, 64), np.float32))
    pj = dict(common, xT=np.ascontiguousarray(xp.T), pos=np.zeros((D, T), np.float32),
              cond=lay8(c_ctx), kap=np.zeros((128, 1), np.float32))
    jobs.append(pj)
    while len(jobs) < N_CORES:
        jobs.append(pj)
    if _NC is None:
        _NC = build_program()
    res = run_bass_kernel_spmd(_NC, jobs, core_ids=list(range(N_CORES)))
    r = res.results
    global _LAST
    _LAST = r
    y_sample = np.stack([np.ascontiguousarray(r[s]["yT"].T) for s in range(2)], axis=0).astype(np.float32)
    y_prompt = np.ascontiguousarray(r[2]["yT"].T).reshape(x_prompt.shape).astype(np.float32)
    B = x_prompt.shape[0]
    nsr = np.ascontiguousarray(np.asarray(r[2]["osr"]).transpose(2, 0, 1, 4, 5, 3)).astype(np.float32)
    nsg = np.ascontiguousarray(np.asarray(r[2]["osg"]).transpose(2, 0, 1, 4, 3, 5)).astype(np.float32)
    return (y_prompt, y_sample, nsr, nsg)
```

```python
import contextlib
import numpy as np
import concourse.bass as bass
import concourse.mybir as mybir
from concourse.bass_utils import run_bass_kernel_spmd

F32 = mybir.dt.float32
BF16 = mybir.dt.bfloat16
ALU = mybir.AluOpType
AF = mybir.ActivationFunctionType
AX = mybir.AxisListType

D = 1024
T = 4096
NT = 8
TN = 512
DEPTH = 2
P_IN = 6432
NMIX = 2336
N_CORES = 8
EPS = 1e-6


class Em:
    ENGS = ("pe", "act", "dve", "pool", "sp")

    def __init__(self, nc):
        self.nc = nc
        self.ops = {e: [] for e in self.ENGS}
        self.cnt = {e: 0 for e in self.ENGS}
        self.seen = {e: {} for e in self.ENGS}
        self.last_w = {}
        self.readers = {}
        self.dma_sems = {}
        self.free_sems = []
        self.sem_names = ["c_" + e for e in self.ENGS]
        self.final_tokens = []
        self.marks = []

    def _deps(self, eng, reads, writes):
        toks = []
        for k in reads:
            t = self.last_w.get(k)
            if t is not None:
                toks.append(t)
        for k in writes:
            t = self.last_w.get(k)
            if t is not None:
                toks.append(t)
            toks.extend(self.readers.get(k, ()))
        seen = self.seen[eng]
        best = {}
        own = "c_" + eng
        for (s, v) in toks:
            if eng == "pe" and s == own:
                continue
            if seen.get(s, 0) < v and best.get(s, 0) < v:
                best[s] = v
        waits = []
        for s, v in best.items():
            seen[s] = v
            waits.append((s, v))
        return waits

    def _commit(self, tok, reads, writes):
        for k in reads:
            self.readers.setdefault(k, []).append(tok)
        for k in writes:
            self.last_w[k] = tok
            self.readers[k] = []

    def op(self, eng, fn, reads=(), writes=()):
        waits = self._deps(eng, reads, writes)
        self.cnt[eng] += 1
        tok = ("c_" + eng, self.cnt[eng])
        self.ops[eng].append((waits, fn, ("c_" + eng, 1)))
        self._commit(tok, reads, writes)
        return tok

    def dma(self, q, fn, semkey, reads=(), writes=(), final=False):
        if semkey not in self.dma_sems:
            if self.free_sems:
                self.dma_sems[semkey] = self.free_sems.pop()
            else:
                name = "d%d" % (len(self.sem_names) - len(self.ENGS))
                self.dma_sems[semkey] = [name, 0]
                self.sem_names.append(name)
        ent = self.dma_sems[semkey]
        waits = self._deps(q, reads, writes)
        ent[1] += 16
        tok = (ent[0], ent[1])
        self.ops[q].append((waits, fn, (ent[0], 16)))
        self._commit(tok, reads, writes)
        if final:
            self.final_tokens.append(tok)
        return tok

    def barrier(self, label=""):
        self.marks.append((label, dict(self.cnt)))
        targets = [("c_" + e, self.cnt[e]) for e in self.ENGS if self.cnt[e] > 0]
        targets += [(n, c) for (n, c) in self.dma_sems.values() if c > 0]
        for e in self.ENGS:
            waits = []
            for (s, v) in targets:
                if e == "pe" and s == "c_pe":
                    continue
                if self.seen[e].get(s, 0) < v:
                    self.seen[e][s] = v
                    waits.append((s, v))
            if waits:
                self.ops[e].append((waits, None, None))
        self.free_sems.extend(self.dma_sems.values())
        self.dma_sems = {}

    def build(self):
        nc = self.nc
        with contextlib.ExitStack() as st:
            print("Em: %d semaphores, ops:" % len(self.sem_names), {e: len(v) for e, v in self.ops.items()})
            sems = {n: st.enter_context(nc.semaphore(n)) for n in self.sem_names}
            fin = {}
            for (s, v) in self.final_tokens:
                fin[s] = max(fin.get(s, 0), v)
            block = st.enter_context(nc.Block())

            def runner(ename):
                def f(e):
                    for waits, fn, inc in self.ops[ename]:
                        for (s, v) in waits:
                            e.wait_ge(sems[s], v)
                        if fn is not None:
                            fn(e).then_inc(sems[inc[0]], inc[1])
                    if ename == "sp":
                        for s, v in fin.items():
                            e.wait_ge(sems[s], v)
                return f
            block.tensor(runner("pe"))
            block.scalar(runner("act"))
            block.vector(runner("dve"))
            block.gpsimd(runner("pool"))
            block.sync(runner("sp"))


class Rec:
    def __init__(self):
        self.items = []

    def op(self, eng, fn, reads=(), writes=()):
        self.items.append(("op", eng, fn, list(reads), list(writes)))

    def dma(self, q, fn, semkey, reads=(), writes=(), final=False):
        self.items.append(("dma", q, fn, semkey, list(reads), list(writes), final))


def merge_streams(em, recs):
    pos = [0] * len(recs)
    tot = [max(1, len(r.items)) for r in recs]
    while True:
        best = None
        for i, r in enumerate(recs):
            if pos[i] < len(r.items):
                frac = pos[i] / tot[i]
                if best is None or frac < best[0]:
                    best = (frac, i)
        if best is None:
            break
        i = best[1]
        it = recs[i].items[pos[i]]; pos[i] += 1
        if it[0] == "op":
            em.op(it[1], it[2], reads=it[3], writes=it[4])
        else:
            em.dma(it[1], it[2], it[3], reads=it[4], writes=it[5], final=it[6])


class Arena:
    def __init__(self, handle_bf16, nelem):
        self.h = handle_bf16
        self.n = nelem
        self.off = 0

    def reset(self):
        self.off = 0

    def alloc(self, shape_free, dt):
        n = int(np.prod(shape_free))
        nb = n * (2 if dt == F32 else 1)
        nb = (nb + 15) // 16 * 16
        assert self.off + nb <= self.n, ("arena overflow", self.off, nb, self.n)
        ap = self.h[:, self.off:self.off + n * (2 if dt == F32 else 1)]
        self.off += nb
        if dt == F32:
            ap = ap.bitcast(F32)
        if len(shape_free) == 2:
            ap = ap.rearrange("p (a b) -> p a b", b=shape_free[1])
        elif len(shape_free) == 3:
            ap = ap.rearrange("p (a b c) -> p a b c", b=shape_free[1], c=shape_free[2])
        return ap


def build_program():
    nc = bass.Bass("TRN2", target_bir_lowering=False)
    dI = lambda n, sh, dt=F32: nc.dram_tensor(n, sh, dt, kind="ExternalInput").ap()
    DBG = False
    NL = DEPTH
    dS = lambda n, sh, dt=F32: nc.dram_tensor(n, sh, dt, kind="Internal").ap()
    dO = lambda n, sh, dt=F32: nc.dram_tensor(n, sh, dt, kind="ExternalOutput").ap()
    xT = dI("xT", [D, T]); pos = dI("pos", [D, T])
    cond = dI("cond", [128, 8]); kap = dI("kap", [128, 1])
    ada_w = dI("ada_w", [DEPTH, D, 6 * D]); ada_b = dI("ada_b", [DEPTH, 128, 48])
    n1g = dI("n1g", [DEPTH, 128, 8]); n2g = dI("n2g", [DEPTH, 128, 8]); fg = dI("fg", [128, 8])
    w_in = dI("w_in", [DEPTH, D, P_IN]); w_br = dI("w_br", [DEPTH, D, D]); w_out = dI("w_out", [DEPTH, D, D])
    w1 = dI("w1", [DEPTH, D, 4 * D]); w2 = dI("w2", [DEPTH, 4 * D, D])
    yT = dO("yT", [D, T])
    PRM_N = 1568; ROW_N = 1280
    prm_d = dI("prm", [DEPTH, 128, PRM_N]); rows_d = dI("rows", [DEPTH, 1, ROW_N])
    lr64_d = dI("lr64", [DEPTH, 64, 768]); lr32_d = dI("lr32", [DEPTH, 32, 512]); lr16_d = dI("lr16", [DEPTH, 16, 256])
    pool_w_d = dI("pool_w", [DEPTH, 4, 64, 64])
    ident_d = dI("ident", [128, 128]); maskc_d = dI("maskc", [128, 8, 512])
    invc_d = dI("invc", [2, 128, T]); csm_d = dI("csm", [2, 128, 512])
    CL_d = dI("CL", [8, 2, 128, 16 * 512], BF16); SL_d = dI("SL", [8, 2, 128, 16 * 512], BF16)
    sr0_d = dI("sr0", [DEPTH, 2, 64, 4, 64]); sg0_d = dI("sg0", [DEPTH, 2, 32, 4, 64])
    osr_d = dO("osr", [DEPTH, 2, 16, 64, 4, 64]); osg_d = dO("osg", [DEPTH, 2, 16, 32, 4, 64])
    ZM = dS("ZM", [768, T])
    xres = dS("xres", [NT, 128, 8 * TN])
    U = dS("U", [19 * 128, T])
    G = dS("G", [NT, 128, 32 * TN], BF16)
    Y = dS("Y", [NT, 128, 8 * TN], BF16)
    H2 = dS("H2", [NT, 128, 8 * TN], BF16)

    with contextlib.ExitStack() as st:
        arena_h = st.enter_context(nc.sbuf_tensor("arena", [128, 104000], BF16))
        cst_h = st.enter_context(nc.sbuf_tensor("cst", [128, 1024], F32))
        onesb = st.enter_context(nc.sbuf_tensor("onesb", [128, 128], BF16))
        psum = [st.enter_context(nc.psum_tensor("ps%d" % i, [128, 512], F32)) for i in range(8)]
        ar = Arena(arena_h, 104000)
        em = Em(nc)
        c_silu = cst_h[:, 0:8]; c_mod = cst_h[:, 8:56]; c_gsc1 = cst_h[:, 56:64]; c_gsc2 = cst_h[:, 64:72]
        c_n1g = cst_h[:, 72:80]; c_n2g = cst_h[:, 80:88]; c_fg = cst_h[:, 88:96]
        c_kap = cst_h[:, 96:97]; c_eps = cst_h[:, 97:98]; c_adab = cst_h[:, 100:148]
        c_nk = cst_h[:, 98:99]; c_gneps = cst_h[:, 99:100]
        em.dma("sp", lambda e: e.dma_start(out=c_silu, in_=cond), "c_silu", writes=["c_silu"])
        em.dma("sp", lambda e: e.dma_start(out=c_kap, in_=kap), "c_kap", writes=["c_kap"])
        em.dma("sp", lambda e: e.dma_start(out=c_fg, in_=fg), "c_fg", writes=["c_fg"])
        em.op("dve", lambda e: e.memset(c_eps, EPS), writes=["c_eps"])
        em.op("dve", lambda e: e.memset(c_gneps, 64e-5), writes=["c_gneps"])
        em.op("dve", lambda e: e.tensor_scalar(out=c_nk, in0=c_kap, scalar1=-1.0, scalar2=None, op0=ALU.add), reads=["c_kap"], writes=["c_nk"])
        em.op("dve", lambda e: e.memset(onesb[:], 1.0 / D), writes=["onesb"])
        em.op("act", lambda e: e.activation(out=c_silu, in_=c_silu, func=AF.Silu), reads=["c_silu"], writes=["c_silu"])

        pctr = [0]

        def pbank():
            pctr[0] = (pctr[0] + 1) % 8
            return pctr[0]

        def rmsnorm_tile(xt, xkey, gsc, shift, out_bf, outkey, tmp, xsq, rstd, tag):
            em.op("act", lambda e: e.activation(out=xsq, in_=xt, func=AF.Square), reads=[xkey], writes=[tag + "xsq"])
            b = pbank()
            for c in range(8):
                em.op("pe", lambda e, c=c, b=b: e.matmul(psum[b][:], lhsT=onesb[:], rhs=xsq[:, c, :], start=(c == 0), stop=(c == 7)),
                      reads=[tag + "xsq", "onesb"], writes=["ps%d" % b])
            em.op("act", lambda e, b=b: e.activation(out=rstd, in_=psum[b][:], func=AF.Sqrt, bias=c_eps, scale=1.0),
                  reads=["ps%d" % b, "c_eps"], writes=[tag + "rstd"])
            em.op("dve", lambda e: e.reciprocal(out=rstd, in_=rstd), reads=[tag + "rstd"], writes=[tag + "rstd"])
            for c in range(8):
                em.op("dve", lambda e, c=c: e.tensor_tensor(out=tmp[:, c, :], in0=xt[:, c, :], in1=rstd, op=ALU.mult),
                      reads=[xkey, tag + "rstd"], writes=[tag + "tmp%d" % c])
                if shift is not None:
                    em.op("act", lambda e, c=c: e.activation(out=out_bf[:, c, :], in_=tmp[:, c, :], func=AF.Identity,
                                                             bias=shift[:, c:c + 1], scale=gsc[:, c:c + 1]),
                          reads=[tag + "tmp%d" % c, "mod"], writes=[outkey])
                else:
                    em.op("act", lambda e, c=c: e.activation(out=out_bf[:, c, :], in_=tmp[:, c, :], func=AF.Identity,
                                                             bias=0.0, scale=gsc[:, c:c + 1]),
                          reads=[tag + "tmp%d" % c, "mod"], writes=[outkey])

        fm = lambda ap: ap.rearrange("(c p) t -> p c t", p=128)
        tv = lambda ap, j: ap[j].rearrange("p (c t) -> p c t", t=TN)
        cx_tv = tv
        cx = CX()
        cx.tv = tv
        cx.nc = nc; cx.em = em; cx.ar = ar; cx.psum = psum; cx.pbank = pbank; cx.U = U; cx.Y = Y; cx.ZM = ZM
        cx.c_kap = c_kap; cx.c_nk = c_nk; cx.c_eps = c_eps; cx.c_gneps = c_gneps
        cx.PRM_N = PRM_N; cx.ROW_N = ROW_N; cx.prm = prm_d; cx.rows = rows_d
        cx.lr64_d = lr64_d; cx.lr32_d = lr32_d; cx.lr16_d = lr16_d; cx.pool_w = pool_w_d
        cx.ident_d = ident_d; cx.maskc_d = maskc_d; cx.invc = invc_d; cx.csm = csm_d; cx.CL = CL_d; cx.SL = SL_d
        cx.sr0 = sr0_d; cx.sg0 = sg0_d; cx.osr = osr_d; cx.osg = osg_d

        for l in range(NL):
            last = (l == NL - 1)
            em.barrier(); ar.reset()
            em.dma("sp", lambda e, l=l: e.dma_start(out=c_adab, in_=ada_b[l]), "c_adab", writes=["c_adab"])
            em.dma("sp", lambda e, l=l: e.dma_start(out=c_n1g, in_=n1g[l]), "c_n1g", writes=["c_n1g"])
            em.dma("sp", lambda e, l=l: e.dma_start(out=c_n2g, in_=n2g[l]), "c_n2g", writes=["c_n2g"])
            awt = [ar.alloc([8, 768], F32) for _ in range(2)]
            aw_v = ada_w[l].rearrange("(c p) n -> p c n", p=128)
            pb = pbank()
            for g in range(8):
                wt = awt[g % 2]; key = "awt%d" % (g % 2)
                em.dma("sp", lambda e, g=g, wt=wt, aw_v=aw_v: e.dma_start(out=wt, in_=aw_v[:, :, g * 768:(g + 1) * 768]), key, writes=[key])
                for m in range(6):
                    col = g * 6 + m
                    for kc in range(8):
                        em.op("pe", lambda e, wt=wt, m=m, kc=kc, col=col, pb=pb: e.matmul(
                            psum[pb][:, col:col + 1], lhsT=wt[:, kc, m * 128:(m + 1) * 128], rhs=c_silu[:, kc:kc + 1],
                            start=(kc == 0), stop=(kc == 7)), reads=[key, "c_silu"], writes=["ps%d" % pb])
            em.op("dve", lambda e, pb=pb: e.tensor_tensor(out=c_mod, in0=psum[pb][:, 0:48], in1=c_adab, op=ALU.add),
                  reads=["ps%d" % pb, "c_adab"], writes=["mod"])
            em.op("dve", lambda e: e.scalar_tensor_tensor(out=c_gsc1, in0=c_mod[:, 8:16], scalar=1.0, in1=c_n1g, op0=ALU.add, op1=ALU.mult),
                  reads=["mod", "c_n1g"], writes=["mod"])
            em.op("dve", lambda e: e.scalar_tensor_tensor(out=c_gsc2, in0=c_mod[:, 32:40], scalar=1.0, in1=c_n2g, op0=ALU.add, op1=ALU.mult),
                  reads=["mod", "c_n2g"], writes=["mod"])
            if DBG and l == 0:
                dbg_mod = nc.dram_tensor("dbg_mod", [128, 64], F32, kind="ExternalOutput").ap()
                em.dma("sp", lambda e: e.dma_start(out=dbg_mod, in_=cst_h[:, 8:72]), "dbgmod", reads=["mod"], final=True)
            sh1 = c_mod[:, 0:8]; g1 = c_mod[:, 16:24]; sh2 = c_mod[:, 24:32]; g2 = c_mod[:, 40:48]

            em.barrier(); ar.reset()
            hT = ar.alloc([8, T], BF16)
            xb = [ar.alloc([8, TN], F32) for _ in range(2)]
            pb_ = [ar.alloc([8, TN], F32) for _ in range(2)]
            tmp = ar.alloc([8, TN], F32); xsq = ar.alloc([8, TN], BF16); rstd = ar.alloc([TN], F32)
            for j in range(NT):
                xt = xb[j % 2]; xk = "xb%d" % (j % 2)
                sl = slice(j * TN, (j + 1) * TN)
                if l == 0:
                    pt = pb_[j % 2]; pk = "pb%d" % (j % 2)
                    em.dma("sp", lambda e, xt=xt, sl=sl: e.dma_start(out=xt, in_=fm(xT)[:, :, sl]), xk, writes=[xk])
                    em.dma("sp", lambda e, pt=pt, sl=sl: e.dma_start(out=pt, in_=fm(pos)[:, :, sl]), pk, writes=[pk])
                    em.op("pool", lambda e, xt=xt, pt=pt: e.tensor_tensor(out=xt, in0=xt, in1=pt, op=ALU.add), reads=[xk, pk], writes=[xk])
                    em.dma("sp", lambda e, xt=xt, j=j: e.dma_start(out=tv(xres, j), in_=xt), xk + "w", reads=[xk], writes=["xres%d" % j])
                else:
                    em.dma("sp", lambda e, xt=xt, j=j: e.dma_start(out=xt, in_=tv(xres, j)), xk, reads=["xres%d" % j], writes=[xk])
                rmsnorm_tile(xt, xk, c_gsc1, sh1, hT[:, :, sl], "hT%d" % j, tmp, xsq, rstd, "n1")

            chunks = [(m * 128, 128, "U", m) for m in range(18)] + [(2304, 32, "U", 18)] + \
                     [(NMIX + m * 128, 128, "G", m) for m in range(32)]
            wbuf = [ar.alloc([8, 512], BF16) for _ in range(2)]
            stg = [ar.alloc([TN], F32) for _ in range(4)]
            stb = [ar.alloc([TN], BF16) for _ in range(4)]
            win_v = w_in[l].rearrange("(c p) n -> p c n", p=128)
            groups = []
            cur = []
            for ch in chunks:
                if cur and (len(cur) == 4 or cur[-1][2] != ch[2] or cur[-1][1] != 128):
                    groups.append(cur); cur = []
                cur.append(ch)
            groups.append(cur)
            sctr = 0
            for gi, grp in enumerate(groups):
                wt = wbuf[gi % 2]; wk = "wbuf%d" % (gi % 2)
                c0 = grp[0][0]; ncol = sum(c[1] for c in grp)
                em.dma("pool", lambda e, wt=wt, c0=c0, ncol=ncol, win_v=win_v: e.dma_start(out=wt[:, :, 0:ncol], in_=win_v[:, :, c0:c0 + ncol]), wk, writes=[wk])
                for j in range(NT):
                    sl = slice(j * TN, (j + 1) * TN)
                    for (cs, cn, kind, mi) in grp:
                        b = pbank(); o = cs - c0
                        for kc in range(8):
                            em.op("pe", lambda e, wt=wt, o=o, cn=cn, kc=kc, b=b, sl=sl: e.matmul(
                                psum[b][0:cn, :], lhsT=wt[:, kc, o:o + cn], rhs=hT[:, kc, sl], start=(kc == 0), stop=(kc == 7)),
                                reads=[wk, "hT%d" % j], writes=["ps%d" % b])
                        si = sctr % 4; sctr += 1
                        if kind == "U":
                            s_ = stg[si]; sk = "stg%d" % si
                            em.op("dve", lambda e, s_=s_, b=b, cn=cn: e.tensor_copy(out=s_[0:cn, :], in_=psum[b][0:cn, :]), reads=["ps%d" % b], writes=[sk])
                            em.dma("sp", lambda e, s_=s_, mi=mi, cn=cn, sl=sl: e.dma_start(out=U[mi * 128:mi * 128 + cn, sl], in_=s_[0:cn, :]),
                                   sk + "w", reads=[sk], writes=["U%d_%d" % (mi, j)])
                        else:
                            s_ = stb[si]; sk = "stb%d" % si
                            em.op("act", lambda e, s_=s_, b=b: e.activation(out=s_, in_=psum[b][:], func=AF.Sigmoid), reads=["ps%d" % b], writes=[sk])
                            em.dma("sp", lambda e, s_=s_, mi=mi, j=j: e.dma_start(out=tv(G, j)[:, mi, :], in_=s_),
                                   sk + "w", reads=[sk], writes=["G%d" % j])

            em.barrier(); ar.reset()
            mixers(cx, l)

            em.barrier(); ar.reset()
            wbr = ar.alloc([8, D], BF16); wo = ar.alloc([8, D], BF16)
            em.dma("pool", lambda e, l=l: e.dma_start(out=wbr, in_=w_br[l].rearrange("(c p) n -> p c n", p=128)), "wbr", writes=["wbr"])
            em.dma("pool", lambda e, l=l: e.dma_start(out=wo, in_=w_out[l].rearrange("(c p) n -> p c n", p=128)), "wo", writes=["wo"])
            Yt = [ar.alloc([8, TN], BF16) for _ in range(2)]
            Gt = [ar.alloc([32, TN], BF16) for _ in range(2)]
            xb = [ar.alloc([8, TN], F32) for _ in range(2)]
            mgs = [ar.alloc([8, TN], BF16) for _ in range(2)]
            accs = [ar.alloc([TN], F32) for _ in range(2)]; tms = [ar.alloc([TN], F32) for _ in range(2)]
            tmp = ar.alloc([8, TN], F32); xsq = ar.alloc([8, TN], BF16); rstd = ar.alloc([TN], F32)
            h2 = [ar.alloc([8, TN], BF16) for _ in range(1)]
            for j in range(NT):
                sl = slice(j * TN, (j + 1) * TN)
                yt = Yt[j % 2]; gt = Gt[j % 2]; xt = xb[j % 2]; hh = h2[0]
                yk = "Yt%d" % (j % 2); gk = "Gt%d" % (j % 2); xk = "x3_%d" % (j % 2); hk = "h2_0"
                em.dma("sp", lambda e, yt=yt, j=j: e.dma_start(out=yt, in_=tv(Y, j)), yk, reads=["Y%d" % j], writes=[yk])
                em.dma("sp", lambda e, gt=gt, j=j: e.dma_start(out=gt, in_=tv(G, j)), gk, reads=["G%d" % j], writes=[gk])
                em.dma("sp", lambda e, xt=xt, j=j: e.dma_start(out=xt, in_=tv(xres, j)), xk, reads=["xres%d" % j], writes=[xk])
                mg = mgs[j % 2]; mgk = "mg%d" % (j % 2)
                for dc in range(8):
                    acc = accs[dc % 2]; acck = "acc%d" % (dc % 2)
                    for i in range(4):
                        b = pbank()
                        for cc in range(2):
                            em.op("pe", lambda e, i=i, cc=cc, dc=dc, b=b, yt=yt: e.matmul(
                                psum[b][:], lhsT=wbr[:, i * 2 + cc, dc * 128:(dc + 1) * 128], rhs=yt[:, i * 2 + cc, :],
                                start=(cc == 0), stop=(cc == 1)), reads=["wbr", yk], writes=["ps%d" % b])
                        dst = acc if i == 0 else tms[(i - 1) % 2]
                        dstk = acck if i == 0 else "tm%d" % ((i - 1) % 2)
                        em.op("dve", lambda e, b=b, i=i, dc=dc, gt=gt, dst=dst: e.tensor_tensor(out=dst, in0=psum[b][:], in1=gt[:, i * 8 + dc, :], op=ALU.mult),
                              reads=["ps%d" % b, gk], writes=[dstk])
                        if i > 0:
                            o_ = mg[:, dc, :] if i == 3 else acc
                            em.op("pool", lambda e, o_=o_, acc=acc, dst=dst: e.tensor_tensor(out=o_, in0=acc, in1=dst, op=ALU.add),
                                  reads=[acck, dstk], writes=[mgk if i == 3 else acck])
                for d2 in range(8):
                    b = pbank()
                    for dc in range(8):
                        em.op("pe", lambda e, d2=d2, dc=dc, b=b, mg=mg: e.matmul(psum[b][:], lhsT=wo[:, dc, d2 * 128:(d2 + 1) * 128], rhs=mg[:, dc, :],
                                                                          start=(dc == 0), stop=(dc == 7)), reads=["wo", mgk], writes=["ps%d" % b])
                    em.op("dve", lambda e, d2=d2, b=b, xt=xt: e.scalar_tensor_tensor(out=xt[:, d2, :], in0=psum[b][:], scalar=g1[:, d2:d2 + 1], in1=xt[:, d2, :],
                                                                                   op0=ALU.mult, op1=ALU.add), reads=["ps%d" % b, xk, "mod"], writes=[xk])
                em.dma("sp", lambda e, xt=xt, j=j: e.dma_start(out=tv(xres, j), in_=xt), xk + "w", reads=[xk], writes=["xres%d" % j])
                rmsnorm_tile(xt, xk, c_gsc2, sh2, hh, hk, tmp, xsq, rstd, "n2")
                em.dma("sp", lambda e, hh=hh, j=j: e.dma_start(out=tv(H2, j), in_=hh), hk + "w", reads=[hk], writes=["H2_%d" % j])

            if DBG and l == 0:
                em.barrier()
                dd = nc.dram_tensor("dY", [D, T], BF16, kind="ExternalOutput").ap()
                em.dma("sp", lambda e, dd=dd: e.dma_start(out=dd, in_=Y), "dbgY", final=True)
                dd2 = nc.dram_tensor("dX", [D, T], F32, kind="ExternalOutput").ap()
                em.dma("sp", lambda e, dd2=dd2: e.dma_start(out=dd2, in_=xres), "dbgX", final=True)
            em.barrier(); ar.reset()
            w1s = ar.alloc([8, 4 * D], BF16); w2s = ar.alloc([32, D], BF16)
            for q in range(4):
                em.dma("pool", lambda e, l=l, q=q: e.dma_start(out=w1s[:, :, q * 1024:(q + 1) * 1024],
                                                               in_=w1[l].rearrange("(c p) n -> p c n", p=128)[:, :, q * 1024:(q + 1) * 1024]), "w1s", writes=["w1s"])
                em.dma("pool", lambda e, l=l, q=q: e.dma_start(out=w2s[:, q * 8:(q + 1) * 8, :],
                                                               in_=w2[l].rearrange("(c p) n -> p c n", p=128)[:, q * 8:(q + 1) * 8, :]), "w2s", writes=["w2s"])
            hid_raw = ar.alloc([32 * TN], BF16)
            hid = hid_raw.rearrange("p (a b) -> p a b", b=TN)
            rl = [ar.alloc([TN], F32) for _ in range(2)]
            xb1 = ar.alloc([8, TN], F32); h2t = ar.alloc([8, TN], BF16)
            if last:
                tmp = hid_raw[:, 0:16 * TN].bitcast(F32).rearrange("p (a b) -> p a b", b=TN)
                xsq = ar.alloc([8, TN], BF16); rstd = ar.alloc([TN], F32)
            for j in range(NT):
                sl = slice(j * TN, (j + 1) * TN)
                em.dma("sp", lambda e, j=j: e.dma_start(out=h2t, in_=tv(H2, j)), "h2t", reads=["H2_%d" % j], writes=["h2t"])
                em.dma("sp", lambda e, j=j: e.dma_start(out=xb1, in_=tv(xres, j)), "xb1", reads=["xres%d" % j], writes=["xb1"])
                for fc in range(32):
                    b = pbank()
                    for kc in range(8):
                        em.op("pe", lambda e, fc=fc, kc=kc, b=b: e.matmul(psum[b][:], lhsT=w1s[:, kc, fc * 128:(fc + 1) * 128], rhs=h2t[:, kc, :],
                                                                          start=(kc == 0), stop=(kc == 7)), reads=["w1s", "h2t"], writes=["ps%d" % b])
                    r_ = rl[fc % 2]; rk = "rl%d" % (fc % 2)
                    em.op("act", lambda e, b=b, r_=r_: e.activation(out=r_, in_=psum[b][:], func=AF.Relu), reads=["ps%d" % b], writes=[rk])
                    em.op("pool", lambda e, r_=r_, fc=fc: e.tensor_tensor(out=hid[:, fc, :], in0=r_, in1=r_, op=ALU.mult), reads=[rk], writes=["hid"])
                for d2 in range(8):
                    b = pbank()
                    for fc in range(32):
                        em.op("pe", lambda e, fc=fc, d2=d2, b=b: e.matmul(psum[b][:], lhsT=w2s[:, fc, d2 * 128:(d2 + 1) * 128], rhs=hid[:, fc, :],
                                                                          start=(fc == 0), stop=(fc == 31)), reads=["w2s", "hid"], writes=["ps%d" % b])
                    em.op("dve", lambda e, d2=d2, b=b: e.scalar_tensor_tensor(out=xb1[:, d2, :], in0=psum[b][:], scalar=g2[:, d2:d2 + 1], in1=xb1[:, d2, :],
                                                                            op0=ALU.mult, op1=ALU.add), reads=["ps%d" % b, "xb1", "mod"], writes=["xb1"])
                if not last:
                    em.dma("sp", lambda e, j=j: e.dma_start(out=tv(xres, j), in_=xb1), "xb1w", reads=["xb1"], writes=["xres%d" % j])
                else:
                    em.op("act", lambda e: e.activation(out=xsq, in_=xb1, func=AF.Square), reads=["xb1"], writes=["fxsq"])
                    b = pbank()
                    for c in range(8):
                        em.op("pe", lambda e, c=c, b=b: e.matmul(psum[b][:], lhsT=onesb[:], rhs=xsq[:, c, :], start=(c == 0), stop=(c == 7)),
                              reads=["fxsq", "onesb"], writes=["ps%d" % b])
                    em.op("act", lambda e, b=b: e.activation(out=rstd, in_=psum[b][:], func=AF.Sqrt, bias=c_eps, scale=1.0),
                          reads=["ps%d" % b, "c_eps"], writes=["frstd"])
                    em.op("dve", lambda e: e.reciprocal(out=rstd, in_=rstd), reads=["frstd"], writes=["frstd"])
                    for c in range(8):
                        em.op("dve", lambda e, c=c: e.scalar_tensor_tensor(out=tmp[:, c, :], in0=xb1[:, c, :], scalar=c_fg[:, c:c + 1], in1=rstd,
                                                                         op0=ALU.mult, op1=ALU.mult), reads=["xb1", "frstd", "c_fg"], writes=["hid"])
                    em.dma("sp", lambda e, sl=sl: e.dma_start(out=fm(yT)[:, :, sl], in_=tmp), "yT_w", reads=["hid"], writes=["yT%d" % j], final=True)
        em.build()
        global _EM
        _EM = em
    return nc


class CX:
    pass


def mixers(cx, l):
    em, ar, psum, pbank, U, Y = cx.em, cx.ar, cx.psum, cx.pbank, cx.U, cx.Y
    c_kap, c_nk, c_eps = cx.c_kap, cx.c_nk, cx.c_eps

    ar.reset()
    c = mk_consts(cx, l, 'p')
    p_psc = c.psc

    zp = ar.alloc([2, 16, 272], F32)
    Wa = ar.alloc([16, 272], F32); Wb = ar.alloc([16, 272], F32)
    pooled = ar.alloc([2, T], F32)
    invc = ar.alloc([2, T], F32)
    PW = ar.alloc([2, 128], F32)
    yst = [ar.alloc([TN], BF16) for _ in range(2)]
    em.op("pool", lambda e: e.memset(zp, 0.0), writes=["zp"])
    em.op("pool", lambda e: e.memset(PW, 0.0), writes=["PW"])
    for ct in range(2):
        em.dma("sp", lambda e, ct=ct: e.dma_start(out=invc[:, ct, :], in_=cx.invc[ct]), "invc", writes=["invc"])
        Uv = U[ct * 128:(ct + 1) * 128, :].rearrange("p (b s) -> p b s", s=256)
        em.dma("sp", lambda e, ct=ct, Uv=Uv: e.dma_start(out=zp[:, ct, :, 8:264], in_=Uv), "zp", writes=["zp"])
        em.dma("sp", lambda e, ct=ct, Uv=Uv: e.dma_start(out=zp[:, ct, 1:16, 0:8], in_=Uv[:, 0:15, 248:256]), "zp", writes=["zp"])
        em.dma("sp", lambda e, ct=ct, Uv=Uv: e.dma_start(out=zp[:, ct, 0:15, 264:272], in_=Uv[:, 1:16, 0:8]), "zp", writes=["zp"])
    for g in range(4):
        r0 = (g % 2) * 64
        em.dma("sp", lambda e, g=g, r0=r0, l=l: e.dma_start(out=PW[r0:r0 + 64, g // 2, r0:r0 + 64], in_=cx.pool_w[l, g]), "PW", writes=["PW"])
    for ct in range(2):
        for (a, b) in ((0, 8), (264, 272)):
            em.op("dve", lambda e, ct=ct, a=a, b=b: e.tensor_scalar(out=zp[:, ct, :, a:b], in0=zp[:, ct, :, a:b], scalar1=c_kap, scalar2=None, op0=ALU.mult),
                  reads=["zp", "c_kap"], writes=["zp"])
    pv = lambda ct: pooled[:, ct, :].rearrange("p (b s) -> p b s", s=256)
    iv = lambda ct: invc[:, ct, :].rearrange("p (b s) -> p b s", s=256)

    def take(W, wk, ct, r0):
        em.op("dve", lambda e: e.tensor_tensor(out=pv(ct)[r0:r0 + 64], in0=W[r0:r0 + 64, :, 8:264], in1=iv(ct)[r0:r0 + 64], op=ALU.mult),
              reads=[wk, "invc"], writes=["pooled"])
        em.op("dve", lambda e: e.tensor_tensor(out=pv(ct)[r0:r0 + 64], in0=pv(ct)[r0:r0 + 64], in1=zp[r0:r0 + 64, ct, :, 8:264], op=ALU.subtract),
              reads=["pooled", "zp"], writes=["pooled"])

    for ct in range(2):
        Z = zp[:, ct]
        em.op("dve", lambda e, Z=Z: e.tensor_tensor(out=Wa[:, :, 1:272], in0=Z[:, :, 0:271], in1=Z[:, :, 1:272], op=ALU.add), reads=["zp"], writes=["Wa"])
        if ct == 0:
            take(Wa, "Wa", 0, 0)
        em.op("dve", lambda e: e.tensor_tensor(out=Wb[:, :, 2:271], in0=Wa[:, :, 1:270], in1=Wa[:, :, 3:272], op=ALU.add), reads=["Wa"], writes=["Wb"])
        if ct == 0:
            take(Wb, "Wb", 0, 64)
        else:
            em.op("dve", lambda e: e.tensor_tensor(out=Wa[:, :, 4:269], in0=Wb[:, :, 2:267], in1=Wb[:, :, 6:271], op=ALU.add), reads=["Wb"], writes=["Wa"])
            take(Wa, "Wa", 1, 0)
            em.op("dve", lambda e: e.tensor_tensor(out=Wb[:, :, 8:265], in0=Wa[:, :, 4:261], in1=Wa[:, :, 12:269], op=ALU.add), reads=["Wa"], writes=["Wb"])
            take(Wb, "Wb", 1, 64)
    n = 0
    for j in range(NT):
        sl = slice(j * TN, (j + 1) * TN)
        for ct in range(2):
            b = pbank(); i = n % 2; n += 1
            em.op("pe", lambda e, b=b, ct=ct, sl=sl: e.matmul(psum[b][:], lhsT=PW[:, ct, :], rhs=pooled[:, ct, sl], start=True, stop=True),
                  reads=["PW", "pooled"], writes=["ps%d" % b])
            em.op("act", lambda e, b=b, ct=ct, i=i: e.activation(out=yst[i], in_=psum[b][:], func=AF.Identity, bias=0.0, scale=p_psc[:, ct:ct + 1]),
                  reads=["ps%d" % b], writes=["yst%d" % i])
            em.dma("sp", lambda e, ct=ct, j=j, i=i: e.dma_start(out=cx.tv(Y, j)[:, ct, :], in_=yst[i]), "yst%dw" % i, reads=["yst%d" % i])

    em.barrier(); ar.reset()
    zf = ar.alloc([2, T], BF16)
    csm = ar.alloc([2, 512], BF16)
    zcs = ar.alloc([32, 512], BF16)
    CLb = [ar.alloc([16, 512], BF16) for _ in range(2)]
    SLb = [ar.alloc([16, 512], BF16) for _ in range(2)]
    fst = [ar.alloc([TN], BF16) for _ in range(2)]
    for ct in range(2):
        em.dma("pool", lambda e, ct=ct: e.dma_start(out=zf[:, ct, :], in_=U[256 + ct * 128:256 + (ct + 1) * 128, :]), "zf", writes=["zf"])
        em.dma("pool", lambda e, ct=ct: e.dma_start(out=csm[:, ct, :], in_=cx.csm[ct]), "csm", writes=["csm"])
    for tc in range(32):
        b = pbank()
        for ct in range(2):
            em.op("pe", lambda e, b=b, ct=ct, tc=tc: e.matmul(psum[b][:], lhsT=zf[:, ct, tc * 128:(tc + 1) * 128], rhs=csm[:, ct, :], start=(ct == 0), stop=(ct == 1)),
                  reads=["zf", "csm"], writes=["ps%d" % b])
        if tc % 2 == 0:
            em.op("act", lambda e, b=b, tc=tc: e.activation(out=zcs[:, tc, :], in_=psum[b][:], func=AF.Identity), reads=["ps%d" % b], writes=["zcs"])
        else:
            em.op("dve", lambda e, b=b, tc=tc: e.tensor_copy(out=zcs[:, tc, :], in_=psum[b][:]), reads=["ps%d" % b], writes=["zcs"])
    CLv = lambda ft, th: cx.CL[ft, th].rearrange("p (a f) -> p a f", f=512)
    SLv = lambda ft, th: cx.SL[ft, th].rearrange("p (a f) -> p a f", f=512)
    n = 0
    for ft in range(8):
        fsl = slice(ft * 512, (ft + 1) * 512)
        bb = [pbank(), pbank()]
        for th in range(2):
            i = n % 2; n += 1
            em.dma("sp", lambda e, i=i, th=th, ft=ft: e.dma_start(out=CLb[i], in_=CLv(ft, th)), "CLb%d" % i, writes=["CLb%d" % i])
            em.dma("sp", lambda e, i=i, th=th, ft=ft: e.dma_start(out=SLb[i], in_=SLv(ft, th)), "SLb%d" % i, writes=["SLb%d" % i])
            for half in range(2):
                b = bb[half]
                for tcl in range(16):
                    tc = th * 16 + tcl
                    em.op("pe", lambda e, b=b, i=i, tc=tc, tcl=tcl, half=half, th=th: e.matmul(
                        psum[b][:], lhsT=zcs[:, tc, half * 128:(half + 1) * 128], rhs=CLb[i][:, tcl, :], start=(th == 0 and tcl == 0), stop=False),
                        reads=["zcs", "CLb%d" % i], writes=["ps%d" % b])
                    em.op("pe", lambda e, b=b, i=i, tc=tc, tcl=tcl, half=half, th=th: e.matmul(
                        psum[b][:], lhsT=zcs[:, tc, 256 + half * 128:256 + (half + 1) * 128], rhs=SLb[i][:, tcl, :], start=False, stop=(th == 1 and tcl == 15)),
                        reads=["zcs", "SLb%d" % i], writes=["ps%d" % b])
        for half in range(2):
            b = bb[half]
            em.op("act", lambda e, b=b, half=half: e.activation(out=fst[half], in_=psum[b][:], func=AF.Identity), reads=["ps%d" % b], writes=["fst%d" % half])
            em.dma("sp", lambda e, half=half, ft=ft: e.dma_start(out=cx.tv(Y, ft)[:, 2 + half, :], in_=fst[half]), "fst%dw" % half, reads=["fst%d" % half])

    em.barrier(); ar.reset()
    c = mk_consts(cx, l, "m")
    rwkv_prepass(cx, l, c)
    recs = []
    for fn, banks in ((mix_gla, [0, 1, 2]), (mix_rwkv, [3, 4, 5, 6, 7])):
        rec = Rec()
        sub = CX(); sub.__dict__.update(cx.__dict__)
        sub.em = rec
        st_ = [0]

        def pb(banks=banks, st_=st_):
            st_[0] = (st_[0] + 1) % len(banks)
            return banks[st_[0]]
        sub.pbank = pb
        fn(sub, l, c)
        recs.append(rec)
    merge_streams(em, recs)


def mk_consts(cx, l, tag):
    em, ar = cx.em, cx.ar
    c = CX()
    c.ident = ar.alloc([128], F32)
    c.ident4 = ar.alloc([512], F32)
    c.maskc = ar.alloc([8, 512], F32)
    c.prm = ar.alloc([cx.PRM_N], F32)
    c.rows = ar.alloc([cx.ROW_N], F32)
    c.lr64 = ar.alloc([768], F32); c.lr32 = ar.alloc([512], F32); c.lr16 = ar.alloc([256], F32)
    c.ones_row = ar.alloc([128], F32); c.ones_col = ar.alloc([4], F32)
    k = "K" + tag
    em.dma("sp", lambda e: e.dma_start(out=c.ident, in_=cx.ident_d), k + "a", writes=[k])
    for h in range(4):
        em.dma("sp", lambda e, h=h: e.dma_start(out=c.ident4[:, h * 128:(h + 1) * 128], in_=cx.ident_d), k + "b", writes=[k + "i4%d" % h])
    em.dma("sp", lambda e: e.dma_start(out=c.maskc, in_=cx.maskc_d), k + "c", writes=[k + "m"])
    em.dma("sp", lambda e, l=l: e.dma_start(out=c.prm, in_=cx.prm[l]), k + "d", writes=[k + "p"])
    em.dma("sp", lambda e, l=l: e.dma_start(out=c.rows[0:1, :], in_=cx.rows[l]), k + "e", writes=[k + "r"])
    em.dma("sp", lambda e, l=l: e.dma_start(out=c.lr64[0:64, :], in_=cx.lr64_d[l]), k + "f", writes=[k + "l64"])
    em.dma("sp", lambda e, l=l: e.dma_start(out=c.lr32[0:32, :], in_=cx.lr32_d[l]), k + "g", writes=[k + "l32"])
    em.dma("sp", lambda e, l=l: e.dma_start(out=c.lr16[0:16, :], in_=cx.lr16_d[l]), k + "h", writes=[k + "l16"])
    em.op("dve", lambda e: e.memset(c.ones_row, 1.0), writes=[k + "o1"])
    em.op("dve", lambda e: e.memset(c.ones_col, 1.0), writes=[k + "o2"])
    BC0 = 32
    p = c.prm
    c.psc = p[:, 0:2]; c.mu = p[:, 2:8]; c.hm = p[:, 8:14]; c.om = p[:, 14:20]
    c.kkw = p[:, BC0:BC0 + 256]; c.ka = p[:, BC0 + 256:BC0 + 512]; c.oka = p[:, BC0 + 512:BC0 + 768]
    c.rk = p[:, BC0 + 768:BC0 + 1024]; c.gn = p[:, BC0 + 1024:BC0 + 1280]; c.gln = p[:, BC0 + 1280:BC0 + 1536]
    em.op("dve", lambda e: e.tensor_scalar(out=c.hm, in0=c.mu, scalar1=0.5, scalar2=None, op0=ALU.mult), reads=[k + "p"], writes=[k + "p"])
    em.op("dve", lambda e: e.tensor_scalar(out=c.om, in0=c.mu, scalar1=-1.0, scalar2=1.0, op0=ALU.mult, op1=ALU.add), reads=[k + "p"], writes=[k + "p"])
    em.op("dve", lambda e: e.tensor_scalar(out=c.oka, in0=c.ka, scalar1=-1.0, scalar2=1.0, op0=ALU.mult, op1=ALU.add), reads=[k + "p"], writes=[k + "p"])
    em.barrier()
    return c


def mix_gla(cx, l, c):
    em, ar, psum, pbank, U, Y = cx.em, cx.ar, cx.psum, cx.pbank, cx.U, cx.Y
    S = [ar.alloc([4, 64], F32) for _ in range(2)]
    OG = ar.alloc([32, 256], F32)
    fmt = [ar.alloc([6, 128], F32) for _ in range(2)]
    cal = [ar.alloc([128], F32) for _ in range(2)]
    tm = [ar.alloc([512], F32) for _ in range(2)]
    go = ar.alloc([256], F32)
    ee = ar.alloc([128], F32); lg = ar.alloc([128], F32)
    Eq = ar.alloc([128], F32); Ek = ar.alloc([128], F32); Eb = ar.alloc([128], F32)
    qt = ar.alloc([128], F32); kh = ar.alloc([128], F32); kb = ar.alloc([128], F32)
    qT = ar.alloc([512], F32); kT = ar.alloc([512], F32)
    AT = ar.alloc([512], F32)
    etot = ar.alloc([4], F32)
    og = ar.alloc([256], F32); sq = ar.alloc([256], F32); ss = ar.alloc([4], F32); sgo = ar.alloc([256], F32)
    yd = ar.alloc([256], F32); ydT = [ar.alloc([2, 128], BF16) for _ in range(2)]
    Ufm = U[1536:2304, :].rearrange("(c p) t -> p c t", p=128)
    Yv = lambda tc: cx.tv(Y, tc // 4)[:, 6:8, (tc % 4) * 128:(tc % 4 + 1) * 128]
    ab = lambda d: c.lr16[0:16, d * 128:(d + 1) * 128]
    abrow = lambda d: c.rows[0:1, 1024 + d * 128:1024 + (d + 1) * 128]
    for d in range(2):
        order = list(range(32)) if d == 0 else list(range(31, -1, -1))
        Sd = S[d]; Sk = "gS%d" % d
        for n, tc in enumerate(order):
            par = n % 2
            tsl = slice(tc * 128, (tc + 1) * 128)
            f_ = fmt[par]; fk = "gfmt%d" % par; ca = cal[par]; ck = "gcal%d" % par; t_ = tm[par]; tk = "gtm%d" % par
            em.dma("sp", lambda e, f_=f_, tsl=tsl: e.dma_start(out=f_, in_=Ufm[:, :, tsl]), fk, writes=[fk])
            em.dma("sp", lambda e, ca=ca, tsl=tsl, d=d: e.dma_start(out=ca[0:16, :], in_=U[2304 + 16 * d:2304 + 16 * d + 16, tsl]), ck, writes=[ck])
            if n == 0:
                em.dma("sp", lambda e, Sd=Sd, d=d, l=l: e.dma_start(out=Sd[0:32], in_=cx.sg0[l, d]), Sk + "i", writes=[Sk])
            elif n % 2 == 0:
                em.op("dve", lambda e, Sd=Sd: e.tensor_scalar(out=Sd[0:32], in0=Sd[0:32], scalar1=cx.c_kap[0:32], scalar2=None, op0=ALU.mult),
                      reads=[Sk], writes=[Sk])
            b = pbank()
            for cc in range(4):
                em.op("pe", lambda e, b=b, cc=cc, f_=f_: e.matmul(psum[b][:, cc * 128:(cc + 1) * 128], lhsT=f_[:, cc, :], rhs=c.ident, start=True, stop=True),
                      reads=[fk], writes=["ps%d" % b])
            em.op("act", lambda e, b=b, t_=t_: e.activation(out=t_, in_=psum[b][:], func=AF.Identity), reads=["ps%d" % b], writes=[tk])
            if d == 1:
                b = pbank()
                for cc in range(2):
                    em.op("pe", lambda e, b=b, cc=cc, f_=f_: e.matmul(psum[b][:, cc * 128:(cc + 1) * 128], lhsT=f_[:, 4 + cc, :], rhs=c.ident, start=True, stop=True),
                          reads=[fk], writes=["ps%d" % b])
                em.op("act", lambda e, b=b: e.activation(out=sgo, in_=psum[b][:, 0:256], func=AF.Silu), reads=["ps%d" % b], writes=["gsgo"])
            b = pbank()
            em.op("pe", lambda e, b=b, ca=ca, d=d: e.matmul(psum[b][:, 0:128], lhsT=ca[0:16, :], rhs=ab(d), start=True, stop=False), reads=[ck], writes=["ps%d" % b])
            em.op("pe", lambda e, b=b, d=d: e.matmul(psum[b][:, 0:128], lhsT=c.ones_row[0:1, :], rhs=abrow(d), start=False, stop=True), reads=[], writes=["ps%d" % b])
            em.op("act", lambda e, b=b: e.activation(out=ee, in_=psum[b][:, 0:128], func=AF.Exp, scale=-1.0), reads=["ps%d" % b], writes=["gee"])
            em.op("act", lambda e: e.activation(out=lg, in_=ee, func=AF.Ln, bias=1.0, scale=1.0), reads=["gee"], writes=["glg"])
            b = pbank()
            em.op("pe", lambda e, b=b, d=d: e.matmul(psum[b][:, 0:128], lhsT=c.maskc[:, d * 4 + 0, 0:128], rhs=lg, start=True, stop=True), reads=["glg"], writes=["ps%d" % b])
            em.op("pe", lambda e, b=b, d=d: e.matmul(psum[b][:, 128:256], lhsT=c.maskc[:, d * 4 + 3, 0:128], rhs=lg, start=True, stop=True), reads=["glg"], writes=["ps%d" % b])
            em.op("act", lambda e, b=b: e.activation(out=Eq, in_=psum[b][:, 0:128], func=AF.Exp, scale=-1.0 / 16), reads=["ps%d" % b], writes=["gEq"])
            em.op("act", lambda e, b=b: e.activation(out=Ek, in_=psum[b][:, 0:128], func=AF.Exp, scale=1.0 / 16), reads=["ps%d" % b], writes=["gEk"])
            em.op("act", lambda e, b=b: e.activation(out=Eb, in_=psum[b][:, 128:256], func=AF.Exp, scale=-1.0 / 16), reads=["ps%d" % b], writes=["gEb"])
            em.op("dve", lambda e, t_=t_: e.scalar_tensor_tensor(out=qt, in0=t_[:, 0:128], scalar=32 ** -0.5, in1=Eq, op0=ALU.mult, op1=ALU.mult), reads=[tk, "gEq"], writes=["gqt"])
            em.op("dve", lambda e, t_=t_: e.tensor_tensor(out=kh, in0=t_[:, 128:256], in1=Ek, op=ALU.mult), reads=[tk, "gEk"], writes=["gkh"])
            em.op("pool", lambda e, t_=t_: e.tensor_tensor(out=kb, in0=t_[:, 128:256], in1=Eb, op=ALU.mult), reads=[tk, "gEb"], writes=["gkb"])
            b = pbank()
            for h in range(4):
                em.op("pe", lambda e, b=b, h=h: e.matmul(psum[b][0:32, h * 128:(h + 1) * 128], lhsT=qt[:, h * 32:(h + 1) * 32], rhs=c.ident, start=True, stop=True),
                      reads=["gqt"], writes=["ps%d" % b])
            em.op("act", lambda e, b=b: e.activation(out=qT[0:32], in_=psum[b][0:32], func=AF.Identity), reads=["ps%d" % b], writes=["gqT"])
            b = pbank()
            for h in range(4):
                em.op("pe", lambda e, b=b, h=h: e.matmul(psum[b][0:32, h * 128:(h + 1) * 128], lhsT=kh[:, h * 32:(h + 1) * 32], rhs=c.ident, start=True, stop=True),
                      reads=["gkh"], writes=["ps%d" % b])
            em.op("dve", lambda e, b=b: e.tensor_copy(out=kT[0:32], in_=psum[b][0:32]), reads=["ps%d" % b], writes=["gkT"])
            b = pbank()
            for h in range(4):
                em.op("pe", lambda e, b=b, h=h: e.matmul(psum[b][:, h * 128:(h + 1) * 128], lhsT=kT[0:32, h * 128:(h + 1) * 128], rhs=qT[0:32, h * 128:(h + 1) * 128], start=True, stop=True),
                      reads=["gkT", "gqT"], writes=["ps%d" % b])
            em.op("dve", lambda e, b=b, d=d: e.tensor_tensor(out=AT, in0=psum[b][:], in1=c.maskc[:, d * 4 + 0, :], op=ALU.mult), reads=["ps%d" % b], writes=["gAT"])
            b = pbank()
            for h in range(4):
                em.op("pe", lambda e, b=b, h=h: e.matmul(psum[b][0:32, h:h + 1], lhsT=lg[:, h * 32:(h + 1) * 32], rhs=c.ones_col[:, 0:1], start=True, stop=True),
                      reads=["glg"], writes=["ps%d" % b])
            em.op("act", lambda e, b=b: e.activation(out=etot[0:32], in_=psum[b][0:32, 0:4], func=AF.Exp, scale=-1.0 / 16), reads=["ps%d" % b], writes=["getot"])
            bO = pbank()
            for h in range(4):
                em.op("pe", lambda e, bO=bO, h=h, t_=t_: e.matmul(psum[bO][:, h * 64:(h + 1) * 64], lhsT=AT[:, h * 128:(h + 1) * 128], rhs=t_[:, 256 + h * 64:256 + (h + 1) * 64], start=True, stop=False),
                      reads=["gAT", tk], writes=["ps%d" % bO])
                em.op("pe", lambda e, bO=bO, h=h, Sd=Sd: e.matmul(psum[bO][:, h * 64:(h + 1) * 64], lhsT=qT[0:32, h * 128:(h + 1) * 128], rhs=Sd[0:32, h, :], start=False, stop=True),
                      reads=["gqT", Sk], writes=["ps%d" % bO])
            bS = pbank()
            for h in range(4):
                em.op("pe", lambda e, bS=bS, h=h, t_=t_: e.matmul(psum[bS][0:32, h * 64:(h + 1) * 64], lhsT=kb[:, h * 32:(h + 1) * 32], rhs=t_[:, 256 + h * 64:256 + (h + 1) * 64], start=True, stop=True),
                      reads=["gkb", tk], writes=["ps%d" % bS])
            for h in range(4):
                em.op("dve", lambda e, bS=bS, h=h, Sd=Sd: e.scalar_tensor_tensor(out=Sd[0:32, h, :], in0=Sd[0:32, h, :], scalar=etot[0:32, h:h + 1], in1=psum[bS][0:32, h * 64:(h + 1) * 64],
                                                                            op0=ALU.mult, op1=ALU.add), reads=[Sk, "getot", "ps%d" % bS], writes=[Sk])
            if n % 2 == 1:
                blk = tc // 2
                em.dma("sp", lambda e, Sd=Sd, d=d, l=l, blk=blk: e.dma_start(out=cx.osg[l, d, blk], in_=Sd[0:32]), Sk + "o", reads=[Sk], final=True)
            if d == 0:
                em.op("act", lambda e, bO=bO, tc=tc: e.activation(out=OG[:, tc, :], in_=psum[bO][:, 0:256], func=AF.Identity), reads=["ps%d" % bO], writes=["gOG"])
            else:
                em.op("dve", lambda e, bO=bO, tc=tc: e.tensor_tensor(out=og, in0=psum[bO][:, 0:256], in1=OG[:, tc, :], op=ALU.add), reads=["ps%d" % bO, "gOG"], writes=["gog"])
                em.op("pool", lambda e: e.tensor_tensor(out=sq, in0=og, in1=og, op=ALU.mult), reads=["gog"], writes=["gsq"])
                em.op("dve", lambda e: e.tensor_reduce(out=ss, in_=sq.rearrange("p (h v) -> p h v", v=64), axis=AX.X, op=ALU.add), reads=["gsq"], writes=["gss"])
                em.op("act", lambda e: e.activation(out=ss, in_=ss, func=AF.Sqrt, bias=cx.c_eps, scale=1.0 / 64), reads=["gss"], writes=["gss"])
                em.op("dve", lambda e: e.reciprocal(out=ss, in_=ss), reads=["gss"], writes=["gss"])
                em.op("dve", lambda e: e.tensor_tensor(out=yd.rearrange("p (h v) -> p h v", v=64), in0=og.rearrange("p (h v) -> p h v", v=64),
                                                       in1=ss.unsqueeze(2).to_broadcast([128, 4, 64]), op=ALU.mult), reads=["gog", "gss"], writes=["gyd"])
                em.op("pool", lambda e: e.tensor_tensor(out=yd, in0=yd, in1=c.gln, op=ALU.mult), reads=["gyd"], writes=["gyd"])
                em.op("pool", lambda e: e.tensor_tensor(out=yd, in0=yd, in1=sgo, op=ALU.mult), reads=["gyd", "gsgo"], writes=["gyd"])
                b = pbank()
                for cc in range(2):
                    em.op("pe", lambda e, b=b, cc=cc: e.matmul(psum[b][:, cc * 128:(cc + 1) * 128], lhsT=yd[:, cc * 128:(cc + 1) * 128], rhs=c.ident, start=True, stop=True),
                          reads=["gyd"], writes=["ps%d" % b])
                yT_ = ydT[par]; yk = "gydT%d" % par
                em.op("act", lambda e, b=b, yT_=yT_: e.activation(out=yT_, in_=psum[b][:, 0:256].rearrange("p (c t) -> p c t", t=128), func=AF.Identity), reads=["ps%d" % b], writes=[yk])
                em.dma("sp", lambda e, yT_=yT_, tc=tc: e.dma_start(out=Yv(tc), in_=yT_), yk + "w", reads=[yk])


def rwkv_prepass(cx, l, c):
    em, ar, psum, pbank, U, Y = cx.em, cx.ar, cx.psum, cx.pbank, cx.U, cx.Y
    ZM = cx.ZM
    mark = ar.off
    zt = [ar.alloc([6, 514], F32) for _ in range(2)]
    sh = ar.alloc([6, 512], F32)
    zo = [ar.alloc([6, 512], F32) for _ in range(2)]
    Uz = U[512:1280, :].rearrange("(c p) t -> p c t", p=128)
    ZMv = ZM.rearrange("(c p) t -> p c t", p=128)
    for j in range(NT):
        z = zt[j % 2]; zk = "rzt%d" % (j % 2); o_ = zo[j % 2]; ok = "rzo%d" % (j % 2)
        t0 = j * 512
        if j == 0:
            em.op("pool", lambda e, z=z: e.memset(z[:, :, 0:1], 0.0), writes=[zk])
            em.dma("sp", lambda e, z=z: e.dma_start(out=z[:, :, 1:514], in_=Uz[:, :, 0:513]), zk, writes=[zk])
        elif j == NT - 1:
            em.op("pool", lambda e, z=z: e.memset(z[:, :, 513:514], 0.0), writes=[zk])
            em.dma("sp", lambda e, z=z, t0=t0: e.dma_start(out=z[:, :, 0:513], in_=Uz[:, :, t0 - 1:t0 + 512]), zk, writes=[zk])
        else:
            em.dma("sp", lambda e, z=z, t0=t0: e.dma_start(out=z, in_=Uz[:, :, t0 - 1:t0 + 513]), zk, writes=[zk])
        em.op("dve", lambda e, z=z: e.tensor_tensor(out=sh, in0=z[:, :, 0:512], in1=z[:, :, 2:514], op=ALU.add), reads=[zk], writes=["rsh"])
        em.op("dve", lambda e, z=z: e.scalar_tensor_tensor(out=sh[:, :, 0:512:256], in0=z[:, :, 0:512:256], scalar=cx.c_nk, in1=sh[:, :, 0:512:256], op0=ALU.mult, op1=ALU.add),
              reads=[zk, "rsh"], writes=["rsh"])
        em.op("dve", lambda e, z=z: e.scalar_tensor_tensor(out=sh[:, :, 255:512:256], in0=z[:, :, 257:514:256], scalar=cx.c_nk, in1=sh[:, :, 255:512:256], op0=ALU.mult, op1=ALU.add),
              reads=[zk, "rsh"], writes=["rsh"])
        for cc in range(6):
            em.op("pool", lambda e, cc=cc: e.tensor_scalar(out=sh[:, cc, :], in0=sh[:, cc, :], scalar1=c.hm[:, cc:cc + 1], scalar2=None, op0=ALU.mult), reads=["rsh"], writes=["rsh"])
            em.op("dve", lambda e, cc=cc, z=z, o_=o_: e.scalar_tensor_tensor(out=o_[:, cc, :], in0=z[:, cc, 1:513], scalar=c.om[:, cc:cc + 1], in1=sh[:, cc, :], op0=ALU.mult, op1=ALU.add),
                  reads=[zk, "rsh"], writes=[ok])
        em.dma("sp", lambda e, o_=o_, t0=t0: e.dma_start(out=ZMv[:, :, t0:t0 + 512], in_=o_), ok + "w", reads=[ok])
    em.barrier()
    ar.off = mark


def mix_rwkv(cx, l, c):
    em, ar, psum, pbank, U, Y = cx.em, cx.ar, cx.psum, cx.pbank, cx.U, cx.Y
    CD = 0.606531
    ZM = cx.ZM
    ZMv = ZM.rearrange("(c p) t -> p c t", p=128)
    A = lambda shp: ar.alloc(shp, F32)
    H = [A([4, 64]) for _ in range(2)]
    OR = A([32, 256]); BON = A([32, 4])
    zc = [A([6, 128]) for _ in range(2)]
    cw = [A([128]) for _ in range(2)]; ca = [A([128]) for _ in range(2)]; cg = [A([128]) for _ in range(2)]
    rk_ = A([512]); v_ = A([256])
    tcw = A([128]); sg = A([256]); aa = A([256])
    kw = A([256]); sq = A([256]); ss = A([4]); kk = A([256]); t1 = A([256]); kdir = A([256]); bb = A([256])
    Ex = A([256]); E1 = A([256]); E2 = A([256]); E3 = A([256]); E4 = A([256])
    kkt = A([256]); bh = A([256]); kh = A([256]); rt = A([256]); Kbar = A([256]); Bbn = A([256]); t2 = A([256]); bsum = A([4])
    kktT = A([512]); bhT = A([512]); khT = A([512]); rtT = A([512])
    Mp = [A([512]) for _ in range(2)]; Np = [A([512]) for _ in range(2)]; X = A([512])
    AkkT = A([512]); ArkT = A([512]); ArbT = A([512])
    WT = A([512]); Y0 = A([256]); U0 = A([256]); Uu = A([256]); pc = A([4])
    o = A([256]); s1 = A([4]); cen = A([256]); s2 = A([4]); bon = A([4]); sgc = A([128]); yc = A([256])
    ycT = [ar.alloc([2, 128], BF16) for _ in range(2)]
    Yv = lambda tc: cx.tv(Y, tc // 4)[:, 4:6, (tc % 4) * 128:(tc % 4 + 1) * 128]
    bw = lambda d: c.lr64[0:64, d * 256:(d + 1) * 256]
    bg = c.lr64[0:64, 512:768]
    ba = lambda d: c.lr32[0:32, d * 256:(d + 1) * 256]
    w0row = lambda d: c.rows[0:1, d * 256:(d + 1) * 256]
    a0row = lambda d: c.rows[0:1, 512 + d * 256:512 + (d + 1) * 256]
    h3 = lambda ap: ap.rearrange("p (h v) -> p h v", v=64)
    bc4 = lambda ap4: ap4.unsqueeze(2).to_broadcast([128, 4, 64])

    def ew(eng, fn, reads, writes):
        em.op(eng, fn, reads=reads, writes=writes)

    def mm4(b, lhs, rhs, reads, npart=128, width=128):
        for h in range(4):
            em.op("pe", lambda e, h=h: e.matmul(psum[b][0:npart, h * width:(h + 1) * width], lhsT=lhs(h), rhs=rhs(h), start=True, stop=True),
                  reads=reads, writes=["ps%d" % b])

    blk128 = lambda t: (lambda h: t[:, h * 128:(h + 1) * 128])
    blk64 = lambda t: (lambda h: t[:, h * 64:(h + 1) * 64])
    blkT = lambda t: (lambda h: t[0:64, h * 128:(h + 1) * 128])

    for d in range(2):
        order = list(range(32)) if d == 0 else list(range(31, -1, -1))
        Hd = H[d]; Hk = "rH%d" % d
        for n, tc in enumerate(order):
            par = n % 2
            tsl = slice(tc * 128, (tc + 1) * 128)
            z = zc[par]; zk = "rzc%d" % par; cw_ = cw[par]; cwk = "rcw%d" % par; ca_ = ca[par]; cak = "rca%d" % par; cg_ = cg[par]; cgk = "rcg%d" % par
            em.dma("sp", lambda e, z=z, tsl=tsl: e.dma_start(out=z, in_=ZMv[:, :, tsl]), zk, writes=[zk])
            em.dma("sp", lambda e, cw_=cw_, tsl=tsl, d=d: e.dma_start(out=cw_[0:64, :], in_=U[1280 + 64 * d:1280 + 64 * d + 64, tsl]), cwk, writes=[cwk])
            em.dma("sp", lambda e, ca_=ca_, tsl=tsl, d=d: e.dma_start(out=ca_[0:32, :], in_=U[1408 + 32 * d:1408 + 32 * d + 32, tsl]), cak, writes=[cak])
            if d == 1:
                em.dma("sp", lambda e, cg_=cg_, tsl=tsl: e.dma_start(out=cg_[0:64, :], in_=U[1472:1536, tsl]), cgk, writes=[cgk])
            if n == 0:
                em.dma("sp", lambda e, Hd=Hd, d=d, l=l: e.dma_start(out=Hd[0:64], in_=cx.sr0[l, d]), Hk + "i", writes=[Hk])
            elif n % 2 == 0:
                ew("dve", lambda e, Hd=Hd: e.tensor_scalar(out=Hd[0:64], in0=Hd[0:64], scalar1=cx.c_kap[0:64], scalar2=None, op0=ALU.mult), [Hk], [Hk])
            b = pbank()
            for cc in range(4):
                em.op("pe", lambda e, b=b, cc=cc, z=z: e.matmul(psum[b][:, cc * 128:(cc + 1) * 128], lhsT=z[:, cc, :], rhs=c.ident, start=True, stop=True), reads=[zk], writes=["ps%d" % b])
            ew("act", lambda e, b=b: e.activation(out=rk_, in_=psum[b][:], func=AF.Identity), ["ps%d" % b], ["rrk"])
            b = pbank()
            for cc in range(2):
                em.op("pe", lambda e, b=b, cc=cc, z=z: e.matmul(psum[b][:, cc * 128:(cc + 1) * 128], lhsT=z[:, 4 + cc, :], rhs=c.ident, start=True, stop=True), reads=[zk], writes=["ps%d" % b])
            ew("act", lambda e, b=b: e.activation(out=v_, in_=psum[b][:, 0:256], func=AF.Identity), ["ps%d" % b], ["rv"])
            r_ = rk_[:, 0:256]; k_ = rk_[:, 256:512]
            ew("act", lambda e, cw_=cw_: e.activation(out=tcw[0:64], in_=cw_[0:64], func=AF.Tanh), [cwk], ["rtcw"])
            b = pbank()
            em.op("pe", lambda e, b=b, d=d: e.matmul(psum[b][:, 0:256], lhsT=tcw[0:64, :], rhs=bw(d), start=True, stop=False), reads=["rtcw"], writes=["ps%d" % b])
            em.op("pe", lambda e, b=b, d=d: e.matmul(psum[b][:, 0:256], lhsT=c.ones_row[0:1, :], rhs=w0row(d), start=False, stop=True), reads=[], writes=["ps%d" % b])
            em.op("pe", lambda e, b=b, d=d, ca_=ca_: e.matmul(psum[b][:, 256:512], lhsT=ca_[0:32, :], rhs=ba(d), start=True, stop=False), reads=[cak], writes=["ps%d" % b])
            em.op("pe", lambda e, b=b, d=d: e.matmul(psum[b][:, 256:512], lhsT=c.ones_row[0:1, :], rhs=a0row(d), start=False, stop=True), reads=[], writes=["ps%d" % b])
            ew("act", lambda e, b=b: e.activation(out=sg, in_=psum[b][:, 0:256], func=AF.Sigmoid), ["ps%d" % b], ["rsg"])
            ew("act", lambda e, b=b: e.activation(out=aa, in_=psum[b][:, 256:512], func=AF.Sigmoid), ["ps%d" % b], ["raa"])
            ew("dve", lambda e: e.tensor_tensor(out=kw, in0=k_, in1=c.kkw, op=ALU.mult), ["rrk"], ["rkw"])
            ew("pool", lambda e: e.tensor_tensor(out=sq, in0=kw, in1=kw, op=ALU.mult), ["rkw"], ["rsq"])
            ew("dve", lambda e: e.tensor_reduce(out=ss, in_=h3(sq), axis=AX.X, op=ALU.add), ["rsq"], ["rss"])
            ew("act", lambda e: e.activation(out=ss, in_=ss, func=AF.Sqrt, bias=cx.c_eps, scale=1.0), ["rss"], ["rss"])
            ew("dve", lambda e: e.reciprocal(out=ss, in_=ss), ["rss"], ["rss"])
            ew("dve", lambda e: e.tensor_tensor(out=h3(kk), in0=h3(kw), in1=bc4(ss), op=ALU.mult), ["rkw", "rss"], ["rkk"])
            ew("pool", lambda e: e.tensor_tensor(out=t1, in0=aa, in1=c.ka, op=ALU.mult), ["raa"], ["rt1"])
            ew("pool", lambda e: e.tensor_tensor(out=t1, in0=t1, in1=c.oka, op=ALU.add), ["rt1"], ["rt1"])
            ew("pool", lambda e: e.tensor_tensor(out=kdir, in0=k_, in1=t1, op=ALU.mult), ["rrk", "rt1"], ["rkdir"])
            ew("dve", lambda e: e.tensor_tensor(out=bb, in0=kk, in1=aa, op=ALU.mult), ["rkk", "raa"], ["rbb"])
            b = pbank()
            em.op("pe", lambda e, b=b, d=d: e.matmul(psum[b][:, 0:256], lhsT=c.maskc[:, d * 4 + 0, 0:128], rhs=sg, start=True, stop=True), reads=["rsg"], writes=["ps%d" % b])
            em.op("pe", lambda e, b=b, d=d: e.matmul(psum[b][:, 256:512], lhsT=c.maskc[:, d * 4 + 3, 0:128], rhs=sg, start=True, stop=True), reads=["rsg"], writes=["ps%d" % b])
            ew("dve", lambda e, b=b: e.tensor_tensor(out=Ex, in0=psum[b][:, 0:256], in1=sg, op=ALU.subtract), ["ps%d" % b, "rsg"], ["rEx"])
            ew("act", lambda e: e.activation(out=E1, in_=Ex, func=AF.Exp, scale=-CD), ["rEx"], ["rE1"])
            ew("act", lambda e, b=b: e.activation(out=E2, in_=psum[b][:, 0:256], func=AF.Exp, scale=CD), ["ps%d" % b], ["rE2"])
            ew("act", lambda e, b=b: e.activation(out=E3, in_=psum[b][:, 0:256], func=AF.Exp, scale=-CD), ["ps%d" % b], ["rE3"])
            ew("act", lambda e, b=b: e.activation(out=E4, in_=psum[b][:, 256:512], func=AF.Exp, scale=-CD), ["ps%d" % b], ["rE4"])
            ew("dve", lambda e: e.tensor_tensor(out=kkt, in0=kk, in1=E1, op=ALU.mult), ["rkk", "rE1"], ["rkkt"])
            ew("pool", lambda e: e.tensor_tensor(out=bh, in0=bb, in1=E2, op=ALU.mult), ["rbb", "rE2"], ["rbh"])
            ew("dve", lambda e: e.tensor_tensor(out=kh, in0=kdir, in1=E2, op=ALU.mult), ["rkdir", "rE2"], ["rkh"])
            ew("pool", lambda e: e.tensor_tensor(out=rt, in0=r_, in1=E3, op=ALU.mult), ["rrk", "rE3"], ["rrt"])
            ew("dve", lambda e: e.tensor_tensor(out=Kbar, in0=kdir, in1=E4, op=ALU.mult), ["rkdir", "rE4"], ["rKbar"])
            ew("dve", lambda e: e.scalar_tensor_tensor(out=Bbn, in0=bb, scalar=-1.0, in1=E4, op0=ALU.mult, op1=ALU.mult), ["rbb", "rE4"], ["rBbn"])
            ew("pool", lambda e: e.tensor_tensor(out=t2, in0=r_, in1=kdir, op=ALU.mult), ["rrk", "rkdir"], ["rt2"])
            ew("pool", lambda e: e.tensor_tensor(out=t2, in0=t2, in1=c.rk, op=ALU.mult), ["rt2"], ["rt2"])
            if d == 0:
                ew("dve", lambda e, tc=tc: e.tensor_reduce(out=BON[:, tc, :], in_=h3(t2), axis=AX.X, op=ALU.add), ["rt2"], ["rBON"])
            else:
                ew("dve", lambda e: e.tensor_reduce(out=bsum, in_=h3(t2), axis=AX.X, op=ALU.add), ["rt2"], ["rbsum"])
            for ti, (src, sk, dst, dk_) in enumerate(((kkt, "rkkt", kktT, "rkktT"), (bh, "rbh", bhT, "rbhT"), (kh, "rkh", khT, "rkhT"), (rt, "rrt", rtT, "rrtT"))):
                b = pbank()
                mm4(b, blk64(src), lambda h: c.ident, [sk], npart=64)
                if ti % 2 == 0:
                    ew("act", lambda e, b=b, dst=dst: e.activation(out=dst[0:64], in_=psum[b][0:64], func=AF.Identity), ["ps%d" % b], [dk_])
                else:
                    ew("dve", lambda e, b=b, dst=dst: e.tensor_copy(out=dst[0:64], in_=psum[b][0:64]), ["ps%d" % b], [dk_])
            def amat(dst, dkey, lhsT_t, lk, rhs_t, rkey, kind, d=d):
                b = pbank()
                mm4(b, blkT(lhsT_t), blkT(rhs_t), [lk, rkey])
                ew("dve", lambda e, b=b: e.tensor_tensor(out=dst, in0=psum[b][:], in1=c.maskc[:, d * 4 + kind, :], op=ALU.mult), ["ps%d" % b], [dkey])
            amat(Mp[0], "rMp0", bhT, "rbhT", kktT, "rkktT", 2)
            amat(Np[0], "rNp0", kktT, "rkktT", bhT, "rbhT", 3)
            amat(AkkT, "rAkkT", khT, "rkhT", kktT, "rkktT", 2)
            amat(ArkT, "rArkT", khT, "rkhT", rtT, "rrtT", 0)
            amat(ArbT, "rArbT", bhT, "rbhT", rtT, "rrtT", 1)
            ew("pool", lambda e: e.tensor_tensor(out=X, in0=c.ident4, in1=Mp[0], op=ALU.subtract), ["rMp0"], ["rX"])
            cur = 0
            for jj in range(1, 7):
                nxt = 1 - cur
                if jj < 6:
                    b = pbank()
                    mm4(b, blk128(Np[cur]), blk128(Mp[cur]), ["rNp%d" % cur, "rMp%d" % cur])
                    ew("act", lambda e, b=b, nxt=nxt: e.activation(out=Mp[nxt], in_=psum[b][:], func=AF.Identity), ["ps%d" % b], ["rMp%d" % nxt])
                b = pbank()
                mm4(b, blk128(Mp[cur]), blk128(Np[cur]), ["rNp%d" % cur, "rMp%d" % cur])
                ew("dve", lambda e, b=b, nxt=nxt: e.tensor_copy(out=Np[nxt], in_=psum[b][:]), ["ps%d" % b], ["rNp%d" % nxt])
                b = pbank()
                mm4(b, blk128(Np[nxt]), blk128(X), ["rNp%d" % nxt, "rX"])
                ew("dve", lambda e, b=b: e.tensor_tensor(out=X, in0=X, in1=psum[b][:], op=ALU.add), ["ps%d" % b, "rX"], ["rX"])
                cur = nxt
            b = pbank()
            mm4(b, blk64(kkt), blk128(X), ["rkkt", "rX"], npart=64)
            ew("act", lambda e, b=b: e.activation(out=WT[0:64], in_=psum[b][0:64], func=AF.Identity), ["ps%d" % b], ["rWT"])
            b = pbank()
            mm4(b, blk128(AkkT), blk64(v_), ["rAkkT", "rv"], width=64)
            ew("act", lambda e, b=b: e.activation(out=Y0, in_=psum[b][:, 0:256], func=AF.Identity), ["ps%d" % b], ["rY0"])
            b = pbank()
            mm4(b, blk128(X), blk64(Y0), ["rX", "rY0"], width=64)
            ew("dve", lambda e, b=b: e.tensor_copy(out=U0, in_=psum[b][:, 0:256]), ["ps%d" % b], ["rU0"])
            b = pbank()
            mm4(b, blk64(sg), lambda h: c.ones_col[:, 0:1], ["rsg"], npart=64, width=1)
            ew("act", lambda e, b=b: e.activation(out=pc[0:64], in_=psum[b][0:64, 0:4], func=AF.Exp, scale=-CD), ["ps%d" % b], ["rpc"])
            b = pbank()
            mm4(b, blkT(WT), lambda h, Hd=Hd: Hd[0:64, h, :], ["rWT", Hk], width=64)
            ew("dve", lambda e, b=b: e.tensor_tensor(out=Uu, in0=psum[b][:, 0:256], in1=U0, op=ALU.add), ["ps%d" % b, "rU0"], ["rUu"])
            bO = pbank()
            for h in range(4):
                osl = slice(h * 64, (h + 1) * 64)
                em.op("pe", lambda e, h=h, osl=osl, Hd=Hd, bO=bO: e.matmul(psum[bO][:, osl], lhsT=rtT[0:64, h * 128:(h + 1) * 128], rhs=Hd[0:64, h, :], start=True, stop=False), reads=["rrtT", Hk], writes=["ps%d" % bO])
                em.op("pe", lambda e, h=h, osl=osl, bO=bO: e.matmul(psum[bO][:, osl], lhsT=ArkT[:, h * 128:(h + 1) * 128], rhs=v_[:, osl], start=False, stop=False), reads=["rArkT", "rv"], writes=["ps%d" % bO])
                em.op("pe", lambda e, h=h, osl=osl, bO=bO: e.matmul(psum[bO][:, osl], lhsT=ArbT[:, h * 128:(h + 1) * 128], rhs=Uu[:, osl], start=False, stop=True), reads=["rArbT", "rUu"], writes=["ps%d" % bO])
            bH = pbank()
            for h in range(4):
                osl = slice(h * 64, (h + 1) * 64)
                em.op("pe", lambda e, h=h, osl=osl, bH=bH: e.matmul(psum[bH][0:64, osl], lhsT=Kbar[:, osl], rhs=v_[:, osl], start=True, stop=False), reads=["rKbar", "rv"], writes=["ps%d" % bH])
                em.op("pe", lambda e, h=h, osl=osl, bH=bH: e.matmul(psum[bH][0:64, osl], lhsT=Bbn[:, osl], rhs=Uu[:, osl], start=False, stop=True), reads=["rBbn", "rUu"], writes=["ps%d" % bH])
            for h in range(4):
                ew("dve", lambda e, h=h, Hd=Hd, bH=bH: e.scalar_tensor_tensor(out=Hd[0:64, h, :], in0=Hd[0:64, h, :], scalar=pc[0:64, h:h + 1], in1=psum[bH][0:64, h * 64:(h + 1) * 64], op0=ALU.mult, op1=ALU.add),
                   [Hk, "rpc", "ps%d" % bH], [Hk])
            if n % 2 == 1:
                blk = tc // 2
                em.dma("sp", lambda e, Hd=Hd, d=d, l=l, blk=blk: e.dma_start(out=cx.osr[l, d, blk], in_=Hd[0:64]), Hk + "o", reads=[Hk], final=True)
            if d == 0:
                ew("act", lambda e, tc=tc, bO=bO: e.activation(out=OR[:, tc, :], in_=psum[bO][:, 0:256], func=AF.Identity), ["ps%d" % bO], ["rOR"])
                continue
            ew("dve", lambda e, tc=tc, bO=bO: e.tensor_tensor(out=o, in0=psum[bO][:, 0:256], in1=OR[:, tc, :], op=ALU.add), ["ps%d" % bO, "rOR"], ["ro"])
            ew("dve", lambda e: e.tensor_reduce(out=s1, in_=h3(o), axis=AX.X, op=ALU.add), ["ro"], ["rs1"])
            ew("dve", lambda e: e.tensor_scalar(out=s1, in0=s1, scalar1=1.0 / 64, scalar2=None, op0=ALU.mult), ["rs1"], ["rs1"])
            ew("dve", lambda e: e.tensor_tensor(out=h3(cen), in0=h3(o), in1=bc4(s1), op=ALU.subtract), ["ro", "rs1"], ["rcen"])
            ew("pool", lambda e: e.tensor_tensor(out=sq, in0=cen, in1=cen, op=ALU.mult), ["rcen"], ["rsq"])
            ew("dve", lambda e: e.tensor_reduce(out=s2, in_=h3(sq), axis=AX.X, op=ALU.add), ["rsq"], ["rs2"])
            ew("act", lambda e: e.activation(out=s2, in_=s2, func=AF.Sqrt, bias=cx.c_gneps, scale=1.0 / 64), ["rs2"], ["rs2"])
            ew("dve", lambda e: e.reciprocal(out=s2, in_=s2), ["rs2"], ["rs2"])
            ew("dve", lambda e: e.tensor_tensor(out=h3(cen), in0=h3(cen), in1=bc4(s2), op=ALU.mult), ["rcen", "rs2"], ["rcen"])
            ew("pool", lambda e: e.tensor_tensor(out=cen, in0=cen, in1=c.gn, op=ALU.mult), ["rcen"], ["rcen"])
            ew("dve", lambda e, tc=tc: e.tensor_tensor(out=bon, in0=BON[:, tc, :], in1=bsum, op=ALU.add), ["rBON", "rbsum"], ["rbon"])
            ew("dve", lambda e: e.tensor_tensor(out=h3(t2), in0=h3(v_), in1=bc4(bon), op=ALU.mult), ["rv", "rbon", "rt2"], ["rt2"])
            ew("pool", lambda e: e.tensor_tensor(out=cen, in0=cen, in1=t2, op=ALU.add), ["rcen", "rt2"], ["rcen"])
            ew("act", lambda e, cg_=cg_: e.activation(out=sgc[0:64], in_=cg_[0:64], func=AF.Sigmoid), [cgk], ["rsgc"])
            b = pbank()
            em.op("pe", lambda e, b=b: e.matmul(psum[b][:, 0:256], lhsT=sgc[0:64, :], rhs=bg, start=True, stop=True), reads=["rsgc"], writes=["ps%d" % b])
            ew("dve", lambda e, b=b: e.tensor_tensor(out=yc, in0=psum[b][:, 0:256], in1=cen, op=ALU.mult), ["ps%d" % b, "rcen"], ["ryc"])
            b = pbank()
            for cc in range(2):
                em.op("pe", lambda e, b=b, cc=cc: e.matmul(psum[b][:, cc * 128:(cc + 1) * 128], lhsT=yc[:, cc * 128:(cc + 1) * 128], rhs=c.ident, start=True, stop=True), reads=["ryc"], writes=["ps%d" % b])
            yT_ = ycT[par]; yk = "rycT%d" % par
            ew("act", lambda e, b=b, yT_=yT_: e.activation(out=yT_, in_=psum[b][:, 0:256].rearrange("p (c t) -> p c t", t=128), func=AF.Identity), ["ps%d" % b], [yk])
            em.dma("sp", lambda e, yT_=yT_, tc=tc: e.dma_start(out=Yv(tc), in_=yT_), yk + "w", reads=[yk])


_NC = None
_EM = None
_LAST = None


def kernel(**inputs):
    global _NC
    f = lambda k: np.ascontiguousarray(np.asarray(inputs[k], dtype=np.float32))
    x_prompt = f("x_prompt"); x_sample = f("x_sample"); c = f("c"); c_ctx = f("c_ctx")
    jobs = []
    ntok = x_sample.shape[1]
    rows = ntok // 64
    rr, cc = np.meshgrid(np.arange(rows, dtype=np.float32), np.arange(64, dtype=np.float32), indexing="ij")
    rr = rr.reshape(-1); cc = cc.reshape(-1)
    quarter = D // 4
    omega = (1.0 / (np.float32(10000.0) ** (np.arange(quarter, dtype=np.float32) / np.float32(quarter)))).astype(np.float32)
    arr = rr[:, None] * omega; acc = cc[:, None] * omega
    pos = np.concatenate([np.sin(arr), np.cos(arr), np.sin(acc), np.cos(acc)], axis=-1).astype(np.float32)
    lay8 = lambda v: np.ascontiguousarray(v.reshape(-1, 128).T)
    common = {
        "ada_w": f("ada_w"),
        "ada_b": np.ascontiguousarray(f("ada_b").reshape(DEPTH, 48, 128).transpose(0, 2, 1)),
        "n1g": np.ascontiguousarray(f("norm1_g").reshape(DEPTH, 8, 128).transpose(0, 2, 1)),
        "n2g": np.ascontiguousarray(f("norm2_g").reshape(DEPTH, 8, 128).transpose(0, 2, 1)),
        "fg": lay8(f("final_g")),
        "w_in": f("w_in"), "w_br": f("w_branch").reshape(DEPTH, D, D), "w_out": f("w_out"),
        "w1": f("mlp_w1"), "w2": f("mlp_w2"),
    }
    import ml_dtypes
    g = lambda k: f(k)
    def bcast(v):
        return np.broadcast_to(v.reshape(1, -1), (128, v.size))
    prm = np.zeros((DEPTH, 128, 1568), np.float32); rows_ = np.zeros((DEPTH, 1, 1280), np.float32)
    lr64 = np.zeros((DEPTH, 64, 768), np.float32); lr32 = np.zeros((DEPTH, 32, 512), np.float32); lr16 = np.zeros((DEPTH, 16, 256), np.float32)
    for l in range(DEPTH):
        prm[l, :, 0:2] = g("pool_scale")[l].reshape(2, 128).T
        prm[l, :, 2:8] = g("rwkv_mu")[l].reshape(6, 128).T
        for i, nm in enumerate(("rwkv_kk", "rwkv_ka", None, "rwkv_rk", "rwkv_gn", "gla_norm")):
            if nm is not None:
                prm[l, :, 32 + i * 256:32 + (i + 1) * 256] = bcast(g(nm)[l])
        rows_[l, 0, 0:512] = g("rwkv_w0")[l].reshape(-1); rows_[l, 0, 512:1024] = g("rwkv_a0")[l].reshape(-1)
        rows_[l, 0, 1024:1280] = g("gla_abias")[l].reshape(-1)
        lr64[l, :, 0:256] = g("rwkv_bw")[l, 0]; lr64[l, :, 256:512] = g("rwkv_bw")[l, 1]; lr64[l, :, 512:768] = g("rwkv_bg")[l]
        lr32[l, :, 0:256] = g("rwkv_ba")[l, 0]; lr32[l, :, 256:512] = g("rwkv_ba")[l, 1]
        lr16[l, :, 0:128] = g("gla_ab")[l, 0]; lr16[l, :, 128:256] = g("gla_ab")[l, 1]
    ii = np.arange(128)[:, None]; jj = np.arange(128)[None, :]
    maskc = np.zeros((128, 8, 512), np.float32)
    for d_ in range(2):
        incl = (ii <= jj) if d_ == 0 else (ii >= jj)
        strict = (ii < jj) if d_ == 0 else (ii > jj)
        after = (ii > jj) if d_ == 0 else (ii < jj)
        for kind, m in enumerate((incl, incl, strict, after)):
            mm_ = np.tile(m.astype(np.float32), (1, 4))
            maskc[:, d_ * 4 + kind, :] = -mm_ if kind == 1 else mm_
    ch = np.arange(256); gch = ch // 64; cidx = ch % 64
    kk_ = np.arange(64)
    csm = np.zeros((256, 512), np.float64)
    for gi in range(4):
        rws = np.where(gch == gi)[0]
        ang = 2 * np.pi * np.outer(cidx[rws], kk_) / 64.0
        csm[np.ix_(rws, gi * 64 + kk_)] = np.cos(ang) / 8.0
        csm[np.ix_(rws, 256 + gi * 64 + kk_)] = -np.sin(ang) / 8.0
    csm = csm.reshape(2, 128, 512).astype(np.float32)

    def job_consts(L):
        tt = np.arange(T)
        s_ = tt % L
        invc = np.zeros((2, 128, T), np.float32)
        for gi, win in enumerate((2, 4, 8, 16)):
            lo = np.clip(s_ - win // 2, 0, L - 1); hi = np.clip(s_ + (win - win // 2) - 1, 0, L - 1)
            invc[gi // 2, (gi % 2) * 64:(gi % 2) * 64 + 64, :] = (1.0 / (hi - lo + 1))[None, :]
        CLm = np.zeros((T, T), np.float32); SLm = np.zeros((T, T), np.float32)
        sl_ = np.arange(L)
        mmod = np.outer(sl_, sl_) % L
        cb = (np.cos(2 * np.pi * mmod / L) / np.sqrt(L)).astype(np.float32)
        sb_ = (np.sin(2 * np.pi * mmod / L) / np.sqrt(L)).astype(np.float32)
        for b_ in range(T // L):
            CLm[b_ * L:(b_ + 1) * L, b_ * L:(b_ + 1) * L] = cb
            SLm[b_ * L:(b_ + 1) * L, b_ * L:(b_ + 1) * L] = sb_
        def lay(m):
            m5 = m.reshape(2, 16, 128, 8, 512).transpose(3, 0, 2, 1, 4)
            return np.ascontiguousarray(m5).reshape(8, 2, 128, 16 * 512).astype(ml_dtypes.bfloat16)
        return invc, lay(CLm), lay(SLm)
    invc_s, CL_s, SL_s = job_consts(T)
    invc_p, CL_p, SL_p = job_consts(256)
    common.update(prm=prm, rows=rows_, lr64=lr64, lr32=lr32, lr16=lr16, pool_w=g("pool_w"),
                  ident=np.eye(128, dtype=np.float32), maskc=maskc, csm=csm)
    st_r = g("state_rwkv"); st_g = g("state_gla")
    for s in range(2):
        common_s = dict(common, invc=invc_s, CL=CL_s, SL=SL_s,
                        sr0=np.ascontiguousarray(st_r[s].transpose(0, 1, 4, 2, 3)),
                        sg0=np.ascontiguousarray(st_g[s].transpose(0, 1, 3, 2, 4)))
        jobs.append(dict(common_s, xT=np.ascontiguousarray(x_sample[s].T), pos=np.ascontiguousarray(pos.T),
                         cond=lay8(c[s]), kap=np.ones((128, 1), np.float32)))
    xp = x_prompt.reshape(-1, D)
    common = dict(common, invc=invc_p, CL=CL_p, SL=SL_p, sr0=np.zeros((DEPTH, 2, 64, 4, 64), np.float32),
                  sg0=np.zeros((DEPTH, 2, 32, 4, 64), np.float32))
    pj = dict(common, xT=np.ascontiguousarray(xp.T), pos=np.zeros((D, T), np.float32),
              cond=lay8(c_ctx), kap=np.zeros((128, 1), np.float32))
    jobs.append(pj)
    while len(jobs) < N_CORES:
        jobs.append(pj)
    if _NC is None:
        _NC = build_program()
    res = run_bass_kernel_spmd(_NC, jobs, core_ids=list(range(N_CORES)))
    r = res.results
    global _LAST
    _LAST = r
    y_sample = np.stack([np.ascontiguousarray(r[s]["yT"].T) for s in range(2)], axis=0).astype(np.float32)
    y_prompt = np.ascontiguousarray(r[2]["yT"].T).reshape(x_prompt.shape).astype(np.float32)
    B = x_prompt.shape[0]
    nsr = np.ascontiguousarray(np.asarray(r[2]["osr"]).transpose(2, 0, 1, 4, 5, 3)).astype(np.float32)
    nsg = np.ascontiguousarray(np.asarray(r[2]["osg"]).transpose(2, 0, 1, 4, 3, 5)).astype(np.float32)
    return (y_prompt, y_sample, nsr, nsg)
```

```python
import contextlib
import numpy as np
import concourse.bass as bass
import concourse.mybir as mybir
from concourse.bass_utils import run_bass_kernel_spmd

F32 = mybir.dt.float32
BF16 = mybir.dt.bfloat16
ALU = mybir.AluOpType
AF = mybir.ActivationFunctionType
AX = mybir.AxisListType

D = 1024
T = 4096
NT = 8
TN = 512
DEPTH = 2
P_IN = 6432
NMIX = 2336
N_CORES = 8
EPS = 1e-6


class Em:
    ENGS = ("pe", "act", "dve", "pool", "sp")

    def __init__(self, nc):
        self.nc = nc
        self.ops = {e: [] for e in self.ENGS}
        self.cnt = {e: 0 for e in self.ENGS}
        self.seen = {e: {} for e in self.ENGS}
        self.last_w = {}
        self.readers = {}
        self.dma_sems = {}
        self.free_sems = []
        self.pool_sems = {}
        self.pool_names = set()
        self.sem_names = ["c_" + e for e in self.ENGS]
        self.final_tokens = []
        self.marks = []

    def _deps(self, eng, reads, writes):
        toks = []
        for k in reads:
            t = self.last_w.get(k)
            if t is not None:
                toks.append(t)
        for k in writes:
            t = self.last_w.get(k)
            if t is not None:
                toks.append(t)
            toks.extend(self.readers.get(k, ()))
        seen = self.seen[eng]
        best = {}
        own = "c_" + eng
        for (s, v) in toks:
            if eng == "pe" and s == own:
                continue
            if seen.get(s, 0) < v and best.get(s, 0) < v:
                best[s] = v
        waits = []
        for s, v in best.items():
            seen[s] = v
            waits.append((s, v))
        return waits

    def _commit(self, tok, reads, writes):
        for k in reads:
            self.readers.setdefault(k, []).append(tok)
        for k in writes:
            self.last_w[k] = tok
            self.readers[k] = []

    def op(self, eng, fn, reads=(), writes=()):
        if eng != "pe":
            pr = [k for k in reads if k.startswith("ps")]
            if pr:
                writes = list(writes) + pr
        waits = self._deps(eng, reads, writes)
        self.cnt[eng] += 1
        tok = ("c_" + eng, self.cnt[eng])
        self.ops[eng].append((waits, fn, ("c_" + eng, 1)))
        self._commit(tok, reads, writes)
        return tok

    def dma(self, q, fn, semkey, reads=(), writes=(), final=False):
        if semkey not in self.dma_sems:
            if q == "sp" and self.free_sems:
                self.dma_sems[semkey] = self.free_sems.pop()
            elif q != "sp" and semkey in self.pool_sems:
                self.dma_sems[semkey] = self.pool_sems[semkey]
            else:
                name = "d%d" % (len(self.sem_names) - len(self.ENGS))
                self.dma_sems[semkey] = [name, 0]
                self.sem_names.append(name)
        ent = self.dma_sems[semkey]
        if q != "sp":
            self.pool_names.add(ent[0])
        waits = self._deps(q, reads, writes)
        ent[1] += 16
        tok = (ent[0], ent[1])
        self.ops[q].append((waits, fn, (ent[0], 16)))
        self._commit(tok, reads, writes)
        if final:
            self.final_tokens.append(tok)
        return tok

    def barrier(self, label=""):
        self.marks.append((label, dict(self.cnt)))
        targets = [("c_" + e, self.cnt[e]) for e in self.ENGS if self.cnt[e] > 0]
        targets += [(n, c) for (n, c) in self.dma_sems.values() if c > 0]
        for e in self.ENGS:
            waits = []
            for (s, v) in targets:
                if e == "pe" and s == "c_pe":
                    continue
                if self.seen[e].get(s, 0) < v:
                    self.seen[e][s] = v
                    waits.append((s, v))
            if waits:
                self.ops[e].append((waits, None, None))
        for k_, ent_ in self.dma_sems.items():
            if ent_[0] in self.pool_names:
                self.pool_sems[k_] = ent_
            else:
                self.free_sems.append(ent_)
        self.dma_sems = {}

    def build(self):
        nc = self.nc
        with contextlib.ExitStack() as st:
            print("Em: %d semaphores, ops:" % len(self.sem_names), {e: len(v) for e, v in self.ops.items()})
            sems = {n: st.enter_context(nc.semaphore(n)) for n in self.sem_names}
            fin = {}
            for (s, v) in self.final_tokens:
                fin[s] = max(fin.get(s, 0), v)
            block = st.enter_context(nc.Block())

            def runner(ename):
                def f(e):
                    for waits, fn, inc in self.ops[ename]:
                        for (s, v) in waits:
                            e.wait_ge(sems[s], v)
                        if fn is not None:
                            fn(e).then_inc(sems[inc[0]], inc[1])
                    if ename == "sp":
                        for s, v in fin.items():
                            e.wait_ge(sems[s], v)
                return f
            block.tensor(runner("pe"))
            block.scalar(runner("act"))
            block.vector(runner("dve"))
            block.gpsimd(runner("pool"))
            block.sync(runner("sp"))


class Rec:
    def __init__(self):
        self.items = []

    def op(self, eng, fn, reads=(), writes=()):
        self.items.append(("op", eng, fn, list(reads), list(writes)))

    def dma(self, q, fn, semkey, reads=(), writes=(), final=False):
        self.items.append(("dma", q, fn, semkey, list(reads), list(writes), final))


def merge_streams(em, recs):
    pos = [0] * len(recs)
    tot = [max(1, len(r.items)) for r in recs]
    while True:
        best = None
        for i, r in enumerate(recs):
            if pos[i] < len(r.items):
                frac = pos[i] / tot[i]
                if best is None or frac < best[0]:
                    best = (frac, i)
        if best is None:
            break
        i = best[1]
        it = recs[i].items[pos[i]]; pos[i] += 1
        if it[0] == "op":
            em.op(it[1], it[2], reads=it[3], writes=it[4])
        else:
            em.dma(it[1], it[2], it[3], reads=it[4], writes=it[5], final=it[6])


class Arena:
    def __init__(self, handle_bf16, nelem):
        self.h = handle_bf16
        self.n = nelem
        self.off = 0

    def reset(self):
        self.off = 0

    def alloc(self, shape_free, dt):
        n = int(np.prod(shape_free))
        nb = n * (2 if dt == F32 else 1)
        nb = (nb + 15) // 16 * 16
        assert self.off + nb <= self.n, ("arena overflow", self.off, nb, self.n)
        ap = self.h[:, self.off:self.off + n * (2 if dt == F32 else 1)]
        self.off += nb
        if dt == F32:
            ap = ap.bitcast(F32)
        if len(shape_free) == 2:
            ap = ap.rearrange("p (a b) -> p a b", b=shape_free[1])
        elif len(shape_free) == 3:
            ap = ap.rearrange("p (a b c) -> p a b c", b=shape_free[1], c=shape_free[2])
        return ap


def build_program():
    nc = bass.Bass("TRN2", target_bir_lowering=False)
    dI = lambda n, sh, dt=F32: nc.dram_tensor(n, sh, dt, kind="ExternalInput").ap()
    DBG = False
    NL = DEPTH
    dS = lambda n, sh, dt=F32: nc.dram_tensor(n, sh, dt, kind="Internal").ap()
    dO = lambda n, sh, dt=F32: nc.dram_tensor(n, sh, dt, kind="ExternalOutput").ap()
    xT = dI("xT", [D, T]); pos = dI("pos", [D, T])
    cond = dI("cond", [128, 8]); kap = dI("kap", [128, 1])
    ada_w = dI("ada_w", [DEPTH, D, 6 * D]); ada_b = dI("ada_b", [DEPTH, 128, 48])
    n1g = dI("n1g", [DEPTH, 128, 8]); n2g = dI("n2g", [DEPTH, 128, 8]); fg = dI("fg", [128, 8])
    w_in = dI("w_in", [DEPTH, D, P_IN]); w_br = dI("w_br", [DEPTH, D, D]); w_out = dI("w_out", [DEPTH, D, D])
    w1 = dI("w1", [DEPTH, D, 4 * D]); w2 = dI("w2", [DEPTH, 4 * D, D])
    yT = dO("yT", [D, T])
    PRM_N = 1568; ROW_N = 1280
    prm_d = dI("prm", [DEPTH, 128, PRM_N]); rows_d = dI("rows", [DEPTH, 1, ROW_N])
    lr64_d = dI("lr64", [DEPTH, 64, 768]); lr32_d = dI("lr32", [DEPTH, 32, 512]); lr16_d = dI("lr16", [DEPTH, 16, 256])
    pool_w_d = dI("pool_w", [DEPTH, 4, 64, 64])
    identb_d = dI("identb", [128, 128], BF16)
    ident_d = dI("ident", [128, 128]); maskc_d = dI("maskc", [128, 8, 512])
    invc_d = dI("invc", [2, 128, T]); csm_d = dI("csm", [2, 128, 512])
    CL_d = dI("CL", [8, 2, 128, 16 * 512], BF16); SL_d = dI("SL", [8, 2, 128, 16 * 512], BF16)
    sr0_d = dI("sr0", [DEPTH, 2, 64, 4, 64]); sg0_d = dI("sg0", [DEPTH, 2, 32, 4, 64])
    osr_d = dO("osr", [DEPTH, 2, 16, 64, 4, 64]); osg_d = dO("osg", [DEPTH, 2, 16, 32, 4, 64])
    ZM = dS("ZM", [768, T])
    xres = dS("xres", [NT, 128, 8 * TN])
    U = dS("U", [19 * 128, T])
    G = dS("G", [NT, 128, 32 * TN], BF16)
    Y = dS("Y", [NT, 128, 8 * TN], BF16)
    H2 = dS("H2", [NT, 128, 8 * TN], BF16)

    with contextlib.ExitStack() as st:
        arena_h = st.enter_context(nc.sbuf_tensor("arena", [128, 104000], BF16))
        cst_h = st.enter_context(nc.sbuf_tensor("cst", [128, 1024], F32))
        onesb = st.enter_context(nc.sbuf_tensor("onesb", [128, 128], BF16))
        psum = [st.enter_context(nc.psum_tensor("ps%d" % i, [128, 512], F32)) for i in range(8)]
        ar = Arena(arena_h, 104000)
        em = Em(nc)
        c_silu = cst_h[:, 0:8]; c_mod = cst_h[:, 8:56]; c_gsc1 = cst_h[:, 56:64]; c_gsc2 = cst_h[:, 64:72]
        c_n1g = cst_h[:, 72:80]; c_n2g = cst_h[:, 80:88]; c_fg = cst_h[:, 88:96]
        c_kap = cst_h[:, 96:97]; c_eps = cst_h[:, 97:98]; c_adab = cst_h[:, 100:148]
        c_nk = cst_h[:, 98:99]; c_gneps = cst_h[:, 99:100]
        em.dma("sp", lambda e: e.dma_start(out=c_silu, in_=cond), "c_silu", writes=["c_silu"])
        em.dma("sp", lambda e: e.dma_start(out=c_kap, in_=kap), "c_kap", writes=["c_kap"])
        em.dma("sp", lambda e: e.dma_start(out=c_fg, in_=fg), "c_fg", writes=["c_fg"])
        em.op("dve", lambda e: e.memset(c_eps, EPS), writes=["c_eps"])
        em.op("dve", lambda e: e.memset(c_gneps, 64e-5), writes=["c_gneps"])
        em.op("dve", lambda e: e.tensor_scalar(out=c_nk, in0=c_kap, scalar1=-1.0, scalar2=None, op0=ALU.add), reads=["c_kap"], writes=["c_nk"])
        em.op("dve", lambda e: e.memset(onesb[:], 1.0 / D), writes=["onesb"])
        em.op("act", lambda e: e.activation(out=c_silu, in_=c_silu, func=AF.Silu), reads=["c_silu"], writes=["c_silu"])

        pctr = [0]

        def pbank():
            pctr[0] = (pctr[0] + 1) % 8
            return pctr[0]

        def rmsnorm_tile(xt, xkey, gsc, shift, out_bf, outkey, tmp, xsq, rstd, tag):
            em.op("act", lambda e: e.activation(out=xsq, in_=xt, func=AF.Square), reads=[xkey], writes=[tag + "xsq"])
            b = pbank()
            for c in range(8):
                em.op("pe", lambda e, c=c, b=b: e.matmul(psum[b][:], lhsT=onesb[:], rhs=xsq[:, c, :], start=(c == 0), stop=(c == 7)),
                      reads=[tag + "xsq", "onesb"], writes=["ps%d" % b])
            em.op("act", lambda e, b=b: e.activation(out=rstd, in_=psum[b][:], func=AF.Sqrt, bias=c_eps, scale=1.0),
                  reads=["ps%d" % b, "c_eps"], writes=[tag + "rstd"])
            em.op("dve", lambda e: e.reciprocal(out=rstd, in_=rstd), reads=[tag + "rstd"], writes=[tag + "rstd"])
            for c in range(8):
                em.op("dve", lambda e, c=c: e.tensor_tensor(out=tmp[:, c, :], in0=xt[:, c, :], in1=rstd, op=ALU.mult),
                      reads=[xkey, tag + "rstd"], writes=[tag + "tmp%d" % c])
                if shift is not None:
                    em.op("act", lambda e, c=c: e.activation(out=out_bf[:, c, :], in_=tmp[:, c, :], func=AF.Identity,
                                                             bias=shift[:, c:c + 1], scale=gsc[:, c:c + 1]),
                          reads=[tag + "tmp%d" % c, "mod"], writes=[outkey])
                else:
                    em.op("act", lambda e, c=c: e.activation(out=out_bf[:, c, :], in_=tmp[:, c, :], func=AF.Identity,
                                                             bias=0.0, scale=gsc[:, c:c + 1]),
                          reads=[tag + "tmp%d" % c, "mod"], writes=[outkey])

        fm = lambda ap: ap.rearrange("(c p) t -> p c t", p=128)
        tv = lambda ap, j: ap[j].rearrange("p (c t) -> p c t", t=TN)
        cx_tv = tv
        cx = CX()
        cx.tv = tv
        cx.nc = nc; cx.em = em; cx.ar = ar; cx.psum = psum; cx.pbank = pbank; cx.U = U; cx.Y = Y; cx.ZM = ZM
        cx.c_kap = c_kap; cx.c_nk = c_nk; cx.c_eps = c_eps; cx.c_gneps = c_gneps
        cx.PRM_N = PRM_N; cx.ROW_N = ROW_N; cx.prm = prm_d; cx.rows = rows_d
        cx.lr64_d = lr64_d; cx.lr32_d = lr32_d; cx.lr16_d = lr16_d; cx.pool_w = pool_w_d
        cx.ident_d = ident_d; cx.identb_d = identb_d; cx.maskc_d = maskc_d; cx.invc = invc_d; cx.csm = csm_d; cx.CL = CL_d; cx.SL = SL_d
        cx.sr0 = sr0_d; cx.sg0 = sg0_d; cx.osr = osr_d; cx.osg = osg_d

        for l in range(NL):
            last = (l == NL - 1)
            em.barrier(); ar.reset()
            em.dma("sp", lambda e, l=l: e.dma_start(out=c_adab, in_=ada_b[l]), "c_adab", writes=["c_adab"])
            em.dma("sp", lambda e, l=l: e.dma_start(out=c_n1g, in_=n1g[l]), "c_n1g", writes=["c_n1g"])
            em.dma("sp", lambda e, l=l: e.dma_start(out=c_n2g, in_=n2g[l]), "c_n2g", writes=["c_n2g"])
            awt = [ar.alloc([8, 768], F32) for _ in range(2)]
            aw_v = ada_w[l].rearrange("(c p) n -> p c n", p=128)
            pb = pbank()
            for g in range(8):
                wt = awt[g % 2]; key = "awt%d" % (g % 2)
                em.dma("sp", lambda e, g=g, wt=wt, aw_v=aw_v: e.dma_start(out=wt, in_=aw_v[:, :, g * 768:(g + 1) * 768]), key, writes=[key])
                for m in range(6):
                    col = g * 6 + m
                    for kc in range(8):
                        em.op("pe", lambda e, wt=wt, m=m, kc=kc, col=col, pb=pb: e.matmul(
                            psum[pb][:, col:col + 1], lhsT=wt[:, kc, m * 128:(m + 1) * 128], rhs=c_silu[:, kc:kc + 1],
                            start=(kc == 0), stop=(kc == 7)), reads=[key, "c_silu"], writes=["ps%d" % pb])
            em.op("dve", lambda e, pb=pb: e.tensor_tensor(out=c_mod, in0=psum[pb][:, 0:48], in1=c_adab, op=ALU.add),
                  reads=["ps%d" % pb, "c_adab"], writes=["mod"])
            em.op("dve", lambda e: e.scalar_tensor_tensor(out=c_gsc1, in0=c_mod[:, 8:16], scalar=1.0, in1=c_n1g, op0=ALU.add, op1=ALU.mult),
                  reads=["mod", "c_n1g"], writes=["mod"])
            em.op("dve", lambda e: e.scalar_tensor_tensor(out=c_gsc2, in0=c_mod[:, 32:40], scalar=1.0, in1=c_n2g, op0=ALU.add, op1=ALU.mult),
                  reads=["mod", "c_n2g"], writes=["mod"])
            if DBG and l == 0:
                dbg_mod = nc.dram_tensor("dbg_mod", [128, 64], F32, kind="ExternalOutput").ap()
                em.dma("sp", lambda e: e.dma_start(out=dbg_mod, in_=cst_h[:, 8:72]), "dbgmod", reads=["mod"], final=True)
            sh1 = c_mod[:, 0:8]; g1 = c_mod[:, 16:24]; sh2 = c_mod[:, 24:32]; g2 = c_mod[:, 40:48]

            em.barrier(); ar.reset()
            hT = ar.alloc([8, T], BF16)
            xb = [ar.alloc([8, TN], F32) for _ in range(2)]
            pb_ = [ar.alloc([8, TN], F32) for _ in range(2)]
            tmp = ar.alloc([8, TN], F32); xsq = ar.alloc([8, TN], BF16); rstd = ar.alloc([TN], F32)
            for j in range(NT):
                xt = xb[j % 2]; xk = "xb%d" % (j % 2)
                sl = slice(j * TN, (j + 1) * TN)
                if l == 0:
                    pt = pb_[j % 2]; pk = "pb%d" % (j % 2)
                    em.dma("sp", lambda e, xt=xt, sl=sl: e.dma_start(out=xt, in_=fm(xT)[:, :, sl]), xk, writes=[xk])
                    em.dma("sp", lambda e, pt=pt, sl=sl: e.dma_start(out=pt, in_=fm(pos)[:, :, sl]), pk, writes=[pk])
                    em.op("pool", lambda e, xt=xt, pt=pt: e.tensor_tensor(out=xt, in0=xt, in1=pt, op=ALU.add), reads=[xk, pk], writes=[xk])
                    em.dma("sp", lambda e, xt=xt, j=j: e.dma_start(out=tv(xres, j), in_=xt), xk + "w", reads=[xk], writes=["xres%d" % j])
                else:
                    em.dma("sp", lambda e, xt=xt, j=j: e.dma_start(out=xt, in_=tv(xres, j)), xk, reads=["xres%d" % j], writes=[xk])
                rmsnorm_tile(xt, xk, c_gsc1, sh1, hT[:, :, sl], "hT%d" % j, tmp, xsq, rstd, "n1")

            chunks = [(m * 128, 128, "U", m) for m in range(18)] + [(2304, 32, "U", 18)] + \
                     [(NMIX + m * 128, 128, "G", m) for m in range(32)]
            wbuf = [ar.alloc([8, 512], BF16) for _ in range(2)]
            stg = [ar.alloc([TN], F32) for _ in range(4)]
            stb = [ar.alloc([TN], BF16) for _ in range(4)]
            win_v = w_in[l].rearrange("(c p) n -> p c n", p=128)
            groups = []
            cur = []
            for ch in chunks:
                if cur and (len(cur) == 4 or cur[-1][2] != ch[2] or cur[-1][1] != 128):
                    groups.append(cur); cur = []
                cur.append(ch)
            groups.append(cur)
            sctr = 0
            for gi, grp in enumerate(groups):
                wt = wbuf[gi % 2]; wk = "wbuf%d" % (gi % 2)
                c0 = grp[0][0]; ncol = sum(c[1] for c in grp)
                em.dma("pool", lambda e, wt=wt, c0=c0, ncol=ncol, win_v=win_v: e.dma_start(out=wt[:, :, 0:ncol], in_=win_v[:, :, c0:c0 + ncol]), wk, writes=[wk])
                for j in range(NT):
                    sl = slice(j * TN, (j + 1) * TN)
                    for (cs, cn, kind, mi) in grp:
                        b = pbank(); o = cs - c0
                        for kc in range(8):
                            em.op("pe", lambda e, wt=wt, o=o, cn=cn, kc=kc, b=b, sl=sl: e.matmul(
                                psum[b][0:cn, :], lhsT=wt[:, kc, o:o + cn], rhs=hT[:, kc, sl], start=(kc == 0), stop=(kc == 7)),
                                reads=[wk, "hT%d" % j], writes=["ps%d" % b])
                        si = sctr % 4; sctr += 1
                        if kind == "U":
                            s_ = stg[si]; sk = "stg%d" % si
                            em.op("dve", lambda e, s_=s_, b=b, cn=cn: e.tensor_copy(out=s_[0:cn, :], in_=psum[b][0:cn, :]), reads=["ps%d" % b], writes=[sk])
                            em.dma("sp", lambda e, s_=s_, mi=mi, cn=cn, sl=sl: e.dma_start(out=U[mi * 128:mi * 128 + cn, sl], in_=s_[0:cn, :]),
                                   sk + "w", reads=[sk], writes=["U%d_%d" % (mi, j)])
                        else:
                            s_ = stb[si]; sk = "stb%d" % si
                            em.op("act", lambda e, s_=s_, b=b: e.activation(out=s_, in_=psum[b][:], func=AF.Sigmoid), reads=["ps%d" % b], writes=[sk])
                            em.dma("sp", lambda e, s_=s_, mi=mi, j=j: e.dma_start(out=tv(G, j)[:, mi, :], in_=s_),
                                   sk + "w", reads=[sk], writes=["G%d" % j])

            em.barrier(); ar.reset()
            mixers(cx, l)

            em.barrier(); ar.reset()
            wbr = ar.alloc([8, D], BF16); wo = ar.alloc([8, D], BF16)
            em.dma("pool", lambda e, l=l: e.dma_start(out=wbr, in_=w_br[l].rearrange("(c p) n -> p c n", p=128)), "wbr", writes=["wbr"])
            em.dma("pool", lambda e, l=l: e.dma_start(out=wo, in_=w_out[l].rearrange("(c p) n -> p c n", p=128)), "wo", writes=["wo"])
            Yt = [ar.alloc([8, TN], BF16) for _ in range(2)]
            Gt = [ar.alloc([32, TN], BF16) for _ in range(2)]
            xb = [ar.alloc([8, TN], F32) for _ in range(2)]
            mgs = [ar.alloc([8, TN], BF16) for _ in range(2)]
            accs = [ar.alloc([TN], F32) for _ in range(2)]; tms = [ar.alloc([TN], F32) for _ in range(2)]
            tmp = ar.alloc([8, TN], F32); xsq = ar.alloc([8, TN], BF16); rstd = ar.alloc([TN], F32)
            h2 = [ar.alloc([8, TN], BF16) for _ in range(1)]
            for j in range(NT):
                sl = slice(j * TN, (j + 1) * TN)
                yt = Yt[j % 2]; gt = Gt[j % 2]; xt = xb[j % 2]; hh = h2[0]
                yk = "Yt%d" % (j % 2); gk = "Gt%d" % (j % 2); xk = "x3_%d" % (j % 2); hk = "h2_0"
                em.dma("sp", lambda e, yt=yt, j=j: e.dma_start(out=yt, in_=tv(Y, j)), yk, reads=["Y%d" % j], writes=[yk])
                em.dma("sp", lambda e, gt=gt, j=j: e.dma_start(out=gt, in_=tv(G, j)), gk, reads=["G%d" % j], writes=[gk])
                em.dma("sp", lambda e, xt=xt, j=j: e.dma_start(out=xt, in_=tv(xres, j)), xk, reads=["xres%d" % j], writes=[xk])
                mg = mgs[j % 2]; mgk = "mg%d" % (j % 2)
                for dc in range(8):
                    acc = accs[dc % 2]; acck = "acc%d" % (dc % 2)
                    for i in range(4):
                        b = pbank()
                        for cc in range(2):
                            em.op("pe", lambda e, i=i, cc=cc, dc=dc, b=b, yt=yt: e.matmul(
                                psum[b][:], lhsT=wbr[:, i * 2 + cc, dc * 128:(dc + 1) * 128], rhs=yt[:, i * 2 + cc, :],
                                start=(cc == 0), stop=(cc == 1)), reads=["wbr", yk], writes=["ps%d" % b])
                        dst = acc if i == 0 else tms[(i - 1) % 2]
                        dstk = acck if i == 0 else "tm%d" % ((i - 1) % 2)
                        em.op("dve", lambda e, b=b, i=i, dc=dc, gt=gt, dst=dst: e.tensor_tensor(out=dst, in0=psum[b][:], in1=gt[:, i * 8 + dc, :], op=ALU.mult),
                              reads=["ps%d" % b, gk], writes=[dstk])
                        if i > 0:
                            o_ = mg[:, dc, :] if i == 3 else acc
                            em.op("pool", lambda e, o_=o_, acc=acc, dst=dst: e.tensor_tensor(out=o_, in0=acc, in1=dst, op=ALU.add),
                                  reads=[acck, dstk], writes=[mgk if i == 3 else acck])
                for d2 in range(8):
                    b = pbank()
                    for dc in range(8):
                        em.op("pe", lambda e, d2=d2, dc=dc, b=b, mg=mg: e.matmul(psum[b][:], lhsT=wo[:, dc, d2 * 128:(d2 + 1) * 128], rhs=mg[:, dc, :],
                                                                          start=(dc == 0), stop=(dc == 7)), reads=["wo", mgk], writes=["ps%d" % b])
                    em.op("dve", lambda e, d2=d2, b=b, xt=xt: e.scalar_tensor_tensor(out=xt[:, d2, :], in0=psum[b][:], scalar=g1[:, d2:d2 + 1], in1=xt[:, d2, :],
                                                                                   op0=ALU.mult, op1=ALU.add), reads=["ps%d" % b, xk, "mod"], writes=[xk])
                em.dma("sp", lambda e, xt=xt, j=j: e.dma_start(out=tv(xres, j), in_=xt), xk + "w", reads=[xk], writes=["xres%d" % j])
                rmsnorm_tile(xt, xk, c_gsc2, sh2, hh, hk, tmp, xsq, rstd, "n2")
                em.dma("sp", lambda e, hh=hh, j=j: e.dma_start(out=tv(H2, j), in_=hh), hk + "w", reads=[hk], writes=["H2_%d" % j])

            if DBG and l == 0:
                em.barrier()
                dd = nc.dram_tensor("dY", [D, T], BF16, kind="ExternalOutput").ap()
                em.dma("sp", lambda e, dd=dd: e.dma_start(out=dd, in_=Y), "dbgY", final=True)
                dd2 = nc.dram_tensor("dX", [D, T], F32, kind="ExternalOutput").ap()
                em.dma("sp", lambda e, dd2=dd2: e.dma_start(out=dd2, in_=xres), "dbgX", final=True)
            em.barrier(); ar.reset()
            w1s = ar.alloc([8, 4 * D], BF16); w2s = ar.alloc([32, D], BF16)
            for q in range(4):
                em.dma("pool", lambda e, l=l, q=q: e.dma_start(out=w1s[:, :, q * 1024:(q + 1) * 1024],
                                                               in_=w1[l].rearrange("(c p) n -> p c n", p=128)[:, :, q * 1024:(q + 1) * 1024]), "w1s", writes=["w1s"])
                em.dma("pool", lambda e, l=l, q=q: e.dma_start(out=w2s[:, q * 8:(q + 1) * 8, :],
                                                               in_=w2[l].rearrange("(c p) n -> p c n", p=128)[:, q * 8:(q + 1) * 8, :]), "w2s", writes=["w2s"])
            hid_raw = ar.alloc([32 * TN], BF16)
            hid = hid_raw.rearrange("p (a b) -> p a b", b=TN)
            rl = [ar.alloc([TN], F32) for _ in range(2)]
            xb1 = ar.alloc([8, TN], F32); h2t = ar.alloc([8, TN], BF16)
            if last:
                tmp = hid_raw[:, 0:16 * TN].bitcast(F32).rearrange("p (a b) -> p a b", b=TN)
                xsq = ar.alloc([8, TN], BF16); rstd = ar.alloc([TN], F32)
            for j in range(NT):
                sl = slice(j * TN, (j + 1) * TN)
                em.dma("sp", lambda e, j=j: e.dma_start(out=h2t, in_=tv(H2, j)), "h2t", reads=["H2_%d" % j], writes=["h2t"])
                em.dma("sp", lambda e, j=j: e.dma_start(out=xb1, in_=tv(xres, j)), "xb1", reads=["xres%d" % j], writes=["xb1"])
                for fc in range(32):
                    b = pbank()
                    for kc in range(8):
                        em.op("pe", lambda e, fc=fc, kc=kc, b=b: e.matmul(psum[b][:], lhsT=w1s[:, kc, fc * 128:(fc + 1) * 128], rhs=h2t[:, kc, :],
                                                                          start=(kc == 0), stop=(kc == 7)), reads=["w1s", "h2t"], writes=["ps%d" % b])
                    r_ = rl[fc % 2]; rk = "rl%d" % (fc % 2)
                    em.op("act", lambda e, b=b, r_=r_: e.activation(out=r_, in_=psum[b][:], func=AF.Relu), reads=["ps%d" % b], writes=[rk])
                    em.op("pool", lambda e, r_=r_, fc=fc: e.tensor_tensor(out=hid[:, fc, :], in0=r_, in1=r_, op=ALU.mult), reads=[rk], writes=["hid"])
                for d2 in range(8):
                    b = pbank()
                    for fc in range(32):
                        em.op("pe", lambda e, fc=fc, d2=d2, b=b: e.matmul(psum[b][:], lhsT=w2s[:, fc, d2 * 128:(d2 + 1) * 128], rhs=hid[:, fc, :],
                                                                          start=(fc == 0), stop=(fc == 31)), reads=["w2s", "hid"], writes=["ps%d" % b])
                    em.op("dve", lambda e, d2=d2, b=b: e.scalar_tensor_tensor(out=xb1[:, d2, :], in0=psum[b][:], scalar=g2[:, d2:d2 + 1], in1=xb1[:, d2, :],
                                                                            op0=ALU.mult, op1=ALU.add), reads=["ps%d" % b, "xb1", "mod"], writes=["xb1"])
                if not last:
                    em.dma("sp", lambda e, j=j: e.dma_start(out=tv(xres, j), in_=xb1), "xb1w", reads=["xb1"], writes=["xres%d" % j])
                else:
                    em.op("act", lambda e: e.activation(out=xsq, in_=xb1, func=AF.Square), reads=["xb1"], writes=["fxsq"])
                    b = pbank()
                    for c in range(8):
                        em.op("pe", lambda e, c=c, b=b: e.matmul(psum[b][:], lhsT=onesb[:], rhs=xsq[:, c, :], start=(c == 0), stop=(c == 7)),
                              reads=["fxsq", "onesb"], writes=["ps%d" % b])
                    em.op("act", lambda e, b=b: e.activation(out=rstd, in_=psum[b][:], func=AF.Sqrt, bias=c_eps, scale=1.0),
                          reads=["ps%d" % b, "c_eps"], writes=["frstd"])
                    em.op("dve", lambda e: e.reciprocal(out=rstd, in_=rstd), reads=["frstd"], writes=["frstd"])
                    for c in range(8):
                        em.op("dve", lambda e, c=c: e.scalar_tensor_tensor(out=tmp[:, c, :], in0=xb1[:, c, :], scalar=c_fg[:, c:c + 1], in1=rstd,
                                                                         op0=ALU.mult, op1=ALU.mult), reads=["xb1", "frstd", "c_fg"], writes=["hid"])
                    em.dma("sp", lambda e, sl=sl: e.dma_start(out=fm(yT)[:, :, sl], in_=tmp), "yT_w", reads=["hid"], writes=["yT%d" % j], final=True)
        em.build()
        global _EM
        _EM = em
    return nc


class CX:
    pass


def mixers(cx, l):
    em, ar, psum, pbank, U, Y = cx.em, cx.ar, cx.psum, cx.pbank, cx.U, cx.Y
    c_kap, c_nk, c_eps = cx.c_kap, cx.c_nk, cx.c_eps

    ar.reset()
    c = mk_consts(cx, l, 'p')
    p_psc = c.psc

    zp = ar.alloc([2, 16, 272], F32)
    Wa = ar.alloc([16, 272], F32); Wb = ar.alloc([16, 272], F32)
    pooled = ar.alloc([2, T], F32)
    invc = ar.alloc([2, T], F32)
    PW = ar.alloc([2, 128], F32)
    yst = [ar.alloc([TN], BF16) for _ in range(2)]
    em.op("pool", lambda e: e.memset(zp, 0.0), writes=["zp"])
    em.op("pool", lambda e: e.memset(PW, 0.0), writes=["PW"])
    for ct in range(2):
        em.dma("sp", lambda e, ct=ct: e.dma_start(out=invc[:, ct, :], in_=cx.invc[ct]), "invc", writes=["invc"])
        Uv = U[ct * 128:(ct + 1) * 128, :].rearrange("p (b s) -> p b s", s=256)
        em.dma("sp", lambda e, ct=ct, Uv=Uv: e.dma_start(out=zp[:, ct, :, 8:264], in_=Uv), "zp", writes=["zp"])
        em.dma("sp", lambda e, ct=ct, Uv=Uv: e.dma_start(out=zp[:, ct, 1:16, 0:8], in_=Uv[:, 0:15, 248:256]), "zp", writes=["zp"])
        em.dma("sp", lambda e, ct=ct, Uv=Uv: e.dma_start(out=zp[:, ct, 0:15, 264:272], in_=Uv[:, 1:16, 0:8]), "zp", writes=["zp"])
    for g in range(4):
        r0 = (g % 2) * 64
        em.dma("sp", lambda e, g=g, r0=r0, l=l: e.dma_start(out=PW[r0:r0 + 64, g // 2, r0:r0 + 64], in_=cx.pool_w[l, g]), "PW", writes=["PW"])
    for ct in range(2):
        for (a, b) in ((0, 8), (264, 272)):
            em.op("dve", lambda e, ct=ct, a=a, b=b: e.tensor_scalar(out=zp[:, ct, :, a:b], in0=zp[:, ct, :, a:b], scalar1=c_kap, scalar2=None, op0=ALU.mult),
                  reads=["zp", "c_kap"], writes=["zp"])
    pv = lambda ct: pooled[:, ct, :].rearrange("p (b s) -> p b s", s=256)
    iv = lambda ct: invc[:, ct, :].rearrange("p (b s) -> p b s", s=256)

    def take(W, wk, ct, r0):
        em.op("dve", lambda e: e.tensor_tensor(out=pv(ct)[r0:r0 + 64], in0=W[r0:r0 + 64, :, 8:264], in1=iv(ct)[r0:r0 + 64], op=ALU.mult),
              reads=[wk, "invc"], writes=["pooled"])
        em.op("dve", lambda e: e.tensor_tensor(out=pv(ct)[r0:r0 + 64], in0=pv(ct)[r0:r0 + 64], in1=zp[r0:r0 + 64, ct, :, 8:264], op=ALU.subtract),
              reads=["pooled", "zp"], writes=["pooled"])

    for ct in range(2):
        Z = zp[:, ct]
        em.op("dve", lambda e, Z=Z: e.tensor_tensor(out=Wa[:, :, 1:272], in0=Z[:, :, 0:271], in1=Z[:, :, 1:272], op=ALU.add), reads=["zp"], writes=["Wa"])
        if ct == 0:
            take(Wa, "Wa", 0, 0)
        em.op("dve", lambda e: e.tensor_tensor(out=Wb[:, :, 2:271], in0=Wa[:, :, 1:270], in1=Wa[:, :, 3:272], op=ALU.add), reads=["Wa"], writes=["Wb"])
        if ct == 0:
            take(Wb, "Wb", 0, 64)
        else:
            em.op("dve", lambda e: e.tensor_tensor(out=Wa[:, :, 4:269], in0=Wb[:, :, 2:267], in1=Wb[:, :, 6:271], op=ALU.add), reads=["Wb"], writes=["Wa"])
            take(Wa, "Wa", 1, 0)
            em.op("dve", lambda e: e.tensor_tensor(out=Wb[:, :, 8:265], in0=Wa[:, :, 4:261], in1=Wa[:, :, 12:269], op=ALU.add), reads=["Wa"], writes=["Wb"])
            take(Wb, "Wb", 1, 64)
    n = 0
    for j in range(NT):
        sl = slice(j * TN, (j + 1) * TN)
        for ct in range(2):
            b = pbank(); i = n % 2; n += 1
            em.op("pe", lambda e, b=b, ct=ct, sl=sl: e.matmul(psum[b][:], lhsT=PW[:, ct, :], rhs=pooled[:, ct, sl], start=True, stop=True),
                  reads=["PW", "pooled"], writes=["ps%d" % b])
            em.op("act", lambda e, b=b, ct=ct, i=i: e.activation(out=yst[i], in_=psum[b][:], func=AF.Identity, bias=0.0, scale=p_psc[:, ct:ct + 1]),
                  reads=["ps%d" % b], writes=["yst%d" % i])
            em.dma("sp", lambda e, ct=ct, j=j, i=i: e.dma_start(out=cx.tv(Y, j)[:, ct, :], in_=yst[i]), "yst%dw" % i, reads=["yst%d" % i])

    em.barrier(); ar.reset()
    zf = ar.alloc([2, T], BF16)
    csm = ar.alloc([2, 512], BF16)
    zcs = ar.alloc([32, 512], BF16)
    CLb = [ar.alloc([16, 512], BF16) for _ in range(2)]
    SLb = [ar.alloc([16, 512], BF16) for _ in range(2)]
    fst = [ar.alloc([TN], BF16) for _ in range(2)]
    for ct in range(2):
        em.dma("pool", lambda e, ct=ct: e.dma_start(out=zf[:, ct, :], in_=U[256 + ct * 128:256 + (ct + 1) * 128, :]), "zf", writes=["zf"])
        em.dma("pool", lambda e, ct=ct: e.dma_start(out=csm[:, ct, :], in_=cx.csm[ct]), "csm", writes=["csm"])
    for tc in range(32):
        b = pbank()
        for ct in range(2):
            em.op("pe", lambda e, b=b, ct=ct, tc=tc: e.matmul(psum[b][:], lhsT=zf[:, ct, tc * 128:(tc + 1) * 128], rhs=csm[:, ct, :], start=(ct == 0), stop=(ct == 1)),
                  reads=["zf", "csm"], writes=["ps%d" % b])
        if tc % 2 == 0:
            em.op("act", lambda e, b=b, tc=tc: e.activation(out=zcs[:, tc, :], in_=psum[b][:], func=AF.Identity), reads=["ps%d" % b], writes=["zcs"])
        else:
            em.op("dve", lambda e, b=b, tc=tc: e.tensor_copy(out=zcs[:, tc, :], in_=psum[b][:]), reads=["ps%d" % b], writes=["zcs"])
    CLv = lambda ft, th: cx.CL[ft, th].rearrange("p (a f) -> p a f", f=512)
    SLv = lambda ft, th: cx.SL[ft, th].rearrange("p (a f) -> p a f", f=512)
    n = 0
    for ft in range(8):
        fsl = slice(ft * 512, (ft + 1) * 512)
        bb = [pbank(), pbank()]
        for th in range(2):
            i = n % 2; n += 1
            em.dma("sp", lambda e, i=i, th=th, ft=ft: e.dma_start(out=CLb[i], in_=CLv(ft, th)), "CLb%d" % i, writes=["CLb%d" % i])
            em.dma("sp", lambda e, i=i, th=th, ft=ft: e.dma_start(out=SLb[i], in_=SLv(ft, th)), "SLb%d" % i, writes=["SLb%d" % i])
            for half in range(2):
                b = bb[half]
                for tcl in range(16):
                    tc = th * 16 + tcl
                    em.op("pe", lambda e, b=b, i=i, tc=tc, tcl=tcl, half=half, th=th: e.matmul(
                        psum[b][:], lhsT=zcs[:, tc, half * 128:(half + 1) * 128], rhs=CLb[i][:, tcl, :], start=(th == 0 and tcl == 0), stop=False),
                        reads=["zcs", "CLb%d" % i], writes=["ps%d" % b])
                    em.op("pe", lambda e, b=b, i=i, tc=tc, tcl=tcl, half=half, th=th: e.matmul(
                        psum[b][:], lhsT=zcs[:, tc, 256 + half * 128:256 + (half + 1) * 128], rhs=SLb[i][:, tcl, :], start=False, stop=(th == 1 and tcl == 15)),
                        reads=["zcs", "SLb%d" % i], writes=["ps%d" % b])
        for half in range(2):
            b = bb[half]
            em.op("act", lambda e, b=b, half=half: e.activation(out=fst[half], in_=psum[b][:], func=AF.Identity), reads=["ps%d" % b], writes=["fst%d" % half])
            em.dma("sp", lambda e, half=half, ft=ft: e.dma_start(out=cx.tv(Y, ft)[:, 2 + half, :], in_=fst[half]), "fst%dw" % half, reads=["fst%d" % half])

    em.barrier(); ar.reset()
    c = mk_consts(cx, l, "m")
    rwkv_prepass(cx, l, c)
    recs = []
    for fn, banks in ((mix_gla, [0, 1, 2]), (mix_rwkv, [3, 4, 5, 6, 7])):
        rec = Rec()
        sub = CX(); sub.__dict__.update(cx.__dict__)
        sub.em = rec
        st_ = [0]

        def pb(banks=banks, st_=st_):
            st_[0] = (st_[0] + 1) % len(banks)
            return banks[st_[0]]
        sub.pbank = pb
        fn(sub, l, c)
        recs.append(rec)
    merge_streams(em, recs)


def mk_consts(cx, l, tag):
    em, ar = cx.em, cx.ar
    c = CX()
    c.ident = ar.alloc([128], F32)
    c.ident4 = ar.alloc([512], F32)
    c.maskc = ar.alloc([8, 512], F32)
    c.prm = ar.alloc([cx.PRM_N], F32)
    c.rows = ar.alloc([cx.ROW_N], F32)
    c.lr64 = ar.alloc([768], F32); c.lr32 = ar.alloc([512], F32); c.lr16 = ar.alloc([256], F32)
    c.ones_row = ar.alloc([128], F32); c.ones_col = ar.alloc([4], F32)
    c.identb = ar.alloc([128], BF16)
    c.ident4b = ar.alloc([512], BF16)
    for h in range(4):
        em.dma("sp", lambda e, h=h: e.dma_start(out=c.ident4b[:, h * 128:(h + 1) * 128], in_=cx.identb_d), "K" + tag + "i4b", writes=["K" + tag + "i4b%d" % h])
    em.dma("sp", lambda e: e.dma_start(out=c.identb, in_=cx.identb_d), "K" + tag + "ib", writes=["K" + tag + "ib"])
    k = "K" + tag
    em.dma("sp", lambda e: e.dma_start(out=c.ident, in_=cx.ident_d), k + "a", writes=[k])
    for h in range(4):
        em.dma("sp", lambda e, h=h: e.dma_start(out=c.ident4[:, h * 128:(h + 1) * 128], in_=cx.ident_d), k + "b", writes=[k + "i4%d" % h])
    em.dma("sp", lambda e: e.dma_start(out=c.maskc, in_=cx.maskc_d), k + "c", writes=[k + "m"])
    em.dma("sp", lambda e, l=l: e.dma_start(out=c.prm, in_=cx.prm[l]), k + "d", writes=[k + "p"])
    em.dma("sp", lambda e, l=l: e.dma_start(out=c.rows[0:1, :], in_=cx.rows[l]), k + "e", writes=[k + "r"])
    em.dma("sp", lambda e, l=l: e.dma_start(out=c.lr64[0:64, :], in_=cx.lr64_d[l]), k + "f", writes=[k + "l64"])
    em.dma("sp", lambda e, l=l: e.dma_start(out=c.lr32[0:32, :], in_=cx.lr32_d[l]), k + "g", writes=[k + "l32"])
    em.dma("sp", lambda e, l=l: e.dma_start(out=c.lr16[0:16, :], in_=cx.lr16_d[l]), k + "h", writes=[k + "l16"])
    em.op("dve", lambda e: e.memset(c.ones_row, 1.0), writes=[k + "o1"])
    em.op("dve", lambda e: e.memset(c.ones_col, 1.0), writes=[k + "o2"])
    BC0 = 32
    p = c.prm
    c.psc = p[:, 0:2]; c.mu = p[:, 2:8]; c.hm = p[:, 8:14]; c.om = p[:, 14:20]
    c.kkw = p[:, BC0:BC0 + 256]; c.ka = p[:, BC0 + 256:BC0 + 512]; c.oka = p[:, BC0 + 512:BC0 + 768]
    c.rk = p[:, BC0 + 768:BC0 + 1024]; c.gn = p[:, BC0 + 1024:BC0 + 1280]; c.gln = p[:, BC0 + 1280:BC0 + 1536]
    em.op("dve", lambda e: e.tensor_scalar(out=c.hm, in0=c.mu, scalar1=0.5, scalar2=None, op0=ALU.mult), reads=[k + "p"], writes=[k + "p"])
    em.op("dve", lambda e: e.tensor_scalar(out=c.om, in0=c.mu, scalar1=-1.0, scalar2=1.0, op0=ALU.mult, op1=ALU.add), reads=[k + "p"], writes=[k + "p"])
    em.op("dve", lambda e: e.tensor_scalar(out=c.oka, in0=c.ka, scalar1=-1.0, scalar2=1.0, op0=ALU.mult, op1=ALU.add), reads=[k + "p"], writes=[k + "p"])
    em.barrier()
    return c


def mix_gla(cx, l, c):
    em, ar, psum, pbank, U, Y = cx.em, cx.ar, cx.psum, cx.pbank, cx.U, cx.Y
    S = [ar.alloc([4, 64], F32) for _ in range(2)]
    OG = ar.alloc([32, 256], F32)
    fmt = [ar.alloc([6, 128], F32) for _ in range(2)]
    cal = [ar.alloc([128], F32) for _ in range(2)]
    tm = [ar.alloc([512], F32) for _ in range(2)]
    go = ar.alloc([256], F32)
    ee = ar.alloc([128], F32); lg = ar.alloc([128], F32)
    Eq = ar.alloc([128], F32); Ek = ar.alloc([128], F32); Eb = ar.alloc([128], F32)
    qt = ar.alloc([128], BF16); kh = ar.alloc([128], BF16); kb = ar.alloc([128], BF16)
    qT = ar.alloc([512], BF16); kT = ar.alloc([512], BF16)
    AT = ar.alloc([512], BF16)
    vb = ar.alloc([256], BF16); Sb = [ar.alloc([4, 64], BF16) for _ in range(2)]
    etot = ar.alloc([4], F32)
    og = ar.alloc([256], F32); sq = ar.alloc([256], F32); ss = ar.alloc([4], F32); sgo = ar.alloc([256], F32)
    yd = ar.alloc([256], F32); ydb = ar.alloc([256], BF16); ydT = [ar.alloc([2, 128], BF16) for _ in range(2)]
    Ufm = U[1536:2304, :].rearrange("(c p) t -> p c t", p=128)
    Yv = lambda tc: cx.tv(Y, tc // 4)[:, 6:8, (tc % 4) * 128:(tc % 4 + 1) * 128]
    ab = lambda d: c.lr16[0:16, d * 128:(d + 1) * 128]
    abrow = lambda d: c.rows[0:1, 1024 + d * 128:1024 + (d + 1) * 128]
    for d in range(2):
        order = list(range(32)) if d == 0 else list(range(31, -1, -1))
        Sd = S[d]; Sk = "gS%d" % d
        for n, tc in enumerate(order):
            par = n % 2
            tsl = slice(tc * 128, (tc + 1) * 128)
            f_ = fmt[par]; fk = "gfmt%d" % par; ca = cal[par]; ck = "gcal%d" % par; t_ = tm[par]; tk = "gtm%d" % par
            em.dma("sp", lambda e, f_=f_, tsl=tsl: e.dma_start(out=f_, in_=Ufm[:, :, tsl]), fk, writes=[fk])
            em.dma("sp", lambda e, ca=ca, tsl=tsl, d=d: e.dma_start(out=ca[0:16, :], in_=U[2304 + 16 * d:2304 + 16 * d + 16, tsl]), ck, writes=[ck])
            if n == 0:
                em.dma("sp", lambda e, Sd=Sd, d=d, l=l: e.dma_start(out=Sd[0:32], in_=cx.sg0[l, d]), Sk + "i", writes=[Sk])
            elif n % 2 == 0:
                em.op("dve", lambda e, Sd=Sd: e.tensor_scalar(out=Sd[0:32], in0=Sd[0:32], scalar1=cx.c_kap[0:32], scalar2=None, op0=ALU.mult),
                      reads=[Sk], writes=[Sk])
            b = pbank()
            for cc in range(4):
                em.op("pe", lambda e, b=b, cc=cc, f_=f_: e.matmul(psum[b][:, cc * 128:(cc + 1) * 128], lhsT=f_[:, cc, :], rhs=c.ident, start=True, stop=True),
                      reads=[fk], writes=["ps%d" % b])
            em.op("act", lambda e, b=b, t_=t_: e.activation(out=t_, in_=psum[b][:], func=AF.Identity), reads=["ps%d" % b], writes=[tk])
            em.op("dve", lambda e, b=b: e.tensor_copy(out=vb, in_=psum[b][:, 256:512]), reads=["ps%d" % b], writes=["gvb"])
            Sb_ = Sb[d]; Sbk = "gSb%d" % d
            em.op("act", lambda e, Sd=Sd, Sb_=Sb_: e.activation(out=Sb_[0:32], in_=Sd[0:32], func=AF.Identity), reads=[Sk], writes=[Sbk])
            if d == 1:
                b = pbank()
                for cc in range(2):
                    em.op("pe", lambda e, b=b, cc=cc, f_=f_: e.matmul(psum[b][:, cc * 128:(cc + 1) * 128], lhsT=f_[:, 4 + cc, :], rhs=c.ident, start=True, stop=True),
                          reads=[fk], writes=["ps%d" % b])
                em.op("act", lambda e, b=b: e.activation(out=sgo, in_=psum[b][:, 0:256], func=AF.Silu), reads=["ps%d" % b], writes=["gsgo"])
            b = pbank()
            em.op("pe", lambda e, b=b, ca=ca, d=d: e.matmul(psum[b][:, 0:128], lhsT=ca[0:16, :], rhs=ab(d), start=True, stop=False), reads=[ck], writes=["ps%d" % b])
            em.op("pe", lambda e, b=b, d=d: e.matmul(psum[b][:, 0:128], lhsT=c.ones_row[0:1, :], rhs=abrow(d), start=False, stop=True), reads=[], writes=["ps%d" % b])
            em.op("act", lambda e, b=b: e.activation(out=ee, in_=psum[b][:, 0:128], func=AF.Exp, scale=-1.0), reads=["ps%d" % b], writes=["gee"])
            em.op("act", lambda e: e.activation(out=lg, in_=ee, func=AF.Ln, bias=1.0, scale=1.0), reads=["gee"], writes=["glg"])
            b = pbank()
            em.op("pe", lambda e, b=b, d=d: e.matmul(psum[b][:, 0:128], lhsT=c.maskc[:, d * 4 + 0, 0:128], rhs=lg, start=True, stop=True), reads=["glg"], writes=["ps%d" % b])
            em.op("pe", lambda e, b=b, d=d: e.matmul(psum[b][:, 128:256], lhsT=c.maskc[:, d * 4 + 3, 0:128], rhs=lg, start=True, stop=True), reads=["glg"], writes=["ps%d" % b])
            em.op("act", lambda e, b=b: e.activation(out=Eq, in_=psum[b][:, 0:128], func=AF.Exp, scale=-1.0 / 16), reads=["ps%d" % b], writes=["gEq"])
            em.op("act", lambda e, b=b: e.activation(out=Ek, in_=psum[b][:, 0:128], func=AF.Exp, scale=1.0 / 16), reads=["ps%d" % b], writes=["gEk"])
            em.op("act", lambda e, b=b: e.activation(out=Eb, in_=psum[b][:, 128:256], func=AF.Exp, scale=-1.0 / 16), reads=["ps%d" % b], writes=["gEb"])
            em.op("dve", lambda e, t_=t_: e.scalar_tensor_tensor(out=qt, in0=t_[:, 0:128], scalar=32 ** -0.5, in1=Eq, op0=ALU.mult, op1=ALU.mult), reads=[tk, "gEq"], writes=["gqt"])
            em.op("dve", lambda e, t_=t_: e.tensor_tensor(out=kh, in0=t_[:, 128:256], in1=Ek, op=ALU.mult), reads=[tk, "gEk"], writes=["gkh"])
            em.op("pool", lambda e, t_=t_: e.tensor_tensor(out=kb, in0=t_[:, 128:256], in1=Eb, op=ALU.mult), reads=[tk, "gEb"], writes=["gkb"])
            b = pbank()
            for h in range(4):
                em.op("pe", lambda e, b=b, h=h: e.matmul(psum[b][0:32, h * 128:(h + 1) * 128], lhsT=qt[:, h * 32:(h + 1) * 32], rhs=c.identb, start=True, stop=True),
                      reads=["gqt"], writes=["ps%d" % b])
            em.op("act", lambda e, b=b: e.activation(out=qT[0:32], in_=psum[b][0:32], func=AF.Identity), reads=["ps%d" % b], writes=["gqT"])
            b = pbank()
            for h in range(4):
                em.op("pe", lambda e, b=b, h=h: e.matmul(psum[b][0:32, h * 128:(h + 1) * 128], lhsT=kh[:, h * 32:(h + 1) * 32], rhs=c.identb, start=True, stop=True),
                      reads=["gkh"], writes=["ps%d" % b])
            em.op("dve", lambda e, b=b: e.tensor_copy(out=kT[0:32], in_=psum[b][0:32]), reads=["ps%d" % b], writes=["gkT"])
            b = pbank()
            for h in range(4):
                em.op("pe", lambda e, b=b, h=h: e.matmul(psum[b][:, h * 128:(h + 1) * 128], lhsT=kT[0:32, h * 128:(h + 1) * 128], rhs=qT[0:32, h * 128:(h + 1) * 128], start=True, stop=True),
                      reads=["gkT", "gqT"], writes=["ps%d" % b])
            em.op("dve", lambda e, b=b, d=d: e.tensor_tensor(out=AT, in0=psum[b][:], in1=c.maskc[:, d * 4 + 0, :], op=ALU.mult), reads=["ps%d" % b], writes=["gAT"])
            b = pbank()
            for h in range(4):
                em.op("pe", lambda e, b=b, h=h: e.matmul(psum[b][0:32, h:h + 1], lhsT=lg[:, h * 32:(h + 1) * 32], rhs=c.ones_col[:, 0:1], start=True, stop=True),
                      reads=["glg"], writes=["ps%d" % b])
            em.op("act", lambda e, b=b: e.activation(out=etot[0:32], in_=psum[b][0:32, 0:4], func=AF.Exp, scale=-1.0 / 16), reads=["ps%d" % b], writes=["getot"])
            bO = pbank()
            for h in range(4):
                em.op("pe", lambda e, bO=bO, h=h: e.matmul(psum[bO][:, h * 64:(h + 1) * 64], lhsT=AT[:, h * 128:(h + 1) * 128], rhs=vb[:, h * 64:(h + 1) * 64], start=True, stop=False),
                      reads=["gAT", "gvb"], writes=["ps%d" % bO])
                em.op("pe", lambda e, bO=bO, h=h, Sb_=Sb_: e.matmul(psum[bO][:, h * 64:(h + 1) * 64], lhsT=qT[0:32, h * 128:(h + 1) * 128], rhs=Sb_[0:32, h, :], start=False, stop=True),
                      reads=["gqT", Sbk], writes=["ps%d" % bO])
            bS = pbank()
            for h in range(4):
                em.op("pe", lambda e, bS=bS, h=h: e.matmul(psum[bS][0:32, h * 64:(h + 1) * 64], lhsT=kb[:, h * 32:(h + 1) * 32], rhs=vb[:, h * 64:(h + 1) * 64], start=True, stop=True),
                      reads=["gkb", "gvb"], writes=["ps%d" % bS])
            for h in range(4):
                em.op("dve", lambda e, bS=bS, h=h, Sd=Sd: e.scalar_tensor_tensor(out=Sd[0:32, h, :], in0=Sd[0:32, h, :], scalar=etot[0:32, h:h + 1], in1=psum[bS][0:32, h * 64:(h + 1) * 64],
                                                                            op0=ALU.mult, op1=ALU.add), reads=[Sk, "getot", "ps%d" % bS], writes=[Sk])
            if n % 2 == 1:
                blk = tc // 2
                em.dma("sp", lambda e, Sd=Sd, d=d, l=l, blk=blk: e.dma_start(out=cx.osg[l, d, blk], in_=Sd[0:32]), Sk + "o", reads=[Sk], final=True)
            if d == 0:
                em.op("act", lambda e, bO=bO, tc=tc: e.activation(out=OG[:, tc, :], in_=psum[bO][:, 0:256], func=AF.Identity), reads=["ps%d" % bO], writes=["gOG"])
            else:
                em.op("dve", lambda e, bO=bO, tc=tc: e.tensor_tensor(out=og, in0=psum[bO][:, 0:256], in1=OG[:, tc, :], op=ALU.add), reads=["ps%d" % bO, "gOG"], writes=["gog"])
                em.op("pool", lambda e: e.tensor_tensor(out=sq, in0=og, in1=og, op=ALU.mult), reads=["gog"], writes=["gsq"])
                em.op("dve", lambda e: e.tensor_reduce(out=ss, in_=sq.rearrange("p (h v) -> p h v", v=64), axis=AX.X, op=ALU.add), reads=["gsq"], writes=["gss"])
                em.op("act", lambda e: e.activation(out=ss, in_=ss, func=AF.Sqrt, bias=cx.c_eps, scale=1.0 / 64), reads=["gss"], writes=["gss"])
                em.op("dve", lambda e: e.reciprocal(out=ss, in_=ss), reads=["gss"], writes=["gss"])
                em.op("dve", lambda e: e.tensor_tensor(out=yd.rearrange("p (h v) -> p h v", v=64), in0=og.rearrange("p (h v) -> p h v", v=64),
                                                       in1=ss.unsqueeze(2).to_broadcast([128, 4, 64]), op=ALU.mult), reads=["gog", "gss"], writes=["gyd"])
                em.op("pool", lambda e: e.tensor_tensor(out=yd, in0=yd, in1=c.gln, op=ALU.mult), reads=["gyd"], writes=["gyd"])
                em.op("pool", lambda e: e.tensor_tensor(out=ydb, in0=yd, in1=sgo, op=ALU.mult), reads=["gyd", "gsgo"], writes=["gydb"])
                b = pbank()
                for cc in range(2):
                    em.op("pe", lambda e, b=b, cc=cc: e.matmul(psum[b][:, cc * 128:(cc + 1) * 128], lhsT=ydb[:, cc * 128:(cc + 1) * 128], rhs=c.identb, start=True, stop=True),
                          reads=["gydb"], writes=["ps%d" % b])
                yT_ = ydT[par]; yk = "gydT%d" % par
                em.op("act", lambda e, b=b, yT_=yT_: e.activation(out=yT_, in_=psum[b][:, 0:256].rearrange("p (c t) -> p c t", t=128), func=AF.Identity), reads=["ps%d" % b], writes=[yk])
                em.dma("sp", lambda e, yT_=yT_, tc=tc: e.dma_start(out=Yv(tc), in_=yT_), yk + "w", reads=[yk])


def rwkv_prepass(cx, l, c):
    em, ar, psum, pbank, U, Y = cx.em, cx.ar, cx.psum, cx.pbank, cx.U, cx.Y
    ZM = cx.ZM
    mark = ar.off
    zt = [ar.alloc([6, 514], F32) for _ in range(2)]
    sh = ar.alloc([6, 512], F32)
    zo = [ar.alloc([6, 512], F32) for _ in range(2)]
    Uz = U[512:1280, :].rearrange("(c p) t -> p c t", p=128)
    ZMv = ZM.rearrange("(c p) t -> p c t", p=128)
    for j in range(NT):
        z = zt[j % 2]; zk = "rzt%d" % (j % 2); o_ = zo[j % 2]; ok = "rzo%d" % (j % 2)
        t0 = j * 512
        if j == 0:
            em.op("pool", lambda e, z=z: e.memset(z[:, :, 0:1], 0.0), writes=[zk])
            em.dma("sp", lambda e, z=z: e.dma_start(out=z[:, :, 1:514], in_=Uz[:, :, 0:513]), zk, writes=[zk])
        elif j == NT - 1:
            em.op("pool", lambda e, z=z: e.memset(z[:, :, 513:514], 0.0), writes=[zk])
            em.dma("sp", lambda e, z=z, t0=t0: e.dma_start(out=z[:, :, 0:513], in_=Uz[:, :, t0 - 1:t0 + 512]), zk, writes=[zk])
        else:
            em.dma("sp", lambda e, z=z, t0=t0: e.dma_start(out=z, in_=Uz[:, :, t0 - 1:t0 + 513]), zk, writes=[zk])
        em.op("dve", lambda e, z=z: e.tensor_tensor(out=sh, in0=z[:, :, 0:512], in1=z[:, :, 2:514], op=ALU.add), reads=[zk], writes=["rsh"])
        em.op("dve", lambda e, z=z: e.scalar_tensor_tensor(out=sh[:, :, 0:512:256], in0=z[:, :, 0:512:256], scalar=cx.c_nk, in1=sh[:, :, 0:512:256], op0=ALU.mult, op1=ALU.add),
              reads=[zk, "rsh"], writes=["rsh"])
        em.op("dve", lambda e, z=z: e.scalar_tensor_tensor(out=sh[:, :, 255:512:256], in0=z[:, :, 257:514:256], scalar=cx.c_nk, in1=sh[:, :, 255:512:256], op0=ALU.mult, op1=ALU.add),
              reads=[zk, "rsh"], writes=["rsh"])
        for cc in range(6):
            em.op("pool", lambda e, cc=cc: e.tensor_scalar(out=sh[:, cc, :], in0=sh[:, cc, :], scalar1=c.hm[:, cc:cc + 1], scalar2=None, op0=ALU.mult), reads=["rsh"], writes=["rsh"])
            em.op("dve", lambda e, cc=cc, z=z, o_=o_: e.scalar_tensor_tensor(out=o_[:, cc, :], in0=z[:, cc, 1:513], scalar=c.om[:, cc:cc + 1], in1=sh[:, cc, :], op0=ALU.mult, op1=ALU.add),
                  reads=[zk, "rsh"], writes=[ok])
        em.dma("sp", lambda e, o_=o_, t0=t0: e.dma_start(out=ZMv[:, :, t0:t0 + 512], in_=o_), ok + "w", reads=[ok])
    em.barrier()
    ar.off = mark


def mix_rwkv(cx, l, c):
    em, ar, psum, pbank, U, Y = cx.em, cx.ar, cx.psum, cx.pbank, cx.U, cx.Y
    CD = 0.606531
    ZM = cx.ZM
    ZMv = ZM.rearrange("(c p) t -> p c t", p=128)
    A = lambda shp: ar.alloc(shp, F32)
    H = [A([4, 64]) for _ in range(2)]
    OR = A([32, 256]); BON = A([32, 4])
    zc = [A([6, 128]) for _ in range(2)]
    cw = [A([128]) for _ in range(2)]; ca = [A([128]) for _ in range(2)]; cg = [A([128]) for _ in range(2)]
    rk_ = A([512]); v_ = A([256])
    tcw = A([128]); sg = A([256]); aa = A([256])
    kw = A([256]); sq = A([256]); ss = A([4]); kk = A([256]); t1 = A([256]); kdir = A([256]); bb = A([256])
    Ex = A([256]); E1 = A([256]); E2 = A([256]); E3 = A([256]); E4 = A([256])
    kkt = A([256]); bh = A([256]); kh = A([256]); rt = A([256]); Kbar = A([256]); Bbn = A([256]); t2 = A([256]); bsum = A([4])
    kktT = A([512]); bhT = A([512]); khT = A([512]); rtT = A([512])
    Mp = [A([512]) for _ in range(2)]; Np = [A([512]) for _ in range(2)]; X = A([512])
    AkkT = A([512]); ArkT = A([512]); ArbT = A([512])
    WT = A([512]); Y0 = A([256]); U0 = A([256]); Uu = A([256]); pc = A([4])
    o = A([256]); s1 = A([4]); cen = A([256]); s2 = A([4]); bon = A([4]); sgc = A([128]); yc = A([256])
    ycT = [ar.alloc([2, 128], BF16) for _ in range(2)]
    Yv = lambda tc: cx.tv(Y, tc // 4)[:, 4:6, (tc % 4) * 128:(tc % 4 + 1) * 128]
    bw = lambda d: c.lr64[0:64, d * 256:(d + 1) * 256]
    bg = c.lr64[0:64, 512:768]
    ba = lambda d: c.lr32[0:32, d * 256:(d + 1) * 256]
    w0row = lambda d: c.rows[0:1, d * 256:(d + 1) * 256]
    a0row = lambda d: c.rows[0:1, 512 + d * 256:512 + (d + 1) * 256]
    h3 = lambda ap: ap.rearrange("p (h v) -> p h v", v=64)
    bc4 = lambda ap4: ap4.unsqueeze(2).to_broadcast([128, 4, 64])

    def ew(eng, fn, reads, writes):
        em.op(eng, fn, reads=reads, writes=writes)

    def mm4(b, lhs, rhs, reads, npart=128, width=128):
        for h in range(4):
            em.op("pe", lambda e, h=h: e.matmul(psum[b][0:npart, h * width:(h + 1) * width], lhsT=lhs(h), rhs=rhs(h), start=True, stop=True),
                  reads=reads, writes=["ps%d" % b])

    blk128 = lambda t: (lambda h: t[:, h * 128:(h + 1) * 128])
    blk64 = lambda t: (lambda h: t[:, h * 64:(h + 1) * 64])
    blkT = lambda t: (lambda h: t[0:64, h * 128:(h + 1) * 128])

    for d in range(2):
        order = list(range(32)) if d == 0 else list(range(31, -1, -1))
        Hd = H[d]; Hk = "rH%d" % d
        for n, tc in enumerate(order):
            par = n % 2
            tsl = slice(tc * 128, (tc + 1) * 128)
            z = zc[par]; zk = "rzc%d" % par; cw_ = cw[par]; cwk = "rcw%d" % par; ca_ = ca[par]; cak = "rca%d" % par; cg_ = cg[par]; cgk = "rcg%d" % par
            em.dma("sp", lambda e, z=z, tsl=tsl: e.dma_start(out=z, in_=ZMv[:, :, tsl]), zk, writes=[zk])
            em.dma("sp", lambda e, cw_=cw_, tsl=tsl, d=d: e.dma_start(out=cw_[0:64, :], in_=U[1280 + 64 * d:1280 + 64 * d + 64, tsl]), cwk, writes=[cwk])
            em.dma("sp", lambda e, ca_=ca_, tsl=tsl, d=d: e.dma_start(out=ca_[0:32, :], in_=U[1408 + 32 * d:1408 + 32 * d + 32, tsl]), cak, writes=[cak])
            if d == 1:
                em.dma("sp", lambda e, cg_=cg_, tsl=tsl: e.dma_start(out=cg_[0:64, :], in_=U[1472:1536, tsl]), cgk, writes=[cgk])
            if n == 0:
                em.dma("sp", lambda e, Hd=Hd, d=d, l=l: e.dma_start(out=Hd[0:64], in_=cx.sr0[l, d]), Hk + "i", writes=[Hk])
            elif n % 2 == 0:
                ew("dve", lambda e, Hd=Hd: e.tensor_scalar(out=Hd[0:64], in0=Hd[0:64], scalar1=cx.c_kap[0:64], scalar2=None, op0=ALU.mult), [Hk], [Hk])
            b = pbank()
            for cc in range(4):
                em.op("pe", lambda e, b=b, cc=cc, z=z: e.matmul(psum[b][:, cc * 128:(cc + 1) * 128], lhsT=z[:, cc, :], rhs=c.ident, start=True, stop=True), reads=[zk], writes=["ps%d" % b])
            ew("act", lambda e, b=b: e.activation(out=rk_, in_=psum[b][:], func=AF.Identity), ["ps%d" % b], ["rrk"])
            b = pbank()
            for cc in range(2):
                em.op("pe", lambda e, b=b, cc=cc, z=z: e.matmul(psum[b][:, cc * 128:(cc + 1) * 128], lhsT=z[:, 4 + cc, :], rhs=c.ident, start=True, stop=True), reads=[zk], writes=["ps%d" % b])
            ew("act", lambda e, b=b: e.activation(out=v_, in_=psum[b][:, 0:256], func=AF.Identity), ["ps%d" % b], ["rv"])
            r_ = rk_[:, 0:256]; k_ = rk_[:, 256:512]
            ew("act", lambda e, cw_=cw_: e.activation(out=tcw[0:64], in_=cw_[0:64], func=AF.Tanh), [cwk], ["rtcw"])
            b = pbank()
            em.op("pe", lambda e, b=b, d=d: e.matmul(psum[b][:, 0:256], lhsT=tcw[0:64, :], rhs=bw(d), start=True, stop=False), reads=["rtcw"], writes=["ps%d" % b])
            em.op("pe", lambda e, b=b, d=d: e.matmul(psum[b][:, 0:256], lhsT=c.ones_row[0:1, :], rhs=w0row(d), start=False, stop=True), reads=[], writes=["ps%d" % b])
            em.op("pe", lambda e, b=b, d=d, ca_=ca_: e.matmul(psum[b][:, 256:512], lhsT=ca_[0:32, :], rhs=ba(d), start=True, stop=False), reads=[cak], writes=["ps%d" % b])
            em.op("pe", lambda e, b=b, d=d: e.matmul(psum[b][:, 256:512], lhsT=c.ones_row[0:1, :], rhs=a0row(d), start=False, stop=True), reads=[], writes=["ps%d" % b])
            ew("act", lambda e, b=b: e.activation(out=sg, in_=psum[b][:, 0:256], func=AF.Sigmoid), ["ps%d" % b], ["rsg"])
            ew("act", lambda e, b=b: e.activation(out=aa, in_=psum[b][:, 256:512], func=AF.Sigmoid), ["ps%d" % b], ["raa"])
            ew("dve", lambda e: e.tensor_tensor(out=kw, in0=k_, in1=c.kkw, op=ALU.mult), ["rrk"], ["rkw"])
            ew("pool", lambda e: e.tensor_tensor(out=sq, in0=kw, in1=kw, op=ALU.mult), ["rkw"], ["rsq"])
            ew("dve", lambda e: e.tensor_reduce(out=ss, in_=h3(sq), axis=AX.X, op=ALU.add), ["rsq"], ["rss"])
            ew("act", lambda e: e.activation(out=ss, in_=ss, func=AF.Sqrt, bias=cx.c_eps, scale=1.0), ["rss"], ["rss"])
            ew("dve", lambda e: e.reciprocal(out=ss, in_=ss), ["rss"], ["rss"])
            ew("dve", lambda e: e.tensor_tensor(out=h3(kk), in0=h3(kw), in1=bc4(ss), op=ALU.mult), ["rkw", "rss"], ["rkk"])
            ew("pool", lambda e: e.tensor_tensor(out=t1, in0=aa, in1=c.ka, op=ALU.mult), ["raa"], ["rt1"])
            ew("pool", lambda e: e.tensor_tensor(out=t1, in0=t1, in1=c.oka, op=ALU.add), ["rt1"], ["rt1"])
            ew("pool", lambda e: e.tensor_tensor(out=kdir, in0=k_, in1=t1, op=ALU.mult), ["rrk", "rt1"], ["rkdir"])
            ew("dve", lambda e: e.tensor_tensor(out=bb, in0=kk, in1=aa, op=ALU.mult), ["rkk", "raa"], ["rbb"])
            b = pbank()
            em.op("pe", lambda e, b=b, d=d: e.matmul(psum[b][:, 0:256], lhsT=c.maskc[:, d * 4 + 0, 0:128], rhs=sg, start=True, stop=True), reads=["rsg"], writes=["ps%d" % b])
            em.op("pe", lambda e, b=b, d=d: e.matmul(psum[b][:, 256:512], lhsT=c.maskc[:, d * 4 + 3, 0:128], rhs=sg, start=True, stop=True), reads=["rsg"], writes=["ps%d" % b])
            ew("dve", lambda e, b=b: e.tensor_tensor(out=Ex, in0=psum[b][:, 0:256], in1=sg, op=ALU.subtract), ["ps%d" % b, "rsg"], ["rEx"])
            ew("act", lambda e: e.activation(out=E1, in_=Ex, func=AF.Exp, scale=-CD), ["rEx"], ["rE1"])
            ew("act", lambda e, b=b: e.activation(out=E2, in_=psum[b][:, 0:256], func=AF.Exp, scale=CD), ["ps%d" % b], ["rE2"])
            ew("act", lambda e, b=b: e.activation(out=E3, in_=psum[b][:, 0:256], func=AF.Exp, scale=-CD), ["ps%d" % b], ["rE3"])
            ew("act", lambda e, b=b: e.activation(out=E4, in_=psum[b][:, 256:512], func=AF.Exp, scale=-CD), ["ps%d" % b], ["rE4"])
            ew("dve", lambda e: e.tensor_tensor(out=kkt, in0=kk, in1=E1, op=ALU.mult), ["rkk", "rE1"], ["rkkt"])
            ew("pool", lambda e: e.tensor_tensor(out=bh, in0=bb, in1=E2, op=ALU.mult), ["rbb", "rE2"], ["rbh"])
            ew("dve", lambda e: e.tensor_tensor(out=kh, in0=kdir, in1=E2, op=ALU.mult), ["rkdir", "rE2"], ["rkh"])
            ew("pool", lambda e: e.tensor_tensor(out=rt, in0=r_, in1=E3, op=ALU.mult), ["rrk", "rE3"], ["rrt"])
            ew("dve", lambda e: e.tensor_tensor(out=Kbar, in0=kdir, in1=E4, op=ALU.mult), ["rkdir", "rE4"], ["rKbar"])
            ew("dve", lambda e: e.scalar_tensor_tensor(out=Bbn, in0=bb, scalar=-1.0, in1=E4, op0=ALU.mult, op1=ALU.mult), ["rbb", "rE4"], ["rBbn"])
            ew("pool", lambda e: e.tensor_tensor(out=t2, in0=r_, in1=kdir, op=ALU.mult), ["rrk", "rkdir"], ["rt2"])
            ew("pool", lambda e: e.tensor_tensor(out=t2, in0=t2, in1=c.rk, op=ALU.mult), ["rt2"], ["rt2"])
            if d == 0:
                ew("dve", lambda e, tc=tc: e.tensor_reduce(out=BON[:, tc, :], in_=h3(t2), axis=AX.X, op=ALU.add), ["rt2"], ["rBON"])
            else:
                ew("dve", lambda e: e.tensor_reduce(out=bsum, in_=h3(t2), axis=AX.X, op=ALU.add), ["rt2"], ["rbsum"])
            for ti, (src, sk, dst, dk_) in enumerate(((kkt, "rkkt", kktT, "rkktT"), (bh, "rbh", bhT, "rbhT"), (kh, "rkh", khT, "rkhT"), (rt, "rrt", rtT, "rrtT"))):
                b = pbank()
                mm4(b, blk64(src), lambda h: c.ident, [sk], npart=64)
                if ti % 2 == 0:
                    ew("act", lambda e, b=b, dst=dst: e.activation(out=dst[0:64], in_=psum[b][0:64], func=AF.Identity), ["ps%d" % b], [dk_])
                else:
                    ew("dve", lambda e, b=b, dst=dst: e.tensor_copy(out=dst[0:64], in_=psum[b][0:64]), ["ps%d" % b], [dk_])
            def amat(dst, dkey, lhsT_t, lk, rhs_t, rkey, kind, d=d):
                b = pbank()
                mm4(b, blkT(lhsT_t), blkT(rhs_t), [lk, rkey])
                ew("dve", lambda e, b=b: e.tensor_tensor(out=dst, in0=psum[b][:], in1=c.maskc[:, d * 4 + kind, :], op=ALU.mult), ["ps%d" % b], [dkey])
            amat(Mp[0], "rMp0", bhT, "rbhT", kktT, "rkktT", 2)
            amat(Np[0], "rNp0", kktT, "rkktT", bhT, "rbhT", 3)
            amat(AkkT, "rAkkT", khT, "rkhT", kktT, "rkktT", 2)
            amat(ArkT, "rArkT", khT, "rkhT", rtT, "rrtT", 0)
            amat(ArbT, "rArbT", bhT, "rbhT", rtT, "rrtT", 1)
            ew("pool", lambda e: e.tensor_tensor(out=X, in0=c.ident4, in1=Mp[0], op=ALU.subtract), ["rMp0"], ["rX"])
            cur = 0
            for jj in range(1, 7):
                nxt = 1 - cur
                if jj < 6:
                    b = pbank()
                    mm4(b, blk128(Np[cur]), blk128(Mp[cur]), ["rNp%d" % cur, "rMp%d" % cur])
                    ew("act", lambda e, b=b, nxt=nxt: e.activation(out=Mp[nxt], in_=psum[b][:], func=AF.Identity), ["ps%d" % b], ["rMp%d" % nxt])
                b = pbank()
                mm4(b, blk128(Mp[cur]), blk128(Np[cur]), ["rNp%d" % cur, "rMp%d" % cur])
                ew("dve", lambda e, b=b, nxt=nxt: e.tensor_copy(out=Np[nxt], in_=psum[b][:]), ["ps%d" % b], ["rNp%d" % nxt])
                b = pbank()
                mm4(b, blk128(Np[nxt]), blk128(X), ["rNp%d" % nxt, "rX"])
                ew("dve", lambda e, b=b: e.tensor_tensor(out=X, in0=X, in1=psum[b][:], op=ALU.add), ["ps%d" % b, "rX"], ["rX"])
                cur = nxt
            b = pbank()
            mm4(b, blk64(kkt), blk128(X), ["rkkt", "rX"], npart=64)
            ew("act", lambda e, b=b: e.activation(out=WT[0:64], in_=psum[b][0:64], func=AF.Identity), ["ps%d" % b], ["rWT"])
            b = pbank()
            mm4(b, blk128(AkkT), blk64(v_), ["rAkkT", "rv"], width=64)
            ew("act", lambda e, b=b: e.activation(out=Y0, in_=psum[b][:, 0:256], func=AF.Identity), ["ps%d" % b], ["rY0"])
            b = pbank()
            mm4(b, blk128(X), blk64(Y0), ["rX", "rY0"], width=64)
            ew("dve", lambda e, b=b: e.tensor_copy(out=U0, in_=psum[b][:, 0:256]), ["ps%d" % b], ["rU0"])
            b = pbank()
            mm4(b, blk64(sg), lambda h: c.ones_col[:, 0:1], ["rsg"], npart=64, width=1)
            ew("act", lambda e, b=b: e.activation(out=pc[0:64], in_=psum[b][0:64, 0:4], func=AF.Exp, scale=-CD), ["ps%d" % b], ["rpc"])
            b = pbank()
            mm4(b, blkT(WT), lambda h, Hd=Hd: Hd[0:64, h, :], ["rWT", Hk], width=64)
            ew("dve", lambda e, b=b: e.tensor_tensor(out=Uu, in0=psum[b][:, 0:256], in1=U0, op=ALU.add), ["ps%d" % b, "rU0"], ["rUu"])
            bO = pbank()
            for h in range(4):
                osl = slice(h * 64, (h + 1) * 64)
                em.op("pe", lambda e, h=h, osl=osl, Hd=Hd, bO=bO: e.matmul(psum[bO][:, osl], lhsT=rtT[0:64, h * 128:(h + 1) * 128], rhs=Hd[0:64, h, :], start=True, stop=False), reads=["rrtT", Hk], writes=["ps%d" % bO])
                em.op("pe", lambda e, h=h, osl=osl, bO=bO: e.matmul(psum[bO][:, osl], lhsT=ArkT[:, h * 128:(h + 1) * 128], rhs=v_[:, osl], start=False, stop=False), reads=["rArkT", "rv"], writes=["ps%d" % bO])
                em.op("pe", lambda e, h=h, osl=osl, bO=bO: e.matmul(psum[bO][:, osl], lhsT=ArbT[:, h * 128:(h + 1) * 128], rhs=Uu[:, osl], start=False, stop=True), reads=["rArbT", "rUu"], writes=["ps%d" % bO])
            bH = pbank()
            for h in range(4):
                osl = slice(h * 64, (h + 1) * 64)
                em.op("pe", lambda e, h=h, osl=osl, bH=bH: e.matmul(psum[bH][0:64, osl], lhsT=Kbar[:, osl], rhs=v_[:, osl], start=True, stop=False), reads=["rKbar", "rv"], writes=["ps%d" % bH])
                em.op("pe", lambda e, h=h, osl=osl, bH=bH: e.matmul(psum[bH][0:64, osl], lhsT=Bbn[:, osl], rhs=Uu[:, osl], start=False, stop=True), reads=["rBbn", "rUu"], writes=["ps%d" % bH])
            for h in range(4):
                ew("dve", lambda e, h=h, Hd=Hd, bH=bH: e.scalar_tensor_tensor(out=Hd[0:64, h, :], in0=Hd[0:64, h, :], scalar=pc[0:64, h:h + 1], in1=psum[bH][0:64, h * 64:(h + 1) * 64], op0=ALU.mult, op1=ALU.add),
                   [Hk, "rpc", "ps%d" % bH], [Hk])
            if n % 2 == 1:
                blk = tc // 2
                em.dma("sp", lambda e, Hd=Hd, d=d, l=l, blk=blk: e.dma_start(out=cx.osr[l, d, blk], in_=Hd[0:64]), Hk + "o", reads=[Hk], final=True)
            if d == 0:
                ew("act", lambda e, tc=tc, bO=bO: e.activation(out=OR[:, tc, :], in_=psum[bO][:, 0:256], func=AF.Identity), ["ps%d" % bO], ["rOR"])
                continue
            ew("dve", lambda e, tc=tc, bO=bO: e.tensor_tensor(out=o, in0=psum[bO][:, 0:256], in1=OR[:, tc, :], op=ALU.add), ["ps%d" % bO, "rOR"], ["ro"])
            ew("dve", lambda e: e.tensor_reduce(out=s1, in_=h3(o), axis=AX.X, op=ALU.add), ["ro"], ["rs1"])
            ew("dve", lambda e: e.tensor_scalar(out=s1, in0=s1, scalar1=1.0 / 64, scalar2=None, op0=ALU.mult), ["rs1"], ["rs1"])
            ew("dve", lambda e: e.tensor_tensor(out=h3(cen), in0=h3(o), in1=bc4(s1), op=ALU.subtract), ["ro", "rs1"], ["rcen"])
            ew("pool", lambda e: e.tensor_tensor(out=sq, in0=cen, in1=cen, op=ALU.mult), ["rcen"], ["rsq"])
            ew("dve", lambda e: e.tensor_reduce(out=s2, in_=h3(sq), axis=AX.X, op=ALU.add), ["rsq"], ["rs2"])
            ew("act", lambda e: e.activation(out=s2, in_=s2, func=AF.Sqrt, bias=cx.c_gneps, scale=1.0 / 64), ["rs2"], ["rs2"])
            ew("dve", lambda e: e.reciprocal(out=s2, in_=s2), ["rs2"], ["rs2"])
            ew("dve", lambda e: e.tensor_tensor(out=h3(cen), in0=h3(cen), in1=bc4(s2), op=ALU.mult), ["rcen", "rs2"], ["rcen"])
            ew("pool", lambda e: e.tensor_tensor(out=cen, in0=cen, in1=c.gn, op=ALU.mult), ["rcen"], ["rcen"])
            ew("dve", lambda e, tc=tc: e.tensor_tensor(out=bon, in0=BON[:, tc, :], in1=bsum, op=ALU.add), ["rBON", "rbsum"], ["rbon"])
            ew("dve", lambda e: e.tensor_tensor(out=h3(t2), in0=h3(v_), in1=bc4(bon), op=ALU.mult), ["rv", "rbon", "rt2"], ["rt2"])
            ew("pool", lambda e: e.tensor_tensor(out=cen, in0=cen, in1=t2, op=ALU.add), ["rcen", "rt2"], ["rcen"])
            ew("act", lambda e, cg_=cg_: e.activation(out=sgc[0:64], in_=cg_[0:64], func=AF.Sigmoid), [cgk], ["rsgc"])
            b = pbank()
            em.op("pe", lambda e, b=b: e.matmul(psum[b][:, 0:256], lhsT=sgc[0:64, :], rhs=bg, start=True, stop=True), reads=["rsgc"], writes=["ps%d" % b])
            ew("dve", lambda e, b=b: e.tensor_tensor(out=yc, in0=psum[b][:, 0:256], in1=cen, op=ALU.mult), ["ps%d" % b, "rcen"], ["ryc"])
            b = pbank()
            for cc in range(2):
                em.op("pe", lambda e, b=b, cc=cc: e.matmul(psum[b][:, cc * 128:(cc + 1) * 128], lhsT=yc[:, cc * 128:(cc + 1) * 128], rhs=c.ident, start=True, stop=True), reads=["ryc"], writes=["ps%d" % b])
            yT_ = ycT[par]; yk = "rycT%d" % par
            ew("act", lambda e, b=b, yT_=yT_: e.activation(out=yT_, in_=psum[b][:, 0:256].rearrange("p (c t) -> p c t", t=128), func=AF.Identity), ["ps%d" % b], [yk])
            em.dma("sp", lambda e, yT_=yT_, tc=tc: e.dma_start(out=Yv(tc), in_=yT_), yk + "w", reads=[yk])


_NC = None
_EM = None
_LAST = None


def kernel(**inputs):
    global _NC
    f = lambda k: np.ascontiguousarray(np.asarray(inputs[k], dtype=np.float32))
    x_prompt = f("x_prompt"); x_sample = f("x_sample"); c = f("c"); c_ctx = f("c_ctx")
    jobs = []
    ntok = x_sample.shape[1]
    rows = ntok // 64
    rr, cc = np.meshgrid(np.arange(rows, dtype=np.float32), np.arange(64, dtype=np.float32), indexing="ij")
    rr = rr.reshape(-1); cc = cc.reshape(-1)
    quarter = D // 4
    omega = (1.0 / (np.float32(10000.0) ** (np.arange(quarter, dtype=np.float32) / np.float32(quarter)))).astype(np.float32)
    arr = rr[:, None] * omega; acc = cc[:, None] * omega
    pos = np.concatenate([np.sin(arr), np.cos(arr), np.sin(acc), np.cos(acc)], axis=-1).astype(np.float32)
    lay8 = lambda v: np.ascontiguousarray(v.reshape(-1, 128).T)
    common = {
        "ada_w": f("ada_w"),
        "ada_b": np.ascontiguousarray(f("ada_b").reshape(DEPTH, 48, 128).transpose(0, 2, 1)),
        "n1g": np.ascontiguousarray(f("norm1_g").reshape(DEPTH, 8, 128).transpose(0, 2, 1)),
        "n2g": np.ascontiguousarray(f("norm2_g").reshape(DEPTH, 8, 128).transpose(0, 2, 1)),
        "fg": lay8(f("final_g")),
        "w_in": f("w_in"), "w_br": f("w_branch").reshape(DEPTH, D, D), "w_out": f("w_out"),
        "w1": f("mlp_w1"), "w2": f("mlp_w2"),
    }
    import ml_dtypes
    g = lambda k: f(k)
    def bcast(v):
        return np.broadcast_to(v.reshape(1, -1), (128, v.size))
    prm = np.zeros((DEPTH, 128, 1568), np.float32); rows_ = np.zeros((DEPTH, 1, 1280), np.float32)
    lr64 = np.zeros((DEPTH, 64, 768), np.float32); lr32 = np.zeros((DEPTH, 32, 512), np.float32); lr16 = np.zeros((DEPTH, 16, 256), np.float32)
    for l in range(DEPTH):
        prm[l, :, 0:2] = g("pool_scale")[l].reshape(2, 128).T
        prm[l, :, 2:8] = g("rwkv_mu")[l].reshape(6, 128).T
        for i, nm in enumerate(("rwkv_kk", "rwkv_ka", None, "rwkv_rk", "rwkv_gn", "gla_norm")):
            if nm is not None:
                prm[l, :, 32 + i * 256:32 + (i + 1) * 256] = bcast(g(nm)[l])
        rows_[l, 0, 0:512] = g("rwkv_w0")[l].reshape(-1); rows_[l, 0, 512:1024] = g("rwkv_a0")[l].reshape(-1)
        rows_[l, 0, 1024:1280] = g("gla_abias")[l].reshape(-1)
        lr64[l, :, 0:256] = g("rwkv_bw")[l, 0]; lr64[l, :, 256:512] = g("rwkv_bw")[l, 1]; lr64[l, :, 512:768] = g("rwkv_bg")[l]
        lr32[l, :, 0:256] = g("rwkv_ba")[l, 0]; lr32[l, :, 256:512] = g("rwkv_ba")[l, 1]
        lr16[l, :, 0:128] = g("gla_ab")[l, 0]; lr16[l, :, 128:256] = g("gla_ab")[l, 1]
    ii = np.arange(128)[:, None]; jj = np.arange(128)[None, :]
    maskc = np.zeros((128, 8, 512), np.float32)
    for d_ in range(2):
        incl = (ii <= jj) if d_ == 0 else (ii >= jj)
        strict = (ii < jj) if d_ == 0 else (ii > jj)
        after = (ii > jj) if d_ == 0 else (ii < jj)
        for kind, m in enumerate((incl, incl, strict, after)):
            mm_ = np.tile(m.astype(np.float32), (1, 4))
            maskc[:, d_ * 4 + kind, :] = -mm_ if kind == 1 else mm_
    ch = np.arange(256); gch = ch // 64; cidx = ch % 64
    kk_ = np.arange(64)
    csm = np.zeros((256, 512), np.float64)
    for gi in range(4):
        rws = np.where(gch == gi)[0]
        ang = 2 * np.pi * np.outer(cidx[rws], kk_) / 64.0
        csm[np.ix_(rws, gi * 64 + kk_)] = np.cos(ang) / 8.0
        csm[np.ix_(rws, 256 + gi * 64 + kk_)] = -np.sin(ang) / 8.0
    csm = csm.reshape(2, 128, 512).astype(np.float32)

    def job_consts(L):
        tt = np.arange(T)
        s_ = tt % L
        invc = np.zeros((2, 128, T), np.float32)
        for gi, win in enumerate((2, 4, 8, 16)):
            lo = np.clip(s_ - win // 2, 0, L - 1); hi = np.clip(s_ + (win - win // 2) - 1, 0, L - 1)
            invc[gi // 2, (gi % 2) * 64:(gi % 2) * 64 + 64, :] = (1.0 / (hi - lo + 1))[None, :]
        CLm = np.zeros((T, T), np.float32); SLm = np.zeros((T, T), np.float32)
        sl_ = np.arange(L)
        mmod = np.outer(sl_, sl_) % L
        cb = (np.cos(2 * np.pi * mmod / L) / np.sqrt(L)).astype(np.float32)
        sb_ = (np.sin(2 * np.pi * mmod / L) / np.sqrt(L)).astype(np.float32)
        for b_ in range(T // L):
            CLm[b_ * L:(b_ + 1) * L, b_ * L:(b_ + 1) * L] = cb
            SLm[b_ * L:(b_ + 1) * L, b_ * L:(b_ + 1) * L] = sb_
        def lay(m):
            m5 = m.reshape(2, 16, 128, 8, 512).transpose(3, 0, 2, 1, 4)
            return np.ascontiguousarray(m5).reshape(8, 2, 128, 16 * 512).astype(ml_dtypes.bfloat16)
        return invc, lay(CLm), lay(SLm)
    invc_s, CL_s, SL_s = job_consts(T)
    invc_p, CL_p, SL_p = job_consts(256)
    common.update(prm=prm, rows=rows_, lr64=lr64, lr32=lr32, lr16=lr16, pool_w=g("pool_w"),
                  ident=np.eye(128, dtype=np.float32), identb=np.eye(128, dtype=np.float32).astype(ml_dtypes.bfloat16), maskc=maskc, csm=csm)
    st_r = g("state_rwkv"); st_g = g("state_gla")
    for s in range(2):
        common_s = dict(common, invc=invc_s, CL=CL_s, SL=SL_s,
                        sr0=np.ascontiguousarray(st_r[s].transpose(0, 1, 4, 2, 3)),
                        sg0=np.ascontiguousarray(st_g[s].transpose(0, 1, 3, 2, 4)))
        jobs.append(dict(common_s, xT=np.ascontiguousarray(x_sample[s].T), pos=np.ascontiguousarray(pos.T),
                         cond=lay8(c[s]), kap=np.ones((128, 1), np.float32)))
    xp = x_prompt.reshape(-1, D)
    common = dict(common, invc=invc_p, CL=CL_p, SL=SL_p, sr0=np.zeros((DEPTH, 2, 64, 4, 64), np.float32),
                  sg0=np.zeros((DEPTH, 2, 32, 4, 64), np.float32))
    pj = dict(common, xT=np.ascontiguousarray(xp.T), pos=np.zeros((D, T), np.float32),
              cond=lay8(c_ctx), kap=np.zeros((128, 1), np.float32))
    jobs.append(pj)
    while len(jobs) < N_CORES:
        jobs.append(pj)
    if _NC is None:
        _NC = build_program()
    res = run_bass_kernel_spmd(_NC, jobs, core_ids=list(range(N_CORES)))
    r = res.results
    global _LAST
    _LAST = r
    y_sample = np.stack([np.ascontiguousarray(r[s]["yT"].T) for s in range(2)], axis=0).astype(np.float32)
    y_prompt = np.ascontiguousarray(r[2]["yT"].T).reshape(x_prompt.shape).astype(np.float32)
    B = x_prompt.shape[0]
    nsr = np.ascontiguousarray(np.asarray(r[2]["osr"]).transpose(2, 0, 1, 4, 5, 3)).astype(np.float32)
    nsg = np.ascontiguousarray(np.asarray(r[2]["osg"]).transpose(2, 0, 1, 4, 3, 5)).astype(np.float32)
    return (y_prompt, y_sample, nsr, nsg)
```

```python
import contextlib
import numpy as np
import concourse.bass as bass
import concourse.mybir as mybir
from concourse.bass_utils import run_bass_kernel_spmd

F32 = mybir.dt.float32
BF16 = mybir.dt.bfloat16
ALU = mybir.AluOpType
AF = mybir.ActivationFunctionType
AX = mybir.AxisListType

D = 1024
T = 4096
NT = 8
TN = 512
DEPTH = 2
P_IN = 6432
NMIX = 2336
N_CORES = 8
EPS = 1e-6


class Em:
    ENGS = ("pe", "act", "dve", "pool", "sp")

    def __init__(self, nc):
        self.nc = nc
        self.ops = {e: [] for e in self.ENGS}
        self.cnt = {e: 0 for e in self.ENGS}
        self.seen = {e: {} for e in self.ENGS}
        self.last_w = {}
        self.readers = {}
        self.dma_sems = {}
        self.free_sems = []
        self.sem_names = ["c_" + e for e in self.ENGS]
        self.final_tokens = []
        self.marks = []

    def _deps(self, eng, reads, writes):
        toks = []
        for k in reads:
            t = self.last_w.get(k)
            if t is not None:
                toks.append(t)
        for k in writes:
            t = self.last_w.get(k)
            if t is not None:
                toks.append(t)
            toks.extend(self.readers.get(k, ()))
        seen = self.seen[eng]
        best = {}
        own = "c_" + eng
        for (s, v) in toks:
            if eng == "pe" and s == own:
                continue
            if seen.get(s, 0) < v and best.get(s, 0) < v:
                best[s] = v
        waits = []
        for s, v in best.items():
            seen[s] = v
            waits.append((s, v))
        return waits

    def _commit(self, tok, reads, writes):
        for k in reads:
            self.readers.setdefault(k, []).append(tok)
        for k in writes:
            self.last_w[k] = tok
            self.readers[k] = []

    def op(self, eng, fn, reads=(), writes=()):
        if eng != "pe":
            pr = [k for k in reads if k.startswith("ps")]
            if pr:
                writes = list(writes) + pr
        waits = self._deps(eng, reads, writes)
        self.cnt[eng] += 1
        tok = ("c_" + eng, self.cnt[eng])
        self.ops[eng].append((waits, fn, ("c_" + eng, 1)))
        self._commit(tok, reads, writes)
        return tok

    def dma(self, q, fn, semkey, reads=(), writes=(), final=False):
        if semkey not in self.dma_sems:
            if self.free_sems:
                self.dma_sems[semkey] = self.free_sems.pop()
            else:
                name = "d%d" % (len(self.sem_names) - len(self.ENGS))
                self.dma_sems[semkey] = [name, 0]
                self.sem_names.append(name)
        ent = self.dma_sems[semkey]
        waits = self._deps(q, reads, writes)
        ent[1] += 16
        tok = (ent[0], ent[1])
        self.ops[q].append((waits, fn, (ent[0], 16)))
        self._commit(tok, reads, writes)
        if final:
            self.final_tokens.append(tok)
        return tok

    def barrier(self, label=""):
        self.marks.append((label, dict(self.cnt)))
        targets = [("c_" + e, self.cnt[e]) for e in self.ENGS if self.cnt[e] > 0]
        targets += [(n, c) for (n, c) in self.dma_sems.values() if c > 0]
        for e in self.ENGS:
            waits = []
            for (s, v) in targets:
                if e == "pe" and s == "c_pe":
                    continue
                if self.seen[e].get(s, 0) < v:
                    self.seen[e][s] = v
                    waits.append((s, v))
            if waits:
                self.ops[e].append((waits, None, None))
        self.free_sems.extend(self.dma_sems.values())
        self.dma_sems = {}

    def build(self):
        nc = self.nc
        with contextlib.ExitStack() as st:
            print("Em: %d semaphores, ops:" % len(self.sem_names), {e: len(v) for e, v in self.ops.items()})
            sems = {n: st.enter_context(nc.semaphore(n)) for n in self.sem_names}
            fin = {}
            for (s, v) in self.final_tokens:
                fin[s] = max(fin.get(s, 0), v)
            block = st.enter_context(nc.Block())

            def runner(ename):
                def f(e):
                    for waits, fn, inc in self.ops[ename]:
                        for (s, v) in waits:
                            e.wait_ge(sems[s], v)
                        if fn is not None:
                            fn(e).then_inc(sems[inc[0]], inc[1])
                    if ename == "sp":
                        for s, v in fin.items():
                            e.wait_ge(sems[s], v)
                return f
            block.tensor(runner("pe"))
            block.scalar(runner("act"))
            block.vector(runner("dve"))
            block.gpsimd(runner("pool"))
            block.sync(runner("sp"))


class Rec:
    def __init__(self):
        self.items = []

    def op(self, eng, fn, reads=(), writes=()):
        self.items.append(("op", eng, fn, list(reads), list(writes)))

    def dma(self, q, fn, semkey, reads=(), writes=(), final=False):
        self.items.append(("dma", q, fn, semkey, list(reads), list(writes), final))


def merge_streams(em, recs):
    pos = [0] * len(recs)
    tot = [max(1, len(r.items)) for r in recs]
    while True:
        best = None
        for i, r in enumerate(recs):
            if pos[i] < len(r.items):
                frac = pos[i] / tot[i]
                if best is None or frac < best[0]:
                    best = (frac, i)
        if best is None:
            break
        i = best[1]
        it = recs[i].items[pos[i]]; pos[i] += 1
        if it[0] == "op":
            em.op(it[1], it[2], reads=it[3], writes=it[4])
        else:
            em.dma(it[1], it[2], it[3], reads=it[4], writes=it[5], final=it[6])


class Arena:
    def __init__(self, handle_bf16, nelem):
        self.h = handle_bf16
        self.n = nelem
        self.off = 0

    def reset(self):
        self.off = 0

    def alloc(self, shape_free, dt):
        n = int(np.prod(shape_free))
        nb = n * (2 if dt == F32 else 1)
        nb = (nb + 15) // 16 * 16
        assert self.off + nb <= self.n, ("arena overflow", self.off, nb, self.n)
        ap = self.h[:, self.off:self.off + n * (2 if dt == F32 else 1)]
        self.off += nb
        if dt == F32:
            ap = ap.bitcast(F32)
        if len(shape_free) == 2:
            ap = ap.rearrange("p (a b) -> p a b", b=shape_free[1])
        elif len(shape_free) == 3:
            ap = ap.rearrange("p (a b c) -> p a b c", b=shape_free[1], c=shape_free[2])
        return ap


def build_program():
    nc = bass.Bass("TRN2", target_bir_lowering=False)
    dI = lambda n, sh, dt=F32: nc.dram_tensor(n, sh, dt, kind="ExternalInput").ap()
    DBG = False
    NL = DEPTH
    dS = lambda n, sh, dt=F32: nc.dram_tensor(n, sh, dt, kind="Internal").ap()
    dO = lambda n, sh, dt=F32: nc.dram_tensor(n, sh, dt, kind="ExternalOutput").ap()
    xT = dI("xT", [D, T]); pos = dI("pos", [D, T])
    cond = dI("cond", [128, 8]); kap = dI("kap", [128, 1])
    ada_w = dI("ada_w", [DEPTH, D, 6 * D]); ada_b = dI("ada_b", [DEPTH, 128, 48])
    n1g = dI("n1g", [DEPTH, 128, 8]); n2g = dI("n2g", [DEPTH, 128, 8]); fg = dI("fg", [128, 8])
    w_in = dI("w_in", [DEPTH, D, P_IN]); w_br = dI("w_br", [DEPTH, D, D]); w_out = dI("w_out", [DEPTH, D, D])
    w1 = dI("w1", [DEPTH, D, 4 * D]); w2 = dI("w2", [DEPTH, 4 * D, D])
    yT = dO("yT", [D, T])
    PRM_N = 1568; ROW_N = 1280
    prm_d = dI("prm", [DEPTH, 128, PRM_N]); rows_d = dI("rows", [DEPTH, 1, ROW_N])
    lr64_d = dI("lr64", [DEPTH, 64, 768]); lr32_d = dI("lr32", [DEPTH, 32, 512]); lr16_d = dI("lr16", [DEPTH, 16, 256])
    pool_w_d = dI("pool_w", [DEPTH, 4, 64, 64])
    identb_d = dI("identb", [128, 128], BF16)
    ident_d = dI("ident", [128, 128]); maskc_d = dI("maskc", [128, 8, 512])
    invc_d = dI("invc", [2, 128, T]); csm_d = dI("csm", [2, 128, 512])
    CL_d = dI("CL", [8, 2, 128, 16 * 512], BF16); SL_d = dI("SL", [8, 2, 128, 16 * 512], BF16)
    sr0_d = dI("sr0", [DEPTH, 2, 64, 4, 64]); sg0_d = dI("sg0", [DEPTH, 2, 32, 4, 64])
    osr_d = dO("osr", [DEPTH, 2, 16, 64, 4, 64]); osg_d = dO("osg", [DEPTH, 2, 16, 32, 4, 64])
    ZM = dS("ZM", [768, T])
    xres = dS("xres", [NT, 128, 8 * TN])
    U = dS("U", [19 * 128, T])
    G = dS("G", [NT, 128, 32 * TN], BF16)
    Y = dS("Y", [NT, 128, 8 * TN], BF16)
    H2 = dS("H2", [NT, 128, 8 * TN], BF16)

    with contextlib.ExitStack() as st:
        arena_h = st.enter_context(nc.sbuf_tensor("arena", [128, 104000], BF16))
        cst_h = st.enter_context(nc.sbuf_tensor("cst", [128, 1024], F32))
        onesb = st.enter_context(nc.sbuf_tensor("onesb", [128, 128], BF16))
        psum = [st.enter_context(nc.psum_tensor("ps%d" % i, [128, 512], F32)) for i in range(8)]
        ar = Arena(arena_h, 104000)
        em = Em(nc)
        c_silu = cst_h[:, 0:8]; c_mod = cst_h[:, 8:56]; c_gsc1 = cst_h[:, 56:64]; c_gsc2 = cst_h[:, 64:72]
        c_n1g = cst_h[:, 72:80]; c_n2g = cst_h[:, 80:88]; c_fg = cst_h[:, 88:96]
        c_kap = cst_h[:, 96:97]; c_eps = cst_h[:, 97:98]; c_adab = cst_h[:, 100:148]
        c_nk = cst_h[:, 98:99]; c_gneps = cst_h[:, 99:100]
        em.dma("sp", lambda e: e.dma_start(out=c_silu, in_=cond), "c_silu", writes=["c_silu"])
        em.dma("sp", lambda e: e.dma_start(out=c_kap, in_=kap), "c_kap", writes=["c_kap"])
        em.dma("sp", lambda e: e.dma_start(out=c_fg, in_=fg), "c_fg", writes=["c_fg"])
        em.op("dve", lambda e: e.memset(c_eps, EPS), writes=["c_eps"])
        em.op("dve", lambda e: e.memset(c_gneps, 64e-5), writes=["c_gneps"])
        em.op("dve", lambda e: e.tensor_scalar(out=c_nk, in0=c_kap, scalar1=-1.0, scalar2=None, op0=ALU.add), reads=["c_kap"], writes=["c_nk"])
        em.op("dve", lambda e: e.memset(onesb[:], 1.0 / D), writes=["onesb"])
        em.op("act", lambda e: e.activation(out=c_silu, in_=c_silu, func=AF.Silu), reads=["c_silu"], writes=["c_silu"])

        pctr = [0]

        def pbank():
            pctr[0] = (pctr[0] + 1) % 8
            return pctr[0]

        def rmsnorm_tile(xt, xkey, gsc, shift, out_bf, outkey, tmp, xsq, rstd, tag):
            em.op("act", lambda e: e.activation(out=xsq, in_=xt, func=AF.Square), reads=[xkey], writes=[tag + "xsq"])
            b = pbank()
            for c in range(8):
                em.op("pe", lambda e, c=c, b=b: e.matmul(psum[b][:], lhsT=onesb[:], rhs=xsq[:, c, :], start=(c == 0), stop=(c == 7)),
                      reads=[tag + "xsq", "onesb"], writes=["ps%d" % b])
            em.op("act", lambda e, b=b: e.activation(out=rstd, in_=psum[b][:], func=AF.Sqrt, bias=c_eps, scale=1.0),
                  reads=["ps%d" % b, "c_eps"], writes=[tag + "rstd"])
            em.op("dve", lambda e: e.reciprocal(out=rstd, in_=rstd), reads=[tag + "rstd"], writes=[tag + "rstd"])
            for c in range(8):
                em.op("dve", lambda e, c=c: e.tensor_tensor(out=tmp[:, c, :], in0=xt[:, c, :], in1=rstd, op=ALU.mult),
                      reads=[xkey, tag + "rstd"], writes=[tag + "tmp%d" % c])
                if shift is not None:
                    em.op("act", lambda e, c=c: e.activation(out=out_bf[:, c, :], in_=tmp[:, c, :], func=AF.Identity,
                                                             bias=shift[:, c:c + 1], scale=gsc[:, c:c + 1]),
                          reads=[tag + "tmp%d" % c, "mod"], writes=[outkey])
                else:
                    em.op("act", lambda e, c=c: e.activation(out=out_bf[:, c, :], in_=tmp[:, c, :], func=AF.Identity,
                                                             bias=0.0, scale=gsc[:, c:c + 1]),
                          reads=[tag + "tmp%d" % c, "mod"], writes=[outkey])

        fm = lambda ap: ap.rearrange("(c p) t -> p c t", p=128)
        tv = lambda ap, j: ap[j].rearrange("p (c t) -> p c t", t=TN)
        cx_tv = tv
        cx = CX()
        cx.tv = tv
        cx.nc = nc; cx.em = em; cx.ar = ar; cx.psum = psum; cx.pbank = pbank; cx.U = U; cx.Y = Y; cx.ZM = ZM
        cx.c_kap = c_kap; cx.c_nk = c_nk; cx.c_eps = c_eps; cx.c_gneps = c_gneps
        cx.PRM_N = PRM_N; cx.ROW_N = ROW_N; cx.prm = prm_d; cx.rows = rows_d
        cx.lr64_d = lr64_d; cx.lr32_d = lr32_d; cx.lr16_d = lr16_d; cx.pool_w = pool_w_d
        cx.ident_d = ident_d; cx.identb_d = identb_d; cx.maskc_d = maskc_d; cx.invc = invc_d; cx.csm = csm_d; cx.CL = CL_d; cx.SL = SL_d
        cx.sr0 = sr0_d; cx.sg0 = sg0_d; cx.osr = osr_d; cx.osg = osg_d

        for l in range(NL):
            last = (l == NL - 1)
            em.barrier(); ar.reset()
            em.dma("sp", lambda e, l=l: e.dma_start(out=c_adab, in_=ada_b[l]), "c_adab", writes=["c_adab"])
            em.dma("sp", lambda e, l=l: e.dma_start(out=c_n1g, in_=n1g[l]), "c_n1g", writes=["c_n1g"])
            em.dma("sp", lambda e, l=l: e.dma_start(out=c_n2g, in_=n2g[l]), "c_n2g", writes=["c_n2g"])
            awt = [ar.alloc([8, 768], F32) for _ in range(2)]
            aw_v = ada_w[l].rearrange("(c p) n -> p c n", p=128)
            pb = pbank()
            for g in range(8):
                wt = awt[g % 2]; key = "awt%d" % (g % 2)
                em.dma("sp", lambda e, g=g, wt=wt, aw_v=aw_v: e.dma_start(out=wt, in_=aw_v[:, :, g * 768:(g + 1) * 768]), key, writes=[key])
                for m in range(6):
                    col = g * 6 + m
                    for kc in range(8):
                        em.op("pe", lambda e, wt=wt, m=m, kc=kc, col=col, pb=pb: e.matmul(
                            psum[pb][:, col:col + 1], lhsT=wt[:, kc, m * 128:(m + 1) * 128], rhs=c_silu[:, kc:kc + 1],
                            start=(kc == 0), stop=(kc == 7)), reads=[key, "c_silu"], writes=["ps%d" % pb])
            em.op("dve", lambda e, pb=pb: e.tensor_tensor(out=c_mod, in0=psum[pb][:, 0:48], in1=c_adab, op=ALU.add),
                  reads=["ps%d" % pb, "c_adab"], writes=["mod"])
            em.op("dve", lambda e: e.scalar_tensor_tensor(out=c_gsc1, in0=c_mod[:, 8:16], scalar=1.0, in1=c_n1g, op0=ALU.add, op1=ALU.mult),
                  reads=["mod", "c_n1g"], writes=["mod"])
            em.op("dve", lambda e: e.scalar_tensor_tensor(out=c_gsc2, in0=c_mod[:, 32:40], scalar=1.0, in1=c_n2g, op0=ALU.add, op1=ALU.mult),
                  reads=["mod", "c_n2g"], writes=["mod"])
            if DBG and l == 0:
                dbg_mod = nc.dram_tensor("dbg_mod", [128, 64], F32, kind="ExternalOutput").ap()
                em.dma("sp", lambda e: e.dma_start(out=dbg_mod, in_=cst_h[:, 8:72]), "dbgmod", reads=["mod"], final=True)
            sh1 = c_mod[:, 0:8]; g1 = c_mod[:, 16:24]; sh2 = c_mod[:, 24:32]; g2 = c_mod[:, 40:48]

            em.barrier(); ar.reset()
            hT = ar.alloc([8, T], BF16)
            xb = [ar.alloc([8, TN], F32) for _ in range(2)]
            pb_ = [ar.alloc([8, TN], F32) for _ in range(2)]
            tmp = ar.alloc([8, TN], F32); xsq = ar.alloc([8, TN], BF16); rstd = ar.alloc([TN], F32)
            for j in range(NT):
                xt = xb[j % 2]; xk = "xb%d" % (j % 2)
                sl = slice(j * TN, (j + 1) * TN)
                if l == 0:
                    pt = pb_[j % 2]; pk = "pb%d" % (j % 2)
                    em.dma("sp", lambda e, xt=xt, sl=sl: e.dma_start(out=xt, in_=fm(xT)[:, :, sl]), xk, writes=[xk])
                    em.dma("sp", lambda e, pt=pt, sl=sl: e.dma_start(out=pt, in_=fm(pos)[:, :, sl]), pk, writes=[pk])
                    em.op("pool", lambda e, xt=xt, pt=pt: e.tensor_tensor(out=xt, in0=xt, in1=pt, op=ALU.add), reads=[xk, pk], writes=[xk])
                    em.dma("sp", lambda e, xt=xt, j=j: e.dma_start(out=tv(xres, j), in_=xt), xk + "w", reads=[xk], writes=["xres%d" % j])
                else:
                    em.dma("sp", lambda e, xt=xt, j=j: e.dma_start(out=xt, in_=tv(xres, j)), xk, reads=["xres%d" % j], writes=[xk])
                rmsnorm_tile(xt, xk, c_gsc1, sh1, hT[:, :, sl], "hT%d" % j, tmp, xsq, rstd, "n1")

            chunks = [(m * 128, 128, "U", m) for m in range(18)] + [(2304, 32, "U", 18)] + \
                     [(NMIX + m * 128, 128, "G", m) for m in range(32)]
            wbuf = [ar.alloc([8, 512], BF16) for _ in range(2)]
            stg = [ar.alloc([TN], F32) for _ in range(4)]
            stb = [ar.alloc([TN], BF16) for _ in range(4)]
            win_v = w_in[l].rearrange("(c p) n -> p c n", p=128)
            groups = []
            cur = []
            for ch in chunks:
                if cur and (len(cur) == 4 or cur[-1][2] != ch[2] or cur[-1][1] != 128):
                    groups.append(cur); cur = []
                cur.append(ch)
            groups.append(cur)
            sctr = 0
            for gi, grp in enumerate(groups):
                wt = wbuf[gi % 2]; wk = "wbuf%d" % (gi % 2)
                c0 = grp[0][0]; ncol = sum(c[1] for c in grp)
                em.dma("pool", lambda e, wt=wt, c0=c0, ncol=ncol, win_v=win_v: e.dma_start(out=wt[:, :, 0:ncol], in_=win_v[:, :, c0:c0 + ncol]), wk, writes=[wk])
                for j in range(NT):
                    sl = slice(j * TN, (j + 1) * TN)
                    for (cs, cn, kind, mi) in grp:
                        b = pbank(); o = cs - c0
                        for kc in range(8):
                            em.op("pe", lambda e, wt=wt, o=o, cn=cn, kc=kc, b=b, sl=sl: e.matmul(
                                psum[b][0:cn, :], lhsT=wt[:, kc, o:o + cn], rhs=hT[:, kc, sl], start=(kc == 0), stop=(kc == 7)),
                                reads=[wk, "hT%d" % j], writes=["ps%d" % b])
                        si = sctr % 4; sctr += 1
                        if kind == "U":
                            s_ = stg[si]; sk = "stg%d" % si
                            em.op("dve", lambda e, s_=s_, b=b, cn=cn: e.tensor_copy(out=s_[0:cn, :], in_=psum[b][0:cn, :]), reads=["ps%d" % b], writes=[sk])
                            em.dma("sp", lambda e, s_=s_, mi=mi, cn=cn, sl=sl: e.dma_start(out=U[mi * 128:mi * 128 + cn, sl], in_=s_[0:cn, :]),
                                   sk + "w", reads=[sk], writes=["U%d_%d" % (mi, j)])
                        else:
                            s_ = stb[si]; sk = "stb%d" % si
                            em.op("act", lambda e, s_=s_, b=b: e.activation(out=s_, in_=psum[b][:], func=AF.Sigmoid), reads=["ps%d" % b], writes=[sk])
                            em.dma("sp", lambda e, s_=s_, mi=mi, j=j: e.dma_start(out=tv(G, j)[:, mi, :], in_=s_),
                                   sk + "w", reads=[sk], writes=["G%d" % j])

            em.barrier(); ar.reset()
            mixers(cx, l)

            em.barrier(); ar.reset()
            wbr = ar.alloc([8, D], BF16); wo = ar.alloc([8, D], BF16)
            em.dma("pool", lambda e, l=l: e.dma_start(out=wbr, in_=w_br[l].rearrange("(c p) n -> p c n", p=128)), "wbr", writes=["wbr"])
            em.dma("pool", lambda e, l=l: e.dma_start(out=wo, in_=w_out[l].rearrange("(c p) n -> p c n", p=128)), "wo", writes=["wo"])
            Yt = [ar.alloc([8, TN], BF16) for _ in range(2)]
            Gt = [ar.alloc([32, TN], BF16) for _ in range(2)]
            xb = [ar.alloc([8, TN], F32) for _ in range(2)]
            mgs = [ar.alloc([8, TN], BF16) for _ in range(2)]
            accs = [ar.alloc([TN], F32) for _ in range(2)]; tms = [ar.alloc([TN], F32) for _ in range(2)]
            tmp = ar.alloc([8, TN], F32); xsq = ar.alloc([8, TN], BF16); rstd = ar.alloc([TN], F32)
            h2 = [ar.alloc([8, TN], BF16) for _ in range(1)]
            for j in range(NT):
                sl = slice(j * TN, (j + 1) * TN)
                yt = Yt[j % 2]; gt = Gt[j % 2]; xt = xb[j % 2]; hh = h2[0]
                yk = "Yt%d" % (j % 2); gk = "Gt%d" % (j % 2); xk = "x3_%d" % (j % 2); hk = "h2_0"
                em.dma("sp", lambda e, yt=yt, j=j: e.dma_start(out=yt, in_=tv(Y, j)), yk, reads=["Y%d" % j], writes=[yk])
                em.dma("sp", lambda e, gt=gt, j=j: e.dma_start(out=gt, in_=tv(G, j)), gk, reads=["G%d" % j], writes=[gk])
                em.dma("sp", lambda e, xt=xt, j=j: e.dma_start(out=xt, in_=tv(xres, j)), xk, reads=["xres%d" % j], writes=[xk])
                mg = mgs[j % 2]; mgk = "mg%d" % (j % 2)
                for dc in range(8):
                    acc = accs[dc % 2]; acck = "acc%d" % (dc % 2)
                    for i in range(4):
                        b = pbank()
                        for cc in range(2):
                            em.op("pe", lambda e, i=i, cc=cc, dc=dc, b=b, yt=yt: e.matmul(
                                psum[b][:], lhsT=wbr[:, i * 2 + cc, dc * 128:(dc + 1) * 128], rhs=yt[:, i * 2 + cc, :],
                                start=(cc == 0), stop=(cc == 1)), reads=["wbr", yk], writes=["ps%d" % b])
                        dst = acc if i == 0 else tms[(i - 1) % 2]
                        dstk = acck if i == 0 else "tm%d" % ((i - 1) % 2)
                        em.op("dve", lambda e, b=b, i=i, dc=dc, gt=gt, dst=dst: e.tensor_tensor(out=dst, in0=psum[b][:], in1=gt[:, i * 8 + dc, :], op=ALU.mult),
                              reads=["ps%d" % b, gk], writes=[dstk])
                        if i > 0:
                            o_ = mg[:, dc, :] if i == 3 else acc
                            em.op("pool", lambda e, o_=o_, acc=acc, dst=dst: e.tensor_tensor(out=o_, in0=acc, in1=dst, op=ALU.add),
                                  reads=[acck, dstk], writes=[mgk if i == 3 else acck])
                for d2 in range(8):
                    b = pbank()
                    for dc in range(8):
                        em.op("pe", lambda e, d2=d2, dc=dc, b=b, mg=mg: e.matmul(psum[b][:], lhsT=wo[:, dc, d2 * 128:(d2 + 1) * 128], rhs=mg[:, dc, :],
                                                                          start=(dc == 0), stop=(dc == 7)), reads=["wo", mgk], writes=["ps%d" % b])
                    em.op("dve", lambda e, d2=d2, b=b, xt=xt: e.scalar_tensor_tensor(out=xt[:, d2, :], in0=psum[b][:], scalar=g1[:, d2:d2 + 1], in1=xt[:, d2, :],
                                                                                   op0=ALU.mult, op1=ALU.add), reads=["ps%d" % b, xk, "mod"], writes=[xk])
                em.dma("sp", lambda e, xt=xt, j=j: e.dma_start(out=tv(xres, j), in_=xt), xk + "w", reads=[xk], writes=["xres%d" % j])
                rmsnorm_tile(xt, xk, c_gsc2, sh2, hh, hk, tmp, xsq, rstd, "n2")
                em.dma("sp", lambda e, hh=hh, j=j: e.dma_start(out=tv(H2, j), in_=hh), hk + "w", reads=[hk], writes=["H2_%d" % j])

            if DBG and l == 0:
                em.barrier()
                dd = nc.dram_tensor("dY", [D, T], BF16, kind="ExternalOutput").ap()
                em.dma("sp", lambda e, dd=dd: e.dma_start(out=dd, in_=Y), "dbgY", final=True)
                dd2 = nc.dram_tensor("dX", [D, T], F32, kind="ExternalOutput").ap()
                em.dma("sp", lambda e, dd2=dd2: e.dma_start(out=dd2, in_=xres), "dbgX", final=True)
            em.barrier(); ar.reset()
            w1s = ar.alloc([8, 4 * D], BF16); w2s = ar.alloc([32, D], BF16)
            for q in range(4):
                em.dma("pool", lambda e, l=l, q=q: e.dma_start(out=w1s[:, :, q * 1024:(q + 1) * 1024],
                                                               in_=w1[l].rearrange("(c p) n -> p c n", p=128)[:, :, q * 1024:(q + 1) * 1024]), "w1s", writes=["w1s"])
                em.dma("pool", lambda e, l=l, q=q: e.dma_start(out=w2s[:, q * 8:(q + 1) * 8, :],
                                                               in_=w2[l].rearrange("(c p) n -> p c n", p=128)[:, q * 8:(q + 1) * 8, :]), "w2s", writes=["w2s"])
            hid_raw = ar.alloc([32 * TN], BF16)
            hid = hid_raw.rearrange("p (a b) -> p a b", b=TN)
            rl = [ar.alloc([TN], F32) for _ in range(2)]
            xb1 = ar.alloc([8, TN], F32); h2t = ar.alloc([8, TN], BF16)
            if last:
                tmp = hid_raw[:, 0:16 * TN].bitcast(F32).rearrange("p (a b) -> p a b", b=TN)
                xsq = ar.alloc([8, TN], BF16); rstd = ar.alloc([TN], F32)
            for j in range(NT):
                sl = slice(j * TN, (j + 1) * TN)
                em.dma("sp", lambda e, j=j: e.dma_start(out=h2t, in_=tv(H2, j)), "h2t", reads=["H2_%d" % j], writes=["h2t"])
                em.dma("sp", lambda e, j=j: e.dma_start(out=xb1, in_=tv(xres, j)), "xb1", reads=["xres%d" % j], writes=["xb1"])
                for fc in range(32):
                    b = pbank()
                    for kc in range(8):
                        em.op("pe", lambda e, fc=fc, kc=kc, b=b: e.matmul(psum[b][:], lhsT=w1s[:, kc, fc * 128:(fc + 1) * 128], rhs=h2t[:, kc, :],
                                                                          start=(kc == 0), stop=(kc == 7)), reads=["w1s", "h2t"], writes=["ps%d" % b])
                    r_ = rl[fc % 2]; rk = "rl%d" % (fc % 2)
                    em.op("act", lambda e, b=b, r_=r_: e.activation(out=r_, in_=psum[b][:], func=AF.Relu), reads=["ps%d" % b], writes=[rk])
                    em.op("pool", lambda e, r_=r_, fc=fc: e.tensor_tensor(out=hid[:, fc, :], in0=r_, in1=r_, op=ALU.mult), reads=[rk], writes=["hid"])
                for d2 in range(8):
                    b = pbank()
                    for fc in range(32):
                        em.op("pe", lambda e, fc=fc, d2=d2, b=b: e.matmul(psum[b][:], lhsT=w2s[:, fc, d2 * 128:(d2 + 1) * 128], rhs=hid[:, fc, :],
                                                                          start=(fc == 0), stop=(fc == 31)), reads=["w2s", "hid"], writes=["ps%d" % b])
                    em.op("dve", lambda e, d2=d2, b=b: e.scalar_tensor_tensor(out=xb1[:, d2, :], in0=psum[b][:], scalar=g2[:, d2:d2 + 1], in1=xb1[:, d2, :],
                                                                            op0=ALU.mult, op1=ALU.add), reads=["ps%d" % b, "xb1", "mod"], writes=["xb1"])
                if not last:
                    em.dma("sp", lambda e, j=j: e.dma_start(out=tv(xres, j), in_=xb1), "xb1w", reads=["xb1"], writes=["xres%d" % j])
                else:
                    em.op("act", lambda e: e.activation(out=xsq, in_=xb1, func=AF.Square), reads=["xb1"], writes=["fxsq"])
                    b = pbank()
                    for c in range(8):
                        em.op("pe", lambda e, c=c, b=b: e.matmul(psum[b][:], lhsT=onesb[:], rhs=xsq[:, c, :], start=(c == 0), stop=(c == 7)),
                              reads=["fxsq", "onesb"], writes=["ps%d" % b])
                    em.op("act", lambda e, b=b: e.activation(out=rstd, in_=psum[b][:], func=AF.Sqrt, bias=c_eps, scale=1.0),
                          reads=["ps%d" % b, "c_eps"], writes=["frstd"])
                    em.op("dve", lambda e: e.reciprocal(out=rstd, in_=rstd), reads=["frstd"], writes=["frstd"])
                    for c in range(8):
                        em.op("dve", lambda e, c=c: e.scalar_tensor_tensor(out=tmp[:, c, :], in0=xb1[:, c, :], scalar=c_fg[:, c:c + 1], in1=rstd,
                                                                         op0=ALU.mult, op1=ALU.mult), reads=["xb1", "frstd", "c_fg"], writes=["hid"])
                    em.dma("sp", lambda e, sl=sl: e.dma_start(out=fm(yT)[:, :, sl], in_=tmp), "yT_w", reads=["hid"], writes=["yT%d" % j], final=True)
        em.build()
        global _EM
        _EM = em
    return nc


class CX:
    pass


def mixers(cx, l):
    em, ar, psum, pbank, U, Y = cx.em, cx.ar, cx.psum, cx.pbank, cx.U, cx.Y
    c_kap, c_nk, c_eps = cx.c_kap, cx.c_nk, cx.c_eps

    ar.reset()
    c = mk_consts(cx, l, 'p')
    p_psc = c.psc

    zp = ar.alloc([2, 16, 272], F32)
    Wa = ar.alloc([16, 272], F32); Wb = ar.alloc([16, 272], F32)
    pooled = ar.alloc([2, T], F32)
    invc = ar.alloc([2, T], F32)
    PW = ar.alloc([2, 128], F32)
    yst = [ar.alloc([TN], BF16) for _ in range(2)]
    em.op("pool", lambda e: e.memset(zp, 0.0), writes=["zp"])
    em.op("pool", lambda e: e.memset(PW, 0.0), writes=["PW"])
    for ct in range(2):
        em.dma("sp", lambda e, ct=ct: e.dma_start(out=invc[:, ct, :], in_=cx.invc[ct]), "invc", writes=["invc"])
        Uv = U[ct * 128:(ct + 1) * 128, :].rearrange("p (b s) -> p b s", s=256)
        em.dma("sp", lambda e, ct=ct, Uv=Uv: e.dma_start(out=zp[:, ct, :, 8:264], in_=Uv), "zp", writes=["zp"])
        em.dma("sp", lambda e, ct=ct, Uv=Uv: e.dma_start(out=zp[:, ct, 1:16, 0:8], in_=Uv[:, 0:15, 248:256]), "zp", writes=["zp"])
        em.dma("sp", lambda e, ct=ct, Uv=Uv: e.dma_start(out=zp[:, ct, 0:15, 264:272], in_=Uv[:, 1:16, 0:8]), "zp", writes=["zp"])
    for g in range(4):
        r0 = (g % 2) * 64
        em.dma("sp", lambda e, g=g, r0=r0, l=l: e.dma_start(out=PW[r0:r0 + 64, g // 2, r0:r0 + 64], in_=cx.pool_w[l, g]), "PW", writes=["PW"])
    for ct in range(2):
        for (a, b) in ((0, 8), (264, 272)):
            em.op("dve", lambda e, ct=ct, a=a, b=b: e.tensor_scalar(out=zp[:, ct, :, a:b], in0=zp[:, ct, :, a:b], scalar1=c_kap, scalar2=None, op0=ALU.mult),
                  reads=["zp", "c_kap"], writes=["zp"])
    pv = lambda ct: pooled[:, ct, :].rearrange("p (b s) -> p b s", s=256)
    iv = lambda ct: invc[:, ct, :].rearrange("p (b s) -> p b s", s=256)

    def take(W, wk, ct, r0):
        em.op("dve", lambda e: e.tensor_tensor(out=pv(ct)[r0:r0 + 64], in0=W[r0:r0 + 64, :, 8:264], in1=iv(ct)[r0:r0 + 64], op=ALU.mult),
              reads=[wk, "invc"], writes=["pooled"])
        em.op("dve", lambda e: e.tensor_tensor(out=pv(ct)[r0:r0 + 64], in0=pv(ct)[r0:r0 + 64], in1=zp[r0:r0 + 64, ct, :, 8:264], op=ALU.subtract),
              reads=["pooled", "zp"], writes=["pooled"])

    for ct in range(2):
        Z = zp[:, ct]
        em.op("dve", lambda e, Z=Z: e.tensor_tensor(out=Wa[:, :, 1:272], in0=Z[:, :, 0:271], in1=Z[:, :, 1:272], op=ALU.add), reads=["zp"], writes=["Wa"])
        if ct == 0:
            take(Wa, "Wa", 0, 0)
        em.op("dve", lambda e: e.tensor_tensor(out=Wb[:, :, 2:271], in0=Wa[:, :, 1:270], in1=Wa[:, :, 3:272], op=ALU.add), reads=["Wa"], writes=["Wb"])
        if ct == 0:
            take(Wb, "Wb", 0, 64)
        else:
            em.op("dve", lambda e: e.tensor_tensor(out=Wa[:, :, 4:269], in0=Wb[:, :, 2:267], in1=Wb[:, :, 6:271], op=ALU.add), reads=["Wb"], writes=["Wa"])
            take(Wa, "Wa", 1, 0)
            em.op("dve", lambda e: e.tensor_tensor(out=Wb[:, :, 8:265], in0=Wa[:, :, 4:261], in1=Wa[:, :, 12:269], op=ALU.add), reads=["Wa"], writes=["Wb"])
            take(Wb, "Wb", 1, 64)
    n = 0
    for j in range(NT):
        sl = slice(j * TN, (j + 1) * TN)
        for ct in range(2):
            b = pbank(); i = n % 2; n += 1
            em.op("pe", lambda e, b=b, ct=ct, sl=sl: e.matmul(psum[b][:], lhsT=PW[:, ct, :], rhs=pooled[:, ct, sl], start=True, stop=True),
                  reads=["PW", "pooled"], writes=["ps%d" % b])
            em.op("act", lambda e, b=b, ct=ct, i=i: e.activation(out=yst[i], in_=psum[b][:], func=AF.Identity, bias=0.0, scale=p_psc[:, ct:ct + 1]),
                  reads=["ps%d" % b], writes=["yst%d" % i])
            em.dma("sp", lambda e, ct=ct, j=j, i=i: e.dma_start(out=cx.tv(Y, j)[:, ct, :], in_=yst[i]), "yst%dw" % i, reads=["yst%d" % i])

    em.barrier(); ar.reset()
    zf = ar.alloc([2, T], BF16)
    csm = ar.alloc([2, 512], BF16)
    zcs = ar.alloc([32, 512], BF16)
    CLb = [ar.alloc([16, 512], BF16) for _ in range(2)]
    SLb = [ar.alloc([16, 512], BF16) for _ in range(2)]
    fst = [ar.alloc([TN], BF16) for _ in range(2)]
    for ct in range(2):
        em.dma("pool", lambda e, ct=ct: e.dma_start(out=zf[:, ct, :], in_=U[256 + ct * 128:256 + (ct + 1) * 128, :]), "zf", writes=["zf"])
        em.dma("pool", lambda e, ct=ct: e.dma_start(out=csm[:, ct, :], in_=cx.csm[ct]), "csm", writes=["csm"])
    for tc in range(32):
        b = pbank()
        for ct in range(2):
            em.op("pe", lambda e, b=b, ct=ct, tc=tc: e.matmul(psum[b][:], lhsT=zf[:, ct, tc * 128:(tc + 1) * 128], rhs=csm[:, ct, :], start=(ct == 0), stop=(ct == 1)),
                  reads=["zf", "csm"], writes=["ps%d" % b])
        if tc % 2 == 0:
            em.op("act", lambda e, b=b, tc=tc: e.activation(out=zcs[:, tc, :], in_=psum[b][:], func=AF.Identity), reads=["ps%d" % b], writes=["zcs"])
        else:
            em.op("dve", lambda e, b=b, tc=tc: e.tensor_copy(out=zcs[:, tc, :], in_=psum[b][:]), reads=["ps%d" % b], writes=["zcs"])
    CLv = lambda ft, th: cx.CL[ft, th].rearrange("p (a f) -> p a f", f=512)
    SLv = lambda ft, th: cx.SL[ft, th].rearrange("p (a f) -> p a f", f=512)
    n = 0
    for ft in range(8):
        fsl = slice(ft * 512, (ft + 1) * 512)
        bb = [pbank(), pbank()]
        for th in range(2):
            i = n % 2; n += 1
            em.dma("sp", lambda e, i=i, th=th, ft=ft: e.dma_start(out=CLb[i], in_=CLv(ft, th)), "CLb%d" % i, writes=["CLb%d" % i])
            em.dma("sp", lambda e, i=i, th=th, ft=ft: e.dma_start(out=SLb[i], in_=SLv(ft, th)), "SLb%d" % i, writes=["SLb%d" % i])
            for half in range(2):
                b = bb[half]
                for tcl in range(16):
                    tc = th * 16 + tcl
                    em.op("pe", lambda e, b=b, i=i, tc=tc, tcl=tcl, half=half, th=th: e.matmul(
                        psum[b][:], lhsT=zcs[:, tc, half * 128:(half + 1) * 128], rhs=CLb[i][:, tcl, :], start=(th == 0 and tcl == 0), stop=False),
                        reads=["zcs", "CLb%d" % i], writes=["ps%d" % b])
                    em.op("pe", lambda e, b=b, i=i, tc=tc, tcl=tcl, half=half, th=th: e.matmul(
                        psum[b][:], lhsT=zcs[:, tc, 256 + half * 128:256 + (half + 1) * 128], rhs=SLb[i][:, tcl, :], start=False, stop=(th == 1 and tcl == 15)),
                        reads=["zcs", "SLb%d" % i], writes=["ps%d" % b])
        for half in range(2):
            b = bb[half]
            em.op("act", lambda e, b=b, half=half: e.activation(out=fst[half], in_=psum[b][:], func=AF.Identity), reads=["ps%d" % b], writes=["fst%d" % half])
            em.dma("sp", lambda e, half=half, ft=ft: e.dma_start(out=cx.tv(Y, ft)[:, 2 + half, :], in_=fst[half]), "fst%dw" % half, reads=["fst%d" % half])

    em.barrier(); ar.reset()
    c = mk_consts(cx, l, "m")
    rwkv_prepass(cx, l, c)
    recs = []
    for fn, banks in ((mix_gla, [0, 1, 2]), (mix_rwkv, [3, 4, 5, 6, 7])):
        rec = Rec()
        sub = CX(); sub.__dict__.update(cx.__dict__)
        sub.em = rec
        st_ = [0]

        def pb(banks=banks, st_=st_):
            st_[0] = (st_[0] + 1) % len(banks)
            return banks[st_[0]]
        sub.pbank = pb
        fn(sub, l, c)
        recs.append(rec)
    merge_streams(em, recs)


def mk_consts(cx, l, tag):
    em, ar = cx.em, cx.ar
    c = CX()
    c.ident = ar.alloc([128], F32)
    c.ident4 = ar.alloc([512], F32)
    c.maskc = ar.alloc([8, 512], F32)
    c.prm = ar.alloc([cx.PRM_N], F32)
    c.rows = ar.alloc([cx.ROW_N], F32)
    c.lr64 = ar.alloc([768], F32); c.lr32 = ar.alloc([512], F32); c.lr16 = ar.alloc([256], F32)
    c.ones_row = ar.alloc([128], F32); c.ones_col = ar.alloc([4], F32)
    c.identb = ar.alloc([128], BF16)
    c.ident4b = ar.alloc([512], BF16)
    for h in range(4):
        em.dma("sp", lambda e, h=h: e.dma_start(out=c.ident4b[:, h * 128:(h + 1) * 128], in_=cx.identb_d), "K" + tag + "i4b", writes=["K" + tag + "i4b%d" % h])
    em.dma("sp", lambda e: e.dma_start(out=c.identb, in_=cx.identb_d), "K" + tag + "ib", writes=["K" + tag + "ib"])
    k = "K" + tag
    em.dma("sp", lambda e: e.dma_start(out=c.ident, in_=cx.ident_d), k + "a", writes=[k])
    for h in range(4):
        em.dma("sp", lambda e, h=h: e.dma_start(out=c.ident4[:, h * 128:(h + 1) * 128], in_=cx.ident_d), k + "b", writes=[k + "i4%d" % h])
    em.dma("sp", lambda e: e.dma_start(out=c.maskc, in_=cx.maskc_d), k + "c", writes=[k + "m"])
    em.dma("sp", lambda e, l=l: e.dma_start(out=c.prm, in_=cx.prm[l]), k + "d", writes=[k + "p"])
    em.dma("sp", lambda e, l=l: e.dma_start(out=c.rows[0:1, :], in_=cx.rows[l]), k + "e", writes=[k + "r"])
    em.dma("sp", lambda e, l=l: e.dma_start(out=c.lr64[0:64, :], in_=cx.lr64_d[l]), k + "f", writes=[k + "l64"])
    em.dma("sp", lambda e, l=l: e.dma_start(out=c.lr32[0:32, :], in_=cx.lr32_d[l]), k + "g", writes=[k + "l32"])
    em.dma("sp", lambda e, l=l: e.dma_start(out=c.lr16[0:16, :], in_=cx.lr16_d[l]), k + "h", writes=[k + "l16"])
    em.op("dve", lambda e: e.memset(c.ones_row, 1.0), writes=[k + "o1"])
    em.op("dve", lambda e: e.memset(c.ones_col, 1.0), writes=[k + "o2"])
    BC0 = 32
    p = c.prm
    c.psc = p[:, 0:2]; c.mu = p[:, 2:8]; c.hm = p[:, 8:14]; c.om = p[:, 14:20]
    c.kkw = p[:, BC0:BC0 + 256]; c.ka = p[:, BC0 + 256:BC0 + 512]; c.oka = p[:, BC0 + 512:BC0 + 768]
    c.rk = p[:, BC0 + 768:BC0 + 1024]; c.gn = p[:, BC0 + 1024:BC0 + 1280]; c.gln = p[:, BC0 + 1280:BC0 + 1536]
    em.op("dve", lambda e: e.tensor_scalar(out=c.hm, in0=c.mu, scalar1=0.5, scalar2=None, op0=ALU.mult), reads=[k + "p"], writes=[k + "p"])
    em.op("dve", lambda e: e.tensor_scalar(out=c.om, in0=c.mu, scalar1=-1.0, scalar2=1.0, op0=ALU.mult, op1=ALU.add), reads=[k + "p"], writes=[k + "p"])
    em.op("dve", lambda e: e.tensor_scalar(out=c.oka, in0=c.ka, scalar1=-1.0, scalar2=1.0, op0=ALU.mult, op1=ALU.add), reads=[k + "p"], writes=[k + "p"])
    em.barrier()
    return c


def mix_gla(cx, l, c):
    em, ar, psum, pbank, U, Y = cx.em, cx.ar, cx.psum, cx.pbank, cx.U, cx.Y
    S = [ar.alloc([4, 64], F32) for _ in range(2)]
    OG = ar.alloc([32, 256], F32)
    fmt = [ar.alloc([6, 128], F32) for _ in range(2)]
    cal = [ar.alloc([128], F32) for _ in range(2)]
    tm = [ar.alloc([512], F32) for _ in range(2)]
    go = ar.alloc([256], F32)
    ee = ar.alloc([128], F32); lg = ar.alloc([128], F32)
    Eq = ar.alloc([128], F32); Ek = ar.alloc([128], F32); Eb = ar.alloc([128], F32)
    qt = ar.alloc([128], BF16); kh = ar.alloc([128], BF16); kb = ar.alloc([128], BF16)
    qT = ar.alloc([512], BF16); kT = ar.alloc([512], BF16)
    AT = ar.alloc([512], BF16)
    vb = ar.alloc([256], BF16); Sb = [ar.alloc([4, 64], BF16) for _ in range(2)]
    etot = ar.alloc([4], F32)
    og = ar.alloc([256], F32); sq = ar.alloc([256], F32); ss = ar.alloc([4], F32); sgo = ar.alloc([256], F32)
    yd = ar.alloc([256], F32); ydb = ar.alloc([256], BF16); ydT = [ar.alloc([2, 128], BF16) for _ in range(2)]
    Ufm = U[1536:2304, :].rearrange("(c p) t -> p c t", p=128)
    Yv = lambda tc: cx.tv(Y, tc // 4)[:, 6:8, (tc % 4) * 128:(tc % 4 + 1) * 128]
    ab = lambda d: c.lr16[0:16, d * 128:(d + 1) * 128]
    abrow = lambda d: c.rows[0:1, 1024 + d * 128:1024 + (d + 1) * 128]
    for d in range(2):
        order = list(range(32)) if d == 0 else list(range(31, -1, -1))
        Sd = S[d]; Sk = "gS%d" % d
        for n, tc in enumerate(order):
            par = n % 2
            tsl = slice(tc * 128, (tc + 1) * 128)
            f_ = fmt[par]; fk = "gfmt%d" % par; ca = cal[par]; ck = "gcal%d" % par; t_ = tm[par]; tk = "gtm%d" % par
            em.dma("sp", lambda e, f_=f_, tsl=tsl: e.dma_start(out=f_, in_=Ufm[:, :, tsl]), fk, writes=[fk])
            em.dma("sp", lambda e, ca=ca, tsl=tsl, d=d: e.dma_start(out=ca[0:16, :], in_=U[2304 + 16 * d:2304 + 16 * d + 16, tsl]), ck, writes=[ck])
            if n == 0:
                em.dma("sp", lambda e, Sd=Sd, d=d, l=l: e.dma_start(out=Sd[0:32], in_=cx.sg0[l, d]), Sk + "i", writes=[Sk])
            elif n % 2 == 0:
                em.op("dve", lambda e, Sd=Sd: e.tensor_scalar(out=Sd[0:32], in0=Sd[0:32], scalar1=cx.c_kap[0:32], scalar2=None, op0=ALU.mult),
                      reads=[Sk], writes=[Sk])
            b = pbank()
            for cc in range(4):
                em.op("pe", lambda e, b=b, cc=cc, f_=f_: e.matmul(psum[b][:, cc * 128:(cc + 1) * 128], lhsT=f_[:, cc, :], rhs=c.ident, start=True, stop=True),
                      reads=[fk], writes=["ps%d" % b])
            em.op("act", lambda e, b=b, t_=t_: e.activation(out=t_, in_=psum[b][:], func=AF.Identity), reads=["ps%d" % b], writes=[tk])
            em.op("dve", lambda e, b=b: e.tensor_copy(out=vb, in_=psum[b][:, 256:512]), reads=["ps%d" % b], writes=["gvb"])
            Sb_ = Sb[d]; Sbk = "gSb%d" % d
            em.op("act", lambda e, Sd=Sd, Sb_=Sb_: e.activation(out=Sb_[0:32], in_=Sd[0:32], func=AF.Identity), reads=[Sk], writes=[Sbk])
            if d == 1:
                b = pbank()
                for cc in range(2):
                    em.op("pe", lambda e, b=b, cc=cc, f_=f_: e.matmul(psum[b][:, cc * 128:(cc + 1) * 128], lhsT=f_[:, 4 + cc, :], rhs=c.ident, start=True, stop=True),
                          reads=[fk], writes=["ps%d" % b])
                em.op("act", lambda e, b=b: e.activation(out=sgo, in_=psum[b][:, 0:256], func=AF.Silu), reads=["ps%d" % b], writes=["gsgo"])
            b = pbank()
            em.op("pe", lambda e, b=b, ca=ca, d=d: e.matmul(psum[b][:, 0:128], lhsT=ca[0:16, :], rhs=ab(d), start=True, stop=False), reads=[ck], writes=["ps%d" % b])
            em.op("pe", lambda e, b=b, d=d: e.matmul(psum[b][:, 0:128], lhsT=c.ones_row[0:1, :], rhs=abrow(d), start=False, stop=True), reads=[], writes=["ps%d" % b])
            em.op("act", lambda e, b=b: e.activation(out=ee, in_=psum[b][:, 0:128], func=AF.Exp, scale=-1.0), reads=["ps%d" % b], writes=["gee"])
            em.op("act", lambda e: e.activation(out=lg, in_=ee, func=AF.Ln, bias=1.0, scale=1.0), reads=["gee"], writes=["glg"])
            b = pbank()
            em.op("pe", lambda e, b=b, d=d: e.matmul(psum[b][:, 0:128], lhsT=c.maskc[:, d * 4 + 0, 0:128], rhs=lg, start=True, stop=True), reads=["glg"], writes=["ps%d" % b])
            em.op("pe", lambda e, b=b, d=d: e.matmul(psum[b][:, 128:256], lhsT=c.maskc[:, d * 4 + 3, 0:128], rhs=lg, start=True, stop=True), reads=["glg"], writes=["ps%d" % b])
            em.op("act", lambda e, b=b: e.activation(out=Eq, in_=psum[b][:, 0:128], func=AF.Exp, scale=-1.0 / 16), reads=["ps%d" % b], writes=["gEq"])
            em.op("act", lambda e, b=b: e.activation(out=Ek, in_=psum[b][:, 0:128], func=AF.Exp, scale=1.0 / 16), reads=["ps%d" % b], writes=["gEk"])
            em.op("act", lambda e, b=b: e.activation(out=Eb, in_=psum[b][:, 128:256], func=AF.Exp, scale=-1.0 / 16), reads=["ps%d" % b], writes=["gEb"])
            em.op("dve", lambda e, t_=t_: e.scalar_tensor_tensor(out=qt, in0=t_[:, 0:128], scalar=32 ** -0.5, in1=Eq, op0=ALU.mult, op1=ALU.mult), reads=[tk, "gEq"], writes=["gqt"])
            em.op("dve", lambda e, t_=t_: e.tensor_tensor(out=kh, in0=t_[:, 128:256], in1=Ek, op=ALU.mult), reads=[tk, "gEk"], writes=["gkh"])
            em.op("pool", lambda e, t_=t_: e.tensor_tensor(out=kb, in0=t_[:, 128:256], in1=Eb, op=ALU.mult), reads=[tk, "gEb"], writes=["gkb"])
            b = pbank()
            for h in range(4):
                em.op("pe", lambda e, b=b, h=h: e.matmul(psum[b][0:32, h * 128:(h + 1) * 128], lhsT=qt[:, h * 32:(h + 1) * 32], rhs=c.identb, start=True, stop=True),
                      reads=["gqt"], writes=["ps%d" % b])
            em.op("act", lambda e, b=b: e.activation(out=qT[0:32], in_=psum[b][0:32], func=AF.Identity), reads=["ps%d" % b], writes=["gqT"])
            b = pbank()
            for h in range(4):
                em.op("pe", lambda e, b=b, h=h: e.matmul(psum[b][0:32, h * 128:(h + 1) * 128], lhsT=kh[:, h * 32:(h + 1) * 32], rhs=c.identb, start=True, stop=True),
                      reads=["gkh"], writes=["ps%d" % b])
            em.op("dve", lambda e, b=b: e.tensor_copy(out=kT[0:32], in_=psum[b][0:32]), reads=["ps%d" % b], writes=["gkT"])
            b = pbank()
            for h in range(4):
                em.op("pe", lambda e, b=b, h=h: e.matmul(psum[b][:, h * 128:(h + 1) * 128], lhsT=kT[0:32, h * 128:(h + 1) * 128], rhs=qT[0:32, h * 128:(h + 1) * 128], start=True, stop=True),
                      reads=["gkT", "gqT"], writes=["ps%d" % b])
            em.op("dve", lambda e, b=b, d=d: e.tensor_tensor(out=AT, in0=psum[b][:], in1=c.maskc[:, d * 4 + 0, :], op=ALU.mult), reads=["ps%d" % b], writes=["gAT"])
            b = pbank()
            for h in range(4):
                em.op("pe", lambda e, b=b, h=h: e.matmul(psum[b][0:32, h:h + 1], lhsT=lg[:, h * 32:(h + 1) * 32], rhs=c.ones_col[:, 0:1], start=True, stop=True),
                      reads=["glg"], writes=["ps%d" % b])
            em.op("act", lambda e, b=b: e.activation(out=etot[0:32], in_=psum[b][0:32, 0:4], func=AF.Exp, scale=-1.0 / 16), reads=["ps%d" % b], writes=["getot"])
            bO = pbank()
            for h in range(4):
                em.op("pe", lambda e, bO=bO, h=h: e.matmul(psum[bO][:, h * 64:(h + 1) * 64], lhsT=AT[:, h * 128:(h + 1) * 128], rhs=vb[:, h * 64:(h + 1) * 64], start=True, stop=False),
                      reads=["gAT", "gvb"], writes=["ps%d" % bO])
                em.op("pe", lambda e, bO=bO, h=h, Sb_=Sb_: e.matmul(psum[bO][:, h * 64:(h + 1) * 64], lhsT=qT[0:32, h * 128:(h + 1) * 128], rhs=Sb_[0:32, h, :], start=False, stop=True),
                      reads=["gqT", Sbk], writes=["ps%d" % bO])
            bS = pbank()
            for h in range(4):
                em.op("pe", lambda e, bS=bS, h=h: e.matmul(psum[bS][0:32, h * 64:(h + 1) * 64], lhsT=kb[:, h * 32:(h + 1) * 32], rhs=vb[:, h * 64:(h + 1) * 64], start=True, stop=True),
                      reads=["gkb", "gvb"], writes=["ps%d" % bS])
            for h in range(4):
                em.op("dve", lambda e, bS=bS, h=h, Sd=Sd: e.scalar_tensor_tensor(out=Sd[0:32, h, :], in0=Sd[0:32, h, :], scalar=etot[0:32, h:h + 1], in1=psum[bS][0:32, h * 64:(h + 1) * 64],
                                                                            op0=ALU.mult, op1=ALU.add), reads=[Sk, "getot", "ps%d" % bS], writes=[Sk])
            if n % 2 == 1:
                blk = tc // 2
                em.dma("sp", lambda e, Sd=Sd, d=d, l=l, blk=blk: e.dma_start(out=cx.osg[l, d, blk], in_=Sd[0:32]), Sk + "o", reads=[Sk], final=True)
            if d == 0:
                em.op("act", lambda e, bO=bO, tc=tc: e.activation(out=OG[:, tc, :], in_=psum[bO][:, 0:256], func=AF.Identity), reads=["ps%d" % bO], writes=["gOG"])
            else:
                em.op("dve", lambda e, bO=bO, tc=tc: e.tensor_tensor(out=og, in0=psum[bO][:, 0:256], in1=OG[:, tc, :], op=ALU.add), reads=["ps%d" % bO, "gOG"], writes=["gog"])
                em.op("pool", lambda e: e.tensor_tensor(out=sq, in0=og, in1=og, op=ALU.mult), reads=["gog"], writes=["gsq"])
                em.op("dve", lambda e: e.tensor_reduce(out=ss, in_=sq.rearrange("p (h v) -> p h v", v=64), axis=AX.X, op=ALU.add), reads=["gsq"], writes=["gss"])
                em.op("act", lambda e: e.activation(out=ss, in_=ss, func=AF.Sqrt, bias=cx.c_eps, scale=1.0 / 64), reads=["gss"], writes=["gss"])
                em.op("dve", lambda e: e.reciprocal(out=ss, in_=ss), reads=["gss"], writes=["gss"])
                em.op("dve", lambda e: e.tensor_tensor(out=yd.rearrange("p (h v) -> p h v", v=64), in0=og.rearrange("p (h v) -> p h v", v=64),
                                                       in1=ss.unsqueeze(2).to_broadcast([128, 4, 64]), op=ALU.mult), reads=["gog", "gss"], writes=["gyd"])
                em.op("pool", lambda e: e.tensor_tensor(out=yd, in0=yd, in1=c.gln, op=ALU.mult), reads=["gyd"], writes=["gyd"])
                em.op("pool", lambda e: e.tensor_tensor(out=ydb, in0=yd, in1=sgo, op=ALU.mult), reads=["gyd", "gsgo"], writes=["gydb"])
                b = pbank()
                for cc in range(2):
                    em.op("pe", lambda e, b=b, cc=cc: e.matmul(psum[b][:, cc * 128:(cc + 1) * 128], lhsT=ydb[:, cc * 128:(cc + 1) * 128], rhs=c.identb, start=True, stop=True),
                          reads=["gydb"], writes=["ps%d" % b])
                yT_ = ydT[par]; yk = "gydT%d" % par
                em.op("act", lambda e, b=b, yT_=yT_: e.activation(out=yT_, in_=psum[b][:, 0:256].rearrange("p (c t) -> p c t", t=128), func=AF.Identity), reads=["ps%d" % b], writes=[yk])
                em.dma("sp", lambda e, yT_=yT_, tc=tc: e.dma_start(out=Yv(tc), in_=yT_), yk + "w", reads=[yk])


def rwkv_prepass(cx, l, c):
    em, ar, psum, pbank, U, Y = cx.em, cx.ar, cx.psum, cx.pbank, cx.U, cx.Y
    ZM = cx.ZM
    mark = ar.off
    zt = [ar.alloc([6, 514], F32) for _ in range(2)]
    sh = ar.alloc([6, 512], F32)
    zo = [ar.alloc([6, 512], F32) for _ in range(2)]
    Uz = U[512:1280, :].rearrange("(c p) t -> p c t", p=128)
    ZMv = ZM.rearrange("(c p) t -> p c t", p=128)
    for j in range(NT):
        z = zt[j % 2]; zk = "rzt%d" % (j % 2); o_ = zo[j % 2]; ok = "rzo%d" % (j % 2)
        t0 = j * 512
        if j == 0:
            em.op("pool", lambda e, z=z: e.memset(z[:, :, 0:1], 0.0), writes=[zk])
            em.dma("sp", lambda e, z=z: e.dma_start(out=z[:, :, 1:514], in_=Uz[:, :, 0:513]), zk, writes=[zk])
        elif j == NT - 1:
            em.op("pool", lambda e, z=z: e.memset(z[:, :, 513:514], 0.0), writes=[zk])
            em.dma("sp", lambda e, z=z, t0=t0: e.dma_start(out=z[:, :, 0:513], in_=Uz[:, :, t0 - 1:t0 + 512]), zk, writes=[zk])
        else:
            em.dma("sp", lambda e, z=z, t0=t0: e.dma_start(out=z, in_=Uz[:, :, t0 - 1:t0 + 513]), zk, writes=[zk])
        em.op("dve", lambda e, z=z: e.tensor_tensor(out=sh, in0=z[:, :, 0:512], in1=z[:, :, 2:514], op=ALU.add), reads=[zk], writes=["rsh"])
        em.op("dve", lambda e, z=z: e.scalar_tensor_tensor(out=sh[:, :, 0:512:256], in0=z[:, :, 0:512:256], scalar=cx.c_nk, in1=sh[:, :, 0:512:256], op0=ALU.mult, op1=ALU.add),
              reads=[zk, "rsh"], writes=["rsh"])
        em.op("dve", lambda e, z=z: e.scalar_tensor_tensor(out=sh[:, :, 255:512:256], in0=z[:, :, 257:514:256], scalar=cx.c_nk, in1=sh[:, :, 255:512:256], op0=ALU.mult, op1=ALU.add),
              reads=[zk, "rsh"], writes=["rsh"])
        for cc in range(6):
            em.op("pool", lambda e, cc=cc: e.tensor_scalar(out=sh[:, cc, :], in0=sh[:, cc, :], scalar1=c.hm[:, cc:cc + 1], scalar2=None, op0=ALU.mult), reads=["rsh"], writes=["rsh"])
            em.op("dve", lambda e, cc=cc, z=z, o_=o_: e.scalar_tensor_tensor(out=o_[:, cc, :], in0=z[:, cc, 1:513], scalar=c.om[:, cc:cc + 1], in1=sh[:, cc, :], op0=ALU.mult, op1=ALU.add),
                  reads=[zk, "rsh"], writes=[ok])
        em.dma("sp", lambda e, o_=o_, t0=t0: e.dma_start(out=ZMv[:, :, t0:t0 + 512], in_=o_), ok + "w", reads=[ok])
    em.barrier()
    ar.off = mark


def mix_rwkv(cx, l, c):
    em, ar, psum, pbank, U, Y = cx.em, cx.ar, cx.psum, cx.pbank, cx.U, cx.Y
    CD = 0.606531
    ZM = cx.ZM
    ZMv = ZM.rearrange("(c p) t -> p c t", p=128)
    A = lambda shp: ar.alloc(shp, F32)
    H = [A([4, 64]) for _ in range(2)]
    OR = A([32, 256]); BON = A([32, 4])
    zc = [A([6, 128]) for _ in range(2)]
    cw = [A([128]) for _ in range(2)]; ca = [A([128]) for _ in range(2)]; cg = [A([128]) for _ in range(2)]
    rk_ = A([512]); v_ = A([256])
    tcw = A([128]); sg = A([256]); aa = A([256])
    kw = A([256]); sq = A([256]); ss = A([4]); kk = A([256]); t1 = A([256]); kdir = A([256]); bb = A([256])
    Ex = A([256]); E1 = A([256]); E2 = A([256]); E3 = A([256]); E4 = A([256])
    kkt = A([256]); bh = A([256]); kh = A([256]); rt = A([256]); Kbar = A([256]); Bbn = A([256]); t2 = A([256]); bsum = A([4])
    kktT = A([512]); bhT = A([512]); khT = A([512]); rtT = A([512])
    Mp = [A([512]) for _ in range(2)]; Np = [A([512]) for _ in range(2)]; X = A([512])
    AkkT = A([512]); ArkT = A([512]); ArbT = A([512])
    WT = A([512]); Y0 = A([256]); U0 = A([256]); Uu = A([256]); pc = A([4])
    o = A([256]); s1 = A([4]); cen = A([256]); s2 = A([4]); bon = A([4]); sgc = A([128]); yc = A([256])
    ycT = [ar.alloc([2, 128], BF16) for _ in range(2)]
    Yv = lambda tc: cx.tv(Y, tc // 4)[:, 4:6, (tc % 4) * 128:(tc % 4 + 1) * 128]
    bw = lambda d: c.lr64[0:64, d * 256:(d + 1) * 256]
    bg = c.lr64[0:64, 512:768]
    ba = lambda d: c.lr32[0:32, d * 256:(d + 1) * 256]
    w0row = lambda d: c.rows[0:1, d * 256:(d + 1) * 256]
    a0row = lambda d: c.rows[0:1, 512 + d * 256:512 + (d + 1) * 256]
    h3 = lambda ap: ap.rearrange("p (h v) -> p h v", v=64)
    bc4 = lambda ap4: ap4.unsqueeze(2).to_broadcast([128, 4, 64])

    def ew(eng, fn, reads, writes):
        em.op(eng, fn, reads=reads, writes=writes)

    def mm4(b, lhs, rhs, reads, npart=128, width=128):
        for h in range(4):
            em.op("pe", lambda e, h=h: e.matmul(psum[b][0:npart, h * width:(h + 1) * width], lhsT=lhs(h), rhs=rhs(h), start=True, stop=True),
                  reads=reads, writes=["ps%d" % b])

    blk128 = lambda t: (lambda h: t[:, h * 128:(h + 1) * 128])
    blk64 = lambda t: (lambda h: t[:, h * 64:(h + 1) * 64])
    blkT = lambda t: (lambda h: t[0:64, h * 128:(h + 1) * 128])

    for d in range(2):
        order = list(range(32)) if d == 0 else list(range(31, -1, -1))
        Hd = H[d]; Hk = "rH%d" % d
        for n, tc in enumerate(order):
            par = n % 2
            tsl = slice(tc * 128, (tc + 1) * 128)
            z = zc[par]; zk = "rzc%d" % par; cw_ = cw[par]; cwk = "rcw%d" % par; ca_ = ca[par]; cak = "rca%d" % par; cg_ = cg[par]; cgk = "rcg%d" % par
            em.dma("sp", lambda e, z=z, tsl=tsl: e.dma_start(out=z, in_=ZMv[:, :, tsl]), zk, writes=[zk])
            em.dma("sp", lambda e, cw_=cw_, tsl=tsl, d=d: e.dma_start(out=cw_[0:64, :], in_=U[1280 + 64 * d:1280 + 64 * d + 64, tsl]), cwk, writes=[cwk])
            em.dma("sp", lambda e, ca_=ca_, tsl=tsl, d=d: e.dma_start(out=ca_[0:32, :], in_=U[1408 + 32 * d:1408 + 32 * d + 32, tsl]), cak, writes=[cak])
            if d == 1:
                em.dma("sp", lambda e, cg_=cg_, tsl=tsl: e.dma_start(out=cg_[0:64, :], in_=U[1472:1536, tsl]), cgk, writes=[cgk])
            if n == 0:
                em.dma("sp", lambda e, Hd=Hd, d=d, l=l: e.dma_start(out=Hd[0:64], in_=cx.sr0[l, d]), Hk + "i", writes=[Hk])
            elif n % 2 == 0:
                ew("dve", lambda e, Hd=Hd: e.tensor_scalar(out=Hd[0:64], in0=Hd[0:64], scalar1=cx.c_kap[0:64], scalar2=None, op0=ALU.mult), [Hk], [Hk])
            b = pbank()
            for cc in range(4):
                em.op("pe", lambda e, b=b, cc=cc, z=z: e.matmul(psum[b][:, cc * 128:(cc + 1) * 128], lhsT=z[:, cc, :], rhs=c.ident, start=True, stop=True), reads=[zk], writes=["ps%d" % b])
            ew("act", lambda e, b=b: e.activation(out=rk_, in_=psum[b][:], func=AF.Identity), ["ps%d" % b], ["rrk"])
            b = pbank()
            for cc in range(2):
                em.op("pe", lambda e, b=b, cc=cc, z=z: e.matmul(psum[b][:, cc * 128:(cc + 1) * 128], lhsT=z[:, 4 + cc, :], rhs=c.ident, start=True, stop=True), reads=[zk], writes=["ps%d" % b])
            ew("act", lambda e, b=b: e.activation(out=v_, in_=psum[b][:, 0:256], func=AF.Identity), ["ps%d" % b], ["rv"])
            r_ = rk_[:, 0:256]; k_ = rk_[:, 256:512]
            ew("act", lambda e, cw_=cw_: e.activation(out=tcw[0:64], in_=cw_[0:64], func=AF.Tanh), [cwk], ["rtcw"])
            b = pbank()
            em.op("pe", lambda e, b=b, d=d: e.matmul(psum[b][:, 0:256], lhsT=tcw[0:64, :], rhs=bw(d), start=True, stop=False), reads=["rtcw"], writes=["ps%d" % b])
            em.op("pe", lambda e, b=b, d=d: e.matmul(psum[b][:, 0:256], lhsT=c.ones_row[0:1, :], rhs=w0row(d), start=False, stop=True), reads=[], writes=["ps%d" % b])
            em.op("pe", lambda e, b=b, d=d, ca_=ca_: e.matmul(psum[b][:, 256:512], lhsT=ca_[0:32, :], rhs=ba(d), start=True, stop=False), reads=[cak], writes=["ps%d" % b])
            em.op("pe", lambda e, b=b, d=d: e.matmul(psum[b][:, 256:512], lhsT=c.ones_row[0:1, :], rhs=a0row(d), start=False, stop=True), reads=[], writes=["ps%d" % b])
            ew("act", lambda e, b=b: e.activation(out=sg, in_=psum[b][:, 0:256], func=AF.Sigmoid), ["ps%d" % b], ["rsg"])
            ew("act", lambda e, b=b: e.activation(out=aa, in_=psum[b][:, 256:512], func=AF.Sigmoid), ["ps%d" % b], ["raa"])
            ew("dve", lambda e: e.tensor_tensor(out=kw, in0=k_, in1=c.kkw, op=ALU.mult), ["rrk"], ["rkw"])
            ew("pool", lambda e: e.tensor_tensor(out=sq, in0=kw, in1=kw, op=ALU.mult), ["rkw"], ["rsq"])
            ew("dve", lambda e: e.tensor_reduce(out=ss, in_=h3(sq), axis=AX.X, op=ALU.add), ["rsq"], ["rss"])
            ew("act", lambda e: e.activation(out=ss, in_=ss, func=AF.Sqrt, bias=cx.c_eps, scale=1.0), ["rss"], ["rss"])
            ew("dve", lambda e: e.reciprocal(out=ss, in_=ss), ["rss"], ["rss"])
            ew("dve", lambda e: e.tensor_tensor(out=h3(kk), in0=h3(kw), in1=bc4(ss), op=ALU.mult), ["rkw", "rss"], ["rkk"])
            ew("pool", lambda e: e.tensor_tensor(out=t1, in0=aa, in1=c.ka, op=ALU.mult), ["raa"], ["rt1"])
            ew("pool", lambda e: e.tensor_tensor(out=t1, in0=t1, in1=c.oka, op=ALU.add), ["rt1"], ["rt1"])
            ew("pool", lambda e: e.tensor_tensor(out=kdir, in0=k_, in1=t1, op=ALU.mult), ["rrk", "rt1"], ["rkdir"])
            ew("dve", lambda e: e.tensor_tensor(out=bb, in0=kk, in1=aa, op=ALU.mult), ["rkk", "raa"], ["rbb"])
            b = pbank()
            em.op("pe", lambda e, b=b, d=d: e.matmul(psum[b][:, 0:256], lhsT=c.maskc[:, d * 4 + 0, 0:128], rhs=sg, start=True, stop=True), reads=["rsg"], writes=["ps%d" % b])
            em.op("pe", lambda e, b=b, d=d: e.matmul(psum[b][:, 256:512], lhsT=c.maskc[:, d * 4 + 3, 0:128], rhs=sg, start=True, stop=True), reads=["rsg"], writes=["ps%d" % b])
            ew("dve", lambda e, b=b: e.tensor_tensor(out=Ex, in0=psum[b][:, 0:256], in1=sg, op=ALU.subtract), ["ps%d" % b, "rsg"], ["rEx"])
            ew("act", lambda e: e.activation(out=E1, in_=Ex, func=AF.Exp, scale=-CD), ["rEx"], ["rE1"])
            ew("act", lambda e, b=b: e.activation(out=E2, in_=psum[b][:, 0:256], func=AF.Exp, scale=CD), ["ps%d" % b], ["rE2"])
            ew("act", lambda e, b=b: e.activation(out=E3, in_=psum[b][:, 0:256], func=AF.Exp, scale=-CD), ["ps%d" % b], ["rE3"])
            ew("act", lambda e, b=b: e.activation(out=E4, in_=psum[b][:, 256:512], func=AF.Exp, scale=-CD), ["ps%d" % b], ["rE4"])
            ew("dve", lambda e: e.tensor_tensor(out=kkt, in0=kk, in1=E1, op=ALU.mult), ["rkk", "rE1"], ["rkkt"])
            ew("pool", lambda e: e.tensor_tensor(out=bh, in0=bb, in1=E2, op=ALU.mult), ["rbb", "rE2"], ["rbh"])
            ew("dve", lambda e: e.tensor_tensor(out=kh, in0=kdir, in1=E2, op=ALU.mult), ["rkdir", "rE2"], ["rkh"])
            ew("pool", lambda e: e.tensor_tensor(out=rt, in0=r_, in1=E3, op=ALU.mult), ["rrk", "rE3"], ["rrt"])
            ew("dve", lambda e: e.tensor_tensor(out=Kbar, in0=kdir, in1=E4, op=ALU.mult), ["rkdir", "rE4"], ["rKbar"])
            ew("dve", lambda e: e.scalar_tensor_tensor(out=Bbn, in0=bb, scalar=-1.0, in1=E4, op0=ALU.mult, op1=ALU.mult), ["rbb", "rE4"], ["rBbn"])
            ew("pool", lambda e: e.tensor_tensor(out=t2, in0=r_, in1=kdir, op=ALU.mult), ["rrk", "rkdir"], ["rt2"])
            ew("pool", lambda e: e.tensor_tensor(out=t2, in0=t2, in1=c.rk, op=ALU.mult), ["rt2"], ["rt2"])
            if d == 0:
                ew("dve", lambda e, tc=tc: e.tensor_reduce(out=BON[:, tc, :], in_=h3(t2), axis=AX.X, op=ALU.add), ["rt2"], ["rBON"])
            else:
                ew("dve", lambda e: e.tensor_reduce(out=bsum, in_=h3(t2), axis=AX.X, op=ALU.add), ["rt2"], ["rbsum"])
            for ti, (src, sk, dst, dk_) in enumerate(((kkt, "rkkt", kktT, "rkktT"), (bh, "rbh", bhT, "rbhT"), (kh, "rkh", khT, "rkhT"), (rt, "rrt", rtT, "rrtT"))):
                b = pbank()
                mm4(b, blk64(src), lambda h: c.ident, [sk], npart=64)
                if ti % 2 == 0:
                    ew("act", lambda e, b=b, dst=dst: e.activation(out=dst[0:64], in_=psum[b][0:64], func=AF.Identity), ["ps%d" % b], [dk_])
                else:
                    ew("dve", lambda e, b=b, dst=dst: e.tensor_copy(out=dst[0:64], in_=psum[b][0:64]), ["ps%d" % b], [dk_])
            def amat(dst, dkey, lhsT_t, lk, rhs_t, rkey, kind, d=d):
                b = pbank()
                mm4(b, blkT(lhsT_t), blkT(rhs_t), [lk, rkey])
                ew("dve", lambda e, b=b: e.tensor_tensor(out=dst, in0=psum[b][:], in1=c.maskc[:, d * 4 + kind, :], op=ALU.mult), ["ps%d" % b], [dkey])
            amat(Mp[0], "rMp0", bhT, "rbhT", kktT, "rkktT", 2)
            amat(Np[0], "rNp0", kktT, "rkktT", bhT, "rbhT", 3)
            amat(AkkT, "rAkkT", khT, "rkhT", kktT, "rkktT", 2)
            amat(ArkT, "rArkT", khT, "rkhT", rtT, "rrtT", 0)
            amat(ArbT, "rArbT", bhT, "rbhT", rtT, "rrtT", 1)
            ew("pool", lambda e: e.tensor_tensor(out=X, in0=c.ident4, in1=Mp[0], op=ALU.subtract), ["rMp0"], ["rX"])
            cur = 0
            for jj in range(1, 7):
                nxt = 1 - cur
                if jj < 6:
                    b = pbank()
                    mm4(b, blk128(Np[cur]), blk128(Mp[cur]), ["rNp%d" % cur, "rMp%d" % cur])
                    ew("act", lambda e, b=b, nxt=nxt: e.activation(out=Mp[nxt], in_=psum[b][:], func=AF.Identity), ["ps%d" % b], ["rMp%d" % nxt])
                b = pbank()
                mm4(b, blk128(Mp[cur]), blk128(Np[cur]), ["rNp%d" % cur, "rMp%d" % cur])
                ew("dve", lambda e, b=b, nxt=nxt: e.tensor_copy(out=Np[nxt], in_=psum[b][:]), ["ps%d" % b], ["rNp%d" % nxt])
                b = pbank()
                mm4(b, blk128(Np[nxt]), blk128(X), ["rNp%d" % nxt, "rX"])
                ew("dve", lambda e, b=b: e.tensor_tensor(out=X, in0=X, in1=psum[b][:], op=ALU.add), ["ps%d" % b, "rX"], ["rX"])
                cur = nxt
            b = pbank()
            mm4(b, blk64(kkt), blk128(X), ["rkkt", "rX"], npart=64)
            ew("act", lambda e, b=b: e.activation(out=WT[0:64], in_=psum[b][0:64], func=AF.Identity), ["ps%d" % b], ["rWT"])
            b = pbank()
            mm4(b, blk128(AkkT), blk64(v_), ["rAkkT", "rv"], width=64)
            ew("act", lambda e, b=b: e.activation(out=Y0, in_=psum[b][:, 0:256], func=AF.Identity), ["ps%d" % b], ["rY0"])
            b = pbank()
            mm4(b, blk128(X), blk64(Y0), ["rX", "rY0"], width=64)
            ew("dve", lambda e, b=b: e.tensor_copy(out=U0, in_=psum[b][:, 0:256]), ["ps%d" % b], ["rU0"])
            b = pbank()
            mm4(b, blk64(sg), lambda h: c.ones_col[:, 0:1], ["rsg"], npart=64, width=1)
            ew("act", lambda e, b=b: e.activation(out=pc[0:64], in_=psum[b][0:64, 0:4], func=AF.Exp, scale=-CD), ["ps%d" % b], ["rpc"])
            b = pbank()
            mm4(b, blkT(WT), lambda h, Hd=Hd: Hd[0:64, h, :], ["rWT", Hk], width=64)
            ew("dve", lambda e, b=b: e.tensor_tensor(out=Uu, in0=psum[b][:, 0:256], in1=U0, op=ALU.add), ["ps%d" % b, "rU0"], ["rUu"])
            bO = pbank()
            for h in range(4):
                osl = slice(h * 64, (h + 1) * 64)
                em.op("pe", lambda e, h=h, osl=osl, Hd=Hd, bO=bO: e.matmul(psum[bO][:, osl], lhsT=rtT[0:64, h * 128:(h + 1) * 128], rhs=Hd[0:64, h, :], start=True, stop=False), reads=["rrtT", Hk], writes=["ps%d" % bO])
                em.op("pe", lambda e, h=h, osl=osl, bO=bO: e.matmul(psum[bO][:, osl], lhsT=ArkT[:, h * 128:(h + 1) * 128], rhs=v_[:, osl], start=False, stop=False), reads=["rArkT", "rv"], writes=["ps%d" % bO])
                em.op("pe", lambda e, h=h, osl=osl, bO=bO: e.matmul(psum[bO][:, osl], lhsT=ArbT[:, h * 128:(h + 1) * 128], rhs=Uu[:, osl], start=False, stop=True), reads=["rArbT", "rUu"], writes=["ps%d" % bO])
            bH = pbank()
            for h in range(4):
                osl = slice(h * 64, (h + 1) * 64)
                em.op("pe", lambda e, h=h, osl=osl, bH=bH: e.matmul(psum[bH][0:64, osl], lhsT=Kbar[:, osl], rhs=v_[:, osl], start=True, stop=False), reads=["rKbar", "rv"], writes=["ps%d" % bH])
                em.op("pe", lambda e, h=h, osl=osl, bH=bH: e.matmul(psum[bH][0:64, osl], lhsT=Bbn[:, osl], rhs=Uu[:, osl], start=False, stop=True), reads=["rBbn", "rUu"], writes=["ps%d" % bH])
            for h in range(4):
                ew("dve", lambda e, h=h, Hd=Hd, bH=bH: e.scalar_tensor_tensor(out=Hd[0:64, h, :], in0=Hd[0:64, h, :], scalar=pc[0:64, h:h + 1], in1=psum[bH][0:64, h * 64:(h + 1) * 64], op0=ALU.mult, op1=ALU.add),
                   [Hk, "rpc", "ps%d" % bH], [Hk])
            if n % 2 == 1:
                blk = tc // 2
                em.dma("sp", lambda e, Hd=Hd, d=d, l=l, blk=blk: e.dma_start(out=cx.osr[l, d, blk], in_=Hd[0:64]), Hk + "o", reads=[Hk], final=True)
            if d == 0:
                ew("act", lambda e, tc=tc, bO=bO: e.activation(out=OR[:, tc, :], in_=psum[bO][:, 0:256], func=AF.Identity), ["ps%d" % bO], ["rOR"])
                continue
            ew("dve", lambda e, tc=tc, bO=bO: e.tensor_tensor(out=o, in0=psum[bO][:, 0:256], in1=OR[:, tc, :], op=ALU.add), ["ps%d" % bO, "rOR"], ["ro"])
            ew("dve", lambda e: e.tensor_reduce(out=s1, in_=h3(o), axis=AX.X, op=ALU.add), ["ro"], ["rs1"])
            ew("dve", lambda e: e.tensor_scalar(out=s1, in0=s1, scalar1=1.0 / 64, scalar2=None, op0=ALU.mult), ["rs1"], ["rs1"])
            ew("dve", lambda e: e.tensor_tensor(out=h3(cen), in0=h3(o), in1=bc4(s1), op=ALU.subtract), ["ro", "rs1"], ["rcen"])
            ew("pool", lambda e: e.tensor_tensor(out=sq, in0=cen, in1=cen, op=ALU.mult), ["rcen"], ["rsq"])
            ew("dve", lambda e: e.tensor_reduce(out=s2, in_=h3(sq), axis=AX.X, op=ALU.add), ["rsq"], ["rs2"])
            ew("act", lambda e: e.activation(out=s2, in_=s2, func=AF.Sqrt, bias=cx.c_gneps, scale=1.0 / 64), ["rs2"], ["rs2"])
            ew("dve", lambda e: e.reciprocal(out=s2, in_=s2), ["rs2"], ["rs2"])
            ew("dve", lambda e: e.tensor_tensor(out=h3(cen), in0=h3(cen), in1=bc4(s2), op=ALU.mult), ["rcen", "rs2"], ["rcen"])
            ew("pool", lambda e: e.tensor_tensor(out=cen, in0=cen, in1=c.gn, op=ALU.mult), ["rcen"], ["rcen"])
            ew("dve", lambda e, tc=tc: e.tensor_tensor(out=bon, in0=BON[:, tc, :], in1=bsum, op=ALU.add), ["rBON", "rbsum"], ["rbon"])
            ew("dve", lambda e: e.tensor_tensor(out=h3(t2), in0=h3(v_), in1=bc4(bon), op=ALU.mult), ["rv", "rbon", "rt2"], ["rt2"])
            ew("pool", lambda e: e.tensor_tensor(out=cen, in0=cen, in1=t2, op=ALU.add), ["rcen", "rt2"], ["rcen"])
            ew("act", lambda e, cg_=cg_: e.activation(out=sgc[0:64], in_=cg_[0:64], func=AF.Sigmoid), [cgk], ["rsgc"])
            b = pbank()
            em.op("pe", lambda e, b=b: e.matmul(psum[b][:, 0:256], lhsT=sgc[0:64, :], rhs=bg, start=True, stop=True), reads=["rsgc"], writes=["ps%d" % b])
            ew("dve", lambda e, b=b: e.tensor_tensor(out=yc, in0=psum[b][:, 0:256], in1=cen, op=ALU.mult), ["ps%d" % b, "rcen"], ["ryc"])
            b = pbank()
            for cc in range(2):
                em.op("pe", lambda e, b=b, cc=cc: e.matmul(psum[b][:, cc * 128:(cc + 1) * 128], lhsT=yc[:, cc * 128:(cc + 1) * 128], rhs=c.ident, start=True, stop=True), reads=["ryc"], writes=["ps%d" % b])
            yT_ = ycT[par]; yk = "rycT%d" % par
            ew("act", lambda e, b=b, yT_=yT_: e.activation(out=yT_, in_=psum[b][:, 0:256].rearrange("p (c t) -> p c t", t=128), func=AF.Identity), ["ps%d" % b], [yk])
            em.dma("sp", lambda e, yT_=yT_, tc=tc: e.dma_start(out=Yv(tc), in_=yT_), yk + "w", reads=[yk])


_NC = None
_EM = None
_LAST = None


def kernel(**inputs):
    global _NC
    f = lambda k: np.ascontiguousarray(np.asarray(inputs[k], dtype=np.float32))
    x_prompt = f("x_prompt"); x_sample = f("x_sample"); c = f("c"); c_ctx = f("c_ctx")
    jobs = []
    ntok = x_sample.shape[1]
    rows = ntok // 64
    rr, cc = np.meshgrid(np.arange(rows, dtype=np.float32), np.arange(64, dtype=np.float32), indexing="ij")
    rr = rr.reshape(-1); cc = cc.reshape(-1)
    quarter = D // 4
    omega = (1.0 / (np.float32(10000.0) ** (np.arange(quarter, dtype=np.float32) / np.float32(quarter)))).astype(np.float32)
    arr = rr[:, None] * omega; acc = cc[:, None] * omega
    pos = np.concatenate([np.sin(arr), np.cos(arr), np.sin(acc), np.cos(acc)], axis=-1).astype(np.float32)
    lay8 = lambda v: np.ascontiguousarray(v.reshape(-1, 128).T)
    common = {
        "ada_w": f("ada_w"),
        "ada_b": np.ascontiguousarray(f("ada_b").reshape(DEPTH, 48, 128).transpose(0, 2, 1)),
        "n1g": np.ascontiguousarray(f("norm1_g").reshape(DEPTH, 8, 128).transpose(0, 2, 1)),
        "n2g": np.ascontiguousarray(f("norm2_g").reshape(DEPTH, 8, 128).transpose(0, 2, 1)),
        "fg": lay8(f("final_g")),
        "w_in": f("w_in"), "w_br": f("w_branch").reshape(DEPTH, D, D), "w_out": f("w_out"),
        "w1": f("mlp_w1"), "w2": f("mlp_w2"),
    }
    import ml_dtypes
    g = lambda k: f(k)
    def bcast(v):
        return np.broadcast_to(v.reshape(1, -1), (128, v.size))
    prm = np.zeros((DEPTH, 128, 1568), np.float32); rows_ = np.zeros((DEPTH, 1, 1280), np.float32)
    lr64 = np.zeros((DEPTH, 64, 768), np.float32); lr32 = np.zeros((DEPTH, 32, 512), np.float32); lr16 = np.zeros((DEPTH, 16, 256), np.float32)
    for l in range(DEPTH):
        prm[l, :, 0:2] = g("pool_scale")[l].reshape(2, 128).T
        prm[l, :, 2:8] = g("rwkv_mu")[l].reshape(6, 128).T
        for i, nm in enumerate(("rwkv_kk", "rwkv_ka", None, "rwkv_rk", "rwkv_gn", "gla_norm")):
            if nm is not None:
                prm[l, :, 32 + i * 256:32 + (i + 1) * 256] = bcast(g(nm)[l])
        rows_[l, 0, 0:512] = g("rwkv_w0")[l].reshape(-1); rows_[l, 0, 512:1024] = g("rwkv_a0")[l].reshape(-1)
        rows_[l, 0, 1024:1280] = g("gla_abias")[l].reshape(-1)
        lr64[l, :, 0:256] = g("rwkv_bw")[l, 0]; lr64[l, :, 256:512] = g("rwkv_bw")[l, 1]; lr64[l, :, 512:768] = g("rwkv_bg")[l]
        lr32[l, :, 0:256] = g("rwkv_ba")[l, 0]; lr32[l, :, 256:512] = g("rwkv_ba")[l, 1]
        lr16[l, :, 0:128] = g("gla_ab")[l, 0]; lr16[l, :, 128:256] = g("gla_ab")[l, 1]
    ii = np.arange(128)[:, None]; jj = np.arange(128)[None, :]
    maskc = np.zeros((128, 8, 512), np.float32)
    for d_ in range(2):
        incl = (ii <= jj) if d_ == 0 else (ii >= jj)
        strict = (ii < jj) if d_ == 0 else (ii > jj)
        after = (ii > jj) if d_ == 0 else (ii < jj)
        for kind, m in enumerate((incl, incl, strict, after)):
            mm_ = np.tile(m.astype(np.float32), (1, 4))
            maskc[:, d_ * 4 + kind, :] = -mm_ if kind == 1 else mm_
    ch = np.arange(256); gch = ch // 64; cidx = ch % 64
    kk_ = np.arange(64)
    csm = np.zeros((256, 512), np.float64)
    for gi in range(4):
        rws = np.where(gch == gi)[0]
        ang = 2 * np.pi * np.outer(cidx[rws], kk_) / 64.0
        csm[np.ix_(rws, gi * 64 + kk_)] = np.cos(ang) / 8.0
        csm[np.ix_(rws, 256 + gi * 64 + kk_)] = -np.sin(ang) / 8.0
    csm = csm.reshape(2, 128, 512).astype(np.float32)

    def job_consts(L):
        tt = np.arange(T)
        s_ = tt % L
        invc = np.zeros((2, 128, T), np.float32)
        for gi, win in enumerate((2, 4, 8, 16)):
            lo = np.clip(s_ - win // 2, 0, L - 1); hi = np.clip(s_ + (win - win // 2) - 1, 0, L - 1)
            invc[gi // 2, (gi % 2) * 64:(gi % 2) * 64 + 64, :] = (1.0 / (hi - lo + 1))[None, :]
        CLm = np.zeros((T, T), np.float32); SLm = np.zeros((T, T), np.float32)
        sl_ = np.arange(L)
        mmod = np.outer(sl_, sl_) % L
        cb = (np.cos(2 * np.pi * mmod / L) / np.sqrt(L)).astype(np.float32)
        sb_ = (np.sin(2 * np.pi * mmod / L) / np.sqrt(L)).astype(np.float32)
        for b_ in range(T // L):
            CLm[b_ * L:(b_ + 1) * L, b_ * L:(b_ + 1) * L] = cb
            SLm[b_ * L:(b_ + 1) * L, b_ * L:(b_ + 1) * L] = sb_
        def lay(m):
            m5 = m.reshape(2, 16, 128, 8, 512).transpose(3, 0, 2, 1, 4)
            return np.ascontiguousarray(m5).reshape(8, 2, 128, 16 * 512).astype(ml_dtypes.bfloat16)
        return invc, lay(CLm), lay(SLm)
    invc_s, CL_s, SL_s = job_consts(T)
    invc_p, CL_p, SL_p = job_consts(256)
    common.update(prm=prm, rows=rows_, lr64=lr64, lr32=lr32, lr16=lr16, pool_w=g("pool_w"),
                  ident=np.eye(128, dtype=np.float32), identb=np.eye(128, dtype=np.float32).astype(ml_dtypes.bfloat16), maskc=maskc, csm=csm)
    st_r = g("state_rwkv"); st_g = g("state_gla")
    for s in range(2):
        common_s = dict(common, invc=invc_s, CL=CL_s, SL=SL_s,
                        sr0=np.ascontiguousarray(st_r[s].transpose(0, 1, 4, 2, 3)),
                        sg0=np.ascontiguousarray(st_g[s].transpose(0, 1, 3, 2, 4)))
        jobs.append(dict(common_s, xT=np.ascontiguousarray(x_sample[s].T), pos=np.ascontiguousarray(pos.T),
                         cond=lay8(c[s]), kap=np.ones((128, 1), np.float32)))
    xp = x_prompt.reshape(-1, D)
    common = dict(common, invc=invc_p, CL=CL_p, SL=SL_p, sr0=np.zeros((DEPTH, 2, 64, 4, 64), np.float32),
                  sg0=np.zeros((DEPTH, 2, 32, 4, 64), np.float32))
    pj = dict(common, xT=np.ascontiguousarray(xp.T), pos=np.zeros((D, T), np.float32),
              cond=lay8(c_ctx), kap=np.zeros((128, 1), np.float32))
    jobs.append(pj)
    while len(jobs) < N_CORES:
        jobs.append(pj)
    if _NC is None:
        _NC = build_program()
    res = run_bass_kernel_spmd(_NC, jobs, core_ids=list(range(N_CORES)))
    r = res.results
    global _LAST
    _LAST = r
    y_sample = np.stack([np.ascontiguousarray(r[s]["yT"].T) for s in range(2)], axis=0).astype(np.float32)
    y_prompt = np.ascontiguousarray(r[2]["yT"].T).reshape(x_prompt.shape).astype(np.float32)
    B = x_prompt.shape[0]
    nsr = np.ascontiguousarray(np.asarray(r[2]["osr"]).transpose(2, 0, 1, 4, 5, 3)).astype(np.float32)
    nsg = np.ascontiguousarray(np.asarray(r[2]["osg"]).transpose(2, 0, 1, 4, 3, 5)).astype(np.float32)
    return (y_prompt, y_sample, nsr, nsg)
```

```python
import contextlib
import numpy as np
import concourse.bass as bass
import concourse.mybir as mybir
from concourse.bass_utils import run_bass_kernel_spmd

F32 = mybir.dt.float32
BF16 = mybir.dt.bfloat16
ALU = mybir.AluOpType
AF = mybir.ActivationFunctionType
AX = mybir.AxisListType

D = 1024
T = 4096
NT = 8
TN = 512
DEPTH = 2
P_IN = 6432
NMIX = 2336
N_CORES = 8
EPS = 1e-6


class Em:
    ENGS = ("pe", "act", "dve", "pool", "sp")

    def __init__(self, nc):
        self.nc = nc
        self.ops = {e: [] for e in self.ENGS}
        self.cnt = {e: 0 for e in self.ENGS}
        self.seen = {e: {} for e in self.ENGS}
        self.last_w = {}
        self.readers = {}
        self.dma_sems = {}
        self.free_sems = []
        self.sem_names = ["c_" + e for e in self.ENGS]
        self.final_tokens = []
        self.marks = []

    def _deps(self, eng, reads, writes):
        toks = []
        for k in reads:
            t = self.last_w.get(k)
            if t is not None:
                toks.append(t)
        for k in writes:
            t = self.last_w.get(k)
            if t is not None:
                toks.append(t)
            toks.extend(self.readers.get(k, ()))
        seen = self.seen[eng]
        best = {}
        own = "c_" + eng
        for (s, v) in toks:
            if eng == "pe" and s == own:
                continue
            if seen.get(s, 0) < v and best.get(s, 0) < v:
                best[s] = v
        waits = []
        for s, v in best.items():
            seen[s] = v
            waits.append((s, v))
        return waits

    def _commit(self, tok, reads, writes):
        for k in reads:
            self.readers.setdefault(k, []).append(tok)
        for k in writes:
            self.last_w[k] = tok
            self.readers[k] = []

    def op(self, eng, fn, reads=(), writes=()):
        if eng != "pe":
            pr = [k for k in reads if k.startswith("ps")]
            if pr:
                writes = list(writes) + pr
        waits = self._deps(eng, reads, writes)
        self.cnt[eng] += 1
        tok = ("c_" + eng, self.cnt[eng])
        self.ops[eng].append((waits, fn, ("c_" + eng, 1)))
        self._commit(tok, reads, writes)
        return tok

    def dma(self, q, fn, semkey, reads=(), writes=(), final=False):
        if semkey not in self.dma_sems:
            if self.free_sems:
                self.dma_sems[semkey] = self.free_sems.pop()
            else:
                name = "d%d" % (len(self.sem_names) - len(self.ENGS))
                self.dma_sems[semkey] = [name, 0]
                self.sem_names.append(name)
        ent = self.dma_sems[semkey]
        waits = self._deps(q, reads, writes)
        ent[1] += 16
        tok = (ent[0], ent[1])
        self.ops[q].append((waits, fn, (ent[0], 16)))
        self._commit(tok, reads, writes)
        if final:
            self.final_tokens.append(tok)
        return tok

    def barrier(self, label=""):
        self.marks.append((label, dict(self.cnt)))
        targets = [("c_" + e, self.cnt[e]) for e in self.ENGS if self.cnt[e] > 0]
        targets += [(n, c) for (n, c) in self.dma_sems.values() if c > 0]
        for e in self.ENGS:
            waits = []
            for (s, v) in targets:
                if e == "pe" and s == "c_pe":
                    continue
                if self.seen[e].get(s, 0) < v:
                    self.seen[e][s] = v
                    waits.append((s, v))
            if waits:
                self.ops[e].append((waits, None, None))
        self.free_sems.extend(self.dma_sems.values())
        self.dma_sems = {}

    def build(self):
        nc = self.nc
        with contextlib.ExitStack() as st:
            print("Em: %d semaphores, ops:" % len(self.sem_names), {e: len(v) for e, v in self.ops.items()})
            sems = {n: st.enter_context(nc.semaphore(n)) for n in self.sem_names}
            fin = {}
            for (s, v) in self.final_tokens:
                fin[s] = max(fin.get(s, 0), v)
            block = st.enter_context(nc.Block())

            def runner(ename):
                def f(e):
                    for waits, fn, inc in self.ops[ename]:
                        for (s, v) in waits:
                            e.wait_ge(sems[s], v)
                        if fn is not None:
                            fn(e).then_inc(sems[inc[0]], inc[1])
                    if ename == "sp":
                        for s, v in fin.items():
                            e.wait_ge(sems[s], v)
                return f
            block.tensor(runner("pe"))
            block.scalar(runner("act"))
            block.vector(runner("dve"))
            block.gpsimd(runner("pool"))
            block.sync(runner("sp"))


class Rec:
    def __init__(self):
        self.items = []

    def op(self, eng, fn, reads=(), writes=()):
        self.items.append(("op", eng, fn, list(reads), list(writes)))

    def dma(self, q, fn, semkey, reads=(), writes=(), final=False):
        self.items.append(("dma", q, fn, semkey, list(reads), list(writes), final))


def merge_streams(em, recs):
    pos = [0] * len(recs)
    tot = [max(1, len(r.items)) for r in recs]
    while True:
        best = None
        for i, r in enumerate(recs):
            if pos[i] < len(r.items):
                frac = pos[i] / tot[i]
                if best is None or frac < best[0]:
                    best = (frac, i)
        if best is None:
            break
        i = best[1]
        it = recs[i].items[pos[i]]; pos[i] += 1
        if it[0] == "op":
            em.op(it[1], it[2], reads=it[3], writes=it[4])
        else:
            em.dma(it[1], it[2], it[3], reads=it[4], writes=it[5], final=it[6])


class Arena:
    def __init__(self, handle_bf16, nelem):
        self.h = handle_bf16
        self.n = nelem
        self.off = 0

    def reset(self):
        self.off = 0

    def alloc(self, shape_free, dt):
        n = int(np.prod(shape_free))
        nb = n * (2 if dt == F32 else 1)
        nb = (nb + 15) // 16 * 16
        assert self.off + nb <= self.n, ("arena overflow", self.off, nb, self.n)
        ap = self.h[:, self.off:self.off + n * (2 if dt == F32 else 1)]
        self.off += nb
        if dt == F32:
            ap = ap.bitcast(F32)
        if len(shape_free) == 2:
            ap = ap.rearrange("p (a b) -> p a b", b=shape_free[1])
        elif len(shape_free) == 3:
            ap = ap.rearrange("p (a b c) -> p a b c", b=shape_free[1], c=shape_free[2])
        return ap


def build_program():
    nc = bass.Bass("TRN2", target_bir_lowering=False)
    dI = lambda n, sh, dt=F32: nc.dram_tensor(n, sh, dt, kind="ExternalInput").ap()
    DBG = False
    NL = DEPTH
    dS = lambda n, sh, dt=F32: nc.dram_tensor(n, sh, dt, kind="Internal").ap()
    dO = lambda n, sh, dt=F32: nc.dram_tensor(n, sh, dt, kind="ExternalOutput").ap()
    xT = dI("xT", [D, T]); pos = dI("pos", [D, T])
    cond = dI("cond", [128, 8]); kap = dI("kap", [128, 1])
    ada_w = dI("ada_w", [DEPTH, D, 6 * D]); ada_b = dI("ada_b", [DEPTH, 128, 48])
    n1g = dI("n1g", [DEPTH, 128, 8]); n2g = dI("n2g", [DEPTH, 128, 8]); fg = dI("fg", [128, 8])
    w_in = dI("w_in", [DEPTH, D, P_IN]); w_br = dI("w_br", [DEPTH, D, D]); w_out = dI("w_out", [DEPTH, D, D])
    w1 = dI("w1", [DEPTH, D, 4 * D]); w2 = dI("w2", [DEPTH, 4 * D, D])
    yT = dO("yT", [D, T])
    PRM_N = 1568; ROW_N = 1280
    prm_d = dI("prm", [DEPTH, 128, PRM_N]); rows_d = dI("rows", [DEPTH, 1, ROW_N])
    lr64_d = dI("lr64", [DEPTH, 64, 768]); lr32_d = dI("lr32", [DEPTH, 32, 512]); lr16_d = dI("lr16", [DEPTH, 16, 256])
    pool_w_d = dI("pool_w", [DEPTH, 4, 64, 64])
    identb_d = dI("identb", [128, 128], BF16)
    ident_d = dI("ident", [128, 128]); maskc_d = dI("maskc", [128, 8, 512])
    invc_d = dI("invc", [2, 128, T]); csm_d = dI("csm", [2, 128, 512])
    CL_d = dI("CL", [8, 2, 128, 16 * 512], BF16); SL_d = dI("SL", [8, 2, 128, 16 * 512], BF16)
    sr0_d = dI("sr0", [DEPTH, 2, 64, 4, 64]); sg0_d = dI("sg0", [DEPTH, 2, 32, 4, 64])
    osr_d = dO("osr", [DEPTH, 2, 16, 64, 4, 64]); osg_d = dO("osg", [DEPTH, 2, 16, 32, 4, 64])
    ZM = dS("ZM", [768, T])
    xres = dS("xres", [NT, 128, 8 * TN])
    U = dS("U", [19 * 128, T])
    G = dS("G", [NT, 128, 32 * TN], BF16)
    Y = dS("Y", [NT, 128, 8 * TN], BF16)
    H2 = dS("H2", [NT, 128, 8 * TN], BF16)

    with contextlib.ExitStack() as st:
        arena_h = st.enter_context(nc.sbuf_tensor("arena", [128, 104000], BF16))
        cst_h = st.enter_context(nc.sbuf_tensor("cst", [128, 1024], F32))
        onesb = st.enter_context(nc.sbuf_tensor("onesb", [128, 128], BF16))
        psum = [st.enter_context(nc.psum_tensor("ps%d" % i, [128, 512], F32)) for i in range(8)]
        ar = Arena(arena_h, 104000)
        em = Em(nc)
        c_silu = cst_h[:, 0:8]; c_mod = cst_h[:, 8:56]; c_gsc1 = cst_h[:, 56:64]; c_gsc2 = cst_h[:, 64:72]
        c_n1g = cst_h[:, 72:80]; c_n2g = cst_h[:, 80:88]; c_fg = cst_h[:, 88:96]
        c_kap = cst_h[:, 96:97]; c_eps = cst_h[:, 97:98]; c_adab = cst_h[:, 100:148]
        c_nk = cst_h[:, 98:99]; c_gneps = cst_h[:, 99:100]
        em.dma("sp", lambda e: e.dma_start(out=c_silu, in_=cond), "c_silu", writes=["c_silu"])
        em.dma("sp", lambda e: e.dma_start(out=c_kap, in_=kap), "c_kap", writes=["c_kap"])
        em.dma("sp", lambda e: e.dma_start(out=c_fg, in_=fg), "c_fg", writes=["c_fg"])
        em.op("dve", lambda e: e.memset(c_eps, EPS), writes=["c_eps"])
        em.op("dve", lambda e: e.memset(c_gneps, 64e-5), writes=["c_gneps"])
        em.op("dve", lambda e: e.tensor_scalar(out=c_nk, in0=c_kap, scalar1=-1.0, scalar2=None, op0=ALU.add), reads=["c_kap"], writes=["c_nk"])
        em.op("dve", lambda e: e.memset(onesb[:], 1.0 / D), writes=["onesb"])
        em.op("act", lambda e: e.activation(out=c_silu, in_=c_silu, func=AF.Silu), reads=["c_silu"], writes=["c_silu"])

        pctr = [0]

        def pbank():
            pctr[0] = (pctr[0] + 1) % 8
            return pctr[0]

        def rmsnorm_tile(xt, xkey, gsc, shift, out_bf, outkey, tmp, xsq, rstd, tag):
            em.op("act", lambda e: e.activation(out=xsq, in_=xt, func=AF.Square), reads=[xkey], writes=[tag + "xsq"])
            b = pbank()
            for c in range(8):
                em.op("pe", lambda e, c=c, b=b: e.matmul(psum[b][:], lhsT=onesb[:], rhs=xsq[:, c, :], start=(c == 0), stop=(c == 7)),
                      reads=[tag + "xsq", "onesb"], writes=["ps%d" % b])
            em.op("act", lambda e, b=b: e.activation(out=rstd, in_=psum[b][:], func=AF.Sqrt, bias=c_eps, scale=1.0),
                  reads=["ps%d" % b, "c_eps"], writes=[tag + "rstd"])
            em.op("dve", lambda e: e.reciprocal(out=rstd, in_=rstd), reads=[tag + "rstd"], writes=[tag + "rstd"])
            for c in range(8):
                em.op("dve", lambda e, c=c: e.tensor_tensor(out=tmp[:, c, :], in0=xt[:, c, :], in1=rstd, op=ALU.mult),
                      reads=[xkey, tag + "rstd"], writes=[tag + "tmp%d" % c])
                if shift is not None:
                    em.op("act", lambda e, c=c: e.activation(out=out_bf[:, c, :], in_=tmp[:, c, :], func=AF.Identity,
                                                             bias=shift[:, c:c + 1], scale=gsc[:, c:c + 1]),
                          reads=[tag + "tmp%d" % c, "mod"], writes=[outkey])
                else:
                    em.op("act", lambda e, c=c: e.activation(out=out_bf[:, c, :], in_=tmp[:, c, :], func=AF.Identity,
                                                             bias=0.0, scale=gsc[:, c:c + 1]),
                          reads=[tag + "tmp%d" % c, "mod"], writes=[outkey])

        fm = lambda ap: ap.rearrange("(c p) t -> p c t", p=128)
        tv = lambda ap, j: ap[j].rearrange("p (c t) -> p c t", t=TN)
        cx_tv = tv
        cx = CX()
        cx.tv = tv
        cx.nc = nc; cx.em = em; cx.ar = ar; cx.psum = psum; cx.pbank = pbank; cx.U = U; cx.Y = Y; cx.ZM = ZM
        cx.c_kap = c_kap; cx.c_nk = c_nk; cx.c_eps = c_eps; cx.c_gneps = c_gneps
        cx.PRM_N = PRM_N; cx.ROW_N = ROW_N; cx.prm = prm_d; cx.rows = rows_d
        cx.lr64_d = lr64_d; cx.lr32_d = lr32_d; cx.lr16_d = lr16_d; cx.pool_w = pool_w_d
        cx.ident_d = ident_d; cx.identb_d = identb_d; cx.maskc_d = maskc_d; cx.invc = invc_d; cx.csm = csm_d; cx.CL = CL_d; cx.SL = SL_d
        cx.sr0 = sr0_d; cx.sg0 = sg0_d; cx.osr = osr_d; cx.osg = osg_d

        for l in range(NL):
            last = (l == NL - 1)
            em.barrier(); ar.reset()
            em.dma("sp", lambda e, l=l: e.dma_start(out=c_adab, in_=ada_b[l]), "c_adab", writes=["c_adab"])
            em.dma("sp", lambda e, l=l: e.dma_start(out=c_n1g, in_=n1g[l]), "c_n1g", writes=["c_n1g"])
            em.dma("sp", lambda e, l=l: e.dma_start(out=c_n2g, in_=n2g[l]), "c_n2g", writes=["c_n2g"])
            awt = [ar.alloc([8, 768], F32) for _ in range(2)]
            aw_v = ada_w[l].rearrange("(c p) n -> p c n", p=128)
            pb = pbank()
            for g in range(8):
                wt = awt[g % 2]; key = "awt%d" % (g % 2)
                em.dma("sp", lambda e, g=g, wt=wt, aw_v=aw_v: e.dma_start(out=wt, in_=aw_v[:, :, g * 768:(g + 1) * 768]), key, writes=[key])
                for m in range(6):
                    col = g * 6 + m
                    for kc in range(8):
                        em.op("pe", lambda e, wt=wt, m=m, kc=kc, col=col, pb=pb: e.matmul(
                            psum[pb][:, col:col + 1], lhsT=wt[:, kc, m * 128:(m + 1) * 128], rhs=c_silu[:, kc:kc + 1],
                            start=(kc == 0), stop=(kc == 7)), reads=[key, "c_silu"], writes=["ps%d" % pb])
            em.op("dve", lambda e, pb=pb: e.tensor_tensor(out=c_mod, in0=psum[pb][:, 0:48], in1=c_adab, op=ALU.add),
                  reads=["ps%d" % pb, "c_adab"], writes=["mod"])
            em.op("dve", lambda e: e.scalar_tensor_tensor(out=c_gsc1, in0=c_mod[:, 8:16], scalar=1.0, in1=c_n1g, op0=ALU.add, op1=ALU.mult),
                  reads=["mod", "c_n1g"], writes=["mod"])
            em.op("dve", lambda e: e.scalar_tensor_tensor(out=c_gsc2, in0=c_mod[:, 32:40], scalar=1.0, in1=c_n2g, op0=ALU.add, op1=ALU.mult),
                  reads=["mod", "c_n2g"], writes=["mod"])
            if DBG and l == 0:
                dbg_mod = nc.dram_tensor("dbg_mod", [128, 64], F32, kind="ExternalOutput").ap()
                em.dma("sp", lambda e: e.dma_start(out=dbg_mod, in_=cst_h[:, 8:72]), "dbgmod", reads=["mod"], final=True)
            sh1 = c_mod[:, 0:8]; g1 = c_mod[:, 16:24]; sh2 = c_mod[:, 24:32]; g2 = c_mod[:, 40:48]

            em.barrier(); ar.reset()
            hT = ar.alloc([8, T], BF16)
            xb = [ar.alloc([8, TN], F32) for _ in range(2)]
            pb_ = [ar.alloc([8, TN], F32) for _ in range(2)]
            tmp = ar.alloc([8, TN], F32); xsq = ar.alloc([8, TN], BF16); rstd = ar.alloc([TN], F32)
            for j in range(NT):
                xt = xb[j % 2]; xk = "xb%d" % (j % 2)
                sl = slice(j * TN, (j + 1) * TN)
                if l == 0:
                    pt = pb_[j % 2]; pk = "pb%d" % (j % 2)
                    em.dma("sp", lambda e, xt=xt, sl=sl: e.dma_start(out=xt, in_=fm(xT)[:, :, sl]), xk, writes=[xk])
                    em.dma("sp", lambda e, pt=pt, sl=sl: e.dma_start(out=pt, in_=fm(pos)[:, :, sl]), pk, writes=[pk])
                    em.op("pool", lambda e, xt=xt, pt=pt: e.tensor_tensor(out=xt, in0=xt, in1=pt, op=ALU.add), reads=[xk, pk], writes=[xk])
                    em.dma("sp", lambda e, xt=xt, j=j: e.dma_start(out=tv(xres, j), in_=xt), xk + "w", reads=[xk], writes=["xres%d" % j])
                else:
                    em.dma("sp", lambda e, xt=xt, j=j: e.dma_start(out=xt, in_=tv(xres, j)), xk, reads=["xres%d" % j], writes=[xk])
                rmsnorm_tile(xt, xk, c_gsc1, sh1, hT[:, :, sl], "hT%d" % j, tmp, xsq, rstd, "n1")

            chunks = [(m * 128, 128, "U", m) for m in range(18)] + [(2304, 32, "U", 18)] + \
                     [(NMIX + m * 128, 128, "G", m) for m in range(32)]
            wbuf = [ar.alloc([8, 512], BF16) for _ in range(2)]
            stg = [ar.alloc([TN], F32) for _ in range(4)]
            stb = [ar.alloc([TN], BF16) for _ in range(4)]
            win_v = w_in[l].rearrange("(c p) n -> p c n", p=128)
            groups = []
            cur = []
            for ch in chunks:
                if cur and (len(cur) == 4 or cur[-1][2] != ch[2] or cur[-1][1] != 128):
                    groups.append(cur); cur = []
                cur.append(ch)
            groups.append(cur)
            sctr = 0
            for gi, grp in enumerate(groups):
                wt = wbuf[gi % 2]; wk = "wbuf%d" % (gi % 2)
                c0 = grp[0][0]; ncol = sum(c[1] for c in grp)
                em.dma("pool", lambda e, wt=wt, c0=c0, ncol=ncol, win_v=win_v: e.dma_start(out=wt[:, :, 0:ncol], in_=win_v[:, :, c0:c0 + ncol]), wk, writes=[wk])
                for j in range(NT):
                    sl = slice(j * TN, (j + 1) * TN)
                    for (cs, cn, kind, mi) in grp:
                        b = pbank(); o = cs - c0
                        for kc in range(8):
                            em.op("pe", lambda e, wt=wt, o=o, cn=cn, kc=kc, b=b, sl=sl: e.matmul(
                                psum[b][0:cn, :], lhsT=wt[:, kc, o:o + cn], rhs=hT[:, kc, sl], start=(kc == 0), stop=(kc == 7)),
                                reads=[wk, "hT%d" % j], writes=["ps%d" % b])
                        si = sctr % 4; sctr += 1
                        if kind == "U":
                            s_ = stg[si]; sk = "stg%d" % si
                            em.op("dve", lambda e, s_=s_, b=b, cn=cn: e.tensor_copy(out=s_[0:cn, :], in_=psum[b][0:cn, :]), reads=["ps%d" % b], writes=[sk])
                            em.dma("sp", lambda e, s_=s_, mi=mi, cn=cn, sl=sl: e.dma_start(out=U[mi * 128:mi * 128 + cn, sl], in_=s_[0:cn, :]),
                                   sk + "w", reads=[sk], writes=["U%d_%d" % (mi, j)])
                        else:
                            s_ = stb[si]; sk = "stb%d" % si
                            em.op("act", lambda e, s_=s_, b=b: e.activation(out=s_, in_=psum[b][:], func=AF.Sigmoid), reads=["ps%d" % b], writes=[sk])
                            em.dma("sp", lambda e, s_=s_, mi=mi, j=j: e.dma_start(out=tv(G, j)[:, mi, :], in_=s_),
                                   sk + "w", reads=[sk], writes=["G%d" % j])

            em.barrier(); ar.reset()
            mixers(cx, l)

            em.barrier(); ar.reset()
            wbr = ar.alloc([8, D], BF16); wo = ar.alloc([8, D], BF16)
            em.dma("pool", lambda e, l=l: e.dma_start(out=wbr, in_=w_br[l].rearrange("(c p) n -> p c n", p=128)), "wbr", writes=["wbr"])
            em.dma("pool", lambda e, l=l: e.dma_start(out=wo, in_=w_out[l].rearrange("(c p) n -> p c n", p=128)), "wo", writes=["wo"])
            Yt = [ar.alloc([8, TN], BF16) for _ in range(2)]
            Gt = [ar.alloc([32, TN], BF16) for _ in range(2)]
            xb = [ar.alloc([8, TN], F32) for _ in range(2)]
            mgs = [ar.alloc([8, TN], BF16) for _ in range(2)]
            accs = [ar.alloc([TN], F32) for _ in range(2)]; tms = [ar.alloc([TN], F32) for _ in range(2)]
            tmp = ar.alloc([8, TN], F32); xsq = ar.alloc([8, TN], BF16); rstd = ar.alloc([TN], F32)
            h2 = [ar.alloc([8, TN], BF16) for _ in range(1)]
            for j in range(NT):
                sl = slice(j * TN, (j + 1) * TN)
                yt = Yt[j % 2]; gt = Gt[j % 2]; xt = xb[j % 2]; hh = h2[0]
                yk = "Yt%d" % (j % 2); gk = "Gt%d" % (j % 2); xk = "x3_%d" % (j % 2); hk = "h2_0"
                em.dma("sp", lambda e, yt=yt, j=j: e.dma_start(out=yt, in_=tv(Y, j)), yk, reads=["Y%d" % j], writes=[yk])
                em.dma("sp", lambda e, gt=gt, j=j: e.dma_start(out=gt, in_=tv(G, j)), gk, reads=["G%d" % j], writes=[gk])
                em.dma("sp", lambda e, xt=xt, j=j: e.dma_start(out=xt, in_=tv(xres, j)), xk, reads=["xres%d" % j], writes=[xk])
                mg = mgs[j % 2]; mgk = "mg%d" % (j % 2)
                for dc in range(8):
                    acc = accs[dc % 2]; acck = "acc%d" % (dc % 2)
                    for i in range(4):
                        b = pbank()
                        for cc in range(2):
                            em.op("pe", lambda e, i=i, cc=cc, dc=dc, b=b, yt=yt: e.matmul(
                                psum[b][:], lhsT=wbr[:, i * 2 + cc, dc * 128:(dc + 1) * 128], rhs=yt[:, i * 2 + cc, :],
                                start=(cc == 0), stop=(cc == 1)), reads=["wbr", yk], writes=["ps%d" % b])
                        dst = acc if i == 0 else tms[(i - 1) % 2]
                        dstk = acck if i == 0 else "tm%d" % ((i - 1) % 2)
                        em.op("dve", lambda e, b=b, i=i, dc=dc, gt=gt, dst=dst: e.tensor_tensor(out=dst, in0=psum[b][:], in1=gt[:, i * 8 + dc, :], op=ALU.mult),
                              reads=["ps%d" % b, gk], writes=[dstk])
                        if i > 0:
                            o_ = mg[:, dc, :] if i == 3 else acc
                            em.op("pool", lambda e, o_=o_, acc=acc, dst=dst: e.tensor_tensor(out=o_, in0=acc, in1=dst, op=ALU.add),
                                  reads=[acck, dstk], writes=[mgk if i == 3 else acck])
                for d2 in range(8):
                    b = pbank()
                    for dc in range(8):
                        em.op("pe", lambda e, d2=d2, dc=dc, b=b, mg=mg: e.matmul(psum[b][:], lhsT=wo[:, dc, d2 * 128:(d2 + 1) * 128], rhs=mg[:, dc, :],
                                                                          start=(dc == 0), stop=(dc == 7)), reads=["wo", mgk], writes=["ps%d" % b])
                    em.op("dve", lambda e, d2=d2, b=b, xt=xt: e.scalar_tensor_tensor(out=xt[:, d2, :], in0=psum[b][:], scalar=g1[:, d2:d2 + 1], in1=xt[:, d2, :],
                                                                                   op0=ALU.mult, op1=ALU.add), reads=["ps%d" % b, xk, "mod"], writes=[xk])
                em.dma("sp", lambda e, xt=xt, j=j: e.dma_start(out=tv(xres, j), in_=xt), xk + "w", reads=[xk], writes=["xres%d" % j])
                rmsnorm_tile(xt, xk, c_gsc2, sh2, hh, hk, tmp, xsq, rstd, "n2")
                em.dma("sp", lambda e, hh=hh, j=j: e.dma_start(out=tv(H2, j), in_=hh), hk + "w", reads=[hk], writes=["H2_%d" % j])

            if DBG and l == 0:
                em.barrier()
                dd = nc.dram_tensor("dY", [D, T], BF16, kind="ExternalOutput").ap()
                em.dma("sp", lambda e, dd=dd: e.dma_start(out=dd, in_=Y), "dbgY", final=True)
                dd2 = nc.dram_tensor("dX", [D, T], F32, kind="ExternalOutput").ap()
                em.dma("sp", lambda e, dd2=dd2: e.dma_start(out=dd2, in_=xres), "dbgX", final=True)
            em.barrier(); ar.reset()
            w1s = ar.alloc([8, 4 * D], BF16); w2s = ar.alloc([32, D], BF16)
            for q in range(4):
                em.dma("pool", lambda e, l=l, q=q: e.dma_start(out=w1s[:, :, q * 1024:(q + 1) * 1024],
                                                               in_=w1[l].rearrange("(c p) n -> p c n", p=128)[:, :, q * 1024:(q + 1) * 1024]), "w1s", writes=["w1s"])
                em.dma("pool", lambda e, l=l, q=q: e.dma_start(out=w2s[:, q * 8:(q + 1) * 8, :],
                                                               in_=w2[l].rearrange("(c p) n -> p c n", p=128)[:, q * 8:(q + 1) * 8, :]), "w2s", writes=["w2s"])
            hid_raw = ar.alloc([32 * TN], BF16)
            hid = hid_raw.rearrange("p (a b) -> p a b", b=TN)
            rl = [ar.alloc([TN], F32) for _ in range(2)]
            xb1 = ar.alloc([8, TN], F32); h2t = ar.alloc([8, TN], BF16)
            if last:
                tmp = hid_raw[:, 0:16 * TN].bitcast(F32).rearrange("p (a b) -> p a b", b=TN)
                xsq = ar.alloc([8, TN], BF16); rstd = ar.alloc([TN], F32)
            for j in range(NT):
                sl = slice(j * TN, (j + 1) * TN)
                em.dma("sp", lambda e, j=j: e.dma_start(out=h2t, in_=tv(H2, j)), "h2t", reads=["H2_%d" % j], writes=["h2t"])
                em.dma("sp", lambda e, j=j: e.dma_start(out=xb1, in_=tv(xres, j)), "xb1", reads=["xres%d" % j], writes=["xb1"])
                for fc in range(32):
                    b = pbank()
                    for kc in range(8):
                        em.op("pe", lambda e, fc=fc, kc=kc, b=b: e.matmul(psum[b][:], lhsT=w1s[:, kc, fc * 128:(fc + 1) * 128], rhs=h2t[:, kc, :],
                                                                          start=(kc == 0), stop=(kc == 7)), reads=["w1s", "h2t"], writes=["ps%d" % b])
                    r_ = rl[fc % 2]; rk = "rl%d" % (fc % 2)
                    em.op("act", lambda e, b=b, r_=r_: e.activation(out=r_, in_=psum[b][:], func=AF.Relu), reads=["ps%d" % b], writes=[rk])
                    em.op("pool", lambda e, r_=r_, fc=fc: e.tensor_tensor(out=hid[:, fc, :], in0=r_, in1=r_, op=ALU.mult), reads=[rk], writes=["hid"])
                for d2 in range(8):
                    b = pbank()
                    for fc in range(32):
                        em.op("pe", lambda e, fc=fc, d2=d2, b=b: e.matmul(psum[b][:], lhsT=w2s[:, fc, d2 * 128:(d2 + 1) * 128], rhs=hid[:, fc, :],
                                                                          start=(fc == 0), stop=(fc == 31)), reads=["w2s", "hid"], writes=["ps%d" % b])
                    em.op("dve", lambda e, d2=d2, b=b: e.scalar_tensor_tensor(out=xb1[:, d2, :], in0=psum[b][:], scalar=g2[:, d2:d2 + 1], in1=xb1[:, d2, :],
                                                                            op0=ALU.mult, op1=ALU.add), reads=["ps%d" % b, "xb1", "mod"], writes=["xb1"])
                if not last:
                    em.dma("sp", lambda e, j=j: e.dma_start(out=tv(xres, j), in_=xb1), "xb1w", reads=["xb1"], writes=["xres%d" % j])
                else:
                    em.op("act", lambda e: e.activation(out=xsq, in_=xb1, func=AF.Square), reads=["xb1"], writes=["fxsq"])
                    b = pbank()
                    for c in range(8):
                        em.op("pe", lambda e, c=c, b=b: e.matmul(psum[b][:], lhsT=onesb[:], rhs=xsq[:, c, :], start=(c == 0), stop=(c == 7)),
                              reads=["fxsq", "onesb"], writes=["ps%d" % b])
                    em.op("act", lambda e, b=b: e.activation(out=rstd, in_=psum[b][:], func=AF.Sqrt, bias=c_eps, scale=1.0),
                          reads=["ps%d" % b, "c_eps"], writes=["frstd"])
                    em.op("dve", lambda e: e.reciprocal(out=rstd, in_=rstd), reads=["frstd"], writes=["frstd"])
                    for c in range(8):
                        em.op("dve", lambda e, c=c: e.scalar_tensor_tensor(out=tmp[:, c, :], in0=xb1[:, c, :], scalar=c_fg[:, c:c + 1], in1=rstd,
                                                                         op0=ALU.mult, op1=ALU.mult), reads=["xb1", "frstd", "c_fg"], writes=["hid"])
                    em.dma("sp", lambda e, sl=sl: e.dma_start(out=fm(yT)[:, :, sl], in_=tmp), "yT_w", reads=["hid"], writes=["yT%d" % j], final=True)
        em.build()
        global _EM
        _EM = em
    return nc


class CX:
    pass


def mixers(cx, l):
    em, ar, psum, pbank, U, Y = cx.em, cx.ar, cx.psum, cx.pbank, cx.U, cx.Y
    c_kap, c_nk, c_eps = cx.c_kap, cx.c_nk, cx.c_eps

    ar.reset()
    c = mk_consts(cx, l, 'p')
    p_psc = c.psc

    zp = ar.alloc([2, 16, 272], F32)
    Wa = ar.alloc([16, 272], F32); Wb = ar.alloc([16, 272], F32)
    pooled = ar.alloc([2, T], F32)
    invc = ar.alloc([2, T], F32)
    PW = ar.alloc([2, 128], F32)
    yst = [ar.alloc([TN], BF16) for _ in range(2)]
    em.op("pool", lambda e: e.memset(zp, 0.0), writes=["zp"])
    em.op("pool", lambda e: e.memset(PW, 0.0), writes=["PW"])
    for ct in range(2):
        em.dma("sp", lambda e, ct=ct: e.dma_start(out=invc[:, ct, :], in_=cx.invc[ct]), "invc", writes=["invc"])
        Uv = U[ct * 128:(ct + 1) * 128, :].rearrange("p (b s) -> p b s", s=256)
        em.dma("sp", lambda e, ct=ct, Uv=Uv: e.dma_start(out=zp[:, ct, :, 8:264], in_=Uv), "zp", writes=["zp"])
        em.dma("sp", lambda e, ct=ct, Uv=Uv: e.dma_start(out=zp[:, ct, 1:16, 0:8], in_=Uv[:, 0:15, 248:256]), "zp", writes=["zp"])
        em.dma("sp", lambda e, ct=ct, Uv=Uv: e.dma_start(out=zp[:, ct, 0:15, 264:272], in_=Uv[:, 1:16, 0:8]), "zp", writes=["zp"])
    for g in range(4):
        r0 = (g % 2) * 64
        em.dma("sp", lambda e, g=g, r0=r0, l=l: e.dma_start(out=PW[r0:r0 + 64, g // 2, r0:r0 + 64], in_=cx.pool_w[l, g]), "PW", writes=["PW"])
    for ct in range(2):
        for (a, b) in ((0, 8), (264, 272)):
            em.op("dve", lambda e, ct=ct, a=a, b=b: e.tensor_scalar(out=zp[:, ct, :, a:b], in0=zp[:, ct, :, a:b], scalar1=c_kap, scalar2=None, op0=ALU.mult),
                  reads=["zp", "c_kap"], writes=["zp"])
    pv = lambda ct: pooled[:, ct, :].rearrange("p (b s) -> p b s", s=256)
    iv = lambda ct: invc[:, ct, :].rearrange("p (b s) -> p b s", s=256)

    def take(W, wk, ct, r0):
        em.op("dve", lambda e: e.tensor_tensor(out=pv(ct)[r0:r0 + 64], in0=W[r0:r0 + 64, :, 8:264], in1=iv(ct)[r0:r0 + 64], op=ALU.mult),
              reads=[wk, "invc"], writes=["pooled"])
        em.op("dve", lambda e: e.tensor_tensor(out=pv(ct)[r0:r0 + 64], in0=pv(ct)[r0:r0 + 64], in1=zp[r0:r0 + 64, ct, :, 8:264], op=ALU.subtract),
              reads=["pooled", "zp"], writes=["pooled"])

    for ct in range(2):
        Z = zp[:, ct]
        em.op("dve", lambda e, Z=Z: e.tensor_tensor(out=Wa[:, :, 1:272], in0=Z[:, :, 0:271], in1=Z[:, :, 1:272], op=ALU.add), reads=["zp"], writes=["Wa"])
        if ct == 0:
            take(Wa, "Wa", 0, 0)
        em.op("dve", lambda e: e.tensor_tensor(out=Wb[:, :, 2:271], in0=Wa[:, :, 1:270], in1=Wa[:, :, 3:272], op=ALU.add), reads=["Wa"], writes=["Wb"])
        if ct == 0:
            take(Wb, "Wb", 0, 64)
        else:
            em.op("dve", lambda e: e.tensor_tensor(out=Wa[:, :, 4:269], in0=Wb[:, :, 2:267], in1=Wb[:, :, 6:271], op=ALU.add), reads=["Wb"], writes=["Wa"])
            take(Wa, "Wa", 1, 0)
            em.op("dve", lambda e: e.tensor_tensor(out=Wb[:, :, 8:265], in0=Wa[:, :, 4:261], in1=Wa[:, :, 12:269], op=ALU.add), reads=["Wa"], writes=["Wb"])
            take(Wb, "Wb", 1, 64)
    n = 0
    for j in range(NT):
        sl = slice(j * TN, (j + 1) * TN)
        for ct in range(2):
            b = pbank(); i = n % 2; n += 1
            em.op("pe", lambda e, b=b, ct=ct, sl=sl: e.matmul(psum[b][:], lhsT=PW[:, ct, :], rhs=pooled[:, ct, sl], start=True, stop=True),
                  reads=["PW", "pooled"], writes=["ps%d" % b])
            em.op("act", lambda e, b=b, ct=ct, i=i: e.activation(out=yst[i], in_=psum[b][:], func=AF.Identity, bias=0.0, scale=p_psc[:, ct:ct + 1]),
                  reads=["ps%d" % b], writes=["yst%d" % i])
            em.dma("sp", lambda e, ct=ct, j=j, i=i: e.dma_start(out=cx.tv(Y, j)[:, ct, :], in_=yst[i]), "yst%dw" % i, reads=["yst%d" % i])

    em.barrier(); ar.reset()
    zf = ar.alloc([2, T], BF16)
    csm = ar.alloc([2, 512], BF16)
    zcs = ar.alloc([32, 512], BF16)
    CLb = [ar.alloc([16, 512], BF16) for _ in range(2)]
    SLb = [ar.alloc([16, 512], BF16) for _ in range(2)]
    fst = [ar.alloc([TN], BF16) for _ in range(2)]
    for ct in range(2):
        em.dma("pool", lambda e, ct=ct: e.dma_start(out=zf[:, ct, :], in_=U[256 + ct * 128:256 + (ct + 1) * 128, :]), "zf", writes=["zf"])
        em.dma("pool", lambda e, ct=ct: e.dma_start(out=csm[:, ct, :], in_=cx.csm[ct]), "csm", writes=["csm"])
    for tc in range(32):
        b = pbank()
        for ct in range(2):
            em.op("pe", lambda e, b=b, ct=ct, tc=tc: e.matmul(psum[b][:], lhsT=zf[:, ct, tc * 128:(tc + 1) * 128], rhs=csm[:, ct, :], start=(ct == 0), stop=(ct == 1)),
                  reads=["zf", "csm"], writes=["ps%d" % b])
        if tc % 2 == 0:
            em.op("act", lambda e, b=b, tc=tc: e.activation(out=zcs[:, tc, :], in_=psum[b][:], func=AF.Identity), reads=["ps%d" % b], writes=["zcs"])
        else:
            em.op("dve", lambda e, b=b, tc=tc: e.tensor_copy(out=zcs[:, tc, :], in_=psum[b][:]), reads=["ps%d" % b], writes=["zcs"])
    CLv = lambda ft, th: cx.CL[ft, th].rearrange("p (a f) -> p a f", f=512)
    SLv = lambda ft, th: cx.SL[ft, th].rearrange("p (a f) -> p a f", f=512)
    n = 0
    for ft in range(8):
        fsl = slice(ft * 512, (ft + 1) * 512)
        bb = [pbank(), pbank()]
        for th in range(2):
            i = n % 2; n += 1
            em.dma("sp", lambda e, i=i, th=th, ft=ft: e.dma_start(out=CLb[i], in_=CLv(ft, th)), "CLb%d" % i, writes=["CLb%d" % i])
            em.dma("sp", lambda e, i=i, th=th, ft=ft: e.dma_start(out=SLb[i], in_=SLv(ft, th)), "SLb%d" % i, writes=["SLb%d" % i])
            for half in range(2):
                b = bb[half]
                for tcl in range(16):
                    tc = th * 16 + tcl
                    em.op("pe", lambda e, b=b, i=i, tc=tc, tcl=tcl, half=half, th=th: e.matmul(
                        psum[b][:], lhsT=zcs[:, tc, half * 128:(half + 1) * 128], rhs=CLb[i][:, tcl, :], start=(th == 0 and tcl == 0), stop=False),
                        reads=["zcs", "CLb%d" % i], writes=["ps%d" % b])
                    em.op("pe", lambda e, b=b, i=i, tc=tc, tcl=tcl, half=half, th=th: e.matmul(
                        psum[b][:], lhsT=zcs[:, tc, 256 + half * 128:256 + (half + 1) * 128], rhs=SLb[i][:, tcl, :], start=False, stop=(th == 1 and tcl == 15)),
                        reads=["zcs", "SLb%d" % i], writes=["ps%d" % b])
        for half in range(2):
            b = bb[half]
            em.op("act", lambda e, b=b, half=half: e.activation(out=fst[half], in_=psum[b][:], func=AF.Identity), reads=["ps%d" % b], writes=["fst%d" % half])
            em.dma("sp", lambda e, half=half, ft=ft: e.dma_start(out=cx.tv(Y, ft)[:, 2 + half, :], in_=fst[half]), "fst%dw" % half, reads=["fst%d" % half])

    em.barrier(); ar.reset()
    c = mk_consts(cx, l, "m")
    rwkv_prepass(cx, l, c)
    recs = []
    for fn, banks in ((mix_gla, [0, 1, 2]), (mix_rwkv, [3, 4, 5, 6, 7])):
        rec = Rec()
        sub = CX(); sub.__dict__.update(cx.__dict__)
        sub.em = rec
        st_ = [0]

        def pb(banks=banks, st_=st_):
            st_[0] = (st_[0] + 1) % len(banks)
            return banks[st_[0]]
        sub.pbank = pb
        fn(sub, l, c)
        recs.append(rec)
    merge_streams(em, recs)


def mk_consts(cx, l, tag):
    em, ar = cx.em, cx.ar
    c = CX()
    c.ident = ar.alloc([128], F32)
    c.ident4 = ar.alloc([512], F32)
    c.maskc = ar.alloc([8, 512], F32)
    c.prm = ar.alloc([cx.PRM_N], F32)
    c.rows = ar.alloc([cx.ROW_N], F32)
    c.lr64 = ar.alloc([768], F32); c.lr32 = ar.alloc([512], F32); c.lr16 = ar.alloc([256], F32)
    c.ones_row = ar.alloc([128], F32); c.ones_col = ar.alloc([4], F32)
    c.identb = ar.alloc([128], BF16)
    c.ident4b = ar.alloc([512], BF16)
    for h in range(4):
        em.dma("sp", lambda e, h=h: e.dma_start(out=c.ident4b[:, h * 128:(h + 1) * 128], in_=cx.identb_d), "K" + tag + "i4b", writes=["K" + tag + "i4b%d" % h])
    em.dma("sp", lambda e: e.dma_start(out=c.identb, in_=cx.identb_d), "K" + tag + "ib", writes=["K" + tag + "ib"])
    k = "K" + tag
    em.dma("sp", lambda e: e.dma_start(out=c.ident, in_=cx.ident_d), k + "a", writes=[k])
    for h in range(4):
        em.dma("sp", lambda e, h=h: e.dma_start(out=c.ident4[:, h * 128:(h + 1) * 128], in_=cx.ident_d), k + "b", writes=[k + "i4%d" % h])
    em.dma("sp", lambda e: e.dma_start(out=c.maskc, in_=cx.maskc_d), k + "c", writes=[k + "m"])
    em.dma("sp", lambda e, l=l: e.dma_start(out=c.prm, in_=cx.prm[l]), k + "d", writes=[k + "p"])
    em.dma("sp", lambda e, l=l: e.dma_start(out=c.rows[0:1, :], in_=cx.rows[l]), k + "e", writes=[k + "r"])
    em.dma("sp", lambda e, l=l: e.dma_start(out=c.lr64[0:64, :], in_=cx.lr64_d[l]), k + "f", writes=[k + "l64"])
    em.dma("sp", lambda e, l=l: e.dma_start(out=c.lr32[0:32, :], in_=cx.lr32_d[l]), k + "g", writes=[k + "l32"])
    em.dma("sp", lambda e, l=l: e.dma_start(out=c.lr16[0:16, :], in_=cx.lr16_d[l]), k + "h", writes=[k + "l16"])
    em.op("dve", lambda e: e.memset(c.ones_row, 1.0), writes=[k + "o1"])
    em.op("dve", lambda e: e.memset(c.ones_col, 1.0), writes=[k + "o2"])
    BC0 = 32
    p = c.prm
    c.psc = p[:, 0:2]; c.mu = p[:, 2:8]; c.hm = p[:, 8:14]; c.om = p[:, 14:20]
    c.kkw = p[:, BC0:BC0 + 256]; c.ka = p[:, BC0 + 256:BC0 + 512]; c.oka = p[:, BC0 + 512:BC0 + 768]
    c.rk = p[:, BC0 + 768:BC0 + 1024]; c.gn = p[:, BC0 + 1024:BC0 + 1280]; c.gln = p[:, BC0 + 1280:BC0 + 1536]
    em.op("dve", lambda e: e.tensor_scalar(out=c.hm, in0=c.mu, scalar1=0.5, scalar2=None, op0=ALU.mult), reads=[k + "p"], writes=[k + "p"])
    em.op("dve", lambda e: e.tensor_scalar(out=c.om, in0=c.mu, scalar1=-1.0, scalar2=1.0, op0=ALU.mult, op1=ALU.add), reads=[k + "p"], writes=[k + "p"])
    em.op("dve", lambda e: e.tensor_scalar(out=c.oka, in0=c.ka, scalar1=-1.0, scalar2=1.0, op0=ALU.mult, op1=ALU.add), reads=[k + "p"], writes=[k + "p"])
    em.barrier()
    return c


def mix_gla(cx, l, c):
    em, ar, psum, pbank, U, Y = cx.em, cx.ar, cx.psum, cx.pbank, cx.U, cx.Y
    S = [ar.alloc([4, 64], F32) for _ in range(2)]
    OG = ar.alloc([32, 256], F32)
    fmt = [ar.alloc([6, 128], F32) for _ in range(2)]
    cal = [ar.alloc([128], F32) for _ in range(2)]
    tm = [ar.alloc([512], F32) for _ in range(2)]
    go = ar.alloc([256], F32)
    ee = ar.alloc([128], F32); lg = ar.alloc([128], F32)
    Eq = ar.alloc([128], F32); Ek = ar.alloc([128], F32); Eb = ar.alloc([128], F32)
    qt = ar.alloc([128], BF16); kh = ar.alloc([128], BF16); kb = ar.alloc([128], BF16)
    qT = ar.alloc([512], BF16); kT = ar.alloc([512], BF16)
    AT = ar.alloc([512], BF16)
    vb = ar.alloc([256], BF16); Sb = [ar.alloc([4, 64], BF16) for _ in range(2)]
    etot = ar.alloc([4], F32)
    og = ar.alloc([256], F32); sq = ar.alloc([256], F32); ss = ar.alloc([4], F32); sgo = ar.alloc([256], F32)
    yd = ar.alloc([256], F32); ydb = ar.alloc([256], BF16); ydT = [ar.alloc([2, 128], BF16) for _ in range(2)]
    Ufm = U[1536:2304, :].rearrange("(c p) t -> p c t", p=128)
    Yv = lambda tc: cx.tv(Y, tc // 4)[:, 6:8, (tc % 4) * 128:(tc % 4 + 1) * 128]
    ab = lambda d: c.lr16[0:16, d * 128:(d + 1) * 128]
    abrow = lambda d: c.rows[0:1, 1024 + d * 128:1024 + (d + 1) * 128]
    for d in range(2):
        order = list(range(32)) if d == 0 else list(range(31, -1, -1))
        Sd = S[d]; Sk = "gS%d" % d
        for n, tc in enumerate(order):
            par = n % 2
            tsl = slice(tc * 128, (tc + 1) * 128)
            f_ = fmt[par]; fk = "gfmt%d" % par; ca = cal[par]; ck = "gcal%d" % par; t_ = tm[par]; tk = "gtm%d" % par
            em.dma("sp", lambda e, f_=f_, tsl=tsl: e.dma_start(out=f_, in_=Ufm[:, :, tsl]), fk, writes=[fk])
            em.dma("sp", lambda e, ca=ca, tsl=tsl, d=d: e.dma_start(out=ca[0:16, :], in_=U[2304 + 16 * d:2304 + 16 * d + 16, tsl]), ck, writes=[ck])
            if n == 0:
                em.dma("sp", lambda e, Sd=Sd, d=d, l=l: e.dma_start(out=Sd[0:32], in_=cx.sg0[l, d]), Sk + "i", writes=[Sk])
            elif n % 2 == 0:
                em.op("dve", lambda e, Sd=Sd: e.tensor_scalar(out=Sd[0:32], in0=Sd[0:32], scalar1=cx.c_kap[0:32], scalar2=None, op0=ALU.mult),
                      reads=[Sk], writes=[Sk])
            b = pbank()
            for cc in range(4):
                em.op("pe", lambda e, b=b, cc=cc, f_=f_: e.matmul(psum[b][:, cc * 128:(cc + 1) * 128], lhsT=f_[:, cc, :], rhs=c.ident, start=True, stop=True),
                      reads=[fk], writes=["ps%d" % b])
            em.op("act", lambda e, b=b, t_=t_: e.activation(out=t_, in_=psum[b][:], func=AF.Identity), reads=["ps%d" % b], writes=[tk])
            em.op("dve", lambda e, b=b: e.tensor_copy(out=vb, in_=psum[b][:, 256:512]), reads=["ps%d" % b], writes=["gvb"])
            Sb_ = Sb[d]; Sbk = "gSb%d" % d
            em.op("act", lambda e, Sd=Sd, Sb_=Sb_: e.activation(out=Sb_[0:32], in_=Sd[0:32], func=AF.Identity), reads=[Sk], writes=[Sbk])
            if d == 1:
                b = pbank()
                for cc in range(2):
                    em.op("pe", lambda e, b=b, cc=cc, f_=f_: e.matmul(psum[b][:, cc * 128:(cc + 1) * 128], lhsT=f_[:, 4 + cc, :], rhs=c.ident, start=True, stop=True),
                          reads=[fk], writes=["ps%d" % b])
                em.op("act", lambda e, b=b: e.activation(out=sgo, in_=psum[b][:, 0:256], func=AF.Silu), reads=["ps%d" % b], writes=["gsgo"])
            b = pbank()
            em.op("pe", lambda e, b=b, ca=ca, d=d: e.matmul(psum[b][:, 0:128], lhsT=ca[0:16, :], rhs=ab(d), start=True, stop=False), reads=[ck], writes=["ps%d" % b])
            em.op("pe", lambda e, b=b, d=d: e.matmul(psum[b][:, 0:128], lhsT=c.ones_row[0:1, :], rhs=abrow(d), start=False, stop=True), reads=[], writes=["ps%d" % b])
            em.op("act", lambda e, b=b: e.activation(out=ee, in_=psum[b][:, 0:128], func=AF.Exp, scale=-1.0), reads=["ps%d" % b], writes=["gee"])
            em.op("act", lambda e: e.activation(out=lg, in_=ee, func=AF.Ln, bias=1.0, scale=1.0), reads=["gee"], writes=["glg"])
            b = pbank()
            em.op("pe", lambda e, b=b, d=d: e.matmul(psum[b][:, 0:128], lhsT=c.maskc[:, d * 4 + 0, 0:128], rhs=lg, start=True, stop=True), reads=["glg"], writes=["ps%d" % b])
            em.op("pe", lambda e, b=b, d=d: e.matmul(psum[b][:, 128:256], lhsT=c.maskc[:, d * 4 + 3, 0:128], rhs=lg, start=True, stop=True), reads=["glg"], writes=["ps%d" % b])
            em.op("act", lambda e, b=b: e.activation(out=Eq, in_=psum[b][:, 0:128], func=AF.Exp, scale=-1.0 / 16), reads=["ps%d" % b], writes=["gEq"])
            em.op("act", lambda e, b=b: e.activation(out=Ek, in_=psum[b][:, 0:128], func=AF.Exp, scale=1.0 / 16), reads=["ps%d" % b], writes=["gEk"])
            em.op("act", lambda e, b=b: e.activation(out=Eb, in_=psum[b][:, 128:256], func=AF.Exp, scale=-1.0 / 16), reads=["ps%d" % b], writes=["gEb"])
            em.op("dve", lambda e, t_=t_: e.scalar_tensor_tensor(out=qt, in0=t_[:, 0:128], scalar=32 ** -0.5, in1=Eq, op0=ALU.mult, op1=ALU.mult), reads=[tk, "gEq"], writes=["gqt"])
            em.op("dve", lambda e, t_=t_: e.tensor_tensor(out=kh, in0=t_[:, 128:256], in1=Ek, op=ALU.mult), reads=[tk, "gEk"], writes=["gkh"])
            em.op("pool", lambda e, t_=t_: e.tensor_tensor(out=kb, in0=t_[:, 128:256], in1=Eb, op=ALU.mult), reads=[tk, "gEb"], writes=["gkb"])
            b = pbank()
            for h in range(4):
                em.op("pe", lambda e, b=b, h=h: e.matmul(psum[b][0:32, h * 128:(h + 1) * 128], lhsT=qt[:, h * 32:(h + 1) * 32], rhs=c.identb, start=True, stop=True),
                      reads=["gqt"], writes=["ps%d" % b])
            em.op("act", lambda e, b=b: e.activation(out=qT[0:32], in_=psum[b][0:32], func=AF.Identity), reads=["ps%d" % b], writes=["gqT"])
            b = pbank()
            for h in range(4):
                em.op("pe", lambda e, b=b, h=h: e.matmul(psum[b][0:32, h * 128:(h + 1) * 128], lhsT=kh[:, h * 32:(h + 1) * 32], rhs=c.identb, start=True, stop=True),
                      reads=["gkh"], writes=["ps%d" % b])
            em.op("dve", lambda e, b=b: e.tensor_copy(out=kT[0:32], in_=psum[b][0:32]), reads=["ps%d" % b], writes=["gkT"])
            b = pbank()
            for h in range(4):
                em.op("pe", lambda e, b=b, h=h: e.matmul(psum[b][:, h * 128:(h + 1) * 128], lhsT=kT[0:32, h * 128:(h + 1) * 128], rhs=qT[0:32, h * 128:(h + 1) * 128], start=True, stop=True),
                      reads=["gkT", "gqT"], writes=["ps%d" % b])
            em.op("dve", lambda e, b=b, d=d: e.tensor_tensor(out=AT, in0=psum[b][:], in1=c.maskc[:, d * 4 + 0, :], op=ALU.mult), reads=["ps%d" % b], writes=["gAT"])
            b = pbank()
            for h in range(4):
                em.op("pe", lambda e, b=b, h=h: e.matmul(psum[b][0:32, h:h + 1], lhsT=lg[:, h * 32:(h + 1) * 32], rhs=c.ones_col[:, 0:1], start=True, stop=True),
                      reads=["glg"], writes=["ps%d" % b])
            em.op("act", lambda e, b=b: e.activation(out=etot[0:32], in_=psum[b][0:32, 0:4], func=AF.Exp, scale=-1.0 / 16), reads=["ps%d" % b], writes=["getot"])
            bO = pbank()
            for h in range(4):
                em.op("pe", lambda e, bO=bO, h=h: e.matmul(psum[bO][:, h * 64:(h + 1) * 64], lhsT=AT[:, h * 128:(h + 1) * 128], rhs=vb[:, h * 64:(h + 1) * 64], start=True, stop=False),
                      reads=["gAT", "gvb"], writes=["ps%d" % bO])
                em.op("pe", lambda e, bO=bO, h=h, Sb_=Sb_: e.matmul(psum[bO][:, h * 64:(h + 1) * 64], lhsT=qT[0:32, h * 128:(h + 1) * 128], rhs=Sb_[0:32, h, :], start=False, stop=True),
                      reads=["gqT", Sbk], writes=["ps%d" % bO])
            bS = pbank()
            for h in range(4):
                em.op("pe", lambda e, bS=bS, h=h: e.matmul(psum[bS][0:32, h * 64:(h + 1) * 64], lhsT=kb[:, h * 32:(h + 1) * 32], rhs=vb[:, h * 64:(h + 1) * 64], start=True, stop=True),
                      reads=["gkb", "gvb"], writes=["ps%d" % bS])
            for h in range(4):
                em.op("dve", lambda e, bS=bS, h=h, Sd=Sd: e.scalar_tensor_tensor(out=Sd[0:32, h, :], in0=Sd[0:32, h, :], scalar=etot[0:32, h:h + 1], in1=psum[bS][0:32, h * 64:(h + 1) * 64],
                                                                            op0=ALU.mult, op1=ALU.add), reads=[Sk, "getot", "ps%d" % bS], writes=[Sk])
            if n % 2 == 1:
                blk = tc // 2
                em.dma("sp", lambda e, Sd=Sd, d=d, l=l, blk=blk: e.dma_start(out=cx.osg[l, d, blk], in_=Sd[0:32]), Sk + "o", reads=[Sk], final=True)
            if d == 0:
                em.op("act", lambda e, bO=bO, tc=tc: e.activation(out=OG[:, tc, :], in_=psum[bO][:, 0:256], func=AF.Identity), reads=["ps%d" % bO], writes=["gOG"])
            else:
                em.op("dve", lambda e, bO=bO, tc=tc: e.tensor_tensor(out=og, in0=psum[bO][:, 0:256], in1=OG[:, tc, :], op=ALU.add), reads=["ps%d" % bO, "gOG"], writes=["gog"])
                em.op("pool", lambda e: e.tensor_tensor(out=sq, in0=og, in1=og, op=ALU.mult), reads=["gog"], writes=["gsq"])
                em.op("dve", lambda e: e.tensor_reduce(out=ss, in_=sq.rearrange("p (h v) -> p h v", v=64), axis=AX.X, op=ALU.add), reads=["gsq"], writes=["gss"])
                em.op("act", lambda e: e.activation(out=ss, in_=ss, func=AF.Sqrt, bias=cx.c_eps, scale=1.0 / 64), reads=["gss"], writes=["gss"])
                em.op("dve", lambda e: e.reciprocal(out=ss, in_=ss), reads=["gss"], writes=["gss"])
                em.op("dve", lambda e: e.tensor_tensor(out=yd.rearrange("p (h v) -> p h v", v=64), in0=og.rearrange("p (h v) -> p h v", v=64),
                                                       in1=ss.unsqueeze(2).to_broadcast([128, 4, 64]), op=ALU.mult), reads=["gog", "gss"], writes=["gyd"])
                em.op("pool", lambda e: e.tensor_tensor(out=yd, in0=yd, in1=c.gln, op=ALU.mult), reads=["gyd"], writes=["gyd"])
                em.op("pool", lambda e: e.tensor_tensor(out=ydb, in0=yd, in1=sgo, op=ALU.mult), reads=["gyd", "gsgo"], writes=["gydb"])
                b = pbank()
                for cc in range(2):
                    em.op("pe", lambda e, b=b, cc=cc: e.matmul(psum[b][:, cc * 128:(cc + 1) * 128], lhsT=ydb[:, cc * 128:(cc + 1) * 128], rhs=c.identb, start=True, stop=True),
                          reads=["gydb"], writes=["ps%d" % b])
                yT_ = ydT[par]; yk = "gydT%d" % par
                em.op("act", lambda e, b=b, yT_=yT_: e.activation(out=yT_, in_=psum[b][:, 0:256].rearrange("p (c t) -> p c t", t=128), func=AF.Identity), reads=["ps%d" % b], writes=[yk])
                em.dma("sp", lambda e, yT_=yT_, tc=tc: e.dma_start(out=Yv(tc), in_=yT_), yk + "w", reads=[yk])


def rwkv_prepass(cx, l, c):
    em, ar, psum, pbank, U, Y = cx.em, cx.ar, cx.psum, cx.pbank, cx.U, cx.Y
    ZM = cx.ZM
    mark = ar.off
    zt = [ar.alloc([6, 514], F32) for _ in range(2)]
    sh = ar.alloc([6, 512], F32)
    zo = [ar.alloc([6, 512], F32) for _ in range(2)]
    Uz = U[512:1280, :].rearrange("(c p) t -> p c t", p=128)
    ZMv = ZM.rearrange("(c p) t -> p c t", p=128)
    for j in range(NT):
        z = zt[j % 2]; zk = "rzt%d" % (j % 2); o_ = zo[j % 2]; ok = "rzo%d" % (j % 2)
        t0 = j * 512
        if j == 0:
            em.op("pool", lambda e, z=z: e.memset(z[:, :, 0:1], 0.0), writes=[zk])
            em.dma("sp", lambda e, z=z: e.dma_start(out=z[:, :, 1:514], in_=Uz[:, :, 0:513]), zk, writes=[zk])
        elif j == NT - 1:
            em.op("pool", lambda e, z=z: e.memset(z[:, :, 513:514], 0.0), writes=[zk])
            em.dma("sp", lambda e, z=z, t0=t0: e.dma_start(out=z[:, :, 0:513], in_=Uz[:, :, t0 - 1:t0 + 512]), zk, writes=[zk])
        else:
            em.dma("sp", lambda e, z=z, t0=t0: e.dma_start(out=z, in_=Uz[:, :, t0 - 1:t0 + 513]), zk, writes=[zk])
        em.op("dve", lambda e, z=z: e.tensor_tensor(out=sh, in0=z[:, :, 0:512], in1=z[:, :, 2:514], op=ALU.add), reads=[zk], writes=["rsh"])
        em.op("dve", lambda e, z=z: e.scalar_tensor_tensor(out=sh[:, :, 0:512:256], in0=z[:, :, 0:512:256], scalar=cx.c_nk, in1=sh[:, :, 0:512:256], op0=ALU.mult, op1=ALU.add),
              reads=[zk, "rsh"], writes=["rsh"])
        em.op("dve", lambda e, z=z: e.scalar_tensor_tensor(out=sh[:, :, 255:512:256], in0=z[:, :, 257:514:256], scalar=cx.c_nk, in1=sh[:, :, 255:512:256], op0=ALU.mult, op1=ALU.add),
              reads=[zk, "rsh"], writes=["rsh"])
        for cc in range(6):
            em.op("pool", lambda e, cc=cc: e.tensor_scalar(out=sh[:, cc, :], in0=sh[:, cc, :], scalar1=c.hm[:, cc:cc + 1], scalar2=None, op0=ALU.mult), reads=["rsh"], writes=["rsh"])
            em.op("dve", lambda e, cc=cc, z=z, o_=o_: e.scalar_tensor_tensor(out=o_[:, cc, :], in0=z[:, cc, 1:513], scalar=c.om[:, cc:cc + 1], in1=sh[:, cc, :], op0=ALU.mult, op1=ALU.add),
                  reads=[zk, "rsh"], writes=[ok])
        em.dma("sp", lambda e, o_=o_, t0=t0: e.dma_start(out=ZMv[:, :, t0:t0 + 512], in_=o_), ok + "w", reads=[ok])
    em.barrier()
    ar.off = mark


def mix_rwkv(cx, l, c):
    em, ar, psum, pbank, U, Y = cx.em, cx.ar, cx.psum, cx.pbank, cx.U, cx.Y
    CD = 0.606531
    ZM = cx.ZM
    ZMv = ZM.rearrange("(c p) t -> p c t", p=128)
    A = lambda shp: ar.alloc(shp, F32)
    H = [A([4, 64]) for _ in range(2)]
    OR = A([32, 256]); BON = A([32, 4])
    zc = [A([6, 128]) for _ in range(2)]
    cw = [A([128]) for _ in range(2)]; ca = [A([128]) for _ in range(2)]; cg = [A([128]) for _ in range(2)]
    rk_ = A([512]); v_ = A([256])
    tcw = A([128]); sg = A([256]); aa = A([256])
    kw = A([256]); sq = A([256]); ss = A([4]); kk = A([256]); t1 = A([256]); kdir = A([256]); bb = A([256])
    Ex = A([256]); E1 = A([256]); E2 = A([256]); E3 = A([256]); E4 = A([256])
    Bf = lambda shp: ar.alloc(shp, BF16)
    kkt = A([256]); bh = A([256]); kh = A([256]); rt = Bf([256]); Kbar = Bf([256]); Bbn = Bf([256]); t2 = A([256]); bsum = A([4])
    bhTb = Bf([512]); khTb = Bf([512]); vb = Bf([256]); Hb = [Bf([4, 64]) for _ in range(2)]
    kktT = A([512]); bhT = A([512]); khT = A([512]); rtT = Bf([512])
    Mp = [A([512]) for _ in range(2)]; Np = [A([512]) for _ in range(2)]; X = A([512])
    AkkT = A([512]); ArkT = Bf([512]); ArbT = Bf([512])
    WT = A([512]); Y0 = A([256]); U0 = A([256]); Uu = Bf([256]); pc = A([4])
    o = A([256]); s1 = A([4]); cen = A([256]); s2 = A([4]); bon = A([4]); sgc = A([128]); yc = Bf([256])
    ycT = [ar.alloc([2, 128], BF16) for _ in range(2)]
    Yv = lambda tc: cx.tv(Y, tc // 4)[:, 4:6, (tc % 4) * 128:(tc % 4 + 1) * 128]
    bw = lambda d: c.lr64[0:64, d * 256:(d + 1) * 256]
    bg = c.lr64[0:64, 512:768]
    ba = lambda d: c.lr32[0:32, d * 256:(d + 1) * 256]
    w0row = lambda d: c.rows[0:1, d * 256:(d + 1) * 256]
    a0row = lambda d: c.rows[0:1, 512 + d * 256:512 + (d + 1) * 256]
    h3 = lambda ap: ap.rearrange("p (h v) -> p h v", v=64)
    bc4 = lambda ap4: ap4.unsqueeze(2).to_broadcast([128, 4, 64])

    def ew(eng, fn, reads, writes):
        em.op(eng, fn, reads=reads, writes=writes)

    def mm4(b, lhs, rhs, reads, npart=128, width=128):
        for h in range(4):
            em.op("pe", lambda e, h=h: e.matmul(psum[b][0:npart, h * width:(h + 1) * width], lhsT=lhs(h), rhs=rhs(h), start=True, stop=True),
                  reads=reads, writes=["ps%d" % b])

    blk128 = lambda t: (lambda h: t[:, h * 128:(h + 1) * 128])
    blk64 = lambda t: (lambda h: t[:, h * 64:(h + 1) * 64])
    blkT = lambda t: (lambda h: t[0:64, h * 128:(h + 1) * 128])

    for d in range(2):
        order = list(range(32)) if d == 0 else list(range(31, -1, -1))
        Hd = H[d]; Hk = "rH%d" % d
        for n, tc in enumerate(order):
            par = n % 2
            tsl = slice(tc * 128, (tc + 1) * 128)
            z = zc[par]; zk = "rzc%d" % par; cw_ = cw[par]; cwk = "rcw%d" % par; ca_ = ca[par]; cak = "rca%d" % par; cg_ = cg[par]; cgk = "rcg%d" % par
            em.dma("sp", lambda e, z=z, tsl=tsl: e.dma_start(out=z, in_=ZMv[:, :, tsl]), zk, writes=[zk])
            em.dma("sp", lambda e, cw_=cw_, tsl=tsl, d=d: e.dma_start(out=cw_[0:64, :], in_=U[1280 + 64 * d:1280 + 64 * d + 64, tsl]), cwk, writes=[cwk])
            em.dma("sp", lambda e, ca_=ca_, tsl=tsl, d=d: e.dma_start(out=ca_[0:32, :], in_=U[1408 + 32 * d:1408 + 32 * d + 32, tsl]), cak, writes=[cak])
            if d == 1:
                em.dma("sp", lambda e, cg_=cg_, tsl=tsl: e.dma_start(out=cg_[0:64, :], in_=U[1472:1536, tsl]), cgk, writes=[cgk])
            if n == 0:
                em.dma("sp", lambda e, Hd=Hd, d=d, l=l: e.dma_start(out=Hd[0:64], in_=cx.sr0[l, d]), Hk + "i", writes=[Hk])
            elif n % 2 == 0:
                ew("dve", lambda e, Hd=Hd: e.tensor_scalar(out=Hd[0:64], in0=Hd[0:64], scalar1=cx.c_kap[0:64], scalar2=None, op0=ALU.mult), [Hk], [Hk])
            b = pbank()
            for cc in range(4):
                em.op("pe", lambda e, b=b, cc=cc, z=z: e.matmul(psum[b][:, cc * 128:(cc + 1) * 128], lhsT=z[:, cc, :], rhs=c.ident, start=True, stop=True), reads=[zk], writes=["ps%d" % b])
            ew("act", lambda e, b=b: e.activation(out=rk_, in_=psum[b][:], func=AF.Identity), ["ps%d" % b], ["rrk"])
            b = pbank()
            for cc in range(2):
                em.op("pe", lambda e, b=b, cc=cc, z=z: e.matmul(psum[b][:, cc * 128:(cc + 1) * 128], lhsT=z[:, 4 + cc, :], rhs=c.ident, start=True, stop=True), reads=[zk], writes=["ps%d" % b])
            ew("act", lambda e, b=b: e.activation(out=v_, in_=psum[b][:, 0:256], func=AF.Identity), ["ps%d" % b], ["rv"])
            ew("dve", lambda e: e.tensor_copy(out=vb, in_=v_), ["rv"], ["rvb"])
            r_ = rk_[:, 0:256]; k_ = rk_[:, 256:512]
            ew("act", lambda e, cw_=cw_: e.activation(out=tcw[0:64], in_=cw_[0:64], func=AF.Tanh), [cwk], ["rtcw"])
            b = pbank()
            em.op("pe", lambda e, b=b, d=d: e.matmul(psum[b][:, 0:256], lhsT=tcw[0:64, :], rhs=bw(d), start=True, stop=False), reads=["rtcw"], writes=["ps%d" % b])
            em.op("pe", lambda e, b=b, d=d: e.matmul(psum[b][:, 0:256], lhsT=c.ones_row[0:1, :], rhs=w0row(d), start=False, stop=True), reads=[], writes=["ps%d" % b])
            em.op("pe", lambda e, b=b, d=d, ca_=ca_: e.matmul(psum[b][:, 256:512], lhsT=ca_[0:32, :], rhs=ba(d), start=True, stop=False), reads=[cak], writes=["ps%d" % b])
            em.op("pe", lambda e, b=b, d=d: e.matmul(psum[b][:, 256:512], lhsT=c.ones_row[0:1, :], rhs=a0row(d), start=False, stop=True), reads=[], writes=["ps%d" % b])
            ew("act", lambda e, b=b: e.activation(out=sg, in_=psum[b][:, 0:256], func=AF.Sigmoid), ["ps%d" % b], ["rsg"])
            ew("act", lambda e, b=b: e.activation(out=aa, in_=psum[b][:, 256:512], func=AF.Sigmoid), ["ps%d" % b], ["raa"])
            ew("dve", lambda e: e.tensor_tensor(out=kw, in0=k_, in1=c.kkw, op=ALU.mult), ["rrk"], ["rkw"])
            ew("pool", lambda e: e.tensor_tensor(out=sq, in0=kw, in1=kw, op=ALU.mult), ["rkw"], ["rsq"])
            ew("dve", lambda e: e.tensor_reduce(out=ss, in_=h3(sq), axis=AX.X, op=ALU.add), ["rsq"], ["rss"])
            ew("act", lambda e: e.activation(out=ss, in_=ss, func=AF.Sqrt, bias=cx.c_eps, scale=1.0), ["rss"], ["rss"])
            ew("dve", lambda e: e.reciprocal(out=ss, in_=ss), ["rss"], ["rss"])
            ew("dve", lambda e: e.tensor_tensor(out=h3(kk), in0=h3(kw), in1=bc4(ss), op=ALU.mult), ["rkw", "rss"], ["rkk"])
            ew("pool", lambda e: e.tensor_tensor(out=t1, in0=aa, in1=c.ka, op=ALU.mult), ["raa"], ["rt1"])
            ew("pool", lambda e: e.tensor_tensor(out=t1, in0=t1, in1=c.oka, op=ALU.add), ["rt1"], ["rt1"])
            ew("pool", lambda e: e.tensor_tensor(out=kdir, in0=k_, in1=t1, op=ALU.mult), ["rrk", "rt1"], ["rkdir"])
            ew("dve", lambda e: e.tensor_tensor(out=bb, in0=kk, in1=aa, op=ALU.mult), ["rkk", "raa"], ["rbb"])
            b = pbank()
            em.op("pe", lambda e, b=b, d=d: e.matmul(psum[b][:, 0:256], lhsT=c.maskc[:, d * 4 + 0, 0:128], rhs=sg, start=True, stop=True), reads=["rsg"], writes=["ps%d" % b])
            em.op("pe", lambda e, b=b, d=d: e.matmul(psum[b][:, 256:512], lhsT=c.maskc[:, d * 4 + 3, 0:128], rhs=sg, start=True, stop=True), reads=["rsg"], writes=["ps%d" % b])
            ew("dve", lambda e, b=b: e.tensor_tensor(out=Ex, in0=psum[b][:, 0:256], in1=sg, op=ALU.subtract), ["ps%d" % b, "rsg"], ["rEx"])
            ew("act", lambda e: e.activation(out=E1, in_=Ex, func=AF.Exp, scale=-CD), ["rEx"], ["rE1"])
            ew("act", lambda e, b=b: e.activation(out=E2, in_=psum[b][:, 0:256], func=AF.Exp, scale=CD), ["ps%d" % b], ["rE2"])
            ew("act", lambda e, b=b: e.activation(out=E3, in_=psum[b][:, 0:256], func=AF.Exp, scale=-CD), ["ps%d" % b], ["rE3"])
            ew("act", lambda e, b=b: e.activation(out=E4, in_=psum[b][:, 256:512], func=AF.Exp, scale=-CD), ["ps%d" % b], ["rE4"])
            ew("dve", lambda e: e.tensor_tensor(out=kkt, in0=kk, in1=E1, op=ALU.mult), ["rkk", "rE1"], ["rkkt"])
            ew("pool", lambda e: e.tensor_tensor(out=bh, in0=bb, in1=E2, op=ALU.mult), ["rbb", "rE2"], ["rbh"])
            ew("dve", lambda e: e.tensor_tensor(out=kh, in0=kdir, in1=E2, op=ALU.mult), ["rkdir", "rE2"], ["rkh"])
            ew("pool", lambda e: e.tensor_tensor(out=rt, in0=r_, in1=E3, op=ALU.mult), ["rrk", "rE3"], ["rrt"])
            ew("dve", lambda e: e.tensor_tensor(out=Kbar, in0=kdir, in1=E4, op=ALU.mult), ["rkdir", "rE4"], ["rKbar"])
            ew("dve", lambda e: e.scalar_tensor_tensor(out=Bbn, in0=bb, scalar=-1.0, in1=E4, op0=ALU.mult, op1=ALU.mult), ["rbb", "rE4"], ["rBbn"])
            ew("pool", lambda e: e.tensor_tensor(out=t2, in0=r_, in1=kdir, op=ALU.mult), ["rrk", "rkdir"], ["rt2"])
            ew("pool", lambda e: e.tensor_tensor(out=t2, in0=t2, in1=c.rk, op=ALU.mult), ["rt2"], ["rt2"])
            if d == 0:
                ew("dve", lambda e, tc=tc: e.tensor_reduce(out=BON[:, tc, :], in_=h3(t2), axis=AX.X, op=ALU.add), ["rt2"], ["rBON"])
            else:
                ew("dve", lambda e: e.tensor_reduce(out=bsum, in_=h3(t2), axis=AX.X, op=ALU.add), ["rt2"], ["rbsum"])
            for ti, (src, sk, dst, dk_) in enumerate(((kkt, "rkkt", kktT, "rkktT"), (bh, "rbh", bhT, "rbhT"), (kh, "rkh", khT, "rkhT"), (rt, "rrt", rtT, "rrtT"))):
                b = pbank()
                mm4(b, blk64(src), (lambda h: c.identb) if ti == 3 else (lambda h: c.ident), [sk], npart=64)
                if ti % 2 == 0:
                    ew("act", lambda e, b=b, dst=dst: e.activation(out=dst[0:64], in_=psum[b][0:64], func=AF.Identity), ["ps%d" % b], [dk_])
                else:
                    ew("dve", lambda e, b=b, dst=dst: e.tensor_copy(out=dst[0:64], in_=psum[b][0:64]), ["ps%d" % b], [dk_])
                if ti == 1:
                    ew("act", lambda e: e.activation(out=bhTb[0:64], in_=bhT[0:64], func=AF.Identity), ["rbhT"], ["rbhTb"])
                if ti == 2:
                    ew("act", lambda e: e.activation(out=khTb[0:64], in_=khT[0:64], func=AF.Identity), ["rkhT"], ["rkhTb"])
            def amat(dst, dkey, lhsT_t, lk, rhs_t, rkey, kind, d=d):
                b = pbank()
                mm4(b, blkT(lhsT_t), blkT(rhs_t), [lk, rkey])
                ew("dve", lambda e, b=b: e.tensor_tensor(out=dst, in0=psum[b][:], in1=c.maskc[:, d * 4 + kind, :], op=ALU.mult), ["ps%d" % b], [dkey])
            amat(Mp[0], "rMp0", bhT, "rbhT", kktT, "rkktT", 2)
            amat(Np[0], "rNp0", kktT, "rkktT", bhT, "rbhT", 3)
            amat(AkkT, "rAkkT", khT, "rkhT", kktT, "rkktT", 2)
            amat(ArkT, "rArkT", khTb, "rkhTb", rtT, "rrtT", 0)
            amat(ArbT, "rArbT", bhTb, "rbhTb", rtT, "rrtT", 1)
            ew("pool", lambda e: e.tensor_tensor(out=X, in0=c.ident4, in1=Mp[0], op=ALU.subtract), ["rMp0"], ["rX"])
            cur = 0
            for jj in range(1, 7):
                nxt = 1 - cur
                if jj < 6:
                    b = pbank()
                    mm4(b, blk128(Np[cur]), blk128(Mp[cur]), ["rNp%d" % cur, "rMp%d" % cur])
                    ew("act", lambda e, b=b, nxt=nxt: e.activation(out=Mp[nxt], in_=psum[b][:], func=AF.Identity), ["ps%d" % b], ["rMp%d" % nxt])
                b = pbank()
                mm4(b, blk128(Mp[cur]), blk128(Np[cur]), ["rNp%d" % cur, "rMp%d" % cur])
                ew("dve", lambda e, b=b, nxt=nxt: e.tensor_copy(out=Np[nxt], in_=psum[b][:]), ["ps%d" % b], ["rNp%d" % nxt])
                b = pbank()
                mm4(b, blk128(Np[nxt]), blk128(X), ["rNp%d" % nxt, "rX"])
                ew("dve", lambda e, b=b: e.tensor_tensor(out=X, in0=X, in1=psum[b][:], op=ALU.add), ["ps%d" % b, "rX"], ["rX"])
                cur = nxt
            b = pbank()
            mm4(b, blk64(kkt), blk128(X), ["rkkt", "rX"], npart=64)
            ew("act", lambda e, b=b: e.activation(out=WT[0:64], in_=psum[b][0:64], func=AF.Identity), ["ps%d" % b], ["rWT"])
            b = pbank()
            mm4(b, blk128(AkkT), blk64(v_), ["rAkkT", "rv"], width=64)
            ew("act", lambda e, b=b: e.activation(out=Y0, in_=psum[b][:, 0:256], func=AF.Identity), ["ps%d" % b], ["rY0"])
            b = pbank()
            mm4(b, blk128(X), blk64(Y0), ["rX", "rY0"], width=64)
            ew("dve", lambda e, b=b: e.tensor_copy(out=U0, in_=psum[b][:, 0:256]), ["ps%d" % b], ["rU0"])
            b = pbank()
            mm4(b, blk64(sg), lambda h: c.ones_col[:, 0:1], ["rsg"], npart=64, width=1)
            ew("act", lambda e, b=b: e.activation(out=pc[0:64], in_=psum[b][0:64, 0:4], func=AF.Exp, scale=-CD), ["ps%d" % b], ["rpc"])
            Hb_ = Hb[d]; Hbk = "rHb%d" % d
            ew("act", lambda e, Hd=Hd, Hb_=Hb_: e.activation(out=Hb_[0:64], in_=Hd[0:64], func=AF.Identity), [Hk], [Hbk])
            b = pbank()
            mm4(b, blkT(WT), lambda h, Hd=Hd: Hd[0:64, h, :], ["rWT", Hk], width=64)
            ew("dve", lambda e, b=b: e.tensor_tensor(out=Uu, in0=psum[b][:, 0:256], in1=U0, op=ALU.add), ["ps%d" % b, "rU0"], ["rUu"])
            bO = pbank()
            for h in range(4):
                osl = slice(h * 64, (h + 1) * 64)
                em.op("pe", lambda e, h=h, osl=osl, Hb_=Hb_, bO=bO: e.matmul(psum[bO][:, osl], lhsT=rtT[0:64, h * 128:(h + 1) * 128], rhs=Hb_[0:64, h, :], start=True, stop=False), reads=["rrtT", Hbk], writes=["ps%d" % bO])
                em.op("pe", lambda e, h=h, osl=osl, bO=bO: e.matmul(psum[bO][:, osl], lhsT=ArkT[:, h * 128:(h + 1) * 128], rhs=vb[:, osl], start=False, stop=False), reads=["rArkT", "rvb"], writes=["ps%d" % bO])
                em.op("pe", lambda e, h=h, osl=osl, bO=bO: e.matmul(psum[bO][:, osl], lhsT=ArbT[:, h * 128:(h + 1) * 128], rhs=Uu[:, osl], start=False, stop=True), reads=["rArbT", "rUu"], writes=["ps%d" % bO])
            bH = pbank()
            for h in range(4):
                osl = slice(h * 64, (h + 1) * 64)
                em.op("pe", lambda e, h=h, osl=osl, bH=bH: e.matmul(psum[bH][0:64, osl], lhsT=Kbar[:, osl], rhs=vb[:, osl], start=True, stop=False), reads=["rKbar", "rvb"], writes=["ps%d" % bH])
                em.op("pe", lambda e, h=h, osl=osl, bH=bH: e.matmul(psum[bH][0:64, osl], lhsT=Bbn[:, osl], rhs=Uu[:, osl], start=False, stop=True), reads=["rBbn", "rUu"], writes=["ps%d" % bH])
            for h in range(4):
                ew("dve", lambda e, h=h, Hd=Hd, bH=bH: e.scalar_tensor_tensor(out=Hd[0:64, h, :], in0=Hd[0:64, h, :], scalar=pc[0:64, h:h + 1], in1=psum[bH][0:64, h * 64:(h + 1) * 64], op0=ALU.mult, op1=ALU.add),
                   [Hk, "rpc", "ps%d" % bH], [Hk])
            if n % 2 == 1:
                blk = tc // 2
                em.dma("sp", lambda e, Hd=Hd, d=d, l=l, blk=blk: e.dma_start(out=cx.osr[l, d, blk], in_=Hd[0:64]), Hk + "o", reads=[Hk], final=True)
            if d == 0:
                ew("act", lambda e, tc=tc, bO=bO: e.activation(out=OR[:, tc, :], in_=psum[bO][:, 0:256], func=AF.Identity), ["ps%d" % bO], ["rOR"])
                continue
            ew("dve", lambda e, tc=tc, bO=bO: e.tensor_tensor(out=o, in0=psum[bO][:, 0:256], in1=OR[:, tc, :], op=ALU.add), ["ps%d" % bO, "rOR"], ["ro"])
            ew("dve", lambda e: e.tensor_reduce(out=s1, in_=h3(o), axis=AX.X, op=ALU.add), ["ro"], ["rs1"])
            ew("dve", lambda e: e.tensor_scalar(out=s1, in0=s1, scalar1=1.0 / 64, scalar2=None, op0=ALU.mult), ["rs1"], ["rs1"])
            ew("dve", lambda e: e.tensor_tensor(out=h3(cen), in0=h3(o), in1=bc4(s1), op=ALU.subtract), ["ro", "rs1"], ["rcen"])
            ew("pool", lambda e: e.tensor_tensor(out=sq, in0=cen, in1=cen, op=ALU.mult), ["rcen"], ["rsq"])
            ew("dve", lambda e: e.tensor_reduce(out=s2, in_=h3(sq), axis=AX.X, op=ALU.add), ["rsq"], ["rs2"])
            ew("act", lambda e: e.activation(out=s2, in_=s2, func=AF.Sqrt, bias=cx.c_gneps, scale=1.0 / 64), ["rs2"], ["rs2"])
            ew("dve", lambda e: e.reciprocal(out=s2, in_=s2), ["rs2"], ["rs2"])
            ew("dve", lambda e: e.tensor_tensor(out=h3(cen), in0=h3(cen), in1=bc4(s2), op=ALU.mult), ["rcen", "rs2"], ["rcen"])
            ew("pool", lambda e: e.tensor_tensor(out=cen, in0=cen, in1=c.gn, op=ALU.mult), ["rcen"], ["rcen"])
            ew("dve", lambda e, tc=tc: e.tensor_tensor(out=bon, in0=BON[:, tc, :], in1=bsum, op=ALU.add), ["rBON", "rbsum"], ["rbon"])
            ew("dve", lambda e: e.tensor_tensor(out=h3(t2), in0=h3(v_), in1=bc4(bon), op=ALU.mult), ["rv", "rbon", "rt2"], ["rt2"])
            ew("pool", lambda e: e.tensor_tensor(out=cen, in0=cen, in1=t2, op=ALU.add), ["rcen", "rt2"], ["rcen"])
            ew("act", lambda e, cg_=cg_: e.activation(out=sgc[0:64], in_=cg_[0:64], func=AF.Sigmoid), [cgk], ["rsgc"])
            b = pbank()
            em.op("pe", lambda e, b=b: e.matmul(psum[b][:, 0:256], lhsT=sgc[0:64, :], rhs=bg, start=True, stop=True), reads=["rsgc"], writes=["ps%d" % b])
            ew("dve", lambda e, b=b: e.tensor_tensor(out=yc, in0=psum[b][:, 0:256], in1=cen, op=ALU.mult), ["ps%d" % b, "rcen"], ["ryc"])
            b = pbank()
            for cc in range(2):
                em.op("pe", lambda e, b=b, cc=cc: e.matmul(psum[b][:, cc * 128:(cc + 1) * 128], lhsT=yc[:, cc * 128:(cc + 1) * 128], rhs=c.identb, start=True, stop=True), reads=["ryc"], writes=["ps%d" % b])
            yT_ = ycT[par]; yk = "rycT%d" % par
            ew("act", lambda e, b=b, yT_=yT_: e.activation(out=yT_, in_=psum[b][:, 0:256].rearrange("p (c t) -> p c t", t=128), func=AF.Identity), ["ps%d" % b], [yk])
            em.dma("sp", lambda e, yT_=yT_, tc=tc: e.dma_start(out=Yv(tc), in_=yT_), yk + "w", reads=[yk])


_NC = None
_EM = None
_LAST = None


def kernel(**inputs):
    global _NC
    f = lambda k: np.ascontiguousarray(np.asarray(inputs[k], dtype=np.float32))
    x_prompt = f("x_prompt"); x_sample = f("x_sample"); c = f("c"); c_ctx = f("c_ctx")
    jobs = []
    ntok = x_sample.shape[1]
    rows = ntok // 64
    rr, cc = np.meshgrid(np.arange(rows, dtype=np.float32), np.arange(64, dtype=np.float32), indexing="ij")
    rr = rr.reshape(-1); cc = cc.reshape(-1)
    quarter = D // 4
    omega = (1.0 / (np.float32(10000.0) ** (np.arange(quarter, dtype=np.float32) / np.float32(quarter)))).astype(np.float32)
    arr = rr[:, None] * omega; acc = cc[:, None] * omega
    pos = np.concatenate([np.sin(arr), np.cos(arr), np.sin(acc), np.cos(acc)], axis=-1).astype(np.float32)
    lay8 = lambda v: np.ascontiguousarray(v.reshape(-1, 128).T)
    common = {
        "ada_w": f("ada_w"),
        "ada_b": np.ascontiguousarray(f("ada_b").reshape(DEPTH, 48, 128).transpose(0, 2, 1)),
        "n1g": np.ascontiguousarray(f("norm1_g").reshape(DEPTH, 8, 128).transpose(0, 2, 1)),
        "n2g": np.ascontiguousarray(f("norm2_g").reshape(DEPTH, 8, 128).transpose(0, 2, 1)),
        "fg": lay8(f("final_g")),
        "w_in": f("w_in"), "w_br": f("w_branch").reshape(DEPTH, D, D), "w_out": f("w_out"),
        "w1": f("mlp_w1"), "w2": f("mlp_w2"),
    }
    import ml_dtypes
    g = lambda k: f(k)
    def bcast(v):
        return np.broadcast_to(v.reshape(1, -1), (128, v.size))
    prm = np.zeros((DEPTH, 128, 1568), np.float32); rows_ = np.zeros((DEPTH, 1, 1280), np.float32)
    lr64 = np.zeros((DEPTH, 64, 768), np.float32); lr32 = np.zeros((DEPTH, 32, 512), np.float32); lr16 = np.zeros((DEPTH, 16, 256), np.float32)
    for l in range(DEPTH):
        prm[l, :, 0:2] = g("pool_scale")[l].reshape(2, 128).T
        prm[l, :, 2:8] = g("rwkv_mu")[l].reshape(6, 128).T
        for i, nm in enumerate(("rwkv_kk", "rwkv_ka", None, "rwkv_rk", "rwkv_gn", "gla_norm")):
            if nm is not None:
                prm[l, :, 32 + i * 256:32 + (i + 1) * 256] = bcast(g(nm)[l])
        rows_[l, 0, 0:512] = g("rwkv_w0")[l].reshape(-1); rows_[l, 0, 512:1024] = g("rwkv_a0")[l].reshape(-1)
        rows_[l, 0, 1024:1280] = g("gla_abias")[l].reshape(-1)
        lr64[l, :, 0:256] = g("rwkv_bw")[l, 0]; lr64[l, :, 256:512] = g("rwkv_bw")[l, 1]; lr64[l, :, 512:768] = g("rwkv_bg")[l]
        lr32[l, :, 0:256] = g("rwkv_ba")[l, 0]; lr32[l, :, 256:512] = g("rwkv_ba")[l, 1]
        lr16[l, :, 0:128] = g("gla_ab")[l, 0]; lr16[l, :, 128:256] = g("gla_ab")[l, 1]
    ii = np.arange(128)[:, None]; jj = np.arange(128)[None, :]
    maskc = np.zeros((128, 8, 512), np.float32)
    for d_ in range(2):
        incl = (ii <= jj) if d_ == 0 else (ii >= jj)
        strict = (ii < jj) if d_ == 0 else (ii > jj)
        after = (ii > jj) if d_ == 0 else (ii < jj)
        for kind, m in enumerate((incl, incl, strict, after)):
            mm_ = np.tile(m.astype(np.float32), (1, 4))
            maskc[:, d_ * 4 + kind, :] = -mm_ if kind == 1 else mm_
    ch = np.arange(256); gch = ch // 64; cidx = ch % 64
    kk_ = np.arange(64)
    csm = np.zeros((256, 512), np.float64)
    for gi in range(4):
        rws = np.where(gch == gi)[0]
        ang = 2 * np.pi * np.outer(cidx[rws], kk_) / 64.0
        csm[np.ix_(rws, gi * 64 + kk_)] = np.cos(ang) / 8.0
        csm[np.ix_(rws, 256 + gi * 64 + kk_)] = -np.sin(ang) / 8.0
    csm = csm.reshape(2, 128, 512).astype(np.float32)

    def job_consts(L):
        tt = np.arange(T)
        s_ = tt % L
        invc = np.zeros((2, 128, T), np.float32)
        for gi, win in enumerate((2, 4, 8, 16)):
            lo = np.clip(s_ - win // 2, 0, L - 1); hi = np.clip(s_ + (win - win // 2) - 1, 0, L - 1)
            invc[gi // 2, (gi % 2) * 64:(gi % 2) * 64 + 64, :] = (1.0 / (hi - lo + 1))[None, :]
        CLm = np.zeros((T, T), np.float32); SLm = np.zeros((T, T), np.float32)
        sl_ = np.arange(L)
        mmod = np.outer(sl_, sl_) % L
        cb = (np.cos(2 * np.pi * mmod / L) / np.sqrt(L)).astype(np.float32)
        sb_ = (np.sin(2 * np.pi * mmod / L) / np.sqrt(L)).astype(np.float32)
        for b_ in range(T // L):
            CLm[b_ * L:(b_ + 1) * L, b_ * L:(b_ + 1) * L] = cb
            SLm[b_ * L:(b_ + 1) * L, b_ * L:(b_ + 1) * L] = sb_
        def lay(m):
            m5 = m.reshape(2, 16, 128, 8, 512).transpose(3, 0, 2, 1, 4)
            return np.ascontiguousarray(m5).reshape(8, 2, 128, 16 * 512).astype(ml_dtypes.bfloat16)
        return invc, lay(CLm), lay(SLm)
    invc_s, CL_s, SL_s = job_consts(T)
    invc_p, CL_p, SL_p = job_consts(256)
    common.update(prm=prm, rows=rows_, lr64=lr64, lr32=lr32, lr16=lr16, pool_w=g("pool_w"),
                  ident=np.eye(128, dtype=np.float32), identb=np.eye(128, dtype=np.float32).astype(ml_dtypes.bfloat16), maskc=maskc, csm=csm)
    st_r = g("state_rwkv"); st_g = g("state_gla")
    for s in range(2):
        common_s = dict(common, invc=invc_s, CL=CL_s, SL=SL_s,
                        sr0=np.ascontiguousarray(st_r[s].transpose(0, 1, 4, 2, 3)),
                        sg0=np.ascontiguousarray(st_g[s].transpose(0, 1, 3, 2, 4)))
        jobs.append(dict(common_s, xT=np.ascontiguousarray(x_sample[s].T), pos=np.ascontiguousarray(pos.T),
                         cond=lay8(c[s]), kap=np.ones((128, 1), np.float32)))
    xp = x_prompt.reshape(-1, D)
    common = dict(common, invc=invc_p, CL=CL_p, SL=SL_p, sr0=np.zeros((DEPTH, 2, 64, 4, 64), np.float32),
                  sg0=np.zeros((DEPTH, 2, 32, 4, 64), np.float32))
    pj = dict(common, xT=np.ascontiguousarray(xp.T), pos=np.zeros((D, T), np.float32),
              cond=lay8(c_ctx), kap=np.zeros((128, 1), np.float32))
    jobs.append(pj)
    while len(jobs) < N_CORES:
        jobs.append(pj)
    if _NC is None:
        _NC = build_program()
    res = run_bass_kernel_spmd(_NC, jobs, core_ids=list(range(N_CORES)))
    r = res.results
    global _LAST
    _LAST = r
    y_sample = np.stack([np.ascontiguousarray(r[s]["yT"].T) for s in range(2)], axis=0).astype(np.float32)
    y_prompt = np.ascontiguousarray(r[2]["yT"].T).reshape(x_prompt.shape).astype(np.float32)
    B = x_prompt.shape[0]
    nsr = np.ascontiguousarray(np.asarray(r[2]["osr"]).transpose(2, 0, 1, 4, 5, 3)).astype(np.float32)
    nsg = np.ascontiguousarray(np.asarray(r[2]["osg"]).transpose(2, 0, 1, 4, 3, 5)).astype(np.float32)
    return (y_prompt, y_sample, nsr, nsg)
```
